# Optimizing a Trainium2 kernel written in Bass

```python
import math
import jax, jax.numpy as jnp
from jax import lax
import numpy as np

D_MODEL = 1024
BATCH = 16
SEQ = 2048
DEPTH = 4

GRID_W = 64
CTX_LEN = 256
A_WIDTH = D_MODEL // 4
A_GROUPS = 4
A_GDIM = A_WIDTH // A_GROUPS
CHUNK = 128
B_HEAD_DIM = 64
B_HEADS = (D_MODEL // 2) // B_HEAD_DIM
B_KV_HEADS = B_HEADS // 4
B_WIDTH = B_HEADS * B_HEAD_DIM
WINDOW = 128
BLOCK = 128
C_V_DIM = 64
C_QK_DIM = C_V_DIM // 2
C_HEADS = (D_MODEL // 4) // C_V_DIM
C_WIDTH = C_HEADS * C_V_DIM
Q_BLOCK = 128
MIX_WIDTH = A_WIDTH + B_WIDTH + C_WIDTH
PROJ_SIZES = (2 * A_WIDTH, B_WIDTH, B_KV_HEADS * B_HEAD_DIM, B_KV_HEADS * B_HEAD_DIM,
              C_HEADS * 2 * C_QK_DIM, C_HEADS * 2 * C_QK_DIM, C_WIDTH)
PROJ_WIDTH = sum(PROJ_SIZES)
D_FF = 256 * ((8 * D_MODEL // 3 + 255) // 256)
CONV_W = 3
ROPE_BASE = 10000.0
EPS = 1e-6

kernel_name = 'hybrid_parallel_heads_diffusion_trunk'


def rms_norm(x, gain=None):
    xf = x.astype(jnp.float32)
    y = xf * lax.rsqrt(jnp.mean(xf * xf, axis=-1, keepdims=True) + EPS)
    if gain is not None:
        y = y * gain.astype(jnp.float32)
    return y.astype(x.dtype)


def modulation(cond, w_ada, b_ada):
    m = jax.nn.silu(cond) @ w_ada + b_ada
    return [t[..., None, :] for t in jnp.split(m, 6, axis=-1)]


def modulate(h, shift, scale):
    return h * (1 + scale) + shift


def split_proj(z):
    idx = np.cumsum(np.array(PROJ_SIZES))[:-1].tolist()
    return jnp.split(z, idx, axis=-1)


def axial_rope(n_tokens, head_dim):
    rows = n_tokens // GRID_W
    row = jnp.repeat(jnp.arange(rows, dtype=jnp.float32), GRID_W)
    col = jnp.tile(jnp.arange(GRID_W, dtype=jnp.float32), rows)
    n_freq = head_dim // 4
    inv_freq = ROPE_BASE ** (-jnp.arange(n_freq, dtype=jnp.float32) / n_freq)
    ang = jnp.stack([row[:, None] * inv_freq, col[:, None] * inv_freq], axis=1)
    return jnp.cos(ang), jnp.sin(ang)


def apply_rope(x, cos, sin):
    b, l, h, d = x.shape
    xr = x.reshape(b, l, h, 2, 2, d // 4).astype(jnp.float32)
    x1, x2 = xr[..., 0, :], xr[..., 1, :]
    c = cos[None, :, None]
    s = sin[None, :, None]
    out = jnp.stack([x1 * c - x2 * s, x2 * c + x1 * s], axis=-2)
    return out.reshape(b, l, h, d).astype(x.dtype)


def rope_maps(t, cos, sin):
    b, l, h, m, d = t.shape
    return apply_rope(t.reshape(b, l, h * m, d), cos, sin).reshape(b, l, h, m, d)


def chunk_gmlp(z, w_s, b_s):
    b, l, _ = z.shape
    z = jax.nn.gelu(z)
    u, v = jnp.split(z, 2, axis=-1)
    v = rms_norm(v.reshape(b, l // CHUNK, CHUNK, A_GROUPS, A_GDIM))
    mixed = jnp.einsum('gpq,bnqgc->bnpgc', w_s, v) + b_s.T[None, None, :, :, None]
    return u * mixed.reshape(b, l, A_WIDTH)


def window_attn_latent(q, k, v, k_ctx, v_ctx, sink):
    b, l, hq, d = q.shape
    hkv = k.shape[2]
    g = hq // hkv
    nb = l // BLOCK
    lc = k_ctx.shape[1]
    scale = d ** -0.5
    qb = q.reshape(b, nb, BLOCK, hkv, g, d)

    def band(t):
        tb = jnp.pad(t.reshape(b, nb, BLOCK, hkv, d), ((0, 0), (1, 1), (0, 0), (0, 0), (0, 0)))
        return jnp.concatenate([tb[:, :-2], tb[:, 1:-1], tb[:, 2:]], axis=2)

    kb, vb = band(k), band(v)
    s_win = jnp.einsum('bnqhgd,bnkhd->bnhgqk', qb, kb).astype(jnp.float32) * scale
    qpos = jnp.arange(nb)[:, None, None] * BLOCK + jnp.arange(BLOCK)[None, :, None]
    kpos = (jnp.arange(nb)[:, None, None] - 1) * BLOCK + jnp.arange(3 * BLOCK)[None, None, :]
    valid = (kpos >= 0) & (kpos < l) & (jnp.abs(qpos - kpos) <= WINDOW)
    s_win = jnp.where(valid[None, :, None, None], s_win, -jnp.inf)
    s_ctx = jnp.einsum('bnqhgd,bkhd->bnhgqk', qb, k_ctx).astype(jnp.float32) * scale
    s_sink = jnp.broadcast_to(sink.astype(jnp.float32).reshape(hkv, g)[None, None, :, :, None, None],
                              s_win.shape[:-1] + (1,))
    p = jax.nn.softmax(jnp.concatenate([s_win, s_ctx, s_sink], axis=-1), axis=-1)
    kw = 3 * BLOCK
    p_win = p[..., :kw].astype(v.dtype)
    p_ctx = p[..., kw:kw + lc].astype(v.dtype)
    o = (jnp.einsum('bnhgqk,bnkhd->bnqhgd', p_win, vb)
         + jnp.einsum('bnhgqk,bkhd->bnqhgd', p_ctx, v_ctx))
    return o.reshape(b, l, hq * d)


def sink_attn_ctx(q, k, v, sink):
    b, lc, hq, d = q.shape
    hkv = k.shape[2]
    g = hq // hkv
    qg = q.reshape(b, lc, hkv, g, d)
    s = jnp.einsum('bqhgd,bkhd->bhgqk', qg, k).astype(jnp.float32) * d ** -0.5
    s_sink = jnp.broadcast_to(sink.astype(jnp.float32).reshape(1, hkv, g, 1, 1), (b, hkv, g, lc, 1))
    p = jax.nn.softmax(jnp.concatenate([s, s_sink], axis=-1), axis=-1)[..., :lc]
    o = jnp.einsum('bhgqk,bkhd->bqhgd', p.astype(v.dtype), v)
    return o.reshape(b, lc, hq * d)


def diff_lambda(lam_params, lam_init):
    lp = lam_params.astype(jnp.float32)
    return jnp.exp(jnp.sum(lp[0] * lp[1])) - jnp.exp(jnp.sum(lp[2] * lp[3])) + lam_init


def diff_attn_block(q, k, v, lam):
    s = jnp.einsum('bqhmd,bkhmd->bhmqk', q, k).astype(jnp.float32) * q.shape[-1] ** -0.5
    p = jax.nn.softmax(s, axis=-1)
    pd = p[:, :, 0] - lam * p[:, :, 1]
    return jnp.einsum('bhqk,bkhd->bqhd', pd.astype(v.dtype), v)


def diff_attn_latent(q, k, v, k_ctx, v_ctx, lam):
    b, l, h, _, dq = q.shape
    k_all = jnp.concatenate([k, k_ctx], axis=1)
    v_all = jnp.concatenate([v, v_ctx], axis=1)
    qb = jnp.moveaxis(q.reshape(b, l // Q_BLOCK, Q_BLOCK, h, 2, dq), 1, 0)
    o = lax.map(lambda qblk: diff_attn_block(qblk, k_all, v_all, lam), qb)
    return jnp.moveaxis(o, 0, 1).reshape(b, l, h, -1)


def conv_ffn(h, w_up, conv_w, conv_b, w_down):
    a = h @ w_up
    a = lax.conv_general_dilated(a, conv_w, window_strides=(1,), padding=((CONV_W // 2, CONV_W // 2),),
                                 dimension_numbers=('NWC', 'WIO', 'NWC'),
                                 feature_group_count=a.shape[-1]) + conv_b
    gate, val = jnp.split(a, 2, axis=-1)
    return (jax.nn.silu(gate) * val) @ w_down


def setup_inputs(seed: int = 0) -> dict:
    key = jax.random.key(seed)
    ks = jax.random.split(key, 24)
    nrm = jax.random.normal
    f32 = jnp.float32
    L = DEPTH
    D = D_MODEL
    return {
        'x': nrm(ks[0], (BATCH, SEQ, D), f32),
        'c': nrm(ks[1], (BATCH, D), f32),
        'ctx': nrm(ks[2], (BATCH, CTX_LEN, D), f32),
        'c_ctx': nrm(ks[3], (D,), f32),
        'w_ada': nrm(ks[4], (L, D, 6 * D), f32) * (0.25 * D ** -0.5),
        'b_ada': nrm(ks[5], (L, 6 * D), f32) * 0.01,
        'norm1_g': 1.0 + 0.02 * nrm(ks[6], (L, D), f32),
        'norm2_g': 1.0 + 0.02 * nrm(ks[7], (L, D), f32),
        'w_in': nrm(ks[8], (L, D, PROJ_WIDTH), f32) * D ** -0.5,
        'a_ws': nrm(ks[9], (L, A_GROUPS, CHUNK, CHUNK), f32) * CHUNK ** -0.5,
        'a_bs': 1.0 + 0.02 * nrm(ks[10], (L, A_GROUPS, CHUNK), f32),
        'b_qnorm': 1.0 + 0.02 * nrm(ks[11], (L, B_HEAD_DIM), f32),
        'b_knorm': 1.0 + 0.02 * nrm(ks[12], (L, B_HEAD_DIM), f32),
        'b_sink': 0.5 * nrm(ks[13], (L, B_HEADS), f32),
        'c_qnorm': 1.0 + 0.02 * nrm(ks[14], (L, C_QK_DIM), f32),
        'c_knorm': 1.0 + 0.02 * nrm(ks[15], (L, C_QK_DIM), f32),
        'c_lam': 0.1 * nrm(ks[16], (L, 4, C_QK_DIM), f32),
        'c_subln': 1.0 + 0.02 * nrm(ks[17], (L, C_V_DIM), f32),
        'w_out': nrm(ks[18], (L, MIX_WIDTH, D), f32) * MIX_WIDTH ** -0.5,
        'w_up': nrm(ks[19], (L, D, 2 * D_FF), f32) * D ** -0.5,
        'conv_w': nrm(ks[20], (L, CONV_W, 1, 2 * D_FF), f32) * CONV_W ** -0.5,
        'conv_b': 0.01 * nrm(ks[21], (L, 2 * D_FF), f32),
        'w_down': nrm(ks[22], (L, D_FF, D), f32) * D_FF ** -0.5,
    }


def reference(x, c, ctx, c_ctx, w_ada, b_ada, norm1_g, norm2_g, w_in, a_ws, a_bs,
              b_qnorm, b_knorm, b_sink, c_qnorm, c_knorm, c_lam, c_subln,
              w_out, w_up, conv_w, conv_b, w_down):
    b, l, _ = x.shape
    lc = ctx.shape[1]
    cos_b, sin_b = axial_rope(l, B_HEAD_DIM)
    cos_c, sin_c = axial_rope(l, C_QK_DIM)
    for i in range(DEPTH):
        last = i == DEPTH - 1
        lam_init = 0.8 - 0.6 * math.exp(-0.3 * i)
        sh1, sc1, g1, sh2, sc2, g2 = modulation(c, w_ada[i], b_ada[i])
        csh1, csc1, cg1, csh2, csc2, cg2 = modulation(c_ctx, w_ada[i], b_ada[i])

        h = modulate(rms_norm(x, norm1_g[i]), sh1, sc1)
        hc = modulate(rms_norm(ctx, norm1_g[i]), csh1, csc1)
        a_z, bq, bk, bv, cq, ck, cv = split_proj(h @ w_in[i])
        a_zc, bqc, bkc, bvc, cqc, ckc, cvc = split_proj(hc @ w_in[i])

        k_bc = rms_norm(bkc.reshape(b, lc, B_KV_HEADS, B_HEAD_DIM), b_knorm[i])
        v_bc = bvc.reshape(b, lc, B_KV_HEADS, B_HEAD_DIM)
        k_cc = rms_norm(ckc.reshape(b, lc, C_HEADS, 2, C_QK_DIM), c_knorm[i])
        v_cc = cvc.reshape(b, lc, C_HEADS, C_V_DIM)
        lam = diff_lambda(c_lam[i], lam_init)

        out_a = chunk_gmlp(a_z, a_ws[i], a_bs[i])
        q_b = apply_rope(rms_norm(bq.reshape(b, l, B_HEADS, B_HEAD_DIM), b_qnorm[i]), cos_b, sin_b)
        k_b = apply_rope(rms_norm(bk.reshape(b, l, B_KV_HEADS, B_HEAD_DIM), b_knorm[i]), cos_b, sin_b)
        v_b = bv.reshape(b, l, B_KV_HEADS, B_HEAD_DIM)
        out_b = window_attn_latent(q_b, k_b, v_b, k_bc, v_bc, b_sink[i])
        q_c = rope_maps(rms_norm(cq.reshape(b, l, C_HEADS, 2, C_QK_DIM), c_qnorm[i]), cos_c, sin_c)
        k_c = rope_maps(rms_norm(ck.reshape(b, l, C_HEADS, 2, C_QK_DIM), c_knorm[i]), cos_c, sin_c)
        v_c = cv.reshape(b, l, C_HEADS, C_V_DIM)
        o_c = diff_attn_latent(q_c, k_c, v_c, k_cc, v_cc, lam)
        out_c = (rms_norm(o_c, c_subln[i]) * (1 - lam_init)).reshape(b, l, C_WIDTH)

        x = x + g1 * (jnp.concatenate([out_a, out_b, out_c], axis=-1) @ w_out[i])
        x = x + g2 * conv_ffn(modulate(rms_norm(x, norm2_g[i]), sh2, sc2),
                              w_up[i], conv_w[i], conv_b[i], w_down[i])

        if not last:
            out_ac = chunk_gmlp(a_zc, a_ws[i], a_bs[i])
            q_bc = rms_norm(bqc.reshape(b, lc, B_HEADS, B_HEAD_DIM), b_qnorm[i])
            out_bc = sink_attn_ctx(q_bc, k_bc, v_bc, b_sink[i])
            q_cc = rms_norm(cqc.reshape(b, lc, C_HEADS, 2, C_QK_DIM), c_qnorm[i])
            o_cc = diff_attn_block(q_cc, k_cc, v_cc, lam)
            out_cc = (rms_norm(o_cc, c_subln[i]) * (1 - lam_init)).reshape(b, lc, C_WIDTH)
            ctx = ctx + cg1 * (jnp.concatenate([out_ac, out_bc, out_cc], axis=-1) @ w_out[i])
            ctx = ctx + cg2 * conv_ffn(modulate(rms_norm(ctx, norm2_g[i]), csh2, csc2),
                                       w_up[i], conv_w[i], conv_b[i], w_down[i])
    return x
```

```python
import math
from contextlib import ExitStack

import numpy as np
import concourse.bass as bass
import concourse.mybir as mybir
from concourse.bass_utils import run_bass_kernel_spmd

F32 = mybir.dt.float32
BF16 = mybir.dt.bfloat16
AF = mybir.ActivationFunctionType
ALU = mybir.AluOpType
AX = mybir.AxisListType

L_ALL = 4
D = 1024
SEQ = 2048
LC = 256
NT = 18
NTOK = NT * 128
DFF = 2816
NCH = 22
EPS = 1e-6
GRID_W = 64
N_CORES = 8
NB = 2


class Res:
    __slots__ = ("name", "lw", "rd", "excl")

    def __init__(self, name, excl=False):
        self.name = name
        self.lw = None
        self.rd = {}
        self.excl = excl


class DmaGroup:
    __slots__ = ("sem", "cnt", "name")

    def __init__(self, sem, name):
        self.sem = sem
        self.cnt = 0
        self.name = name


class Sched:
    ENG = ("pe", "act", "dve", "pool", "sp")

    def __init__(self, nc, stack):
        self.nc = nc
        self.stack = stack
        self.eng = {"pe": nc.tensor, "act": nc.scalar, "dve": nc.vector,
                    "pool": nc.gpsimd, "sp": nc.sync}
        self.sem = {e: stack.enter_context(nc.semaphore("sem_" + e)) for e in self.ENG}
        self.cnt = {e: 0 for e in self.ENG}
        self.seen = {e: {} for e in self.ENG}
        self.nwait = 0
        self.ninst = 0
        self.groups = []

    def group(self, name):
        sem = self.stack.enter_context(self.nc.semaphore("dg_" + name))
        g = DmaGroup(sem, name)
        self.groups.append(g)
        return g

    def finish(self):
        for g in self.groups:
            if g.cnt:
                self.nc.sync.wait_ge(g.sem, g.cnt)

    def _deps(self, e, reads, writes):
        need = {}

        def add(ev, raw):
            if ev is None:
                return
            kind, src, count = ev
            if kind == "eng" and src == e:
                if e in ("pe", "sp"):
                    return
            key = (kind, src)
            if need.get(key, (None, 0))[1] < count:
                need[key] = (ev, count)

        for r in reads:
            add(r.lw, True)
            if r.excl:
                for k2, ev in r.rd.items():
                    if k2 != ("eng", e):
                        add(ev, False)
        for w in writes:
            add(w.lw, False)
            for ev in w.rd.values():
                add(ev, False)
        out = []
        for key, (ev, count) in need.items():
            if self.seen[e].get(key, 0) >= count:
                continue
            out.append((key, ev, count))
        return out

    def _emit_waits(self, e, deps):
        eng = self.eng[e]
        for key, ev, count in deps:
            kind, src, _ = ev
            if kind == "eng":
                assert count <= self.cnt[src], f"wait on un-inc'd instr {src} {count}>{self.cnt[src]}"
                sem = self.sem[src]
            else:
                sem = src.sem
            eng.wait_ge(sem, count)
            self.nwait += 1
            self.seen[e][key] = count

    def _mark(self, ev, key, reads, writes):
        for r in reads:
            r.rd[key] = ev
        for w in writes:
            w.lw = ev
            w.rd = {}

    def op(self, e, fn, reads=(), writes=(), inc=True):
        import os as _os
        self.nops = getattr(self, "nops", 0) + 1
        if self.nops > int(_os.environ.get("P1_OPLIMIT", 10 ** 9)):
            if not inc:
                return None
            return None
        self._emit_waits(e, self._deps(e, reads, writes))
        ins = fn()
        self.ninst += 1
        if inc:
            self.cnt[e] += 1
            ins.then_inc(self.sem[e], 1)
            ev = ("eng", e, self.cnt[e])
        else:
            ev = ("eng", e, self.cnt[e] + 1)
        self._mark(ev, ("eng", e), reads, writes)
        return ins

    def dma(self, q, grp, out, in_, reads=(), writes=(), **kw):
        self._emit_waits(q, self._deps(q, reads, writes))
        ins = self.eng[q].dma_start(out=out, in_=in_, **kw)
        grp.cnt += 16
        ins.then_inc(grp.sem, 16)
        self.ninst += 1
        ev = ("dma", grp, grp.cnt)
        self._mark(ev, ("dma", grp), reads, writes)
        return ins

    def dma_batch(self, q, grp, items):
        allr, allw = [], []
        for it in items:
            allr += list(it.get("reads", ()))
            allw += list(it.get("writes", ()))
        self._emit_waits(q, self._deps(q, allr, allw))
        for it in items:
            ins = self.eng[q].dma_start(out=it["out"], in_=it["in_"], **it.get("kw", {}))
            grp.cnt += 16
            ins.then_inc(grp.sem, 16)
            self.ninst += 1
        ev = ("dma", grp, grp.cnt)
        self._mark(ev, ("dma", grp), allr, allw)

    def barrier(self):
        for e in self.ENG:
            for f in self.ENG:
                if self.cnt[f] == 0 or (f == e and e in ("pe", "sp")):
                    continue
                key = ("eng", f)
                if self.seen[e].get(key, 0) >= self.cnt[f]:
                    continue
                self.eng[e].wait_ge(self.sem[f], self.cnt[f])
                self.seen[e][key] = self.cnt[f]
                self.nwait += 1


class Ring:
    def __init__(self, items):
        self.items = list(items)
        self.i = 0

    def next(self):
        it = self.items[self.i % len(self.items)]
        self.i += 1
        return it


def pipeline(n, stages):
    mx = max(s for _, s in stages)
    for step in range(n + mx):
        for fn, sk in stages:
            i = step - sk
            if 0 <= i < n:
                fn(i)


def build_nc(depth=L_ALL, nb=NB, dbg=None, stop=None):
    nc = bass.Bass("TRN2", target_bir_lowering=False)

    def din(name, shape, dt=F32):
        return nc.dram_tensor(name, list(shape), dt, kind="ExternalInput").ap()

    x_d = din("x", [nb, SEQ, D])
    ctx_d = din("ctx", [nb, LC, D])
    crow_d = din("crow", [3, D])
    wada_d = din("w_ada_r", [L_ALL, 12, 128, 8, 512])
    bada_d = din("b_ada", [L_ALL, 6 * D])
    gn_d = din("gn", [L_ALL, 2, D])
    win_d = din("w_in", [L_ALL, D, 2048])
    wout_d = din("w_out", [L_ALL, D, D])
    wup_d = din("w_up_r", [L_ALL, NCH, 128, 8, 256])
    wdn_d = din("w_down", [L_ALL, DFF, D])
    awsT_d = din("a_wsT", [L_ALL, 128, 4, 128])
    abs_d = din("a_bs", [L_ALL, 4, 128])
    smallg_d = din("smallg", [L_ALL, 256])
    sink_d = din("b_sink", [1, L_ALL * 8])
    clam_d = din("c_lam", [1, L_ALL * 128])
    sublnT_d = din("sublnT", [64, L_ALL])
    cw_d = din("cw_r", [128, L_ALL, 3, 44])
    cb_d = din("cb_r", [128, L_ALL, 44])
    ident_d = din("ident", [128, 128])
    maskL_d = din("maskL", [128, 128])
    maskR_d = din("maskR", [128, 128])
    rbc_d = din("ropeB_C2", [128, 16, 64])
    rbs_d = din("ropeB_S", [128, 16, 2, 32])
    rcc_d = din("ropeC_C2", [128, 16, 32])
    rcs_d = din("ropeC_S", [128, 16, 2, 16])
    y_d = nc.dram_tensor("y", [nb, SEQ, D], F32, kind="ExternalOutput").ap()
    ctxs_d = nc.dram_tensor("ctx_s", [nb, LC, D], F32).ap()
    mod_d = nc.dram_tensor("mod_s", [3, L_ALL, 6, D], F32).ap()

    with ExitStack() as st:
        S = Sched(nc, st)

        uid = [0]

        def T(name, shape, dt, stack=st):
            uid[0] += 1
            return stack.enter_context(nc.sbuf_tensor(f"sb{uid[0]}_{name}", list(shape), dt))

        PB = [st.enter_context(nc.psum_tensor(f"pb{i}", [128, 512], F32)) for i in range(8)]
        RB = [Res(f"pb{i}", excl=True) for i in range(8)]

        def bank_bf(i):
            return PB[i][:].bitcast(BF16)

        g_const = S.group("const")
        g_xin = [S.group(f"xin{i}") for i in range(2)]
        g_xout = [S.group(f"xout{i}") for i in range(2)]
        g_modb = [S.group(f"modb{i}") for i in range(4)]
        g_w = [S.group(f"w{i}") for i in range(6)]
        g_wdn = S.group("wdn")
        g_pre = [S.group(f"pre{i}") for i in range(6)]
        g_zst = [S.group(f"zst{i}") for i in range(2)]
        g_mod = S.group("modw")
        g_dbg = S.group("dbg")

        dbg_groups = []

        def dbg_wait():
            S.finish()

        def dump(name, ap, res):
            if dbg is None:
                return
            d = nc.dram_tensor("dbg_" + name, list(ap.shape), ap.dtype, kind="ExternalOutput").ap()
            gg = S.group("dbg_" + name)
            dbg_groups.append(gg)
            S.dma("sp", gg, d, ap, reads=[res])
            dbg.append(name)

        ident_f = T("ident_f", [128, 128], F32); r_ident = Res("ident")
        ident_b = T("ident_b", [128, 128], BF16)
        ones_f = T("ones_f", [128, 128], F32)
        onesmean = T("onesmean", [128, 128], F32)
        mask_f = T("mask_f", [128, 2, 128], F32)
        mask_b = T("mask_b", [128, 2, 128], BF16)
        ropeB_C = T("ropeB_C", [128, 16, 64], F32)
        ropeB_S = T("ropeB_S", [128, 16, 2, 32], F32)
        ropeC_C = T("ropeC_C", [128, 16, 32], F32)
        ropeC_S = T("ropeC_S", [128, 16, 2, 16], F32)
        smallg = T("smallg", [128, L_ALL, 256], F32)
        biasT = T("biasT", [128, L_ALL, 2, 128], F32)
        cw = T("cw", [128, L_ALL, 3, 44], F32)
        cb = T("cb", [128, L_ALL, 44], F32)
        esink = T("esink", [128, L_ALL * 8], F32)
        clam = T("clam", [128, L_ALL, 4, 32], F32)
        lamt = T("lamt", [128, L_ALL, 2, 32], F32)
        lam2 = T("lam2", [128, L_ALL, 2], F32)
        neglam = T("neglam", [128, L_ALL], F32)
        gsub = T("gsub", [128, L_ALL], F32)
        r_const = Res("const")

        items = [
            dict(out=ident_f[:], in_=ident_d),
            dict(out=mask_f[:, 0, :], in_=maskL_d),
            dict(out=mask_f[:, 1, :], in_=maskR_d),
            dict(out=ropeB_C[:], in_=rbc_d),
            dict(out=ropeB_S[:], in_=rbs_d),
            dict(out=ropeC_C[:], in_=rcc_d),
            dict(out=ropeC_S[:], in_=rcs_d),
            dict(out=smallg[:].rearrange("p l c -> p (l c)"),
                 in_=smallg_d.rearrange("l c -> (l c)").partition_broadcast(128)),
            dict(out=cw[:], in_=cw_d),
            dict(out=cb[:], in_=cb_d),
            dict(out=esink[:], in_=sink_d[0, :].partition_broadcast(128)),
            dict(out=clam[:].rearrange("p l a c -> p (l a c)"), in_=clam_d[0, :].partition_broadcast(128)),
            dict(out=gsub[0:64, :], in_=sublnT_d),
            dict(out=gsub[64:128, :], in_=sublnT_d),
        ]
        for l in range(L_ALL):
            for g in range(4):
                items.append(dict(out=biasT[(g % 2) * 64:(g % 2) * 64 + 64, l, g // 2, :],
                                  in_=abs_d[l, g, :].partition_broadcast(64)))
        for it in items:
            it["writes"] = [r_const]
        S.dma_batch("sp", g_const, items)

        S.op("dve", lambda: nc.vector.tensor_copy(ident_b[:], ident_f[:]), reads=[r_const], writes=[r_ident])
        S.op("dve", lambda: nc.vector.tensor_copy(mask_b[:], mask_f[:]), reads=[r_const], writes=[r_ident])
        S.op("dve", lambda: nc.vector.memset(ones_f[:], 1.0), writes=[r_ident])
        S.op("dve", lambda: nc.vector.memset(onesmean[:], 1.0 / 64.0), writes=[r_ident])
        S.op("act", lambda: nc.scalar.activation(esink[:], esink[:], AF.Exp), reads=[r_const], writes=[r_const])
        S.op("dve", lambda: nc.vector.tensor_tensor(lamt[:], clam[:, :, 0:4:2, :], clam[:, :, 1:4:2, :], ALU.mult),
             reads=[r_const], writes=[r_const])
        S.op("dve", lambda: nc.vector.tensor_reduce(lam2[:], lamt[:], AX.X, ALU.add), reads=[r_const], writes=[r_const])
        S.op("act", lambda: nc.scalar.activation(lam2[:], lam2[:], AF.Exp), reads=[r_const], writes=[r_const])
        S.op("dve", lambda: nc.vector.tensor_tensor(neglam[:], lam2[:, :, 1], lam2[:, :, 0], ALU.subtract),
             reads=[r_const], writes=[r_const])
        for l in range(L_ALL):
            lam_init = 0.8 - 0.6 * math.exp(-0.3 * l)
            S.op("dve", lambda l=l, li=lam_init: nc.vector.tensor_scalar(
                neglam[:, l:l + 1], neglam[:, l:l + 1], -li, None, ALU.add), reads=[r_const], writes=[r_const])
            S.op("dve", lambda l=l, li=lam_init: nc.vector.tensor_scalar(
                gsub[:, l:l + 1], gsub[:, l:l + 1], 1.0 - li, None, ALU.mult), reads=[r_const], writes=[r_const])

        if stop == "const":
            dump("neglam", neglam[:], r_const); dump("gsub", gsub[:], r_const); dump("esink", esink[:], r_const)
            dump("biasT", biasT[:], r_const); dump("mask_b", mask_b[:], r_ident)
            dbg_wait()
            return nc
        with ExitStack() as pp:
            crow = T("crow_sb", [3, D], F32, pp); r_crow = Res("crow")
            scT = T("scT", [128, 8, 3], F32, pp); r_scT = Res("scT")
            rows = T("rows", [3, 6 * D], F32, pp); r_rows = Res("rows")
            bada = T("bada", [3, 6 * D], F32, pp); r_bada = Res("bada")
            gnb = T("gnb", [3, 2, D], F32, pp); r_gnb = Res("gnb")
            wslots = [T(f"wada{i}", [128, 8, 512], F32, pp) for i in range(3)]
            r_wslots = [Res(f"wada{i}") for i in range(3)]
            g_ws = g_pre[0:3]
            g_pp = g_pre[3]
            S.dma("sp", g_pp, crow[:], crow_d, writes=[r_crow])
            S.op("act", lambda: nc.scalar.activation(crow[:], crow[:], AF.Silu), reads=[r_crow], writes=[r_crow])
            for c in range(8):
                S.op("pe", lambda c=c: nc.tensor.transpose(PB[0][:, c * 3:c * 3 + 3], crow[0:3, c * 128:(c + 1) * 128],
                                                           ident_f[0:3, 0:3]),
                     reads=[r_crow, r_const], writes=[RB[0]], inc=(c == 7))
            S.op("dve", lambda: nc.vector.tensor_copy(scT[:].rearrange("p c r -> p (c r)"), PB[0][:, 0:24]),
                 reads=[RB[0]], writes=[r_scT])
            pring = Ring([1, 2, 3])
            k = 0
            for l in range(depth):
                S.dma("sp", g_pre[4], bada[:], bada_d[l, :].partition_broadcast(3), writes=[r_bada])
                S.dma("sp", g_pre[5], gnb[:].rearrange("p a d -> p (a d)"),
                      gn_d[l].rearrange("a d -> (a d)").partition_broadcast(3), writes=[r_gnb])
                for n in range(12):
                    si = k % 3
                    k += 1
                    S.dma("sp", g_ws[si], wslots[si][:],
                          wada_d[l, n],
                          writes=[r_wslots[si]])
                    bi = pring.next()
                    for kc in range(8):
                        S.op("pe", lambda kc=kc, si=si, bi=bi: nc.tensor.matmul(
                            PB[bi][0:3, :], lhsT=scT[:, kc, :], rhs=wslots[si][:, kc, :],
                            start=(kc == 0), stop=(kc == 7)),
                            reads=[r_scT, r_wslots[si]], writes=[RB[bi]], inc=(kc == 7))
                    S.op("dve", lambda n=n, bi=bi: nc.vector.tensor_tensor(
                        rows[:, n * 512:(n + 1) * 512], PB[bi][0:3, :], bada[:, n * 512:(n + 1) * 512], ALU.add),
                        reads=[RB[bi], r_bada], writes=[r_rows])
                for a, kidx in ((0, 1), (1, 4)):
                    S.op("dve", lambda a=a, kidx=kidx: nc.vector.scalar_tensor_tensor(
                        rows[:, kidx * D:(kidx + 1) * D], rows[:, kidx * D:(kidx + 1) * D], 1.0, gnb[:, a, :],
                        ALU.add, ALU.mult), reads=[r_rows, r_gnb], writes=[r_rows])
                S.dma("sp", g_mod, mod_d[:, l].rearrange("r k d -> r (k d)"), rows[:], reads=[r_rows], writes=[])
            r_mod = Res("mod_d")
            r_mod.lw = ("dma", g_mod, g_mod.cnt)
            S.barrier()

        if stop == "prepass":
            if dbg is not None:
                d = nc.dram_tensor("dbg_mod", [3, 1, 6, D], F32, kind="ExternalOutput").ap()
                S.dma("sp", g_dbg, d, mod_d[:, 0:1], reads=[r_mod])
                dbg.append("mod")
            dbg_wait()
            return nc
        xin = [T(f"xin{i}", [128, D], F32) for i in range(2)]
        r_xin = [Res(f"xin{i}") for i in range(2)]
        xin_ring = Ring(range(2))
        xout = [T(f"xout{i}", [128, D], F32) for i in range(2)]
        r_xout = [Res(f"xout{i}") for i in range(2)]
        xout_ring = Ring(range(2))
        modb = [T(f"modb{i}", [128, D], F32) for i in range(4)]
        r_modb = [Res(f"modb{i}") for i in range(4)]
        stat = T("stat", [128, 64], F32)
        r_dx = [[Res(f"dx{b}_{t}") for t in range(NT)] for b in range(nb)]

        def x_src(b, l, t):
            if t < 16:
                base = x_d if l == 0 else y_d
                return base[b, t * 128:(t + 1) * 128, :]
            base = ctx_d if l == 0 else ctxs_d
            return base[b, (t - 16) * 128:(t - 15) * 128, :]

        def x_dst(b, t):
            if t < 16:
                return y_d[b, t * 128:(t + 1) * 128, :]
            return ctxs_d[b, (t - 16) * 128:(t - 15) * 128, :]

        def load_x(b, lsrc, t):
            i = xin_ring.next()
            S.dma("sp", g_xin[i], xin[i][:], x_src(b, lsrc, t), reads=[r_dx[b][t]], writes=[r_xin[i]])
            return i

        def load_modb(i, row, l, kind):
            S.dma("sp", g_modb[i], modb[i][:], mod_d[row, l, kind, :].partition_broadcast(128),
                  reads=[r_mod], writes=[r_modb[i]])

        ev_toggle = [0]

        def evac(out, in_, reads, writes):
            ev_toggle[0] ^= 1
            if ev_toggle[0]:
                S.op("act", lambda: nc.scalar.copy(out, in_), reads=reads, writes=writes)
            else:
                S.op("dve", lambda: nc.vector.tensor_copy(out, in_), reads=reads, writes=writes)

        def norm_tile(xi, mi, shi, hb, r_hb, t1, r_t1, ss_col):
            ss = stat[:, ss_col:ss_col + 1]
            rt = stat[:, ss_col + 1:ss_col + 2]
            r_st = r_stat[ss_col // 2]
            import os as _os
            _k = int(_os.environ.get("P1_S0", 99))
            if _k < 2:
                return
            S.op("act", lambda: nc.scalar.activation(t1, xin[xi][:], AF.Square),
                 reads=[r_xin[xi]], writes=[r_t1])
            S.op("dve", lambda: nc.vector.tensor_reduce(ss, t1, AX.X, ALU.add), reads=[r_t1], writes=[r_st])
            if _k < 3:
                return
            S.op("act", lambda: nc.scalar.activation(rt, ss, AF.Sqrt, scale=1.0 / D, bias=eps_t[:, 0:1]),
                 reads=[r_st, r_ident], writes=[r_st])
            if _k < 4:
                return
            S.op("dve", lambda: nc.vector.reciprocal(rt, rt), reads=[r_st], writes=[r_st])
            if _k < 5:
                return
            S.op("dve", lambda: nc.vector.scalar_tensor_tensor(t1, xin[xi][:], rt, modb[mi][:], ALU.mult, ALU.mult),
                 reads=[r_xin[xi], r_st, r_modb[mi]], writes=[r_t1])
            if _k < 6:
                return
            S.op("dve", lambda: nc.vector.tensor_tensor(hb, t1, modb[shi][:], ALU.add),
                 reads=[r_t1, r_modb[shi]], writes=[r_hb])

        r_stat = [Res(f"stat{i}") for i in range(8)]
        eps_t = T("eps_t", [128, 1], F32)
        S.op("dve", lambda: nc.vector.memset(eps_t[:], EPS), writes=[r_ident])

        def transpose8(hb, r_hb, bi, dst, r_dst):
            pv = bank_bf(bi)
            import os as _os
            _k = int(_os.environ.get("P1_S0", 99))
            if _k < 7:
                return
            for c in range(8):
                S.op("pe", lambda c=c: nc.tensor.transpose(pv[:, c * 128:(c + 1) * 128], hb[:, c * 128:(c + 1) * 128],
                                                           ident_b[:]),
                     reads=[r_hb, r_ident], writes=[RB[bi]], inc=(c == 7))
            if _k < 8:
                return
            evac(dst, pv[:, 0:1024].rearrange("p (c t) -> p c t", c=8), [RB[bi]], [r_dst])

        for b in range(nb):
            for l in range(depth):
                last = (l == depth - 1)
                ntq = 16 if last else 18
                lsrc = l
                load_modb(0, b, l, 1)
                load_modb(1, b, l, 0)
                load_modb(2, 2, l, 1)
                load_modb(3, 2, l, 0)
                with ExitStack() as s1:
                    kbT = T("kbT", [128, NTOK], BF16, s1); r_kbT = [Res(f"kbT{t}") for t in range(NT)]
                    vb = T("vb", [128, NT, 2, 65], BF16, s1); r_vb = [Res(f"vb{t}") for t in range(NT)]
                    kcT = T("kcT", [128, 2, NTOK], BF16, s1); r_kcT = [Res(f"kcT{t}") for t in range(NT)]
                    vc = T("vc", [128, NT, 4, 65], BF16, s1); r_vc = [Res(f"vc{t}") for t in range(NT)]
                    qbT = T("qbT", [128, 4, NTOK], BF16, s1); r_qbT = [Res(f"qbT{t}") for t in range(NT)]
                    qcT = T("qcT", [128, 2, NTOK], BF16, s1); r_qcT = [Res(f"qcT{t}") for t in range(NT)]
                    catA = T("catA", [128, 2, NTOK], BF16, s1)
                    r_catA = [Res(f"catA{t}") for t in range(NT)]
                    r_catB = [Res(f"catB{t}") for t in range(NT)]
                    r_catC = [Res(f"catC{t}") for t in range(NT)]
                    r_vones = Res("vones")
                    S.op("dve", lambda: nc.vector.memset(vb[:, :, :, 64:65], 1.0), writes=[r_vones])
                    S.op("dve", lambda: nc.vector.memset(vc[:, :, :, 64:65], 1.0), writes=[r_vones])

                    with ExitStack() as p1:
                        w_in = T("w_in", [128, 8, 2048], BF16, p1); r_win = Res("w_in")
                        awsT = T("awsT", [128, 4, 128], BF16, p1); r_aws = Res("awsT")
                        for kc in range(8):
                            S.dma("pool", g_w[0], w_in[:, kc, :], win_d[l, kc * 128:(kc + 1) * 128, :], writes=[r_win])
                        S.dma("pool", g_w[1], awsT[:], awsT_d[l], writes=[r_aws])
                        NS = 2
                        hb = [T(f"hb{i}", [128, D], BF16, p1) for i in range(NS)]; r_hb = [Res(f"hb{i}") for i in range(NS)]
                        t1 = [T("t1_0", [128, D], F32, p1)] * NS; r_t1 = [Res("t1_0")] * NS
                        hT = [T(f"hT{i}", [128, 8, 128], BF16, p1) for i in range(3)]; r_hT = [Res(f"hT{i}") for i in range(3)]
                        uT = [T(f"uT{i}", [128, 2, 128], BF16, p1) for i in range(NS)]; r_uT = [Res(f"uT{i}") for i in range(NS)]
                        gv = [T(f"gv{i}", [128, 256], F32, p1) for i in range(NS)]; r_gv = [Res(f"gv{i}") for i in range(NS)]
                        vpad = [T(f"vpad{i}", [128, 4, 128], BF16, p1) for i in range(NS)]; r_vpad = [Res(f"vpad{i}") for i in range(NS)]
                        sq = [T("sq0", [128, 1152], F32, p1)] * NS; r_sq = [Res("sq0")] * NS
                        qn = [T("qn0", [128, 1152], F32, p1)] * NS; r_qn = [Res("qn0")] * NS
                        tb = [T("tb0", [128, 1152], F32, p1)] * NS; r_tb = [Res("tb0")] * NS
                        qr = [T("qr0", [128, 1152], BF16, p1)] * NS; r_qr = [Res("qr0")] * NS
                        st2 = [T(f"st2_{i}", [128, 32], F32, p1) for i in range(NS)]; r_st2 = [Res(f"st2_{i}") for i in range(NS)]
                        ta = [T(f"ta{i}", [128, 256], F32, p1) for i in range(NS)]; r_ta = [Res(f"ta{i}") for i in range(NS)]
                        for i in range(NS):
                            S.op("dve", lambda i=i: nc.vector.memset(vpad[i][:], 0.0), writes=[r_vpad[i]])
                        order = [16, 17] + list(range(16))
                        trp_ring = Ring([0, 1, 2])
                        prj_ring = Ring([3, 4, 5, 6, 7])
                        xi_of = {}
                        banks_of = {}

                        def s0(i):
                            t = order[i]
                            isctx = t >= 16
                            xi = load_x(b, lsrc, t)
                            xi_of[i] = xi
                            sl = i % NS
                            norm_tile(xi, 2 if isctx else 0, 3 if isctx else 1, hb[sl][:], r_hb[sl], t1[sl][:], r_t1[sl], (i % 4) * 2)
                            transpose8(hb[sl], r_hb[sl], trp_ring.next(), hT[i % 3][:], r_hT[i % 3])
                            if stop == "p1dbg" and t == 0:
                                dump("xin", xin[xi][:], r_xin[xi]); dump("hb", hb[sl][:], r_hb[sl]); dump("hT", hT[i % 3][:], r_hT[i % 3])
                                dump("m1b", modb[0][:], r_modb[0]); dump("sh1b", modb[1][:], r_modb[1]); dump("w_in", w_in[:], r_win)

                        def s1f(i):
                            t = order[i]
                            isctx = t >= 16
                            full = (not isctx) or (not last)
                            h = hT[i % 3]; rh = r_hT[i % 3]
                            bk = {}
                            if full:
                                bu = prj_ring.next(); bk["u"] = bu
                                for cc in range(2):
                                    for kc in range(8):
                                        S.op("pe", lambda cc=cc, kc=kc: nc.tensor.matmul(
                                            PB[bu][:, cc * 128:(cc + 1) * 128], lhsT=w_in[:, kc, cc * 128:(cc + 1) * 128],
                                            rhs=h[:, kc, :], start=(kc == 0), stop=(kc == 7)),
                                            reads=[r_win, rh], writes=[RB[bu]], inc=(kc == 7 and cc == 1))
                                bv = prj_ring.next(); bk["v"] = bv
                                for kc in range(8):
                                    S.op("pe", lambda kc=kc: nc.tensor.matmul(
                                        PB[bv][:, 0:256], lhsT=h[:, kc, :], rhs=w_in[:, kc, 256:512],
                                        start=(kc == 0), stop=(kc == 7)), reads=[r_win, rh], writes=[RB[bv]], inc=(kc == 7))
                                bq = prj_ring.next(); bk["q"] = bq
                                for kc in range(8):
                                    S.op("pe", lambda kc=kc: nc.tensor.matmul(
                                        PB[bq][:, :], lhsT=h[:, kc, :], rhs=w_in[:, kc, 512:1024],
                                        start=(kc == 0), stop=(kc == 7)), reads=[r_win, rh], writes=[RB[bq]], inc=(kc == 7))
                            bkk = prj_ring.next(); bk["k"] = bkk
                            for kc in range(8):
                                S.op("pe", lambda kc=kc: nc.tensor.matmul(
                                    PB[bkk][:, :], lhsT=h[:, kc, :], rhs=w_in[:, kc, 1024:1536],
                                    start=(kc == 0), stop=(kc == 7)), reads=[r_win, rh], writes=[RB[bkk]], inc=(kc == 7))
                            bc = prj_ring.next(); bk["c"] = bc
                            for kc in range(8):
                                S.op("pe", lambda kc=kc: nc.tensor.matmul(
                                    PB[bc][:, :], lhsT=h[:, kc, :], rhs=w_in[:, kc, 1536:2048],
                                    start=(kc == 0), stop=(kc == 7)), reads=[r_win, rh], writes=[RB[bc]], inc=(kc == 7))
                            banks_of[i] = bk

                        def s2(i):
                            t = order[i]
                            isctx = t >= 16
                            full = (not isctx) or (not last)
                            bk = banks_of[i]
                            sl = i % NS
                            g0 = l * 256
                            if full:
                                bu, bv = bk["u"], bk["v"]
                                S.op("act", lambda: nc.scalar.activation(
                                    uT[sl][:].rearrange("p c t -> p (c t)"), PB[bu][:, 0:256], AF.Gelu_apprx_tanh),
                                    reads=[RB[bu]], writes=[r_uT[sl]])
                                S.op("act", lambda: nc.scalar.activation(gv[sl][:], PB[bv][:, 0:256], AF.Gelu_apprx_tanh),
                                     reads=[RB[bv]], writes=[r_gv[sl]])
                                S.op("dve", lambda: nc.vector.tensor_tensor(ta[sl][:], gv[sl][:], gv[sl][:], ALU.mult),
                                     reads=[r_gv[sl]], writes=[r_ta[sl]])
                                S.op("dve", lambda: nc.vector.tensor_reduce(
                                    st2[sl][:, 26:30], ta[sl][:].rearrange("p (g c) -> p g c", g=4), AX.X, ALU.add),
                                    reads=[r_ta[sl]], writes=[r_st2[sl]])
                                S.op("act", lambda: nc.scalar.activation(st2[sl][:, 26:30], st2[sl][:, 26:30], AF.Sqrt,
                                                                         scale=1.0 / 64, bias=eps_t[:, 0:1]),
                                     reads=[r_st2[sl], r_ident], writes=[r_st2[sl]])
                                S.op("dve", lambda: nc.vector.reciprocal(st2[sl][:, 26:30], st2[sl][:, 26:30]),
                                     reads=[r_st2[sl]], writes=[r_st2[sl]])
                                for par in range(2):
                                    S.op("dve", lambda par=par: nc.vector.tensor_tensor(
                                        vpad[sl][:, par:4:2, par * 64:par * 64 + 64],
                                        gv[sl][:].rearrange("p (g c) -> p g c", g=4)[:, par:4:2, :],
                                        st2[sl][:, 26 + par:30:2].unsqueeze(2).to_broadcast([128, 2, 64]), ALU.mult),
                                        reads=[r_gv[sl], r_st2[sl]], writes=[r_vpad[sl]])
                                bm = prj_ring.next()
                                for g in range(4):
                                    S.op("pe", lambda g=g: nc.tensor.matmul(
                                        PB[bm][:, (g // 2) * 128:(g // 2) * 128 + 128], lhsT=vpad[sl][:, g, :], rhs=awsT[:, g, :],
                                        start=(g % 2 == 0), stop=(g % 2 == 1)),
                                        reads=[r_vpad[sl], r_aws], writes=[RB[bm]], inc=(g == 3))
                                S.op("dve", lambda: nc.vector.tensor_tensor(
                                    ta[sl][:], PB[bm][:, 0:256], biasT[:, l].rearrange("p c t -> p (c t)"), ALU.add),
                                    reads=[RB[bm], r_const], writes=[r_ta[sl]])
                                S.op("dve", lambda: nc.vector.tensor_tensor(
                                    catA[:, 0:2, t * 128:(t + 1) * 128], ta[sl][:].rearrange("p (c t) -> p c t", c=2),
                                    uT[sl][:], ALU.mult), reads=[r_ta[sl], r_uT[sl]], writes=[r_catA[t]])
                            bkk, bc = bk["k"], bk["c"]
                            if full:
                                bq = bk["q"]
                                S.op("act", lambda: nc.scalar.activation(sq[sl][:, 0:512], PB[bq][:, :], AF.Square),
                                     reads=[RB[bq]], writes=[r_sq[sl]])
                                S.op("act", lambda: nc.scalar.activation(sq[sl][:, 640:896], PB[bkk][:, 256:512], AF.Square),
                                     reads=[RB[bkk]], writes=[r_sq[sl]])
                            else:
                                S.op("dve", lambda: nc.vector.memset(sq[sl][:, 0:512], 1.0), writes=[r_sq[sl]])
                                S.op("dve", lambda: nc.vector.memset(sq[sl][:, 640:896], 1.0), writes=[r_sq[sl]])
                            S.op("act", lambda: nc.scalar.activation(sq[sl][:, 512:640], PB[bkk][:, 0:128], AF.Square),
                                 reads=[RB[bkk]], writes=[r_sq[sl]])
                            S.op("act", lambda: nc.scalar.activation(sq[sl][:, 896:1152], PB[bc][:, 0:256], AF.Square),
                                 reads=[RB[bc]], writes=[r_sq[sl]])
                            S.op("dve", lambda: nc.vector.tensor_reduce(
                                st2[sl][:, 0:10], sq[sl][:, 0:640].rearrange("p (h d) -> p h d", d=64), AX.X, ALU.add),
                                reads=[r_sq[sl]], writes=[r_st2[sl]])
                            S.op("dve", lambda: nc.vector.tensor_reduce(
                                st2[sl][:, 10:26], sq[sl][:, 640:1152].rearrange("p (h d) -> p h d", d=32), AX.X, ALU.add),
                                reads=[r_sq[sl]], writes=[r_st2[sl]])
                            S.op("act", lambda: nc.scalar.activation(st2[sl][:, 0:10], st2[sl][:, 0:10], AF.Sqrt,
                                                                     scale=1.0 / 64, bias=eps_t[:, 0:1]),
                                 reads=[r_st2[sl], r_ident], writes=[r_st2[sl]])
                            S.op("act", lambda: nc.scalar.activation(st2[sl][:, 10:26], st2[sl][:, 10:26], AF.Sqrt,
                                                                     scale=1.0 / 32, bias=eps_t[:, 0:1]),
                                 reads=[r_st2[sl], r_ident], writes=[r_st2[sl]])
                            S.op("dve", lambda: nc.vector.reciprocal(st2[sl][:, 0:26], st2[sl][:, 0:26]),
                                 reads=[r_st2[sl]], writes=[r_st2[sl]])
                            specs = []
                            if full:
                                specs.append(("bq", PB[bk["q"]][:, :], RB[bk["q"]], 0, 512, 8, 64, 0, 0, ropeB_C, ropeB_S))
                            specs.append(("bk", PB[bkk][:, 0:128], RB[bkk], 512, 128, 2, 64, 8, 64, ropeB_C, ropeB_S))
                            if full:
                                specs.append(("cq", PB[bkk][:, 256:512], RB[bkk], 640, 256, 8, 32, 10, 128, ropeC_C, ropeC_S))
                            specs.append(("ck", PB[bc][:, 0:256], RB[bc], 896, 256, 8, 32, 18, 160, ropeC_C, ropeC_S))
                            for (nm, src, rsrc, c0, wd, nh, hd, sc0, gc0, rC, rS) in specs:
                                v3 = lambda ap, nh=nh: ap.rearrange("p (h d) -> p h d", h=nh)
                                qv = qn[sl][:, c0:c0 + wd]
                                S.op("dve", lambda src=src, qv=qv, v3=v3, sc0=sc0, nh=nh, hd=hd: nc.vector.tensor_tensor(
                                    v3(qv), v3(src), st2[sl][:, sc0:sc0 + nh].unsqueeze(2).to_broadcast([128, nh, hd]), ALU.mult),
                                    reads=[rsrc, r_st2[sl]], writes=[r_qn[sl]])
                                gain_b = smallg[:, l, gc0:gc0 + hd].unsqueeze(1).to_broadcast([128, nh, hd])
                                if isctx:
                                    if nm == "bq":
                                        outv = qr[sl][:, 0:512].rearrange("p (c hi d) -> p hi c d", c=4, hi=2)
                                        inv = qv.rearrange("p (hi c d) -> p hi c d", hi=2, c=4)
                                        gb = smallg[:, l, gc0:gc0 + hd].unsqueeze(1).unsqueeze(1).to_broadcast([128, 2, 4, hd])
                                        S.op("dve", lambda outv=outv, inv=inv, gb=gb: nc.vector.tensor_tensor(outv, inv, gb, ALU.mult),
                                             reads=[r_qn[sl], r_const], writes=[r_qr[sl]])
                                    else:
                                        S.op("dve", lambda qv=qv, v3=v3, gain_b=gain_b, c0=c0, wd=wd: nc.vector.tensor_tensor(
                                            v3(qr[sl][:, c0:c0 + wd]), v3(qv), gain_b, ALU.mult),
                                            reads=[r_qn[sl], r_const], writes=[r_qr[sl]])
                                    continue
                                S.op("dve", lambda qv=qv, v3=v3, gain_b=gain_b: nc.vector.tensor_tensor(v3(qv), v3(qv), gain_b, ALU.mult),
                                     reads=[r_qn[sl], r_const], writes=[r_qn[sl]])
                                nf = hd // 4
                                v4 = lambda ap, nf=nf: ap.rearrange("p (ha pr f) -> p ha pr f", pr=2, f=nf)
                                tbv = tb[sl][:, c0:c0 + wd]
                                Sn = rS[:, t, 0, :].rearrange("p (a f) -> p a f", a=2).unsqueeze(1).to_broadcast([128, nh, 2, nf])
                                Sp = rS[:, t, 1, :].rearrange("p (a f) -> p a f", a=2).unsqueeze(1).to_broadcast([128, nh, 2, nf])
                                v5 = lambda ap, nh=nh, nf=nf: ap.rearrange("p (h a pr f) -> p h a pr f", h=nh, a=2, pr=2)
                                S.op("dve", lambda tbv=tbv, qv=qv, v5=v5, Sn=Sn: nc.vector.tensor_tensor(
                                    v5(tbv)[:, :, :, 0, :], v5(qv)[:, :, :, 1, :], Sn, ALU.mult),
                                    reads=[r_qn[sl], r_const], writes=[r_tb[sl]])
                                S.op("dve", lambda tbv=tbv, qv=qv, v5=v5, Sp=Sp: nc.vector.tensor_tensor(
                                    v5(tbv)[:, :, :, 1, :], v5(qv)[:, :, :, 0, :], Sp, ALU.mult),
                                    reads=[r_qn[sl], r_const], writes=[r_tb[sl]])
                                Cb = rC[:, t, :].unsqueeze(1).to_broadcast([128, nh, hd])
                                S.op("dve", lambda qv=qv, v3=v3, Cb=Cb: nc.vector.tensor_tensor(v3(qv), v3(qv), Cb, ALU.mult),
                                     reads=[r_qn[sl], r_const], writes=[r_qn[sl]])
                                if nm == "bq":
                                    outv = qr[sl][:, 0:512].rearrange("p (c hi d) -> p hi c d", c=4, hi=2)
                                    a0 = qv.rearrange("p (hi c d) -> p hi c d", hi=2, c=4)
                                    a1 = tbv.rearrange("p (hi c d) -> p hi c d", hi=2, c=4)
                                    S.op("dve", lambda outv=outv, a0=a0, a1=a1: nc.vector.tensor_tensor(outv, a0, a1, ALU.add),
                                         reads=[r_qn[sl], r_tb[sl]], writes=[r_qr[sl]])
                                else:
                                    S.op("dve", lambda qv=qv, tbv=tbv, c0=c0, wd=wd: nc.vector.tensor_tensor(
                                        qr[sl][:, c0:c0 + wd], qv, tbv, ALU.add),
                                        reads=[r_qn[sl], r_tb[sl]], writes=[r_qr[sl]])
                            S.op("act", lambda: nc.scalar.copy(vb[:, t, :, 0:64], PB[bkk][:, 128:256].rearrange("p (h d) -> p h d", h=2)),
                                 reads=[RB[bkk]], writes=[r_vb[t]])
                            S.op("act", lambda: nc.scalar.copy(vc[:, t, :, 0:64], PB[bc][:, 256:512].rearrange("p (h d) -> p h d", h=4)),
                                 reads=[RB[bc]], writes=[r_vc[t]])
                            bt = trp_ring.next()
                            pv = bank_bf(bt)
                            lst = []
                            if full:
                                for c in range(4):
                                    lst.append((c, qr[sl][:, c * 128:(c + 1) * 128]))
                            lst.append((4, qr[sl][:, 512:640]))
                            for c, src in lst:
                                S.op("pe", lambda c=c, src=src: nc.tensor.transpose(pv[:, c * 128:(c + 1) * 128], src, ident_b[:]),
                                     reads=[r_qr[sl], r_ident], writes=[RB[bt]], inc=(c == 4))
                            if full:
                                evac(qbT[:, :, t * 128:(t + 1) * 128], pv[:, 0:512].rearrange("p (c t) -> p c t", c=4),
                                     [RB[bt]], [r_qbT[t]])
                            evac(kbT[:, t * 128:(t + 1) * 128], pv[:, 512:640], [RB[bt]], [r_kbT[t]])
                            bt2 = trp_ring.next()
                            pv2 = bank_bf(bt2)
                            lst = []
                            if full:
                                lst += [(0, qr[sl][:, 640:768]), (1, qr[sl][:, 768:896])]
                            lst += [(2, qr[sl][:, 896:1024]), (3, qr[sl][:, 1024:1152])]
                            for c, src in lst:
                                S.op("pe", lambda c=c, src=src: nc.tensor.transpose(pv2[:, c * 128:(c + 1) * 128], src, ident_b[:]),
                                     reads=[r_qr[sl], r_ident], writes=[RB[bt2]], inc=(c == 3))
                            if full:
                                evac(qcT[:, :, t * 128:(t + 1) * 128], pv2[:, 0:256].rearrange("p (c t) -> p c t", c=2),
                                     [RB[bt2]], [r_qcT[t]])
                            evac(kcT[:, :, t * 128:(t + 1) * 128], pv2[:, 256:512].rearrange("p (c t) -> p c t", c=2),
                                 [RB[bt2]], [r_kcT[t]])

                        import os as _os
                        _ntl = int(_os.environ.get("P1_TILES", NT))
                        _nst = int(_os.environ.get("P1_STAGES", 3))
                        pipeline(_ntl, [(s2, 2), (s1f, 1), (s0, 0)][3 - _nst:])
                        if b == 0 and l == 0:
                            print("sbuf remaining p1", nc.sbuf_bytes_remaining)
                        S.barrier()
                    if stop == "p1x":
                        print("NOPS", getattr(S, "nops", 0))
                        dump("hT0", hT[0][:], r_hT[0])
                        dbg_wait()
                        return nc
                    if stop in ("p1", "p1dbg") and dbg is not None:
                        dump("kbT", kbT[:], r_kbT[0]); dump("kcT", kcT[:], r_kcT[0]); dump("qbT", qbT[:, :, 0:ntq * 128], r_qbT[0])
                        dump("qcT", qcT[:, :, 0:ntq * 128], r_qcT[0]); dump("vb", vb[:], r_vb[0]); dump("vc", vc[:], r_vc[0])
                        dump("catA", catA[:, :, 0:ntq * 128], r_catA[0])
                        dbg_wait()
                        return nc

                    with ExitStack() as p2:
                        catBC = T("catBC", [128, 6, NTOK], BF16, p2)
                        w_out = T("w_out", [128, 8, D], BF16, p2); r_wout = Res("w_out")
                        for kc in range(8):
                            S.dma("pool", g_w[2], w_out[:, kc, :], wout_d[l, kc * 128:(kc + 1) * 128, :], writes=[r_wout])
                        NP = 6
                        pT = [T(f"pT{i}", [128, 512], BF16, p2) for i in range(NP)]; r_pT = [Res(f"pT{i}") for i in range(NP)]
                        pT_ring = Ring(range(NP))
                        zst = [T(f"zst{i}", [128, 512], BF16, p2) for i in range(2)]; r_zst = [Res(f"zst{i}") for i in range(2)]
                        zst_ring = Ring(range(2))
                        rr = [T("rr0", [128, 1024], F32, p2)] * 2; r_rr = [Res("rr0")] * 2
                        Rsb = [T("Rsb0", [128, 1024], F32, p2)] * 2; r_Rsb = [Res("Rsb0")] * 2
                        yy = [T(f"yy{i}", [128, 512], F32, p2) for i in range(2)]; r_yy = [Res(f"yy{i}") for i in range(2)]
                        y1 = [T(f"y1{i}", [128, 512], F32, p2) for i in range(2)]; r_y1 = [Res(f"y1{i}") for i in range(2)]
                        ysq = [T(f"ysq{i}", [128, 512], F32, p2) for i in range(2)]; r_ysq = [Res(f"ysq{i}") for i in range(2)]
                        rsd = [T(f"rsd{i}", [128, 512], F32, p2) for i in range(2)]; r_rsd = [Res(f"rsd{i}") for i in range(2)]

                        s_ring = Ring([0, 1, 2, 3])
                        acc_ring = Ring([4, 5])
                        R_ring = Ring([6, 7])
                        units = [(kvh, n) for n in range(ntq) for kvh in range(2)]
                        bstate = {}

                        def b_scores(ui):
                            kvh, n = units[ui]
                            if n < 16:
                                kts = []
                                if n > 0:
                                    kts.append((n - 1, 0))
                                kts.append((n, None))
                                if n < 15:
                                    kts.append((n + 1, 1))
                                kts += [(16, None), (17, None)]
                            else:
                                kts = [(16, None), (17, None)]
                            pb0 = kvh * 64
                            plist = []
                            for (kt, mk) in kts:
                                sb = s_ring.next()
                                S.op("pe", lambda sb=sb, kt=kt: nc.tensor.matmul(
                                    PB[sb][:, :], lhsT=kbT[pb0:pb0 + 64, kt * 128:(kt + 1) * 128],
                                    rhs=qbT[pb0:pb0 + 64, :, n * 128:(n + 1) * 128], start=True, stop=True),
                                    reads=[r_kbT[kt], r_qbT[n]], writes=[RB[sb]])
                                pi = pT_ring.next()
                                S.op("act", lambda sb=sb, pi=pi: nc.scalar.activation(pT[pi][:], PB[sb][:, :], AF.Exp, scale=0.125),
                                     reads=[RB[sb]], writes=[r_pT[pi]])
                                if mk is not None:
                                    S.op("dve", lambda pi=pi, mk=mk: nc.vector.tensor_tensor(
                                        pT[pi][:].rearrange("p (g q) -> p g q", g=4), pT[pi][:].rearrange("p (g q) -> p g q", g=4),
                                        mask_b[:, mk, :].unsqueeze(1).to_broadcast([128, 4, 128]), ALU.mult),
                                        reads=[r_pT[pi], r_ident], writes=[r_pT[pi]])
                                plist.append((kt, pi))
                            bstate[ui] = plist

                        def b_pv(ui):
                            kvh, n = units[ui]
                            plist = bstate.pop(ui)
                            ab = acc_ring.next()
                            nk = len(plist)
                            for j, (kt, pi) in enumerate(plist):
                                S.op("pe", lambda j=j, kt=kt, pi=pi: nc.tensor.matmul(
                                    PB[ab][0:65, :], lhsT=vb[:, kt, kvh, :], rhs=pT[pi][:], start=(j == 0), stop=(j == nk - 1)),
                                    reads=[r_vb[kt], r_vones, r_pT[pi]], writes=[RB[ab]], inc=(j == nk - 1))
                            ri = ui % 2
                            hd0 = kvh * 4
                            S.op("dve", lambda: nc.vector.tensor_tensor(
                                rr[ri][64:65, 0:512].rearrange("p (g q) -> p g q", g=4),
                                PB[ab][64:65, :].rearrange("p (g q) -> p g q", g=4),
                                esink[64:65, l * 8 + hd0:l * 8 + hd0 + 4].unsqueeze(2).to_broadcast([1, 4, 128]),
                                ALU.add), reads=[RB[ab], r_const], writes=[r_rr[ri]])
                            S.op("dve", lambda: nc.vector.reciprocal(rr[ri][64:65, 0:512], rr[ri][64:65, 0:512]),
                                 reads=[r_rr[ri]], writes=[r_rr[ri]])
                            rb = R_ring.next()
                            S.op("pe", lambda: nc.tensor.matmul(PB[rb][0:64, :], lhsT=ones_f[64:65, 0:64], rhs=rr[ri][64:65, 0:512],
                                                                start=True, stop=True),
                                 reads=[r_rr[ri], r_ident], writes=[RB[rb]])
                            S.op("act", lambda: nc.scalar.copy(Rsb[ri][0:64, 0:512], PB[rb][0:64, :]), reads=[RB[rb]], writes=[r_Rsb[ri]])
                            c0 = kvh * 2
                            v4 = lambda ap: ap.rearrange("p (g q) -> p g q", g=4)
                            S.op("dve", lambda: nc.vector.tensor_tensor(
                                catBC[0:64, c0:c0 + 2, n * 128:(n + 1) * 128], v4(PB[ab][0:64, :])[:, 0:4:2, :],
                                v4(Rsb[ri][0:64, 0:512])[:, 0:4:2, :], ALU.mult),
                                reads=[RB[ab], r_Rsb[ri]], writes=[r_catB[n]])
                            zi = zst_ring.next()
                            S.op("dve", lambda: nc.vector.tensor_tensor(
                                zst[zi][0:64, 0:256].rearrange("p (g q) -> p g q", g=2), v4(PB[ab][0:64, :])[:, 1:4:2, :],
                                v4(Rsb[ri][0:64, 0:512])[:, 1:4:2, :], ALU.mult),
                                reads=[RB[ab], r_Rsb[ri]], writes=[r_zst[zi]])
                            S.dma("sp", g_zst[zi], catBC[64:128, c0:c0 + 2, n * 128:(n + 1) * 128],
                                  zst[zi][0:64, 0:256].rearrange("p (g q) -> p g q", g=2), reads=[r_zst[zi]], writes=[r_catB[n]])

                        if b == 0 and l == 0:
                            print("sbuf remaining p2", nc.sbuf_bytes_remaining)
                        pipeline(len(units), [(b_pv, 1), (b_scores, 0)])

                        qgroups = [(g * 512, 512, list(range(NT)), list(range(4 * g, 4 * g + 4))) for g in range(4)]
                        if not last:
                            qgroups.append((2048, 256, [16, 17], [16, 17]))
                        cunits = [(h, qg) for qg in qgroups for h in range(4)]
                        s_ring = Ring([0, 1, 2])
                        scl = 32 ** -0.5
                        for ui, (h, (q0, W, kts, qtiles)) in enumerate(cunits):
                            hc = h // 2
                            accb = (3, 4)
                            nk = len(kts)
                            sbs = {}

                            def c_score(ki, h=h, q0=q0, W=W, kts=kts, qtiles=qtiles, hc=hc, sbs=sbs):
                                kt = kts[ki]
                                for m in range(2):
                                    pb = 32 * (2 * (h % 2) + m)
                                    sb = s_ring.next()
                                    kw = dict(tile_position=(96, 0)) if pb == 96 else {}
                                    S.op("pe", lambda sb=sb, pb=pb, kw=kw: nc.tensor.matmul(
                                        PB[sb][:, 0:W], lhsT=kcT[pb:pb + 32, hc, kt * 128:(kt + 1) * 128],
                                        rhs=qcT[pb:pb + 32, hc, q0:q0 + W], start=True, stop=True, **kw),
                                        reads=[r_kcT[kt]] + [r_qcT[qt] for qt in qtiles], writes=[RB[sb]])
                                    pi = pT_ring.next()
                                    S.op("act", lambda sb=sb, pi=pi: nc.scalar.activation(pT[pi][:, 0:W], PB[sb][:, 0:W], AF.Exp, scale=scl),
                                         reads=[RB[sb]], writes=[r_pT[pi]])
                                    sbs[(ki, m)] = pi

                            def c_pv(ki, h=h, W=W, kts=kts, nk=nk, sbs=sbs, accb=accb):
                                kt = kts[ki]
                                lhs = vc[:, kt, h, :]
                                for m in range(2):
                                    pi = sbs.pop((ki, m))
                                    S.op("pe", lambda m=m, pi=pi: nc.tensor.matmul(
                                        PB[accb[m]][0:65, 0:W], lhsT=lhs, rhs=pT[pi][:, 0:W], start=(ki == 0), stop=(ki == nk - 1)),
                                        reads=[r_vc[kt], r_vones, r_pT[pi]], writes=[RB[accb[m]]], inc=(ki == nk - 1))

                            pipeline(nk, [(c_score, 0), (c_pv, 1)])
                            ri = ui % 2
                            po = 0
                            pr_ = 64
                            for m in range(2):
                                S.op("dve", lambda m=m: nc.vector.reciprocal(rr[ri][pr_:pr_ + 1, m * 512:m * 512 + W],
                                                                              PB[accb[m]][pr_:pr_ + 1, 0:W]),
                                     reads=[RB[accb[m]]], writes=[r_rr[ri]])
                            S.op("dve", lambda: nc.vector.tensor_scalar(rr[ri][pr_:pr_ + 1, 512:512 + W], rr[ri][pr_:pr_ + 1, 512:512 + W],
                                                                         neglam[pr_:pr_ + 1, l:l + 1], None, ALU.mult),
                                 reads=[r_rr[ri], r_const], writes=[r_rr[ri]])
                            for m in range(2):
                                S.op("pe", lambda m=m: nc.tensor.matmul(PB[5 + m][0:64, 0:W], lhsT=ones_f[pr_:pr_ + 1, 0:64],
                                                                        rhs=rr[ri][pr_:pr_ + 1, m * 512:m * 512 + W], start=True, stop=True),
                                     reads=[r_rr[ri], r_ident], writes=[RB[5 + m]])
                                S.op("act", lambda m=m: nc.scalar.copy(Rsb[ri][po:po + 64, m * 512:m * 512 + W], PB[5 + m][po:po + 64, 0:W]),
                                     reads=[RB[5 + m]], writes=[r_Rsb[ri]])
                            S.op("dve", lambda: nc.vector.tensor_tensor(yy[ri][po:po + 64, 0:W], PB[accb[0]][po:po + 64, 0:W],
                                                                         Rsb[ri][po:po + 64, 0:W], ALU.mult),
                                 reads=[RB[accb[0]], r_Rsb[ri]], writes=[r_yy[ri]])
                            S.op("dve", lambda: nc.vector.tensor_tensor(y1[ri][po:po + 64, 0:W], PB[accb[1]][po:po + 64, 0:W],
                                                                         Rsb[ri][po:po + 64, 512:512 + W], ALU.mult),
                                 reads=[RB[accb[1]], r_Rsb[ri]], writes=[r_y1[ri]])
                            S.op("dve", lambda: nc.vector.tensor_tensor(yy[ri][po:po + 64, 0:W], yy[ri][po:po + 64, 0:W],
                                                                         y1[ri][po:po + 64, 0:W], ALU.add),
                                 reads=[r_yy[ri], r_y1[ri]], writes=[r_yy[ri]])
                            S.op("act", lambda: nc.scalar.activation(ysq[ri][po:po + 64, 0:W], yy[ri][po:po + 64, 0:W], AF.Square),
                                 reads=[r_yy[ri]], writes=[r_ysq[ri]])
                            S.op("pe", lambda: nc.tensor.matmul(PB[7][0:64, 0:W], lhsT=onesmean[po:po + 64, 0:64], rhs=ysq[ri][po:po + 64, 0:W],
                                                                start=True, stop=True),
                                 reads=[r_ysq[ri], r_ident], writes=[RB[7]])
                            S.op("act", lambda: nc.scalar.activation(rsd[ri][po:po + 64, 0:W], PB[7][po:po + 64, 0:W], AF.Sqrt,
                                                                     bias=eps_t[po:po + 64, 0:1]),
                                 reads=[RB[7], r_ident], writes=[r_rsd[ri]])
                            S.op("dve", lambda: nc.vector.reciprocal(rsd[ri][po:po + 64, 0:W], rsd[ri][po:po + 64, 0:W]),
                                 reads=[r_rsd[ri]], writes=[r_rsd[ri]])
                            if h % 2 == 0:
                                S.op("dve", lambda: nc.vector.scalar_tensor_tensor(
                                    catBC[0:64, 4 + hc, q0:q0 + W], yy[ri][0:64, 0:W], gsub[0:64, l:l + 1],
                                    rsd[ri][0:64, 0:W], ALU.mult, ALU.mult),
                                    reads=[r_yy[ri], r_rsd[ri], r_const], writes=[r_catC[qt] for qt in qtiles])
                            else:
                                zi = zst_ring.next()
                                S.op("dve", lambda: nc.vector.scalar_tensor_tensor(
                                    zst[zi][0:64, 0:W], yy[ri][0:64, 0:W], gsub[0:64, l:l + 1],
                                    rsd[ri][0:64, 0:W], ALU.mult, ALU.mult),
                                    reads=[r_yy[ri], r_rsd[ri], r_const], writes=[r_zst[zi]])
                                S.dma("sp", g_zst[zi], catBC[64:128, 4 + hc, q0:q0 + W], zst[zi][0:64, 0:W],
                                      reads=[r_zst[zi]], writes=[r_catC[qt] for qt in qtiles])

                        if stop == "p2" and dbg is not None:
                            dump("catA", catA[:, :, 0:ntq * 128], r_catA[0])
                            for t in range(ntq):
                                S._emit_waits("sp", S._deps("sp", [r_catB[t], r_catC[t], r_catA[t]], []))
                            dump("catBC", catBC[:, :, 0:ntq * 128], r_catA[0])
                            for t in range(0):
                                S._emit_waits("sp", S._deps("sp", [r_catB[t], r_catC[t], r_catA[t]], []))
                            dbg_wait()
                            return nc

                        load_modb(0, b, l, 2)
                        load_modb(2, 2, l, 2)
                        pair_ring = Ring([(0, 1), (2, 3), (4, 5), (6, 7)])
                        pstate = {}

                        def o_mm(t):
                            b0, b1 = pair_ring.next()
                            for nh, bi in enumerate((b0, b1)):
                                for kc in range(8):
                                    S.op("pe", lambda nh=nh, bi=bi, kc=kc: nc.tensor.matmul(
                                        PB[bi][:, :], lhsT=(catA[:, kc, t * 128:(t + 1) * 128] if kc < 2 else catBC[:, kc - 2, t * 128:(t + 1) * 128]),
                                        rhs=w_out[:, kc, nh * 512:(nh + 1) * 512],
                                        start=(kc == 0), stop=(kc == 7)),
                                        reads=[r_catA[t], r_catB[t], r_catC[t], r_wout], writes=[RB[bi]], inc=(kc == 7))
                            pstate[t] = (b0, b1)

                        def o_res(t):
                            b0, b1 = pstate.pop(t)
                            xi = load_x(b, lsrc, t)
                            xo = xout_ring.next()
                            gi = 0 if t < 16 else 2
                            for nh, bi in enumerate((b0, b1)):
                                S.op("dve", lambda nh=nh, bi=bi: nc.vector.tensor_tensor(
                                    xout[xo][:, nh * 512:(nh + 1) * 512], PB[bi][:, :], modb[gi][:, nh * 512:(nh + 1) * 512], ALU.mult),
                                    reads=[RB[bi], r_modb[gi]], writes=[r_xout[xo]])
                            S.op("dve", lambda: nc.vector.tensor_tensor(xout[xo][:], xout[xo][:], xin[xi][:], ALU.add),
                                 reads=[r_xout[xo], r_xin[xi]], writes=[r_xout[xo]])
                            S.dma("sp", g_xout[xo], x_dst(b, t), xout[xo][:], reads=[r_xout[xo]], writes=[r_dx[b][t]])

                        pipeline(ntq, [(o_mm, 0), (o_res, 1)])
                        S.barrier()
                if stop == "s1":
                    break

                load_modb(0, b, l, 4)
                load_modb(1, b, l, 3)
                load_modb(2, 2, l, 4)
                load_modb(3, 2, l, 3)
                ncols = ntq * 128
                with ExitStack() as s2c:
                    h2T = T("h2T", [128, 8, NTOK], BF16, s2c); r_h2T = [Res(f"h2T{t}") for t in range(NT)]
                    hid = T("hid", [128, 8, NTOK], BF16, s2c); r_hid = [Res(f"hid{j}") for j in range(8)]
                    wdn = T("wdn", [128, 8, D], BF16, s2c); r_wdn = Res("wdn")
                    wup = [T(f"wup{i}", [128, 8, 256], BF16, s2c) for i in range(3)]; r_wup = [Res(f"wup{i}") for i in range(3)]
                    tbuf = [T(f"tbuf{i}", [128, NTOK], F32, s2c) for i in range(2)]; r_tbuf = [Res(f"tbuf{i}") for i in range(2)]
                    tb_ring = Ring(range(2))
                    hb2 = [T(f"hb2_{i}", [128, D], BF16, s2c) for i in range(2)]; r_hb2 = [Res(f"hb2_{i}") for i in range(2)]
                    t12 = [T("t12_0", [128, D], F32, s2c)] * 2; r_t12 = [Res("t12_0")] * 2
                    trp_ring = Ring([0, 1, 2, 3])
                    if b == 0 and l == 0:
                        print("sbuf remaining ffn", nc.sbuf_bytes_remaining)
                    for t in range(ntq):
                        isctx = t >= 16
                        xi = load_x(b, l + 1, t)
                        sl = t % 2
                        norm_tile(xi, 2 if isctx else 0, 3 if isctx else 1, hb2[sl][:], r_hb2[sl], t12[sl][:], r_t12[sl], (t % 4) * 2)
                        transpose8(hb2[sl], r_hb2[sl], trp_ring.next(), h2T[:, :, t * 128:(t + 1) * 128], r_h2T[t])
                    blocks = [(g * 512, 512, list(range(4 * g, 4 * g + 4)), g > 0) for g in range(4)]
                    if not last:
                        blocks.append((2048, 256, [16, 17], False))
                    wk = 0
                    load_modb(0, b, l, 5)
                    load_modb(2, 2, l, 5)
                    for (j0, npart) in ((0, 8), (8, 7), (15, 7)):
                        for part in range(npart):
                            jj = j0 + part
                            S.dma("pool", g_wdn, wdn[:, part, :], wdn_d[l, jj * 128:(jj + 1) * 128, :],
                                  reads=[], writes=[r_wdn])
                        up_ring = Ring([0, 1, 2, 3, 4, 5])
                        for j in range(npart):
                            jj = j0 + j
                            wi = wk % 3
                            wk += 1
                            S.dma("pool", g_w[3 + wi], wup[wi][:], wup_d[l, jj], writes=[r_wup[wi]])
                            tbs = []
                            for row in range(2):
                                ch = jj + row * NCH
                                ti_ = tb_ring.next()
                                tbs.append(ti_)
                                tbv = tbuf[ti_]
                                prev = None
                                for (c0, W, tiles, cont) in blocks:
                                    bi = up_ring.next()
                                    for kc in range(8):
                                        S.op("pe", lambda kc=kc, bi=bi, c0=c0, W=W, row=row: nc.tensor.matmul(
                                            PB[bi][:, 0:W], lhsT=wup[wi][:, kc, row * 128:(row + 1) * 128], rhs=h2T[:, kc, c0:c0 + W],
                                            start=(kc == 0), stop=(kc == 7)),
                                            reads=[r_wup[wi]] + [r_h2T[t] for t in tiles], writes=[RB[bi]], inc=(kc == 7))
                                    S.op("act", lambda bi=bi, c0=c0, W=W, ch=ch, tbv=tbv: nc.scalar.activation(
                                        tbv[:, c0:c0 + W], PB[bi][:, 0:W], AF.Identity, scale=cw[:, l, 1, ch:ch + 1], bias=cb[:, l, ch:ch + 1]),
                                        reads=[RB[bi], r_const], writes=[r_tbuf[ti_]])
                                    S.op("dve", lambda bi=bi, c0=c0, W=W, ch=ch, tbv=tbv: nc.vector.scalar_tensor_tensor(
                                        tbv[:, c0 + 1:c0 + W], PB[bi][:, 0:W - 1], cw[:, l, 0, ch:ch + 1], tbv[:, c0 + 1:c0 + W],
                                        ALU.mult, ALU.add), reads=[RB[bi], r_const, r_tbuf[ti_]], writes=[r_tbuf[ti_]])
                                    S.op("dve", lambda bi=bi, c0=c0, W=W, ch=ch, tbv=tbv: nc.vector.scalar_tensor_tensor(
                                        tbv[:, c0:c0 + W - 1], PB[bi][:, 1:W], cw[:, l, 2, ch:ch + 1], tbv[:, c0:c0 + W - 1],
                                        ALU.mult, ALU.add), reads=[RB[bi], r_const, r_tbuf[ti_]], writes=[r_tbuf[ti_]])
                                    if cont:
                                        pbi, pW = prev
                                        S.op("dve", lambda bi=bi, c0=c0, ch=ch, tbv=tbv, pbi=pbi, pW=pW: nc.vector.scalar_tensor_tensor(
                                            tbv[:, c0:c0 + 1], PB[pbi][:, pW - 1:pW], cw[:, l, 0, ch:ch + 1], tbv[:, c0:c0 + 1],
                                            ALU.mult, ALU.add), reads=[RB[pbi], r_const, r_tbuf[ti_]], writes=[r_tbuf[ti_]])
                                        S.op("dve", lambda bi=bi, c0=c0, ch=ch, tbv=tbv: nc.vector.scalar_tensor_tensor(
                                            tbv[:, c0 - 1:c0], PB[bi][:, 0:1], cw[:, l, 2, ch:ch + 1], tbv[:, c0 - 1:c0],
                                            ALU.mult, ALU.add), reads=[RB[bi], r_const, r_tbuf[ti_]], writes=[r_tbuf[ti_]])
                                    prev = (bi, W)
                            tg, tv = tbs
                            S.op("act", lambda tg=tg: nc.scalar.activation(tbuf[tg][:, 0:ncols], tbuf[tg][:, 0:ncols], AF.Silu),
                                 reads=[r_tbuf[tg]], writes=[r_tbuf[tg]])
                            S.op("dve", lambda tg=tg, tv=tv, j=j: nc.vector.tensor_tensor(
                                hid[:, j, 0:ncols], tbuf[tg][:, 0:ncols], tbuf[tv][:, 0:ncols], ALU.mult),
                                reads=[r_tbuf[tg], r_tbuf[tv]], writes=[r_hid[j]])
                        pair_ring = Ring([(0, 1), (2, 3), (4, 5), (6, 7)])
                        pstate = {}

                        def d_mm(t):
                            b0, b1 = pair_ring.next()
                            for nh, bi in enumerate((b0, b1)):
                                for j in range(npart):
                                    S.op("pe", lambda nh=nh, bi=bi, j=j: nc.tensor.matmul(
                                        PB[bi][:, :], lhsT=hid[:, j, t * 128:(t + 1) * 128], rhs=wdn[:, j, nh * 512:(nh + 1) * 512],
                                        start=(j == 0), stop=(j == npart - 1)),
                                        reads=[r_hid[j], r_wdn], writes=[RB[bi]], inc=(j == npart - 1))
                            pstate[t] = (b0, b1)

                        def d_res(t):
                            b0, b1 = pstate.pop(t)
                            xi = load_x(b, l + 1, t)
                            xo = xout_ring.next()
                            gi = 0 if t < 16 else 2
                            for nh, bi in enumerate((b0, b1)):
                                S.op("dve", lambda nh=nh, bi=bi: nc.vector.tensor_tensor(
                                    xout[xo][:, nh * 512:(nh + 1) * 512], PB[bi][:, :], modb[gi][:, nh * 512:(nh + 1) * 512], ALU.mult),
                                    reads=[RB[bi], r_modb[gi]], writes=[r_xout[xo]])
                            S.op("dve", lambda: nc.vector.tensor_tensor(xout[xo][:], xout[xo][:], xin[xi][:], ALU.add),
                                 reads=[r_xout[xo], r_xin[xi]], writes=[r_xout[xo]])
                            S.dma("sp", g_xout[xo], x_dst(b, t), xout[xo][:], reads=[r_xout[xo]], writes=[r_dx[b][t]])

                        pipeline(ntq, [(d_mm, 0), (d_res, 1)])
                    S.barrier()
        S.finish()
        build_nc.stats = (S.ninst, S.nwait)
    return nc


def _rope_tables(head_dim):
    rows = SEQ // GRID_W
    row = np.repeat(np.arange(rows, dtype=np.float32), GRID_W)
    col = np.tile(np.arange(GRID_W, dtype=np.float32), rows)
    n_freq = head_dim // 4
    inv_freq = (np.float32(10000.0) ** (-np.arange(n_freq, dtype=np.float32) / np.float32(n_freq))).astype(np.float32)
    ang = np.stack([row[:, None] * inv_freq, col[:, None] * inv_freq], axis=1).astype(np.float32)
    cos = np.cos(ang).astype(np.float32)
    sin = np.sin(ang).astype(np.float32)
    C2 = np.stack([cos, cos], axis=2).reshape(SEQ, 4 * n_freq)
    S2 = np.stack([-sin.reshape(SEQ, 2 * n_freq), sin.reshape(SEQ, 2 * n_freq)], axis=1)
    C2 = C2.reshape(16, 128, 4 * n_freq).transpose(1, 0, 2)
    S2 = S2.reshape(16, 128, 2, 2 * n_freq).transpose(1, 0, 2, 3)
    return np.ascontiguousarray(C2), np.ascontiguousarray(S2)


def prep_shared(inp):
    f = lambda a: np.ascontiguousarray(np.asarray(a, dtype=np.float32))
    w_up = f(inp["w_up"])
    wu = w_up.reshape(L_ALL, 8, 128, 2, NCH, 128)
    w_up_r = np.ascontiguousarray(wu.transpose(0, 4, 2, 1, 3, 5)).reshape(L_ALL, NCH, 128, 8, 256)
    a_wsT = np.ascontiguousarray(f(inp["a_ws"]).transpose(0, 3, 1, 2))
    smallg = np.concatenate([f(inp["b_qnorm"]), f(inp["b_knorm"]), f(inp["c_qnorm"]), f(inp["c_knorm"]),
                             f(inp["c_subln"])], axis=1)
    cwr = f(inp["conv_w"]).reshape(L_ALL, 3, 44, 128).transpose(3, 0, 1, 2)
    cbr = f(inp["conv_b"]).reshape(L_ALL, 44, 128).transpose(2, 0, 1)
    kk = np.arange(128)[:, None]
    qq = np.arange(128)[None, :]
    rbc, rbs = _rope_tables(64)
    rcc, rcs = _rope_tables(32)
    return {
        "w_ada_r": np.ascontiguousarray(f(inp["w_ada"]).reshape(L_ALL, 8, 128, 12, 512).transpose(0, 3, 2, 1, 4)),
        "b_ada": f(inp["b_ada"]),
        "gn": np.ascontiguousarray(np.stack([f(inp["norm1_g"]), f(inp["norm2_g"])], axis=1)),
        "w_in": f(inp["w_in"]), "w_out": f(inp["w_out"]), "w_up_r": w_up_r, "w_down": f(inp["w_down"]),
        "a_wsT": a_wsT, "a_bs": f(inp["a_bs"]), "smallg": np.ascontiguousarray(smallg),
        "b_sink": f(inp["b_sink"]).reshape(1, -1), "c_lam": f(inp["c_lam"]).reshape(1, -1),
        "sublnT": np.ascontiguousarray(f(inp["c_subln"]).T),
        "cw_r": np.ascontiguousarray(cwr), "cb_r": np.ascontiguousarray(cbr),
        "ident": np.eye(128, dtype=np.float32),
        "maskL": (qq <= kk).astype(np.float32), "maskR": (kk <= qq).astype(np.float32),
        "ropeB_C2": rbc, "ropeB_S": rbs, "ropeC_C2": rcc, "ropeC_S": rcs,
    }


def core_inputs(inp, shared, core, nb=NB):
    f = lambda a: np.ascontiguousarray(np.asarray(a, dtype=np.float32))
    b0 = core * nb
    d = dict(shared)
    d["x"] = f(inp["x"][b0:b0 + nb])
    d["ctx"] = f(inp["ctx"][b0:b0 + nb])
    crow = np.zeros((3, D), np.float32)
    crow[0:nb] = np.asarray(inp["c"], np.float32)[b0:b0 + nb]
    crow[2] = np.asarray(inp["c_ctx"], np.float32)
    d["crow"] = crow
    return d


_NC_CACHE = {}


def kernel(**inputs):
    inp = {k: np.asarray(v) for k, v in inputs.items()}
    shared = prep_shared(inp)
    if "nc" not in _NC_CACHE:
        _NC_CACHE["nc"] = build_nc()
    nc = _NC_CACHE["nc"]
    in_maps = [core_inputs(inp, shared, c) for c in range(N_CORES)]
    res = run_bass_kernel_spmd(nc, in_maps, core_ids=list(range(N_CORES)))
    out = np.concatenate([np.asarray(r["y"], dtype=np.float32) for r in res.results], axis=0)
    return out
```

```python
import math
from contextlib import ExitStack

import numpy as np
import concourse.bass as bass
import concourse.mybir as mybir
from concourse.bass_utils import run_bass_kernel_spmd

F32 = mybir.dt.float32
BF16 = mybir.dt.bfloat16
AF = mybir.ActivationFunctionType
ALU = mybir.AluOpType
AX = mybir.AxisListType

L_ALL = 4
D = 1024
SEQ = 2048
LC = 256
NT = 18
NTOK = NT * 128
DFF = 2816
NCH = 22
EPS = 1e-6
GRID_W = 64
N_CORES = 8
NB = 2


class Res:
    __slots__ = ("name", "lw", "rd", "excl")

    def __init__(self, name, excl=False):
        self.name = name
        self.lw = None
        self.rd = {}
        self.excl = excl


class DmaGroup:
    __slots__ = ("sem", "cnt", "name")

    def __init__(self, sem, name):
        self.sem = sem
        self.cnt = 0
        self.name = name


class Sched:
    ENG = ("pe", "act", "dve", "pool", "sp")

    def __init__(self, nc, stack):
        self.nc = nc
        self.stack = stack
        self.eng = {"pe": nc.tensor, "act": nc.scalar, "dve": nc.vector,
                    "pool": nc.gpsimd, "sp": nc.sync}
        self.sem = {e: stack.enter_context(nc.semaphore("sem_" + e)) for e in self.ENG}
        self.cnt = {e: 0 for e in self.ENG}
        self.seen = {e: {} for e in self.ENG}
        self.nwait = 0
        self.ninst = 0
        self.groups = []

    def group(self, name):
        sem = self.stack.enter_context(self.nc.semaphore("dg_" + name))
        g = DmaGroup(sem, name)
        self.groups.append(g)
        return g

    def finish(self):
        for g in self.groups:
            if g.cnt:
                self.nc.sync.wait_ge(g.sem, g.cnt)

    def _deps(self, e, reads, writes):
        need = {}

        def add(ev, raw):
            if ev is None:
                return
            kind, src, count = ev
            if kind == "eng" and src == e:
                if e in ("pe", "sp"):
                    return
            key = (kind, src)
            if need.get(key, (None, 0))[1] < count:
                need[key] = (ev, count)

        for r in reads:
            add(r.lw, True)
            if r.excl:
                for k2, ev in r.rd.items():
                    if k2 != ("eng", e):
                        add(ev, False)
        for w in writes:
            add(w.lw, False)
            for ev in w.rd.values():
                add(ev, False)
        out = []
        for key, (ev, count) in need.items():
            if self.seen[e].get(key, 0) >= count:
                continue
            out.append((key, ev, count))
        return out

    def _emit_waits(self, e, deps):
        eng = self.eng[e]
        for key, ev, count in deps:
            kind, src, _ = ev
            if kind == "eng":
                assert count <= self.cnt[src], f"wait on un-inc'd instr {src} {count}>{self.cnt[src]}"
                sem = self.sem[src]
            else:
                sem = src.sem
            eng.wait_ge(sem, count)
            self.nwait += 1
            self.seen[e][key] = count

    def _mark(self, ev, key, reads, writes):
        for r in reads:
            r.rd[key] = ev
        for w in writes:
            w.lw = ev
            w.rd = {}

    def op(self, e, fn, reads=(), writes=(), inc=True):
        import os as _os
        self.nops = getattr(self, "nops", 0) + 1
        if self.nops > int(_os.environ.get("P1_OPLIMIT", 10 ** 9)):
            if not inc:
                return None
            return None
        self._emit_waits(e, self._deps(e, reads, writes))
        ins = fn()
        self.ninst += 1
        if inc:
            self.cnt[e] += 1
            ins.then_inc(self.sem[e], 1)
            ev = ("eng", e, self.cnt[e])
        else:
            ev = ("eng", e, self.cnt[e] + 1)
        self._mark(ev, ("eng", e), reads, writes)
        return ins

    def dma(self, q, grp, out, in_, reads=(), writes=(), **kw):
        self._emit_waits(q, self._deps(q, reads, writes))
        ins = self.eng[q].dma_start(out=out, in_=in_, **kw)
        grp.cnt += 16
        ins.then_inc(grp.sem, 16)
        self.ninst += 1
        ev = ("dma", grp, grp.cnt)
        self._mark(ev, ("dma", grp), reads, writes)
        return ins

    def dma_batch(self, q, grp, items):
        allr, allw = [], []
        for it in items:
            allr += list(it.get("reads", ()))
            allw += list(it.get("writes", ()))
        self._emit_waits(q, self._deps(q, allr, allw))
        for it in items:
            ins = self.eng[q].dma_start(out=it["out"], in_=it["in_"], **it.get("kw", {}))
            grp.cnt += 16
            ins.then_inc(grp.sem, 16)
            self.ninst += 1
        ev = ("dma", grp, grp.cnt)
        self._mark(ev, ("dma", grp), allr, allw)

    def barrier(self):
        for e in self.ENG:
            for f in self.ENG:
                if self.cnt[f] == 0 or (f == e and e in ("pe", "sp")):
                    continue
                key = ("eng", f)
                if self.seen[e].get(key, 0) >= self.cnt[f]:
                    continue
                self.eng[e].wait_ge(self.sem[f], self.cnt[f])
                self.seen[e][key] = self.cnt[f]
                self.nwait += 1


class Ring:
    def __init__(self, items):
        self.items = list(items)
        self.i = 0

    def next(self):
        it = self.items[self.i % len(self.items)]
        self.i += 1
        return it


def pipeline(n, stages):
    mx = max(s for _, s in stages)
    for step in range(n + mx):
        for fn, sk in stages:
            i = step - sk
            if 0 <= i < n:
                fn(i)


def build_nc(depth=L_ALL, nb=NB, dbg=None, stop=None):
    nc = bass.Bass("TRN2", target_bir_lowering=False)

    def din(name, shape, dt=F32):
        return nc.dram_tensor(name, list(shape), dt, kind="ExternalInput").ap()

    x_d = din("x", [nb, SEQ, D])
    ctx_d = din("ctx", [nb, LC, D])
    crow_d = din("crow", [3, D])
    wada_d = din("w_ada_r", [L_ALL, 12, 128, 8, 512])
    bada_d = din("b_ada", [L_ALL, 6 * D])
    gn_d = din("gn", [L_ALL, 2, D])
    win_d = din("w_in", [L_ALL, D, 2048])
    wout_d = din("w_out", [L_ALL, D, D])
    wup_d = din("w_up_r", [L_ALL, NCH, 128, 8, 256])
    wdn_d = din("w_down", [L_ALL, DFF, D])
    awsT_d = din("a_wsT", [L_ALL, 128, 4, 128])
    abs_d = din("a_bs", [L_ALL, 4, 128])
    smallg_d = din("smallg", [L_ALL, 256])
    sink_d = din("b_sink", [1, L_ALL * 8])
    clam_d = din("c_lam", [1, L_ALL * 128])
    sublnT_d = din("sublnT", [64, L_ALL])
    cw_d = din("cw_r", [128, L_ALL, 3, 44])
    cb_d = din("cb_r", [128, L_ALL, 44])
    ident_d = din("ident", [128, 128])
    maskL_d = din("maskL", [128, 128])
    maskR_d = din("maskR", [128, 128])
    rbc_d = din("ropeB_C2", [128, 16, 64])
    rbs_d = din("ropeB_S", [128, 16, 2, 32])
    rcc_d = din("ropeC_C2", [128, 16, 32])
    rcs_d = din("ropeC_S", [128, 16, 2, 16])
    y_d = nc.dram_tensor("y", [nb, SEQ, D], F32, kind="ExternalOutput").ap()
    ctxs_d = nc.dram_tensor("ctx_s", [nb, LC, D], F32).ap()
    mod_d = nc.dram_tensor("mod_s", [3, L_ALL, 6, D], F32).ap()

    with ExitStack() as st:
        S = Sched(nc, st)

        uid = [0]

        def T(name, shape, dt, stack=st):
            uid[0] += 1
            return stack.enter_context(nc.sbuf_tensor(f"sb{uid[0]}_{name}", list(shape), dt))

        PB = [st.enter_context(nc.psum_tensor(f"pb{i}", [128, 512], F32)) for i in range(8)]
        RB = [Res(f"pb{i}", excl=True) for i in range(8)]

        def bank_bf(i):
            return PB[i][:].bitcast(BF16)

        g_const = S.group("const")
        g_xin = [S.group(f"xin{i}") for i in range(2)]
        g_xout = [S.group(f"xout{i}") for i in range(2)]
        g_modb = [S.group(f"modb{i}") for i in range(4)]
        g_w = [S.group(f"w{i}") for i in range(6)]
        g_wdn = S.group("wdn")
        g_pre = [S.group(f"pre{i}") for i in range(6)]
        g_zst = [S.group(f"zst{i}") for i in range(2)]
        g_mod = S.group("modw")
        g_dbg = S.group("dbg")

        dbg_groups = []

        def dbg_wait():
            S.finish()

        def dump(name, ap, res):
            if dbg is None:
                return
            d = nc.dram_tensor("dbg_" + name, list(ap.shape), ap.dtype, kind="ExternalOutput").ap()
            gg = S.group("dbg_" + name)
            dbg_groups.append(gg)
            S.dma("sp", gg, d, ap, reads=[res])
            dbg.append(name)

        ident_f = T("ident_f", [128, 128], F32); r_ident = Res("ident")
        ident_b = T("ident_b", [128, 128], BF16)
        ones_f = T("ones_f", [128, 128], F32)
        onesmean = T("onesmean", [128, 128], F32)
        mask_f = T("mask_f", [128, 2, 128], F32)
        mask_b = T("mask_b", [128, 2, 128], BF16)
        ropeB_C = T("ropeB_C", [128, 16, 64], F32)
        ropeB_S = T("ropeB_S", [128, 16, 2, 32], F32)
        ropeC_C = T("ropeC_C", [128, 16, 32], F32)
        ropeC_S = T("ropeC_S", [128, 16, 2, 16], F32)
        smallg = T("smallg", [128, L_ALL, 256], F32)
        biasT = T("biasT", [128, L_ALL, 2, 128], F32)
        cw = T("cw", [128, L_ALL, 3, 44], F32)
        cb = T("cb", [128, L_ALL, 44], F32)
        esink = T("esink", [128, L_ALL * 8], F32)
        clam = T("clam", [128, L_ALL, 4, 32], F32)
        lamt = T("lamt", [128, L_ALL, 2, 32], F32)
        lam2 = T("lam2", [128, L_ALL, 2], F32)
        neglam = T("neglam", [128, L_ALL], F32)
        gsub = T("gsub", [128, L_ALL], F32)
        r_const = Res("const")

        items = [
            dict(out=ident_f[:], in_=ident_d),
            dict(out=mask_f[:, 0, :], in_=maskL_d),
            dict(out=mask_f[:, 1, :], in_=maskR_d),
            dict(out=ropeB_C[:], in_=rbc_d),
            dict(out=ropeB_S[:], in_=rbs_d),
            dict(out=ropeC_C[:], in_=rcc_d),
            dict(out=ropeC_S[:], in_=rcs_d),
            dict(out=smallg[:].rearrange("p l c -> p (l c)"),
                 in_=smallg_d.rearrange("l c -> (l c)").partition_broadcast(128)),
            dict(out=cw[:], in_=cw_d),
            dict(out=cb[:], in_=cb_d),
            dict(out=esink[:], in_=sink_d[0, :].partition_broadcast(128)),
            dict(out=clam[:].rearrange("p l a c -> p (l a c)"), in_=clam_d[0, :].partition_broadcast(128)),
            dict(out=gsub[0:64, :], in_=sublnT_d),
            dict(out=gsub[64:128, :], in_=sublnT_d),
        ]
        for l in range(L_ALL):
            for g in range(4):
                items.append(dict(out=biasT[(g % 2) * 64:(g % 2) * 64 + 64, l, g // 2, :],
                                  in_=abs_d[l, g, :].partition_broadcast(64)))
        for it in items:
            it["writes"] = [r_const]
        S.dma_batch("sp", g_const, items)

        S.op("dve", lambda: nc.vector.tensor_copy(ident_b[:], ident_f[:]), reads=[r_const], writes=[r_ident])
        S.op("dve", lambda: nc.vector.tensor_copy(mask_b[:], mask_f[:]), reads=[r_const], writes=[r_ident])
        S.op("dve", lambda: nc.vector.memset(ones_f[:], 1.0), writes=[r_ident])
        S.op("dve", lambda: nc.vector.memset(onesmean[:], 1.0 / 64.0), writes=[r_ident])
        S.op("act", lambda: nc.scalar.activation(esink[:], esink[:], AF.Exp), reads=[r_const], writes=[r_const])
        S.op("dve", lambda: nc.vector.tensor_tensor(lamt[:], clam[:, :, 0:4:2, :], clam[:, :, 1:4:2, :], ALU.mult),
             reads=[r_const], writes=[r_const])
        S.op("dve", lambda: nc.vector.tensor_reduce(lam2[:], lamt[:], AX.X, ALU.add), reads=[r_const], writes=[r_const])
        S.op("act", lambda: nc.scalar.activation(lam2[:], lam2[:], AF.Exp), reads=[r_const], writes=[r_const])
        S.op("dve", lambda: nc.vector.tensor_tensor(neglam[:], lam2[:, :, 1], lam2[:, :, 0], ALU.subtract),
             reads=[r_const], writes=[r_const])
        for l in range(L_ALL):
            lam_init = 0.8 - 0.6 * math.exp(-0.3 * l)
            S.op("dve", lambda l=l, li=lam_init: nc.vector.tensor_scalar(
                neglam[:, l:l + 1], neglam[:, l:l + 1], -li, None, ALU.add), reads=[r_const], writes=[r_const])
            S.op("dve", lambda l=l, li=lam_init: nc.vector.tensor_scalar(
                gsub[:, l:l + 1], gsub[:, l:l + 1], 1.0 - li, None, ALU.mult), reads=[r_const], writes=[r_const])

        if stop == "const":
            dump("neglam", neglam[:], r_const); dump("gsub", gsub[:], r_const); dump("esink", esink[:], r_const)
            dump("biasT", biasT[:], r_const); dump("mask_b", mask_b[:], r_ident)
            dbg_wait()
            return nc
        with ExitStack() as pp:
            crow = T("crow_sb", [3, D], F32, pp); r_crow = Res("crow")
            scT = T("scT", [128, 8, 3], F32, pp); r_scT = Res("scT")
            rows = T("rows", [3, 6 * D], F32, pp); r_rows = Res("rows")
            bada = T("bada", [3, 6 * D], F32, pp); r_bada = Res("bada")
            gnb = T("gnb", [3, 2, D], F32, pp); r_gnb = Res("gnb")
            wslots = [T(f"wada{i}", [128, 8, 512], F32, pp) for i in range(3)]
            r_wslots = [Res(f"wada{i}") for i in range(3)]
            g_ws = g_pre[0:3]
            g_pp = g_pre[3]
            S.dma("sp", g_pp, crow[:], crow_d, writes=[r_crow])
            S.op("act", lambda: nc.scalar.activation(crow[:], crow[:], AF.Silu), reads=[r_crow], writes=[r_crow])
            for c in range(8):
                S.op("pe", lambda c=c: nc.tensor.transpose(PB[0][:, c * 3:c * 3 + 3], crow[0:3, c * 128:(c + 1) * 128],
                                                           ident_f[0:3, 0:3]),
                     reads=[r_crow, r_const], writes=[RB[0]], inc=(c == 7))
            S.op("dve", lambda: nc.vector.tensor_copy(scT[:].rearrange("p c r -> p (c r)"), PB[0][:, 0:24]),
                 reads=[RB[0]], writes=[r_scT])
            pring = Ring([1, 2, 3])
            k = 0
            for l in range(depth):
                S.dma("sp", g_pre[4], bada[:], bada_d[l, :].partition_broadcast(3), writes=[r_bada])
                S.dma("sp", g_pre[5], gnb[:].rearrange("p a d -> p (a d)"),
                      gn_d[l].rearrange("a d -> (a d)").partition_broadcast(3), writes=[r_gnb])
                for n in range(12):
                    si = k % 3
                    k += 1
                    S.dma("sp", g_ws[si], wslots[si][:],
                          wada_d[l, n],
                          writes=[r_wslots[si]])
                    bi = pring.next()
                    for kc in range(8):
                        S.op("pe", lambda kc=kc, si=si, bi=bi: nc.tensor.matmul(
                            PB[bi][0:3, :], lhsT=scT[:, kc, :], rhs=wslots[si][:, kc, :],
                            start=(kc == 0), stop=(kc == 7)),
                            reads=[r_scT, r_wslots[si]], writes=[RB[bi]], inc=(kc == 7))
                    S.op("dve", lambda n=n, bi=bi: nc.vector.tensor_tensor(
                        rows[:, n * 512:(n + 1) * 512], PB[bi][0:3, :], bada[:, n * 512:(n + 1) * 512], ALU.add),
                        reads=[RB[bi], r_bada], writes=[r_rows])
                for a, kidx in ((0, 1), (1, 4)):
                    S.op("dve", lambda a=a, kidx=kidx: nc.vector.scalar_tensor_tensor(
                        rows[:, kidx * D:(kidx + 1) * D], rows[:, kidx * D:(kidx + 1) * D], 1.0, gnb[:, a, :],
                        ALU.add, ALU.mult), reads=[r_rows, r_gnb], writes=[r_rows])
                S.dma("sp", g_mod, mod_d[:, l].rearrange("r k d -> r (k d)"), rows[:], reads=[r_rows], writes=[])
            r_mod = Res("mod_d")
            r_mod.lw = ("dma", g_mod, g_mod.cnt)
            S.barrier()

        if stop == "prepass":
            if dbg is not None:
                d = nc.dram_tensor("dbg_mod", [3, 1, 6, D], F32, kind="ExternalOutput").ap()
                S.dma("sp", g_dbg, d, mod_d[:, 0:1], reads=[r_mod])
                dbg.append("mod")
            dbg_wait()
            return nc
        xin = [T(f"xin{i}", [128, D], F32) for i in range(2)]
        r_xin = [Res(f"xin{i}") for i in range(2)]
        xin_ring = Ring(range(2))
        xout = [T(f"xout{i}", [128, D], F32) for i in range(2)]
        r_xout = [Res(f"xout{i}") for i in range(2)]
        xout_ring = Ring(range(2))
        modb = [T(f"modb{i}", [128, D], F32) for i in range(4)]
        r_modb = [Res(f"modb{i}") for i in range(4)]
        stat = T("stat", [128, 64], F32)
        r_dx = [[Res(f"dx{b}_{t}") for t in range(NT)] for b in range(nb)]

        def x_src(b, l, t):
            if t < 16:
                base = x_d if l == 0 else y_d
                return base[b, t * 128:(t + 1) * 128, :]
            base = ctx_d if l == 0 else ctxs_d
            return base[b, (t - 16) * 128:(t - 15) * 128, :]

        def x_dst(b, t):
            if t < 16:
                return y_d[b, t * 128:(t + 1) * 128, :]
            return ctxs_d[b, (t - 16) * 128:(t - 15) * 128, :]

        def load_x(b, lsrc, t):
            i = xin_ring.next()
            S.dma("sp", g_xin[i], xin[i][:], x_src(b, lsrc, t), reads=[r_dx[b][t]], writes=[r_xin[i]])
            return i

        def load_modb(i, row, l, kind):
            S.dma("sp", g_modb[i], modb[i][:], mod_d[row, l, kind, :].partition_broadcast(128),
                  reads=[r_mod], writes=[r_modb[i]])

        ev_toggle = [0]

        def evac(out, in_, reads, writes):
            ev_toggle[0] ^= 1
            if ev_toggle[0]:
                S.op("act", lambda: nc.scalar.copy(out, in_), reads=reads, writes=writes)
            else:
                S.op("dve", lambda: nc.vector.tensor_copy(out, in_), reads=reads, writes=writes)

        def norm_tile(xi, mi, shi, hb, r_hb, t1, r_t1, ss_col):
            ss = stat[:, ss_col:ss_col + 1]
            rt = stat[:, ss_col + 1:ss_col + 2]
            r_st = r_stat[ss_col // 2]
            import os as _os
            _k = int(_os.environ.get("P1_S0", 99))
            if _k < 2:
                return
            S.op("act", lambda: nc.scalar.activation(t1, xin[xi][:], AF.Square),
                 reads=[r_xin[xi]], writes=[r_t1])
            S.op("dve", lambda: nc.vector.tensor_reduce(ss, t1, AX.X, ALU.add), reads=[r_t1], writes=[r_st])
            if _k < 3:
                return
            S.op("act", lambda: nc.scalar.activation(rt, ss, AF.Sqrt, scale=1.0 / D, bias=eps_t[:, 0:1]),
                 reads=[r_st, r_ident], writes=[r_st])
            if _k < 4:
                return
            S.op("dve", lambda: nc.vector.reciprocal(rt, rt), reads=[r_st], writes=[r_st])
            if _k < 5:
                return
            S.op("dve", lambda: nc.vector.scalar_tensor_tensor(t1, xin[xi][:], rt, modb[mi][:], ALU.mult, ALU.mult),
                 reads=[r_xin[xi], r_st, r_modb[mi]], writes=[r_t1])
            if _k < 6:
                return
            S.op("dve", lambda: nc.vector.tensor_tensor(hb, t1, modb[shi][:], ALU.add),
                 reads=[r_t1, r_modb[shi]], writes=[r_hb])

        r_stat = [Res(f"stat{i}") for i in range(8)]
        eps_t = T("eps_t", [128, 1], F32)
        S.op("dve", lambda: nc.vector.memset(eps_t[:], EPS), writes=[r_ident])

        def transpose8(hb, r_hb, bi, dst, r_dst):
            pv = bank_bf(bi)
            import os as _os
            _k = int(_os.environ.get("P1_S0", 99))
            if _k < 7:
                return
            for c in range(8):
                S.op("pe", lambda c=c: nc.tensor.transpose(pv[:, c * 128:(c + 1) * 128], hb[:, c * 128:(c + 1) * 128],
                                                           ident_b[:]),
                     reads=[r_hb, r_ident], writes=[RB[bi]], inc=(c == 7))
            if _k < 8:
                return
            evac(dst, pv[:, 0:1024].rearrange("p (c t) -> p c t", c=8), [RB[bi]], [r_dst])

        for b in range(nb):
            for l in range(depth):
                last = (l == depth - 1)
                ntq = 16 if last else 18
                lsrc = l
                load_modb(0, b, l, 1)
                load_modb(1, b, l, 0)
                load_modb(2, 2, l, 1)
                load_modb(3, 2, l, 0)
                with ExitStack() as s1:
                    kbT = T("kbT", [128, NTOK], BF16, s1); r_kbT = [Res(f"kbT{t}") for t in range(NT)]
                    vb = T("vb", [128, NT, 2, 65], BF16, s1); r_vb = [Res(f"vb{t}") for t in range(NT)]
                    kcT = T("kcT", [128, 2, NTOK], BF16, s1); r_kcT = [Res(f"kcT{t}") for t in range(NT)]
                    vc = T("vc", [128, NT, 4, 65], BF16, s1); r_vc = [Res(f"vc{t}") for t in range(NT)]
                    qbT = T("qbT", [128, 4, NTOK], BF16, s1); r_qbT = [Res(f"qbT{t}") for t in range(NT)]
                    qcT = T("qcT", [128, 2, NTOK], BF16, s1); r_qcT = [Res(f"qcT{t}") for t in range(NT)]
                    catA = T("catA", [128, 2, NTOK], BF16, s1)
                    r_catA = [Res(f"catA{t}") for t in range(NT)]
                    r_catB = [Res(f"catB{t}") for t in range(NT)]
                    r_catC = [Res(f"catC{t}") for t in range(NT)]
                    r_vones = Res("vones")
                    S.op("dve", lambda: nc.vector.memset(vb[:, :, :, 64:65], 1.0), writes=[r_vones])
                    S.op("dve", lambda: nc.vector.memset(vc[:, :, :, 64:65], 1.0), writes=[r_vones])

                    with ExitStack() as p1:
                        w_in = T("w_in", [128, 8, 2048], BF16, p1); r_win = Res("w_in")
                        awsT = T("awsT", [128, 4, 128], BF16, p1); r_aws = Res("awsT")
                        for kc in range(8):
                            S.dma("pool", g_w[0], w_in[:, kc, :], win_d[l, kc * 128:(kc + 1) * 128, :], writes=[r_win])
                        S.dma("pool", g_w[1], awsT[:], awsT_d[l], writes=[r_aws])
                        NS = 2
                        hb = [T(f"hb{i}", [128, D], BF16, p1) for i in range(NS)]; r_hb = [Res(f"hb{i}") for i in range(NS)]
                        t1 = [T("t1_0", [128, D], F32, p1)] * NS; r_t1 = [Res("t1_0")] * NS
                        hT = [T(f"hT{i}", [128, 8, 128], BF16, p1) for i in range(3)]; r_hT = [Res(f"hT{i}") for i in range(3)]
                        uT = [T(f"uT{i}", [128, 2, 128], BF16, p1) for i in range(NS)]; r_uT = [Res(f"uT{i}") for i in range(NS)]
                        gv = [T(f"gv{i}", [128, 256], F32, p1) for i in range(NS)]; r_gv = [Res(f"gv{i}") for i in range(NS)]
                        vpad = [T(f"vpad{i}", [128, 4, 128], BF16, p1) for i in range(NS)]; r_vpad = [Res(f"vpad{i}") for i in range(NS)]
                        sq = [T("sq0", [128, 1152], F32, p1)] * NS; r_sq = [Res("sq0")] * NS
                        qn = [T("qn0", [128, 1152], F32, p1)] * NS; r_qn = [Res("qn0")] * NS
                        tb = [T("tb0", [128, 1152], F32, p1)] * NS; r_tb = [Res("tb0")] * NS
                        qr = [T("qr0", [128, 1152], BF16, p1)] * NS; r_qr = [Res("qr0")] * NS
                        st2 = [T(f"st2_{i}", [128, 32], F32, p1) for i in range(NS)]; r_st2 = [Res(f"st2_{i}") for i in range(NS)]
                        ta = [T(f"ta{i}", [128, 256], F32, p1) for i in range(NS)]; r_ta = [Res(f"ta{i}") for i in range(NS)]
                        for i in range(NS):
                            S.op("dve", lambda i=i: nc.vector.memset(vpad[i][:], 0.0), writes=[r_vpad[i]])
                        order = [16, 17] + list(range(16))
                        trp_ring = Ring([0, 1, 2])
                        prj_ring = Ring([3, 4, 5, 6, 7])
                        xi_of = {}
                        banks_of = {}

                        def s0(i):
                            t = order[i]
                            isctx = t >= 16
                            xi = load_x(b, lsrc, t)
                            xi_of[i] = xi
                            sl = i % NS
                            norm_tile(xi, 2 if isctx else 0, 3 if isctx else 1, hb[sl][:], r_hb[sl], t1[sl][:], r_t1[sl], (i % 4) * 2)
                            transpose8(hb[sl], r_hb[sl], trp_ring.next(), hT[i % 3][:], r_hT[i % 3])
                            if stop == "p1dbg" and t == 0:
                                dump("xin", xin[xi][:], r_xin[xi]); dump("hb", hb[sl][:], r_hb[sl]); dump("hT", hT[i % 3][:], r_hT[i % 3])
                                dump("m1b", modb[0][:], r_modb[0]); dump("sh1b", modb[1][:], r_modb[1]); dump("w_in", w_in[:], r_win)

                        def s1f(i):
                            t = order[i]
                            isctx = t >= 16
                            full = (not isctx) or (not last)
                            h = hT[i % 3]; rh = r_hT[i % 3]
                            bk = {}
                            if full:
                                bu = prj_ring.next(); bk["u"] = bu
                                for cc in range(2):
                                    for kc in range(8):
                                        S.op("pe", lambda cc=cc, kc=kc: nc.tensor.matmul(
                                            PB[bu][:, cc * 128:(cc + 1) * 128], lhsT=w_in[:, kc, cc * 128:(cc + 1) * 128],
                                            rhs=h[:, kc, :], start=(kc == 0), stop=(kc == 7)),
                                            reads=[r_win, rh], writes=[RB[bu]], inc=(kc == 7 and cc == 1))
                                bv = prj_ring.next(); bk["v"] = bv
                                for kc in range(8):
                                    S.op("pe", lambda kc=kc: nc.tensor.matmul(
                                        PB[bv][:, 0:256], lhsT=h[:, kc, :], rhs=w_in[:, kc, 256:512],
                                        start=(kc == 0), stop=(kc == 7)), reads=[r_win, rh], writes=[RB[bv]], inc=(kc == 7))
                                bq = prj_ring.next(); bk["q"] = bq
                                for kc in range(8):
                                    S.op("pe", lambda kc=kc: nc.tensor.matmul(
                                        PB[bq][:, :], lhsT=h[:, kc, :], rhs=w_in[:, kc, 512:1024],
                                        start=(kc == 0), stop=(kc == 7)), reads=[r_win, rh], writes=[RB[bq]], inc=(kc == 7))
                            bkk = prj_ring.next(); bk["k"] = bkk
                            for kc in range(8):
                                S.op("pe", lambda kc=kc: nc.tensor.matmul(
                                    PB[bkk][:, :], lhsT=h[:, kc, :], rhs=w_in[:, kc, 1024:1536],
                                    start=(kc == 0), stop=(kc == 7)), reads=[r_win, rh], writes=[RB[bkk]], inc=(kc == 7))
                            bc = prj_ring.next(); bk["c"] = bc
                            for kc in range(8):
                                S.op("pe", lambda kc=kc: nc.tensor.matmul(
                                    PB[bc][:, :], lhsT=h[:, kc, :], rhs=w_in[:, kc, 1536:2048],
                                    start=(kc == 0), stop=(kc == 7)), reads=[r_win, rh], writes=[RB[bc]], inc=(kc == 7))
                            banks_of[i] = bk

                        def s2(i):
                            t = order[i]
                            isctx = t >= 16
                            full = (not isctx) or (not last)
                            bk = banks_of[i]
                            sl = i % NS
                            g0 = l * 256
                            if full:
                                bu, bv = bk["u"], bk["v"]
                                S.op("act", lambda: nc.scalar.activation(
                                    uT[sl][:].rearrange("p c t -> p (c t)"), PB[bu][:, 0:256], AF.Gelu_apprx_tanh),
                                    reads=[RB[bu]], writes=[r_uT[sl]])
                                S.op("act", lambda: nc.scalar.activation(gv[sl][:], PB[bv][:, 0:256], AF.Gelu_apprx_tanh),
                                     reads=[RB[bv]], writes=[r_gv[sl]])
                                S.op("dve", lambda: nc.vector.tensor_tensor(ta[sl][:], gv[sl][:], gv[sl][:], ALU.mult),
                                     reads=[r_gv[sl]], writes=[r_ta[sl]])
                                S.op("dve", lambda: nc.vector.tensor_reduce(
                                    st2[sl][:, 26:30], ta[sl][:].rearrange("p (g c) -> p g c", g=4), AX.X, ALU.add),
                                    reads=[r_ta[sl]], writes=[r_st2[sl]])
                                S.op("act", lambda: nc.scalar.activation(st2[sl][:, 26:30], st2[sl][:, 26:30], AF.Sqrt,
                                                                         scale=1.0 / 64, bias=eps_t[:, 0:1]),
                                     reads=[r_st2[sl], r_ident], writes=[r_st2[sl]])
                                S.op("dve", lambda: nc.vector.reciprocal(st2[sl][:, 26:30], st2[sl][:, 26:30]),
                                     reads=[r_st2[sl]], writes=[r_st2[sl]])
                                for par in range(2):
                                    S.op("dve", lambda par=par: nc.vector.tensor_tensor(
                                        vpad[sl][:, par:4:2, par * 64:par * 64 + 64],
                                        gv[sl][:].rearrange("p (g c) -> p g c", g=4)[:, par:4:2, :],
                                        st2[sl][:, 26 + par:30:2].unsqueeze(2).to_broadcast([128, 2, 64]), ALU.mult),
                                        reads=[r_gv[sl], r_st2[sl]], writes=[r_vpad[sl]])
                                bm = prj_ring.next()
                                for g in range(4):
                                    S.op("pe", lambda g=g: nc.tensor.matmul(
                                        PB[bm][:, (g // 2) * 128:(g // 2) * 128 + 128], lhsT=vpad[sl][:, g, :], rhs=awsT[:, g, :],
                                        start=(g % 2 == 0), stop=(g % 2 == 1)),
                                        reads=[r_vpad[sl], r_aws], writes=[RB[bm]], inc=(g == 3))
                                S.op("dve", lambda: nc.vector.tensor_tensor(
                                    ta[sl][:], PB[bm][:, 0:256], biasT[:, l].rearrange("p c t -> p (c t)"), ALU.add),
                                    reads=[RB[bm], r_const], writes=[r_ta[sl]])
                                S.op("dve", lambda: nc.vector.tensor_tensor(
                                    catA[:, 0:2, t * 128:(t + 1) * 128], ta[sl][:].rearrange("p (c t) -> p c t", c=2),
                                    uT[sl][:], ALU.mult), reads=[r_ta[sl], r_uT[sl]], writes=[r_catA[t]])
                            bkk, bc = bk["k"], bk["c"]
                            if full:
                                bq = bk["q"]
                                S.op("act", lambda: nc.scalar.activation(sq[sl][:, 0:512], PB[bq][:, :], AF.Square),
                                     reads=[RB[bq]], writes=[r_sq[sl]])
                                S.op("act", lambda: nc.scalar.activation(sq[sl][:, 640:896], PB[bkk][:, 256:512], AF.Square),
                                     reads=[RB[bkk]], writes=[r_sq[sl]])
                            else:
                                S.op("dve", lambda: nc.vector.memset(sq[sl][:, 0:512], 1.0), writes=[r_sq[sl]])
                                S.op("dve", lambda: nc.vector.memset(sq[sl][:, 640:896], 1.0), writes=[r_sq[sl]])
                            S.op("act", lambda: nc.scalar.activation(sq[sl][:, 512:640], PB[bkk][:, 0:128], AF.Square),
                                 reads=[RB[bkk]], writes=[r_sq[sl]])
                            S.op("act", lambda: nc.scalar.activation(sq[sl][:, 896:1152], PB[bc][:, 0:256], AF.Square),
                                 reads=[RB[bc]], writes=[r_sq[sl]])
                            S.op("dve", lambda: nc.vector.tensor_reduce(
                                st2[sl][:, 0:10], sq[sl][:, 0:640].rearrange("p (h d) -> p h d", d=64), AX.X, ALU.add),
                                reads=[r_sq[sl]], writes=[r_st2[sl]])
                            S.op("dve", lambda: nc.vector.tensor_reduce(
                                st2[sl][:, 10:26], sq[sl][:, 640:1152].rearrange("p (h d) -> p h d", d=32), AX.X, ALU.add),
                                reads=[r_sq[sl]], writes=[r_st2[sl]])
                            S.op("act", lambda: nc.scalar.activation(st2[sl][:, 0:10], st2[sl][:, 0:10], AF.Sqrt,
                                                                     scale=1.0 / 64, bias=eps_t[:, 0:1]),
                                 reads=[r_st2[sl], r_ident], writes=[r_st2[sl]])
                            S.op("act", lambda: nc.scalar.activation(st2[sl][:, 10:26], st2[sl][:, 10:26], AF.Sqrt,
                                                                     scale=1.0 / 32, bias=eps_t[:, 0:1]),
                                 reads=[r_st2[sl], r_ident], writes=[r_st2[sl]])
                            S.op("dve", lambda: nc.vector.reciprocal(st2[sl][:, 0:26], st2[sl][:, 0:26]),
                                 reads=[r_st2[sl]], writes=[r_st2[sl]])
                            specs = []
                            if full:
                                specs.append(("bq", PB[bk["q"]][:, :], RB[bk["q"]], 0, 512, 8, 64, 0, 0, ropeB_C, ropeB_S))
                            specs.append(("bk", PB[bkk][:, 0:128], RB[bkk], 512, 128, 2, 64, 8, 64, ropeB_C, ropeB_S))
                            if full:
                                specs.append(("cq", PB[bkk][:, 256:512], RB[bkk], 640, 256, 8, 32, 10, 128, ropeC_C, ropeC_S))
                            specs.append(("ck", PB[bc][:, 0:256], RB[bc], 896, 256, 8, 32, 18, 160, ropeC_C, ropeC_S))
                            for (nm, src, rsrc, c0, wd, nh, hd, sc0, gc0, rC, rS) in specs:
                                v3 = lambda ap, nh=nh: ap.rearrange("p (h d) -> p h d", h=nh)
                                qv = qn[sl][:, c0:c0 + wd]
                                S.op("dve", lambda src=src, qv=qv, v3=v3, sc0=sc0, nh=nh, hd=hd: nc.vector.tensor_tensor(
                                    v3(qv), v3(src), st2[sl][:, sc0:sc0 + nh].unsqueeze(2).to_broadcast([128, nh, hd]), ALU.mult),
                                    reads=[rsrc, r_st2[sl]], writes=[r_qn[sl]])
                                gain_b = smallg[:, l, gc0:gc0 + hd].unsqueeze(1).to_broadcast([128, nh, hd])
                                if isctx:
                                    if nm == "bq":
                                        outv = qr[sl][:, 0:512].rearrange("p (c hi d) -> p hi c d", c=4, hi=2)
                                        inv = qv.rearrange("p (hi c d) -> p hi c d", hi=2, c=4)
                                        gb = smallg[:, l, gc0:gc0 + hd].unsqueeze(1).unsqueeze(1).to_broadcast([128, 2, 4, hd])
                                        S.op("dve", lambda outv=outv, inv=inv, gb=gb: nc.vector.tensor_tensor(outv, inv, gb, ALU.mult),
                                             reads=[r_qn[sl], r_const], writes=[r_qr[sl]])
                                    else:
                                        S.op("dve", lambda qv=qv, v3=v3, gain_b=gain_b, c0=c0, wd=wd: nc.vector.tensor_tensor(
                                            v3(qr[sl][:, c0:c0 + wd]), v3(qv), gain_b, ALU.mult),
                                            reads=[r_qn[sl], r_const], writes=[r_qr[sl]])
                                    continue
                                S.op("dve", lambda qv=qv, v3=v3, gain_b=gain_b: nc.vector.tensor_tensor(v3(qv), v3(qv), gain_b, ALU.mult),
                                     reads=[r_qn[sl], r_const], writes=[r_qn[sl]])
                                nf = hd // 4
                                v4 = lambda ap, nf=nf: ap.rearrange("p (ha pr f) -> p ha pr f", pr=2, f=nf)
                                tbv = tb[sl][:, c0:c0 + wd]
                                Sn = rS[:, t, 0, :].rearrange("p (a f) -> p a f", a=2).unsqueeze(1).to_broadcast([128, nh, 2, nf])
                                Sp = rS[:, t, 1, :].rearrange("p (a f) -> p a f", a=2).unsqueeze(1).to_broadcast([128, nh, 2, nf])
                                v5 = lambda ap, nh=nh, nf=nf: ap.rearrange("p (h a pr f) -> p h a pr f", h=nh, a=2, pr=2)
                                S.op("dve", lambda tbv=tbv, qv=qv, v5=v5, Sn=Sn: nc.vector.tensor_tensor(
                                    v5(tbv)[:, :, :, 0, :], v5(qv)[:, :, :, 1, :], Sn, ALU.mult),
                                    reads=[r_qn[sl], r_const], writes=[r_tb[sl]])
                                S.op("dve", lambda tbv=tbv, qv=qv, v5=v5, Sp=Sp: nc.vector.tensor_tensor(
                                    v5(tbv)[:, :, :, 1, :], v5(qv)[:, :, :, 0, :], Sp, ALU.mult),
                                    reads=[r_qn[sl], r_const], writes=[r_tb[sl]])
                                Cb = rC[:, t, :].unsqueeze(1).to_broadcast([128, nh, hd])
                                S.op("dve", lambda qv=qv, v3=v3, Cb=Cb: nc.vector.tensor_tensor(v3(qv), v3(qv), Cb, ALU.mult),
                                     reads=[r_qn[sl], r_const], writes=[r_qn[sl]])
                                if nm == "bq":
                                    outv = qr[sl][:, 0:512].rearrange("p (c hi d) -> p hi c d", c=4, hi=2)
                                    a0 = qv.rearrange("p (hi c d) -> p hi c d", hi=2, c=4)
                                    a1 = tbv.rearrange("p (hi c d) -> p hi c d", hi=2, c=4)
                                    S.op("dve", lambda outv=outv, a0=a0, a1=a1: nc.vector.tensor_tensor(outv, a0, a1, ALU.add),
                                         reads=[r_qn[sl], r_tb[sl]], writes=[r_qr[sl]])
                                else:
                                    S.op("dve", lambda qv=qv, tbv=tbv, c0=c0, wd=wd: nc.vector.tensor_tensor(
                                        qr[sl][:, c0:c0 + wd], qv, tbv, ALU.add),
                                        reads=[r_qn[sl], r_tb[sl]], writes=[r_qr[sl]])
                            S.op("act", lambda: nc.scalar.copy(vb[:, t, :, 0:64], PB[bkk][:, 128:256].rearrange("p (h d) -> p h d", h=2)),
                                 reads=[RB[bkk]], writes=[r_vb[t]])
                            S.op("act", lambda: nc.scalar.copy(vc[:, t, :, 0:64], PB[bc][:, 256:512].rearrange("p (h d) -> p h d", h=4)),
                                 reads=[RB[bc]], writes=[r_vc[t]])
                            bt = trp_ring.next()
                            pv = bank_bf(bt)
                            lst = []
                            if full:
                                for c in range(4):
                                    lst.append((c, qr[sl][:, c * 128:(c + 1) * 128]))
                            lst.append((4, qr[sl][:, 512:640]))
                            for c, src in lst:
                                S.op("pe", lambda c=c, src=src: nc.tensor.transpose(pv[:, c * 128:(c + 1) * 128], src, ident_b[:]),
                                     reads=[r_qr[sl], r_ident], writes=[RB[bt]], inc=(c == 4))
                            if full:
                                evac(qbT[:, :, t * 128:(t + 1) * 128], pv[:, 0:512].rearrange("p (c t) -> p c t", c=4),
                                     [RB[bt]], [r_qbT[t]])
                            evac(kbT[:, t * 128:(t + 1) * 128], pv[:, 512:640], [RB[bt]], [r_kbT[t]])
                            bt2 = trp_ring.next()
                            pv2 = bank_bf(bt2)
                            lst = []
                            if full:
                                lst += [(0, qr[sl][:, 640:768]), (1, qr[sl][:, 768:896])]
                            lst += [(2, qr[sl][:, 896:1024]), (3, qr[sl][:, 1024:1152])]
                            for c, src in lst:
                                S.op("pe", lambda c=c, src=src: nc.tensor.transpose(pv2[:, c * 128:(c + 1) * 128], src, ident_b[:]),
                                     reads=[r_qr[sl], r_ident], writes=[RB[bt2]], inc=(c == 3))
                            if full:
                                evac(qcT[:, :, t * 128:(t + 1) * 128], pv2[:, 0:256].rearrange("p (c t) -> p c t", c=2),
                                     [RB[bt2]], [r_qcT[t]])
                            evac(kcT[:, :, t * 128:(t + 1) * 128], pv2[:, 256:512].rearrange("p (c t) -> p c t", c=2),
                                 [RB[bt2]], [r_kcT[t]])

                        import os as _os
                        _ntl = int(_os.environ.get("P1_TILES", NT))
                        _nst = int(_os.environ.get("P1_STAGES", 3))
                        pipeline(_ntl, [(s2, 2), (s1f, 1), (s0, 0)][3 - _nst:])
                        if b == 0 and l == 0:
                            print("sbuf remaining p1", nc.sbuf_bytes_remaining)
                        S.barrier()
                    if stop == "p1x":
                        print("NOPS", getattr(S, "nops", 0))
                        dump("hT0", hT[0][:], r_hT[0])
                        dbg_wait()
                        return nc
                    if stop in ("p1", "p1dbg") and dbg is not None:
                        dump("kbT", kbT[:], r_kbT[0]); dump("kcT", kcT[:], r_kcT[0]); dump("qbT", qbT[:, :, 0:ntq * 128], r_qbT[0])
                        dump("qcT", qcT[:, :, 0:ntq * 128], r_qcT[0]); dump("vb", vb[:], r_vb[0]); dump("vc", vc[:], r_vc[0])
                        dump("catA", catA[:, :, 0:ntq * 128], r_catA[0])
                        dbg_wait()
                        return nc

                    with ExitStack() as p2:
                        catBC = T("catBC", [128, 6, NTOK], BF16, p2)
                        w_out = T("w_out", [128, 8, D], BF16, p2); r_wout = Res("w_out")
                        for kc in range(8):
                            S.dma("pool", g_w[2], w_out[:, kc, :], wout_d[l, kc * 128:(kc + 1) * 128, :], writes=[r_wout])
                        NP = 10
                        pT = [T(f"pT{i}", [128, 512], BF16, p2) for i in range(NP)]; r_pT = [Res(f"pT{i}") for i in range(NP)]
                        pT_ring = Ring(range(NP))
                        zst = [T(f"zst{i}", [128, 512], BF16, p2) for i in range(2)]; r_zst = [Res(f"zst{i}") for i in range(2)]
                        zst_ring = Ring(range(2))
                        rr = [T("rr0", [128, 1024], F32, p2)] * 2; r_rr = [Res("rr0")] * 2
                        Rsb = [T("Rsb0", [128, 1024], F32, p2)] * 2; r_Rsb = [Res("Rsb0")] * 2
                        yy = [T(f"yy{i}", [128, 512], F32, p2) for i in range(2)]; r_yy = [Res(f"yy{i}") for i in range(2)]
                        y1 = [T(f"y1{i}", [128, 512], F32, p2) for i in range(2)]; r_y1 = [Res(f"y1{i}") for i in range(2)]
                        ysq = [T(f"ysq{i}", [128, 512], F32, p2) for i in range(2)]; r_ysq = [Res(f"ysq{i}") for i in range(2)]
                        rsd = [T(f"rsd{i}", [128, 512], F32, p2) for i in range(2)]; r_rsd = [Res(f"rsd{i}") for i in range(2)]

                        s_ring = Ring([0, 1, 2, 3])
                        acc_ring = Ring([4, 5])
                        R_ring = Ring([6, 7])
                        units = [(kvh, n) for n in range(ntq) for kvh in range(2)]
                        bstate = {}
                        bacc = {}

                        def b_scores(ui):
                            kvh, n = units[ui]
                            if n < 16:
                                kts = []
                                if n > 0:
                                    kts.append((n - 1, 0))
                                kts.append((n, None))
                                if n < 15:
                                    kts.append((n + 1, 1))
                                kts += [(16, None), (17, None)]
                            else:
                                kts = [(16, None), (17, None)]
                            pb0 = kvh * 64
                            plist = []
                            for (kt, mk) in kts:
                                sb = s_ring.next()
                                S.op("pe", lambda sb=sb, kt=kt: nc.tensor.matmul(
                                    PB[sb][:, :], lhsT=kbT[pb0:pb0 + 64, kt * 128:(kt + 1) * 128],
                                    rhs=qbT[pb0:pb0 + 64, :, n * 128:(n + 1) * 128], start=True, stop=True),
                                    reads=[r_kbT[kt], r_qbT[n]], writes=[RB[sb]])
                                pi = pT_ring.next()
                                S.op("act", lambda sb=sb, pi=pi: nc.scalar.activation(pT[pi][:], PB[sb][:, :], AF.Exp, scale=0.125),
                                     reads=[RB[sb]], writes=[r_pT[pi]])
                                if mk is not None:
                                    S.op("dve", lambda pi=pi, mk=mk: nc.vector.tensor_tensor(
                                        pT[pi][:].rearrange("p (g q) -> p g q", g=4), pT[pi][:].rearrange("p (g q) -> p g q", g=4),
                                        mask_b[:, mk, :].unsqueeze(1).to_broadcast([128, 4, 128]), ALU.mult),
                                        reads=[r_pT[pi], r_ident], writes=[r_pT[pi]])
                                plist.append((kt, pi))
                            bstate[ui] = plist

                        def b_pv(ui):
                            kvh, n = units[ui]
                            plist = bstate.pop(ui)
                            ab = acc_ring.next()
                            nk = len(plist)
                            for j, (kt, pi) in enumerate(plist):
                                S.op("pe", lambda j=j, kt=kt, pi=pi: nc.tensor.matmul(
                                    PB[ab][0:65, :], lhsT=vb[:, kt, kvh, :], rhs=pT[pi][:], start=(j == 0), stop=(j == nk - 1)),
                                    reads=[r_vb[kt], r_vones, r_pT[pi]], writes=[RB[ab]], inc=(j == nk - 1))
                            bacc[ui] = ab

                        def b_post(ui):
                            kvh, n = units[ui]
                            ab = bacc.pop(ui)
                            ri = ui % 2
                            hd0 = kvh * 4
                            S.op("dve", lambda: nc.vector.tensor_tensor(
                                rr[ri][64:65, 0:512].rearrange("p (g q) -> p g q", g=4),
                                PB[ab][64:65, :].rearrange("p (g q) -> p g q", g=4),
                                esink[64:65, l * 8 + hd0:l * 8 + hd0 + 4].unsqueeze(2).to_broadcast([1, 4, 128]),
                                ALU.add), reads=[RB[ab], r_const], writes=[r_rr[ri]])
                            S.op("dve", lambda: nc.vector.reciprocal(rr[ri][64:65, 0:512], rr[ri][64:65, 0:512]),
                                 reads=[r_rr[ri]], writes=[r_rr[ri]])
                            rb = R_ring.next()
                            S.op("pe", lambda: nc.tensor.matmul(PB[rb][0:64, :], lhsT=ones_f[64:65, 0:64], rhs=rr[ri][64:65, 0:512],
                                                                start=True, stop=True),
                                 reads=[r_rr[ri], r_ident], writes=[RB[rb]])
                            S.op("act", lambda: nc.scalar.copy(Rsb[ri][0:64, 0:512], PB[rb][0:64, :]), reads=[RB[rb]], writes=[r_Rsb[ri]])
                            c0 = kvh * 2
                            v4 = lambda ap: ap.rearrange("p (g q) -> p g q", g=4)
                            S.op("dve", lambda: nc.vector.tensor_tensor(
                                catBC[0:64, c0:c0 + 2, n * 128:(n + 1) * 128], v4(PB[ab][0:64, :])[:, 0:4:2, :],
                                v4(Rsb[ri][0:64, 0:512])[:, 0:4:2, :], ALU.mult),
                                reads=[RB[ab], r_Rsb[ri]], writes=[r_catB[n]])
                            zi = zst_ring.next()
                            S.op("dve", lambda: nc.vector.tensor_tensor(
                                zst[zi][0:64, 0:256].rearrange("p (g q) -> p g q", g=2), v4(PB[ab][0:64, :])[:, 1:4:2, :],
                                v4(Rsb[ri][0:64, 0:512])[:, 1:4:2, :], ALU.mult),
                                reads=[RB[ab], r_Rsb[ri]], writes=[r_zst[zi]])
                            S.dma("sp", g_zst[zi], catBC[64:128, c0:c0 + 2, n * 128:(n + 1) * 128],
                                  zst[zi][0:64, 0:256].rearrange("p (g q) -> p g q", g=2), reads=[r_zst[zi]], writes=[r_catB[n]])

                        if b == 0 and l == 0:
                            print("sbuf remaining p2", nc.sbuf_bytes_remaining)
                        pipeline(len(units), [(b_post, 2), (b_scores, 0), (b_pv, 1)])

                        qgroups = [(g * 512, 512, list(range(NT)), list(range(4 * g, 4 * g + 4))) for g in range(4)]
                        if not last:
                            qgroups.append((2048, 256, [16, 17], [16, 17]))
                        cunits = [(h, qg) for qg in qgroups for h in range(4)]
                        s_ring = Ring([0, 1])
                        scl = 32 ** -0.5
                        acc_sets = [(2, 3), (4, 5)]
                        pending = []

                        def c_post(ui, h, q0, W, qtiles, accb):
                            hc = h // 2
                            ri = ui % 2
                            po = 0
                            pr_ = 64
                            for m in range(2):
                                S.op("dve", lambda m=m: nc.vector.reciprocal(rr[ri][pr_:pr_ + 1, m * 512:m * 512 + W],
                                                                              PB[accb[m]][pr_:pr_ + 1, 0:W]),
                                     reads=[RB[accb[m]]], writes=[r_rr[ri]])
                            S.op("dve", lambda: nc.vector.tensor_scalar(rr[ri][pr_:pr_ + 1, 512:512 + W], rr[ri][pr_:pr_ + 1, 512:512 + W],
                                                                         neglam[pr_:pr_ + 1, l:l + 1], None, ALU.mult),
                                 reads=[r_rr[ri], r_const], writes=[r_rr[ri]])
                            for m in range(2):
                                S.op("pe", lambda m=m: nc.tensor.matmul(PB[6 + m][0:64, 0:W], lhsT=ones_f[pr_:pr_ + 1, 0:64],
                                                                        rhs=rr[ri][pr_:pr_ + 1, m * 512:m * 512 + W], start=True, stop=True),
                                     reads=[r_rr[ri], r_ident], writes=[RB[6 + m]])
                                S.op("act", lambda m=m: nc.scalar.copy(Rsb[ri][po:po + 64, m * 512:m * 512 + W], PB[6 + m][po:po + 64, 0:W]),
                                     reads=[RB[6 + m]], writes=[r_Rsb[ri]])
                            S.op("dve", lambda: nc.vector.tensor_tensor(yy[ri][po:po + 64, 0:W], PB[accb[0]][po:po + 64, 0:W],
                                                                         Rsb[ri][po:po + 64, 0:W], ALU.mult),
                                 reads=[RB[accb[0]], r_Rsb[ri]], writes=[r_yy[ri]])
                            S.op("dve", lambda: nc.vector.tensor_tensor(y1[ri][po:po + 64, 0:W], PB[accb[1]][po:po + 64, 0:W],
                                                                         Rsb[ri][po:po + 64, 512:512 + W], ALU.mult),
                                 reads=[RB[accb[1]], r_Rsb[ri]], writes=[r_y1[ri]])
                            S.op("dve", lambda: nc.vector.tensor_tensor(yy[ri][po:po + 64, 0:W], yy[ri][po:po + 64, 0:W],
                                                                         y1[ri][po:po + 64, 0:W], ALU.add),
                                 reads=[r_yy[ri], r_y1[ri]], writes=[r_yy[ri]])
                            S.op("act", lambda: nc.scalar.activation(ysq[ri][po:po + 64, 0:W], yy[ri][po:po + 64, 0:W], AF.Square),
                                 reads=[r_yy[ri]], writes=[r_ysq[ri]])
                            S.op("pe", lambda: nc.tensor.matmul(PB[6][0:64, 0:W], lhsT=onesmean[po:po + 64, 0:64], rhs=ysq[ri][po:po + 64, 0:W],
                                                                start=True, stop=True),
                                 reads=[r_ysq[ri], r_ident], writes=[RB[6]])
                            S.op("act", lambda: nc.scalar.activation(rsd[ri][po:po + 64, 0:W], PB[6][po:po + 64, 0:W], AF.Sqrt,
                                                                     bias=eps_t[po:po + 64, 0:1]),
                                 reads=[RB[6], r_ident], writes=[r_rsd[ri]])
                            S.op("dve", lambda: nc.vector.reciprocal(rsd[ri][po:po + 64, 0:W], rsd[ri][po:po + 64, 0:W]),
                                 reads=[r_rsd[ri]], writes=[r_rsd[ri]])
                            if h % 2 == 0:
                                S.op("dve", lambda: nc.vector.scalar_tensor_tensor(
                                    catBC[0:64, 4 + hc, q0:q0 + W], yy[ri][0:64, 0:W], gsub[0:64, l:l + 1],
                                    rsd[ri][0:64, 0:W], ALU.mult, ALU.mult),
                                    reads=[r_yy[ri], r_rsd[ri], r_const], writes=[r_catC[qt] for qt in qtiles])
                            else:
                                zi = zst_ring.next()
                                S.op("dve", lambda: nc.vector.scalar_tensor_tensor(
                                    zst[zi][0:64, 0:W], yy[ri][0:64, 0:W], gsub[0:64, l:l + 1],
                                    rsd[ri][0:64, 0:W], ALU.mult, ALU.mult),
                                    reads=[r_yy[ri], r_rsd[ri], r_const], writes=[r_zst[zi]])
                                S.dma("sp", g_zst[zi], catBC[64:128, 4 + hc, q0:q0 + W], zst[zi][0:64, 0:W],
                                      reads=[r_zst[zi]], writes=[r_catC[qt] for qt in qtiles])

                        for ui, (h, (q0, W, kts, qtiles)) in enumerate(cunits):
                            hc = h // 2
                            accb = acc_sets[ui % 2]
                            nk = len(kts)
                            sbs = {}

                            def c_score(ki, h=h, q0=q0, W=W, kts=kts, qtiles=qtiles, hc=hc, sbs=sbs, nk=nk):
                                if ki == min(2, nk - 1) and pending:
                                    c_post(*pending.pop())
                                kt = kts[ki]
                                for m in range(2):
                                    pb = 32 * (2 * (h % 2) + m)
                                    sb = s_ring.next()
                                    kw = dict(tile_position=(96, 0)) if pb == 96 else {}
                                    S.op("pe", lambda sb=sb, pb=pb, kw=kw: nc.tensor.matmul(
                                        PB[sb][:, 0:W], lhsT=kcT[pb:pb + 32, hc, kt * 128:(kt + 1) * 128],
                                        rhs=qcT[pb:pb + 32, hc, q0:q0 + W], start=True, stop=True, **kw),
                                        reads=[r_kcT[kt]] + [r_qcT[qt] for qt in qtiles], writes=[RB[sb]])
                                    pi = pT_ring.next()
                                    S.op("act", lambda sb=sb, pi=pi: nc.scalar.activation(pT[pi][:, 0:W], PB[sb][:, 0:W], AF.Exp, scale=scl),
                                         reads=[RB[sb]], writes=[r_pT[pi]])
                                    sbs[(ki, m)] = pi

                            def c_pv(ki, h=h, W=W, kts=kts, nk=nk, sbs=sbs, accb=accb):
                                kt = kts[ki]
                                lhs = vc[:, kt, h, :]
                                for m in range(2):
                                    pi = sbs.pop((ki, m))
                                    S.op("pe", lambda m=m, pi=pi: nc.tensor.matmul(
                                        PB[accb[m]][0:65, 0:W], lhsT=lhs, rhs=pT[pi][:, 0:W], start=(ki == 0), stop=(ki == nk - 1)),
                                        reads=[r_vc[kt], r_vones, r_pT[pi]], writes=[RB[accb[m]]], inc=(ki == nk - 1))

                            pipeline(nk, [(c_score, 0), (c_pv, 1)])
                            pending.append((ui, h, q0, W, qtiles, accb))
                        while pending:
                            c_post(*pending.pop())

                        if stop == "p2" and dbg is not None:
                            dump("catA", catA[:, :, 0:ntq * 128], r_catA[0])
                            for t in range(ntq):
                                S._emit_waits("sp", S._deps("sp", [r_catB[t], r_catC[t], r_catA[t]], []))
                            dump("catBC", catBC[:, :, 0:ntq * 128], r_catA[0])
                            for t in range(0):
                                S._emit_waits("sp", S._deps("sp", [r_catB[t], r_catC[t], r_catA[t]], []))
                            dbg_wait()
                            return nc

                        load_modb(0, b, l, 2)
                        load_modb(2, 2, l, 2)
                        pair_ring = Ring([(0, 1), (2, 3), (4, 5), (6, 7)])
                        pstate = {}

                        def o_mm(t):
                            b0, b1 = pair_ring.next()
                            for nh, bi in enumerate((b0, b1)):
                                for kc in range(8):
                                    S.op("pe", lambda nh=nh, bi=bi, kc=kc: nc.tensor.matmul(
                                        PB[bi][:, :], lhsT=(catA[:, kc, t * 128:(t + 1) * 128] if kc < 2 else catBC[:, kc - 2, t * 128:(t + 1) * 128]),
                                        rhs=w_out[:, kc, nh * 512:(nh + 1) * 512],
                                        start=(kc == 0), stop=(kc == 7)),
                                        reads=[r_catA[t], r_catB[t], r_catC[t], r_wout], writes=[RB[bi]], inc=(kc == 7))
                            pstate[t] = (b0, b1)

                        def o_res(t):
                            b0, b1 = pstate.pop(t)
                            xi = load_x(b, lsrc, t)
                            xo = xout_ring.next()
                            gi = 0 if t < 16 else 2
                            for nh, bi in enumerate((b0, b1)):
                                S.op("dve", lambda nh=nh, bi=bi: nc.vector.tensor_tensor(
                                    xout[xo][:, nh * 512:(nh + 1) * 512], PB[bi][:, :], modb[gi][:, nh * 512:(nh + 1) * 512], ALU.mult),
                                    reads=[RB[bi], r_modb[gi]], writes=[r_xout[xo]])
                            S.op("dve", lambda: nc.vector.tensor_tensor(xout[xo][:], xout[xo][:], xin[xi][:], ALU.add),
                                 reads=[r_xout[xo], r_xin[xi]], writes=[r_xout[xo]])
                            S.dma("sp", g_xout[xo], x_dst(b, t), xout[xo][:], reads=[r_xout[xo]], writes=[r_dx[b][t]])

                        pipeline(ntq, [(o_mm, 0), (o_res, 1)])
                        S.barrier()
                if stop == "s1":
                    break

                load_modb(0, b, l, 4)
                load_modb(1, b, l, 3)
                load_modb(2, 2, l, 4)
                load_modb(3, 2, l, 3)
                ncols = ntq * 128
                with ExitStack() as s2c:
                    h2T = T("h2T", [128, 8, NTOK], BF16, s2c); r_h2T = [Res(f"h2T{t}") for t in range(NT)]
                    hid = T("hid", [128, 8, NTOK], BF16, s2c); r_hid = [Res(f"hid{j}") for j in range(8)]
                    wdn = T("wdn", [128, 8, D], BF16, s2c); r_wdn = Res("wdn")
                    wup = [T(f"wup{i}", [128, 8, 256], BF16, s2c) for i in range(3)]; r_wup = [Res(f"wup{i}") for i in range(3)]
                    tbuf = [T(f"tbuf{i}", [128, NTOK], F32, s2c) for i in range(3)]; r_tbuf = [Res(f"tbuf{i}") for i in range(3)]
                    tb_ring = Ring(range(3))
                    hb2 = [T(f"hb2_{i}", [128, D], BF16, s2c) for i in range(2)]; r_hb2 = [Res(f"hb2_{i}") for i in range(2)]
                    t12 = [T("t12_0", [128, D], F32, s2c)] * 2; r_t12 = [Res("t12_0")] * 2
                    trp_ring = Ring([0, 1, 2, 3])
                    if b == 0 and l == 0:
                        print("sbuf remaining ffn", nc.sbuf_bytes_remaining)
                    def f0a(t):
                        isctx = t >= 16
                        xi = load_x(b, l + 1, t)
                        sl = t % 2
                        norm_tile(xi, 2 if isctx else 0, 3 if isctx else 1, hb2[sl][:], r_hb2[sl], t12[sl][:], r_t12[sl], (t % 4) * 2)

                    def f0b(t):
                        sl = t % 2
                        transpose8(hb2[sl], r_hb2[sl], trp_ring.next(), h2T[:, :, t * 128:(t + 1) * 128], r_h2T[t])

                    pipeline(ntq, [(f0b, 1), (f0a, 0)])
                    blocks = [(g * 512, 512, list(range(4 * g, 4 * g + 4)), g > 0) for g in range(4)]
                    if not last:
                        blocks.append((2048, 256, [16, 17], False))
                    wk = 0
                    load_modb(0, b, l, 5)
                    load_modb(2, 2, l, 5)
                    for (j0, npart) in ((0, 8), (8, 7), (15, 7)):
                        for part in range(npart):
                            jj = j0 + part
                            S.dma("pool", g_wdn, wdn[:, part, :], wdn_d[l, jj * 128:(jj + 1) * 128, :],
                                  reads=[], writes=[r_wdn])
                        up_ring = Ring([0, 1, 2, 3, 4, 5])
                        for j in range(npart):
                            jj = j0 + j
                            wi = wk % 3
                            wk += 1
                            S.dma("pool", g_w[3 + wi], wup[wi][:], wup_d[l, jj], writes=[r_wup[wi]])
                            tbs = []
                            for row in range(2):
                                ch = jj + row * NCH
                                ti_ = tb_ring.next()
                                tbs.append(ti_)
                                tbv = tbuf[ti_]
                                prev = None
                                for (c0, W, tiles, cont) in blocks:
                                    bi = up_ring.next()
                                    for kc in range(8):
                                        S.op("pe", lambda kc=kc, bi=bi, c0=c0, W=W, row=row: nc.tensor.matmul(
                                            PB[bi][:, 0:W], lhsT=wup[wi][:, kc, row * 128:(row + 1) * 128], rhs=h2T[:, kc, c0:c0 + W],
                                            start=(kc == 0), stop=(kc == 7)),
                                            reads=[r_wup[wi]] + [r_h2T[t] for t in tiles], writes=[RB[bi]], inc=(kc == 7))
                                    S.op("act", lambda bi=bi, c0=c0, W=W, ch=ch, tbv=tbv: nc.scalar.activation(
                                        tbv[:, c0:c0 + W], PB[bi][:, 0:W], AF.Identity, scale=cw[:, l, 1, ch:ch + 1], bias=cb[:, l, ch:ch + 1]),
                                        reads=[RB[bi], r_const], writes=[r_tbuf[ti_]])
                                    S.op("dve", lambda bi=bi, c0=c0, W=W, ch=ch, tbv=tbv: nc.vector.scalar_tensor_tensor(
                                        tbv[:, c0 + 1:c0 + W], PB[bi][:, 0:W - 1], cw[:, l, 0, ch:ch + 1], tbv[:, c0 + 1:c0 + W],
                                        ALU.mult, ALU.add), reads=[RB[bi], r_const, r_tbuf[ti_]], writes=[r_tbuf[ti_]])
                                    S.op("dve", lambda bi=bi, c0=c0, W=W, ch=ch, tbv=tbv: nc.vector.scalar_tensor_tensor(
                                        tbv[:, c0:c0 + W - 1], PB[bi][:, 1:W], cw[:, l, 2, ch:ch + 1], tbv[:, c0:c0 + W - 1],
                                        ALU.mult, ALU.add), reads=[RB[bi], r_const, r_tbuf[ti_]], writes=[r_tbuf[ti_]])
                                    if cont:
                                        pbi, pW = prev
                                        S.op("dve", lambda bi=bi, c0=c0, ch=ch, tbv=tbv, pbi=pbi, pW=pW: nc.vector.scalar_tensor_tensor(
                                            tbv[:, c0:c0 + 1], PB[pbi][:, pW - 1:pW], cw[:, l, 0, ch:ch + 1], tbv[:, c0:c0 + 1],
                                            ALU.mult, ALU.add), reads=[RB[pbi], r_const, r_tbuf[ti_]], writes=[r_tbuf[ti_]])
                                        S.op("dve", lambda bi=bi, c0=c0, ch=ch, tbv=tbv: nc.vector.scalar_tensor_tensor(
                                            tbv[:, c0 - 1:c0], PB[bi][:, 0:1], cw[:, l, 2, ch:ch + 1], tbv[:, c0 - 1:c0],
                                            ALU.mult, ALU.add), reads=[RB[bi], r_const, r_tbuf[ti_]], writes=[r_tbuf[ti_]])
                                    prev = (bi, W)
                            tg, tv = tbs
                            S.op("act", lambda tg=tg: nc.scalar.activation(tbuf[tg][:, 0:ncols], tbuf[tg][:, 0:ncols], AF.Silu),
                                 reads=[r_tbuf[tg]], writes=[r_tbuf[tg]])
                            S.op("dve", lambda tg=tg, tv=tv, j=j: nc.vector.tensor_tensor(
                                hid[:, j, 0:ncols], tbuf[tg][:, 0:ncols], tbuf[tv][:, 0:ncols], ALU.mult),
                                reads=[r_tbuf[tg], r_tbuf[tv]], writes=[r_hid[j]])
                        pair_ring = Ring([(0, 1), (2, 3), (4, 5), (6, 7)])
                        pstate = {}

                        def d_mm(t):
                            b0, b1 = pair_ring.next()
                            for nh, bi in enumerate((b0, b1)):
                                for j in range(npart):
                                    S.op("pe", lambda nh=nh, bi=bi, j=j: nc.tensor.matmul(
                                        PB[bi][:, :], lhsT=hid[:, j, t * 128:(t + 1) * 128], rhs=wdn[:, j, nh * 512:(nh + 1) * 512],
                                        start=(j == 0), stop=(j == npart - 1)),
                                        reads=[r_hid[j], r_wdn], writes=[RB[bi]], inc=(j == npart - 1))
                            pstate[t] = (b0, b1)

                        def d_res(t):
                            b0, b1 = pstate.pop(t)
                            xi = load_x(b, l + 1, t)
                            xo = xout_ring.next()
                            gi = 0 if t < 16 else 2
                            for nh, bi in enumerate((b0, b1)):
                                S.op("dve", lambda nh=nh, bi=bi: nc.vector.tensor_tensor(
                                    xout[xo][:, nh * 512:(nh + 1) * 512], PB[bi][:, :], modb[gi][:, nh * 512:(nh + 1) * 512], ALU.mult),
                                    reads=[RB[bi], r_modb[gi]], writes=[r_xout[xo]])
                            S.op("dve", lambda: nc.vector.tensor_tensor(xout[xo][:], xout[xo][:], xin[xi][:], ALU.add),
                                 reads=[r_xout[xo], r_xin[xi]], writes=[r_xout[xo]])
                            S.dma("sp", g_xout[xo], x_dst(b, t), xout[xo][:], reads=[r_xout[xo]], writes=[r_dx[b][t]])

                        pipeline(ntq, [(d_mm, 0), (d_res, 1)])
                    S.barrier()
        S.finish()
        build_nc.stats = (S.ninst, S.nwait)
    return nc


def _rope_tables(head_dim):
    rows = SEQ // GRID_W
    row = np.repeat(np.arange(rows, dtype=np.float32), GRID_W)
    col = np.tile(np.arange(GRID_W, dtype=np.float32), rows)
    n_freq = head_dim // 4
    inv_freq = (np.float32(10000.0) ** (-np.arange(n_freq, dtype=np.float32) / np.float32(n_freq))).astype(np.float32)
    ang = np.stack([row[:, None] * inv_freq, col[:, None] * inv_freq], axis=1).astype(np.float32)
    cos = np.cos(ang).astype(np.float32)
    sin = np.sin(ang).astype(np.float32)
    C2 = np.stack([cos, cos], axis=2).reshape(SEQ, 4 * n_freq)
    S2 = np.stack([-sin.reshape(SEQ, 2 * n_freq), sin.reshape(SEQ, 2 * n_freq)], axis=1)
    C2 = C2.reshape(16, 128, 4 * n_freq).transpose(1, 0, 2)
    S2 = S2.reshape(16, 128, 2, 2 * n_freq).transpose(1, 0, 2, 3)
    return np.ascontiguousarray(C2), np.ascontiguousarray(S2)


def prep_shared(inp):
    f = lambda a: np.ascontiguousarray(np.asarray(a, dtype=np.float32))
    w_up = f(inp["w_up"])
    wu = w_up.reshape(L_ALL, 8, 128, 2, NCH, 128)
    w_up_r = np.ascontiguousarray(wu.transpose(0, 4, 2, 1, 3, 5)).reshape(L_ALL, NCH, 128, 8, 256)
    a_wsT = np.ascontiguousarray(f(inp["a_ws"]).transpose(0, 3, 1, 2))
    smallg = np.concatenate([f(inp["b_qnorm"]), f(inp["b_knorm"]), f(inp["c_qnorm"]), f(inp["c_knorm"]),
                             f(inp["c_subln"])], axis=1)
    cwr = f(inp["conv_w"]).reshape(L_ALL, 3, 44, 128).transpose(3, 0, 1, 2)
    cbr = f(inp["conv_b"]).reshape(L_ALL, 44, 128).transpose(2, 0, 1)
    kk = np.arange(128)[:, None]
    qq = np.arange(128)[None, :]
    rbc, rbs = _rope_tables(64)
    rcc, rcs = _rope_tables(32)
    return {
        "w_ada_r": np.ascontiguousarray(f(inp["w_ada"]).reshape(L_ALL, 8, 128, 12, 512).transpose(0, 3, 2, 1, 4)),
        "b_ada": f(inp["b_ada"]),
        "gn": np.ascontiguousarray(np.stack([f(inp["norm1_g"]), f(inp["norm2_g"])], axis=1)),
        "w_in": f(inp["w_in"]), "w_out": f(inp["w_out"]), "w_up_r": w_up_r, "w_down": f(inp["w_down"]),
        "a_wsT": a_wsT, "a_bs": f(inp["a_bs"]), "smallg": np.ascontiguousarray(smallg),
        "b_sink": f(inp["b_sink"]).reshape(1, -1), "c_lam": f(inp["c_lam"]).reshape(1, -1),
        "sublnT": np.ascontiguousarray(f(inp["c_subln"]).T),
        "cw_r": np.ascontiguousarray(cwr), "cb_r": np.ascontiguousarray(cbr),
        "ident": np.eye(128, dtype=np.float32),
        "maskL": (qq <= kk).astype(np.float32), "maskR": (kk <= qq).astype(np.float32),
        "ropeB_C2": rbc, "ropeB_S": rbs, "ropeC_C2": rcc, "ropeC_S": rcs,
    }


def core_inputs(inp, shared, core, nb=NB):
    f = lambda a: np.ascontiguousarray(np.asarray(a, dtype=np.float32))
    b0 = core * nb
    d = dict(shared)
    d["x"] = f(inp["x"][b0:b0 + nb])
    d["ctx"] = f(inp["ctx"][b0:b0 + nb])
    crow = np.zeros((3, D), np.float32)
    crow[0:nb] = np.asarray(inp["c"], np.float32)[b0:b0 + nb]
    crow[2] = np.asarray(inp["c_ctx"], np.float32)
    d["crow"] = crow
    return d


_NC_CACHE = {}


def kernel(**inputs):
    inp = {k: np.asarray(v) for k, v in inputs.items()}
    shared = prep_shared(inp)
    if "nc" not in _NC_CACHE:
        _NC_CACHE["nc"] = build_nc()
    nc = _NC_CACHE["nc"]
    in_maps = [core_inputs(inp, shared, c) for c in range(N_CORES)]
    res = run_bass_kernel_spmd(nc, in_maps, core_ids=list(range(N_CORES)))
    out = np.concatenate([np.asarray(r["y"], dtype=np.float32) for r in res.results], axis=0)
    return out
```

```python
import math
from contextlib import ExitStack

import numpy as np
import concourse.bass as bass
import concourse.mybir as mybir
from concourse.bass_utils import run_bass_kernel_spmd

F32 = mybir.dt.float32
BF16 = mybir.dt.bfloat16
AF = mybir.ActivationFunctionType
ALU = mybir.AluOpType
AX = mybir.AxisListType

L_ALL = 4
D = 1024
SEQ = 2048
LC = 256
NT = 18
NTOK = NT * 128
DFF = 2816
NCH = 22
EPS = 1e-6
GRID_W = 64
N_CORES = 8
NB = 2


class Res:
    __slots__ = ("name", "lw", "rd", "excl")

    def __init__(self, name, excl=False):
        self.name = name
        self.lw = None
        self.rd = {}
        self.excl = excl


class DmaGroup:
    __slots__ = ("sem", "cnt", "name")

    def __init__(self, sem, name):
        self.sem = sem
        self.cnt = 0
        self.name = name


class Sched:
    ENG = ("pe", "act", "dve", "pool", "sp")

    def __init__(self, nc, stack):
        self.nc = nc
        self.stack = stack
        self.eng = {"pe": nc.tensor, "act": nc.scalar, "dve": nc.vector,
                    "pool": nc.gpsimd, "sp": nc.sync}
        self.sem = {e: stack.enter_context(nc.semaphore("sem_" + e)) for e in self.ENG}
        self.cnt = {e: 0 for e in self.ENG}
        self.seen = {e: {} for e in self.ENG}
        self.nwait = 0
        self.ninst = 0
        self.groups = []

    def group(self, name):
        sem = self.stack.enter_context(self.nc.semaphore("dg_" + name))
        g = DmaGroup(sem, name)
        self.groups.append(g)
        return g

    def finish(self):
        for g in self.groups:
            if g.cnt:
                self.nc.sync.wait_ge(g.sem, g.cnt)

    def _deps(self, e, reads, writes):
        need = {}

        def add(ev, raw):
            if ev is None:
                return
            kind, src, count = ev
            if kind == "eng" and src == e:
                if e in ("pe", "sp"):
                    return
            key = (kind, src)
            if need.get(key, (None, 0))[1] < count:
                need[key] = (ev, count)

        for r in reads:
            add(r.lw, True)
            if r.excl:
                for k2, ev in r.rd.items():
                    if k2 != ("eng", e):
                        add(ev, False)
        for w in writes:
            add(w.lw, False)
            for ev in w.rd.values():
                add(ev, False)
        out = []
        for key, (ev, count) in need.items():
            if self.seen[e].get(key, 0) >= count:
                continue
            out.append((key, ev, count))
        return out

    def _emit_waits(self, e, deps):
        eng = self.eng[e]
        for key, ev, count in deps:
            kind, src, _ = ev
            if kind == "eng":
                assert count <= self.cnt[src], f"wait on un-inc'd instr {src} {count}>{self.cnt[src]}"
                sem = self.sem[src]
            else:
                sem = src.sem
            eng.wait_ge(sem, count)
            self.nwait += 1
            self.seen[e][key] = count

    def _mark(self, ev, key, reads, writes):
        for r in reads:
            r.rd[key] = ev
        for w in writes:
            w.lw = ev
            w.rd = {}

    def op(self, e, fn, reads=(), writes=(), inc=True):
        import os as _os
        self.nops = getattr(self, "nops", 0) + 1
        if self.nops > int(_os.environ.get("P1_OPLIMIT", 10 ** 9)):
            if not inc:
                return None
            return None
        self._emit_waits(e, self._deps(e, reads, writes))
        ins = fn()
        self.ninst += 1
        if inc:
            self.cnt[e] += 1
            ins.then_inc(self.sem[e], 1)
            ev = ("eng", e, self.cnt[e])
        else:
            ev = ("eng", e, self.cnt[e] + 1)
        self._mark(ev, ("eng", e), reads, writes)
        return ins

    def dma(self, q, grp, out, in_, reads=(), writes=(), **kw):
        self._emit_waits(q, self._deps(q, reads, writes))
        ins = self.eng[q].dma_start(out=out, in_=in_, **kw)
        grp.cnt += 16
        ins.then_inc(grp.sem, 16)
        self.ninst += 1
        ev = ("dma", grp, grp.cnt)
        self._mark(ev, ("dma", grp), reads, writes)
        return ins

    def dma_batch(self, q, grp, items):
        allr, allw = [], []
        for it in items:
            allr += list(it.get("reads", ()))
            allw += list(it.get("writes", ()))
        self._emit_waits(q, self._deps(q, allr, allw))
        for it in items:
            ins = self.eng[q].dma_start(out=it["out"], in_=it["in_"], **it.get("kw", {}))
            grp.cnt += 16
            ins.then_inc(grp.sem, 16)
            self.ninst += 1
        ev = ("dma", grp, grp.cnt)
        self._mark(ev, ("dma", grp), allr, allw)

    def barrier(self):
        for e in self.ENG:
            for f in self.ENG:
                if self.cnt[f] == 0 or (f == e and e in ("pe", "sp")):
                    continue
                key = ("eng", f)
                if self.seen[e].get(key, 0) >= self.cnt[f]:
                    continue
                self.eng[e].wait_ge(self.sem[f], self.cnt[f])
                self.seen[e][key] = self.cnt[f]
                self.nwait += 1


class Ring:
    def __init__(self, items):
        self.items = list(items)
        self.i = 0

    def next(self):
        it = self.items[self.i % len(self.items)]
        self.i += 1
        return it


def pipeline(n, stages):
    mx = max(s for _, s in stages)
    for step in range(n + mx):
        for fn, sk in stages:
            i = step - sk
            if 0 <= i < n:
                fn(i)


def build_nc(depth=L_ALL, nb=NB, dbg=None, stop=None):
    nc = bass.Bass("TRN2", target_bir_lowering=False)

    def din(name, shape, dt=F32):
        return nc.dram_tensor(name, list(shape), dt, kind="ExternalInput").ap()

    x_d = din("x", [nb, SEQ, D])
    ctx_d = din("ctx", [nb, LC, D])
    crow_d = din("crow", [3, D])
    wada_d = din("w_ada_r", [L_ALL, 12, 128, 8, 512])
    bada_d = din("b_ada", [L_ALL, 6 * D])
    gn_d = din("gn", [L_ALL, 2, D])
    win_d = din("w_in", [L_ALL, D, 2048])
    wout_d = din("w_out", [L_ALL, D, D])
    wup_d = din("w_up_r", [L_ALL, NCH, 128, 8, 256])
    wdn_d = din("w_down", [L_ALL, DFF, D])
    awsT_d = din("a_wsT", [L_ALL, 128, 4, 128])
    abs_d = din("a_bs", [L_ALL, 4, 128])
    smallg_d = din("smallg", [L_ALL, 256])
    sink_d = din("b_sink", [1, L_ALL * 8])
    clam_d = din("c_lam", [1, L_ALL * 128])
    sublnT_d = din("sublnT", [64, L_ALL])
    cw_d = din("cw_r", [128, L_ALL, 3, 44])
    cb_d = din("cb_r", [128, L_ALL, 44])
    ident_d = din("ident", [128, 128])
    maskL_d = din("maskL", [128, 128])
    maskR_d = din("maskR", [128, 128])
    rbc_d = din("ropeB_C2", [128, 16, 64])
    rbs_d = din("ropeB_S", [128, 16, 2, 32])
    rcc_d = din("ropeC_C2", [128, 16, 32])
    rcs_d = din("ropeC_S", [128, 16, 2, 16])
    y_d = nc.dram_tensor("y", [nb, SEQ, D], F32, kind="ExternalOutput").ap()
    ctxs_d = nc.dram_tensor("ctx_s", [nb, LC, D], F32).ap()
    mod_d = nc.dram_tensor("mod_s", [3, L_ALL, 6, D], F32).ap()

    with ExitStack() as st:
        S = Sched(nc, st)

        uid = [0]

        def T(name, shape, dt, stack=st):
            uid[0] += 1
            return stack.enter_context(nc.sbuf_tensor(f"sb{uid[0]}_{name}", list(shape), dt))

        PB = [st.enter_context(nc.psum_tensor(f"pb{i}", [128, 512], F32)) for i in range(8)]
        RB = [Res(f"pb{i}", excl=True) for i in range(8)]

        def bank_bf(i):
            return PB[i][:].bitcast(BF16)

        g_const = S.group("const")
        g_xin = [S.group(f"xin{i}") for i in range(2)]
        g_xout = [S.group(f"xout{i}") for i in range(2)]
        g_modb = [S.group(f"modb{i}") for i in range(4)]
        g_w = [S.group(f"w{i}") for i in range(6)]
        g_wdn = S.group("wdn")
        g_pre = [S.group(f"pre{i}") for i in range(6)]
        g_zst = [S.group(f"zst{i}") for i in range(2)]
        g_mod = S.group("modw")
        g_dbg = S.group("dbg")

        dbg_groups = []

        def dbg_wait():
            S.finish()

        def dump(name, ap, res):
            if dbg is None:
                return
            d = nc.dram_tensor("dbg_" + name, list(ap.shape), ap.dtype, kind="ExternalOutput").ap()
            gg = S.group("dbg_" + name)
            dbg_groups.append(gg)
            S.dma("sp", gg, d, ap, reads=[res])
            dbg.append(name)

        ident_f = T("ident_f", [128, 128], F32); r_ident = Res("ident")
        ident_b = T("ident_b", [128, 128], BF16)
        ones_f = T("ones_f", [128, 128], F32)
        onesmean = T("onesmean", [128, 128], F32)
        mask_f = T("mask_f", [128, 2, 128], F32)
        mask_b = T("mask_b", [128, 2, 128], BF16)
        ropeB_C = T("ropeB_C", [128, 16, 64], F32)
        ropeB_S = T("ropeB_S", [128, 16, 2, 32], F32)
        ropeC_C = T("ropeC_C", [128, 16, 32], F32)
        ropeC_S = T("ropeC_S", [128, 16, 2, 16], F32)
        smallg = T("smallg", [128, L_ALL, 256], F32)
        biasT = T("biasT", [128, L_ALL, 2, 128], F32)
        cw = T("cw", [128, L_ALL, 3, 44], F32)
        cb = T("cb", [128, L_ALL, 44], F32)
        esink = T("esink", [128, L_ALL * 8], F32)
        clam = T("clam", [128, L_ALL, 4, 32], F32)
        lamt = T("lamt", [128, L_ALL, 2, 32], F32)
        lam2 = T("lam2", [128, L_ALL, 2], F32)
        neglam = T("neglam", [128, L_ALL], F32)
        gsub = T("gsub", [128, L_ALL], F32)
        r_const = Res("const")

        items = [
            dict(out=ident_f[:], in_=ident_d),
            dict(out=mask_f[:, 0, :], in_=maskL_d),
            dict(out=mask_f[:, 1, :], in_=maskR_d),
            dict(out=ropeB_C[:], in_=rbc_d),
            dict(out=ropeB_S[:], in_=rbs_d),
            dict(out=ropeC_C[:], in_=rcc_d),
            dict(out=ropeC_S[:], in_=rcs_d),
            dict(out=smallg[:].rearrange("p l c -> p (l c)"),
                 in_=smallg_d.rearrange("l c -> (l c)").partition_broadcast(128)),
            dict(out=cw[:], in_=cw_d),
            dict(out=cb[:], in_=cb_d),
            dict(out=esink[:], in_=sink_d[0, :].partition_broadcast(128)),
            dict(out=clam[:].rearrange("p l a c -> p (l a c)"), in_=clam_d[0, :].partition_broadcast(128)),
            dict(out=gsub[0:64, :], in_=sublnT_d),
            dict(out=gsub[64:128, :], in_=sublnT_d),
        ]
        for l in range(L_ALL):
            for g in range(4):
                items.append(dict(out=biasT[(g % 2) * 64:(g % 2) * 64 + 64, l, g // 2, :],
                                  in_=abs_d[l, g, :].partition_broadcast(64)))
        for it in items:
            it["writes"] = [r_const]
        S.dma_batch("sp", g_const, items)

        S.op("dve", lambda: nc.vector.tensor_copy(ident_b[:], ident_f[:]), reads=[r_const], writes=[r_ident])
        S.op("dve", lambda: nc.vector.tensor_copy(mask_b[:], mask_f[:]), reads=[r_const], writes=[r_ident])
        S.op("dve", lambda: nc.vector.memset(ones_f[:], 1.0), writes=[r_ident])
        S.op("dve", lambda: nc.vector.memset(onesmean[:], 1.0 / 64.0), writes=[r_ident])
        S.op("act", lambda: nc.scalar.activation(esink[:], esink[:], AF.Exp), reads=[r_const], writes=[r_const])
        S.op("dve", lambda: nc.vector.tensor_tensor(lamt[:], clam[:, :, 0:4:2, :], clam[:, :, 1:4:2, :], ALU.mult),
             reads=[r_const], writes=[r_const])
        S.op("dve", lambda: nc.vector.tensor_reduce(lam2[:], lamt[:], AX.X, ALU.add), reads=[r_const], writes=[r_const])
        S.op("act", lambda: nc.scalar.activation(lam2[:], lam2[:], AF.Exp), reads=[r_const], writes=[r_const])
        S.op("dve", lambda: nc.vector.tensor_tensor(neglam[:], lam2[:, :, 1], lam2[:, :, 0], ALU.subtract),
             reads=[r_const], writes=[r_const])
        for l in range(L_ALL):
            lam_init = 0.8 - 0.6 * math.exp(-0.3 * l)
            S.op("dve", lambda l=l, li=lam_init: nc.vector.tensor_scalar(
                neglam[:, l:l + 1], neglam[:, l:l + 1], -li, None, ALU.add), reads=[r_const], writes=[r_const])
            S.op("dve", lambda l=l, li=lam_init: nc.vector.tensor_scalar(
                gsub[:, l:l + 1], gsub[:, l:l + 1], 1.0 - li, None, ALU.mult), reads=[r_const], writes=[r_const])

        if stop == "const":
            dump("neglam", neglam[:], r_const); dump("gsub", gsub[:], r_const); dump("esink", esink[:], r_const)
            dump("biasT", biasT[:], r_const); dump("mask_b", mask_b[:], r_ident)
            dbg_wait()
            return nc
        with ExitStack() as pp:
            crow = T("crow_sb", [3, D], F32, pp); r_crow = Res("crow")
            scT = T("scT", [128, 8, 3], F32, pp); r_scT = Res("scT")
            rows = T("rows", [3, 6 * D], F32, pp); r_rows = Res("rows")
            bada = T("bada", [3, 6 * D], F32, pp); r_bada = Res("bada")
            gnb = T("gnb", [3, 2, D], F32, pp); r_gnb = Res("gnb")
            wslots = [T(f"wada{i}", [128, 8, 512], F32, pp) for i in range(3)]
            r_wslots = [Res(f"wada{i}") for i in range(3)]
            g_ws = g_pre[0:3]
            g_pp = g_pre[3]
            S.dma("sp", g_pp, crow[:], crow_d, writes=[r_crow])
            S.op("act", lambda: nc.scalar.activation(crow[:], crow[:], AF.Silu), reads=[r_crow], writes=[r_crow])
            for c in range(8):
                S.op("pe", lambda c=c: nc.tensor.transpose(PB[0][:, c * 3:c * 3 + 3], crow[0:3, c * 128:(c + 1) * 128],
                                                           ident_f[0:3, 0:3]),
                     reads=[r_crow, r_const], writes=[RB[0]], inc=(c == 7))
            S.op("dve", lambda: nc.vector.tensor_copy(scT[:].rearrange("p c r -> p (c r)"), PB[0][:, 0:24]),
                 reads=[RB[0]], writes=[r_scT])
            pring = Ring([1, 2, 3])
            k = 0
            for l in range(depth):
                S.dma("sp", g_pre[4], bada[:], bada_d[l, :].partition_broadcast(3), writes=[r_bada])
                S.dma("sp", g_pre[5], gnb[:].rearrange("p a d -> p (a d)"),
                      gn_d[l].rearrange("a d -> (a d)").partition_broadcast(3), writes=[r_gnb])
                for n in range(12):
                    si = k % 3
                    k += 1
                    S.dma("sp", g_ws[si], wslots[si][:],
                          wada_d[l, n],
                          writes=[r_wslots[si]])
                    bi = pring.next()
                    for kc in range(8):
                        S.op("pe", lambda kc=kc, si=si, bi=bi: nc.tensor.matmul(
                            PB[bi][0:3, :], lhsT=scT[:, kc, :], rhs=wslots[si][:, kc, :],
                            start=(kc == 0), stop=(kc == 7)),
                            reads=[r_scT, r_wslots[si]], writes=[RB[bi]], inc=(kc == 7))
                    S.op("dve", lambda n=n, bi=bi: nc.vector.tensor_tensor(
                        rows[:, n * 512:(n + 1) * 512], PB[bi][0:3, :], bada[:, n * 512:(n + 1) * 512], ALU.add),
                        reads=[RB[bi], r_bada], writes=[r_rows])
                for a, kidx in ((0, 1), (1, 4)):
                    S.op("dve", lambda a=a, kidx=kidx: nc.vector.scalar_tensor_tensor(
                        rows[:, kidx * D:(kidx + 1) * D], rows[:, kidx * D:(kidx + 1) * D], 1.0, gnb[:, a, :],
                        ALU.add, ALU.mult), reads=[r_rows, r_gnb], writes=[r_rows])
                S.dma("sp", g_mod, mod_d[:, l].rearrange("r k d -> r (k d)"), rows[:], reads=[r_rows], writes=[])
            r_mod = Res("mod_d")
            r_mod.lw = ("dma", g_mod, g_mod.cnt)
            S.barrier()

        if stop == "prepass":
            if dbg is not None:
                d = nc.dram_tensor("dbg_mod", [3, 1, 6, D], F32, kind="ExternalOutput").ap()
                S.dma("sp", g_dbg, d, mod_d[:, 0:1], reads=[r_mod])
                dbg.append("mod")
            dbg_wait()
            return nc
        xin = [T(f"xin{i}", [128, D], F32) for i in range(2)]
        r_xin = [Res(f"xin{i}") for i in range(2)]
        xin_ring = Ring(range(2))
        xout = [T(f"xout{i}", [128, D], F32) for i in range(2)]
        r_xout = [Res(f"xout{i}") for i in range(2)]
        xout_ring = Ring(range(2))
        modb = [T(f"modb{i}", [128, D], F32) for i in range(4)]
        r_modb = [Res(f"modb{i}") for i in range(4)]
        stat = T("stat", [128, 64], F32)
        r_dx = [[Res(f"dx{b}_{t}") for t in range(NT)] for b in range(nb)]

        def x_src(b, l, t):
            if t < 16:
                base = x_d if l == 0 else y_d
                return base[b, t * 128:(t + 1) * 128, :]
            base = ctx_d if l == 0 else ctxs_d
            return base[b, (t - 16) * 128:(t - 15) * 128, :]

        def x_dst(b, t):
            if t < 16:
                return y_d[b, t * 128:(t + 1) * 128, :]
            return ctxs_d[b, (t - 16) * 128:(t - 15) * 128, :]

        def load_x(b, lsrc, t):
            i = xin_ring.next()
            S.dma("sp", g_xin[i], xin[i][:], x_src(b, lsrc, t), reads=[r_dx[b][t]], writes=[r_xin[i]])
            return i

        def load_modb(i, row, l, kind):
            S.dma("sp", g_modb[i], modb[i][:], mod_d[row, l, kind, :].partition_broadcast(128),
                  reads=[r_mod], writes=[r_modb[i]])

        ev_toggle = [0]

        def evac(out, in_, reads, writes):
            ev_toggle[0] ^= 1
            if ev_toggle[0]:
                S.op("act", lambda: nc.scalar.copy(out, in_), reads=reads, writes=writes)
            else:
                S.op("dve", lambda: nc.vector.tensor_copy(out, in_), reads=reads, writes=writes)

        def norm_tile(xi, mi, shi, hb, r_hb, t1, r_t1, ss_col):
            ss = stat[:, ss_col:ss_col + 1]
            rt = stat[:, ss_col + 1:ss_col + 2]
            r_st = r_stat[ss_col // 2]
            import os as _os
            _k = int(_os.environ.get("P1_S0", 99))
            if _k < 2:
                return
            S.op("act", lambda: nc.scalar.activation(t1, xin[xi][:], AF.Square),
                 reads=[r_xin[xi]], writes=[r_t1])
            S.op("dve", lambda: nc.vector.tensor_reduce(ss, t1, AX.X, ALU.add), reads=[r_t1], writes=[r_st])
            if _k < 3:
                return
            S.op("act", lambda: nc.scalar.activation(rt, ss, AF.Sqrt, scale=1.0 / D, bias=eps_t[:, 0:1]),
                 reads=[r_st, r_ident], writes=[r_st])
            if _k < 4:
                return
            S.op("dve", lambda: nc.vector.reciprocal(rt, rt), reads=[r_st], writes=[r_st])
            if _k < 5:
                return
            S.op("dve", lambda: nc.vector.scalar_tensor_tensor(t1, xin[xi][:], rt, modb[mi][:], ALU.mult, ALU.mult),
                 reads=[r_xin[xi], r_st, r_modb[mi]], writes=[r_t1])
            if _k < 6:
                return
            S.op("dve", lambda: nc.vector.tensor_tensor(hb, t1, modb[shi][:], ALU.add),
                 reads=[r_t1, r_modb[shi]], writes=[r_hb])

        r_stat = [Res(f"stat{i}") for i in range(8)]
        eps_t = T("eps_t", [128, 1], F32)
        S.op("dve", lambda: nc.vector.memset(eps_t[:], EPS), writes=[r_ident])

        def transpose8(hb, r_hb, bi, dst, r_dst):
            pv = bank_bf(bi)
            import os as _os
            _k = int(_os.environ.get("P1_S0", 99))
            if _k < 7:
                return
            for c in range(8):
                S.op("pe", lambda c=c: nc.tensor.transpose(pv[:, c * 128:(c + 1) * 128], hb[:, c * 128:(c + 1) * 128],
                                                           ident_b[:]),
                     reads=[r_hb, r_ident], writes=[RB[bi]], inc=(c == 7))
            if _k < 8:
                return
            evac(dst, pv[:, 0:1024].rearrange("p (c t) -> p c t", c=8), [RB[bi]], [r_dst])

        for b in range(nb):
            for l in range(depth):
                last = (l == depth - 1)
                ntq = 16 if last else 18
                lsrc = l
                load_modb(0, b, l, 1)
                load_modb(1, b, l, 0)
                load_modb(2, 2, l, 1)
                load_modb(3, 2, l, 0)
                with ExitStack() as s1:
                    kbT = T("kbT", [128, NTOK], BF16, s1); r_kbT = [Res(f"kbT{t}") for t in range(NT)]
                    vb = T("vb", [128, NT, 2, 65], BF16, s1); r_vb = [Res(f"vb{t}") for t in range(NT)]
                    kcT = T("kcT", [128, 2, NTOK], BF16, s1); r_kcT = [Res(f"kcT{t}") for t in range(NT)]
                    vc = T("vc", [128, NT, 4, 65], BF16, s1); r_vc = [Res(f"vc{t}") for t in range(NT)]
                    qbT = T("qbT", [128, 4, NTOK], BF16, s1); r_qbT = [Res(f"qbT{t}") for t in range(NT)]
                    qcT = T("qcT", [128, 2, NTOK], BF16, s1); r_qcT = [Res(f"qcT{t}") for t in range(NT)]
                    catA = T("catA", [128, 2, NTOK], BF16, s1)
                    r_catA = [Res(f"catA{t}") for t in range(NT)]
                    r_catB = [Res(f"catB{t}") for t in range(NT)]
                    r_catC = [Res(f"catC{t}") for t in range(NT)]
                    r_vones = Res("vones")
                    S.op("dve", lambda: nc.vector.memset(vb[:, :, :, 64:65], 1.0), writes=[r_vones])
                    S.op("dve", lambda: nc.vector.memset(vc[:, :, :, 64:65], 1.0), writes=[r_vones])

                    with ExitStack() as p1:
                        w_in = T("w_in", [128, 8, 2048], BF16, p1); r_win = Res("w_in")
                        awsT = T("awsT", [128, 4, 128], BF16, p1); r_aws = Res("awsT")
                        for kc in range(8):
                            S.dma("pool", g_w[0], w_in[:, kc, :], win_d[l, kc * 128:(kc + 1) * 128, :], writes=[r_win])
                        S.dma("pool", g_w[1], awsT[:], awsT_d[l], writes=[r_aws])
                        NS = 2
                        hb = [T(f"hb{i}", [128, D], BF16, p1) for i in range(NS)]; r_hb = [Res(f"hb{i}") for i in range(NS)]
                        t1 = [T("t1_0", [128, D], F32, p1)] * NS; r_t1 = [Res("t1_0")] * NS
                        hT = [T(f"hT{i}", [128, 8, 128], BF16, p1) for i in range(3)]; r_hT = [Res(f"hT{i}") for i in range(3)]
                        uT = [T(f"uT{i}", [128, 2, 128], BF16, p1) for i in range(NS)]; r_uT = [Res(f"uT{i}") for i in range(NS)]
                        gv = [T(f"gv{i}", [128, 256], F32, p1) for i in range(NS)]; r_gv = [Res(f"gv{i}") for i in range(NS)]
                        vpad = [T(f"vpad{i}", [128, 4, 128], BF16, p1) for i in range(NS)]; r_vpad = [Res(f"vpad{i}") for i in range(NS)]
                        sq = [T("sq0", [128, 1152], F32, p1)] * NS; r_sq = [Res("sq0")] * NS
                        qn = [T("qn0", [128, 1152], F32, p1)] * NS; r_qn = [Res("qn0")] * NS
                        tb = [T("tb0", [128, 1152], F32, p1)] * NS; r_tb = [Res("tb0")] * NS
                        qr = [T("qr0", [128, 1152], BF16, p1)] * NS; r_qr = [Res("qr0")] * NS
                        st2 = [T(f"st2_{i}", [128, 32], F32, p1) for i in range(NS)]; r_st2 = [Res(f"st2_{i}") for i in range(NS)]
                        ta = [T(f"ta{i}", [128, 256], F32, p1) for i in range(NS)]; r_ta = [Res(f"ta{i}") for i in range(NS)]
                        for i in range(NS):
                            S.op("dve", lambda i=i: nc.vector.memset(vpad[i][:], 0.0), writes=[r_vpad[i]])
                        order = [16, 17] + list(range(16))
                        trp_ring = Ring([0, 1, 2])
                        prj_ring = Ring([3, 4, 5, 6, 7])
                        xi_of = {}
                        banks_of = {}

                        def s0(i):
                            t = order[i]
                            isctx = t >= 16
                            xi = load_x(b, lsrc, t)
                            xi_of[i] = xi
                            sl = i % NS
                            norm_tile(xi, 2 if isctx else 0, 3 if isctx else 1, hb[sl][:], r_hb[sl], t1[sl][:], r_t1[sl], (i % 4) * 2)
                            transpose8(hb[sl], r_hb[sl], trp_ring.next(), hT[i % 3][:], r_hT[i % 3])
                            if stop == "p1dbg" and t == 0:
                                dump("xin", xin[xi][:], r_xin[xi]); dump("hb", hb[sl][:], r_hb[sl]); dump("hT", hT[i % 3][:], r_hT[i % 3])
                                dump("m1b", modb[0][:], r_modb[0]); dump("sh1b", modb[1][:], r_modb[1]); dump("w_in", w_in[:], r_win)

                        def s1f(i):
                            t = order[i]
                            isctx = t >= 16
                            full = (not isctx) or (not last)
                            h = hT[i % 3]; rh = r_hT[i % 3]
                            bk = {}
                            if full:
                                bu = prj_ring.next(); bk["u"] = bu
                                for cc in range(2):
                                    for kc in range(8):
                                        S.op("pe", lambda cc=cc, kc=kc: nc.tensor.matmul(
                                            PB[bu][:, cc * 128:(cc + 1) * 128], lhsT=w_in[:, kc, cc * 128:(cc + 1) * 128],
                                            rhs=h[:, kc, :], start=(kc == 0), stop=(kc == 7)),
                                            reads=[r_win, rh], writes=[RB[bu]], inc=(kc == 7 and cc == 1))
                                bv = prj_ring.next(); bk["v"] = bv
                                for kc in range(8):
                                    S.op("pe", lambda kc=kc: nc.tensor.matmul(
                                        PB[bv][:, 0:256], lhsT=h[:, kc, :], rhs=w_in[:, kc, 256:512],
                                        start=(kc == 0), stop=(kc == 7)), reads=[r_win, rh], writes=[RB[bv]], inc=(kc == 7))
                                bq = prj_ring.next(); bk["q"] = bq
                                for kc in range(8):
                                    S.op("pe", lambda kc=kc: nc.tensor.matmul(
                                        PB[bq][:, :], lhsT=h[:, kc, :], rhs=w_in[:, kc, 512:1024],
                                        start=(kc == 0), stop=(kc == 7)), reads=[r_win, rh], writes=[RB[bq]], inc=(kc == 7))
                            bkk = prj_ring.next(); bk["k"] = bkk
                            for kc in range(8):
                                S.op("pe", lambda kc=kc: nc.tensor.matmul(
                                    PB[bkk][:, :], lhsT=h[:, kc, :], rhs=w_in[:, kc, 1024:1536],
                                    start=(kc == 0), stop=(kc == 7)), reads=[r_win, rh], writes=[RB[bkk]], inc=(kc == 7))
                            bc = prj_ring.next(); bk["c"] = bc
                            for kc in range(8):
                                S.op("pe", lambda kc=kc: nc.tensor.matmul(
                                    PB[bc][:, :], lhsT=h[:, kc, :], rhs=w_in[:, kc, 1536:2048],
                                    start=(kc == 0), stop=(kc == 7)), reads=[r_win, rh], writes=[RB[bc]], inc=(kc == 7))
                            banks_of[i] = bk

                        def s2(i):
                            t = order[i]
                            isctx = t >= 16
                            full = (not isctx) or (not last)
                            bk = banks_of[i]
                            sl = i % NS
                            g0 = l * 256
                            if full:
                                bu, bv = bk["u"], bk["v"]
                                S.op("act", lambda: nc.scalar.activation(
                                    uT[sl][:].rearrange("p c t -> p (c t)"), PB[bu][:, 0:256], AF.Gelu_apprx_tanh),
                                    reads=[RB[bu]], writes=[r_uT[sl]])
                                S.op("act", lambda: nc.scalar.activation(gv[sl][:], PB[bv][:, 0:256], AF.Gelu_apprx_tanh),
                                     reads=[RB[bv]], writes=[r_gv[sl]])
                                S.op("dve", lambda: nc.vector.tensor_tensor(ta[sl][:], gv[sl][:], gv[sl][:], ALU.mult),
                                     reads=[r_gv[sl]], writes=[r_ta[sl]])
                                S.op("dve", lambda: nc.vector.tensor_reduce(
                                    st2[sl][:, 26:30], ta[sl][:].rearrange("p (g c) -> p g c", g=4), AX.X, ALU.add),
                                    reads=[r_ta[sl]], writes=[r_st2[sl]])
                                S.op("act", lambda: nc.scalar.activation(st2[sl][:, 26:30], st2[sl][:, 26:30], AF.Sqrt,
                                                                         scale=1.0 / 64, bias=eps_t[:, 0:1]),
                                     reads=[r_st2[sl], r_ident], writes=[r_st2[sl]])
                                S.op("dve", lambda: nc.vector.reciprocal(st2[sl][:, 26:30], st2[sl][:, 26:30]),
                                     reads=[r_st2[sl]], writes=[r_st2[sl]])
                                for par in range(2):
                                    S.op("dve", lambda par=par: nc.vector.tensor_tensor(
                                        vpad[sl][:, par:4:2, par * 64:par * 64 + 64],
                                        gv[sl][:].rearrange("p (g c) -> p g c", g=4)[:, par:4:2, :],
                                        st2[sl][:, 26 + par:30:2].unsqueeze(2).to_broadcast([128, 2, 64]), ALU.mult),
                                        reads=[r_gv[sl], r_st2[sl]], writes=[r_vpad[sl]])
                                bm = prj_ring.next()
                                for g in range(4):
                                    S.op("pe", lambda g=g: nc.tensor.matmul(
                                        PB[bm][:, (g // 2) * 128:(g // 2) * 128 + 128], lhsT=vpad[sl][:, g, :], rhs=awsT[:, g, :],
                                        start=(g % 2 == 0), stop=(g % 2 == 1)),
                                        reads=[r_vpad[sl], r_aws], writes=[RB[bm]], inc=(g == 3))
                                S.op("dve", lambda: nc.vector.tensor_tensor(
                                    ta[sl][:], PB[bm][:, 0:256], biasT[:, l].rearrange("p c t -> p (c t)"), ALU.add),
                                    reads=[RB[bm], r_const], writes=[r_ta[sl]])
                                S.op("dve", lambda: nc.vector.tensor_tensor(
                                    catA[:, 0:2, t * 128:(t + 1) * 128], ta[sl][:].rearrange("p (c t) -> p c t", c=2),
                                    uT[sl][:], ALU.mult), reads=[r_ta[sl], r_uT[sl]], writes=[r_catA[t]])
                            bkk, bc = bk["k"], bk["c"]
                            if full:
                                bq = bk["q"]
                                S.op("act", lambda: nc.scalar.activation(sq[sl][:, 0:512], PB[bq][:, :], AF.Square),
                                     reads=[RB[bq]], writes=[r_sq[sl]])
                                S.op("act", lambda: nc.scalar.activation(sq[sl][:, 640:896], PB[bkk][:, 256:512], AF.Square),
                                     reads=[RB[bkk]], writes=[r_sq[sl]])
                            else:
                                S.op("dve", lambda: nc.vector.memset(sq[sl][:, 0:512], 1.0), writes=[r_sq[sl]])
                                S.op("dve", lambda: nc.vector.memset(sq[sl][:, 640:896], 1.0), writes=[r_sq[sl]])
                            S.op("act", lambda: nc.scalar.activation(sq[sl][:, 512:640], PB[bkk][:, 0:128], AF.Square),
                                 reads=[RB[bkk]], writes=[r_sq[sl]])
                            S.op("act", lambda: nc.scalar.activation(sq[sl][:, 896:1152], PB[bc][:, 0:256], AF.Square),
                                 reads=[RB[bc]], writes=[r_sq[sl]])
                            S.op("dve", lambda: nc.vector.tensor_reduce(
                                st2[sl][:, 0:10], sq[sl][:, 0:640].rearrange("p (h d) -> p h d", d=64), AX.X, ALU.add),
                                reads=[r_sq[sl]], writes=[r_st2[sl]])
                            S.op("dve", lambda: nc.vector.tensor_reduce(
                                st2[sl][:, 10:26], sq[sl][:, 640:1152].rearrange("p (h d) -> p h d", d=32), AX.X, ALU.add),
                                reads=[r_sq[sl]], writes=[r_st2[sl]])
                            S.op("act", lambda: nc.scalar.activation(st2[sl][:, 0:10], st2[sl][:, 0:10], AF.Sqrt,
                                                                     scale=1.0 / 64, bias=eps_t[:, 0:1]),
                                 reads=[r_st2[sl], r_ident], writes=[r_st2[sl]])
                            S.op("act", lambda: nc.scalar.activation(st2[sl][:, 10:26], st2[sl][:, 10:26], AF.Sqrt,
                                                                     scale=1.0 / 32, bias=eps_t[:, 0:1]),
                                 reads=[r_st2[sl], r_ident], writes=[r_st2[sl]])
                            S.op("dve", lambda: nc.vector.reciprocal(st2[sl][:, 0:26], st2[sl][:, 0:26]),
                                 reads=[r_st2[sl]], writes=[r_st2[sl]])
                            specs = []
                            if full:
                                specs.append(("bq", PB[bk["q"]][:, :], RB[bk["q"]], 0, 512, 8, 64, 0, 0, ropeB_C, ropeB_S))
                            specs.append(("bk", PB[bkk][:, 0:128], RB[bkk], 512, 128, 2, 64, 8, 64, ropeB_C, ropeB_S))
                            if full:
                                specs.append(("cq", PB[bkk][:, 256:512], RB[bkk], 640, 256, 8, 32, 10, 128, ropeC_C, ropeC_S))
                            specs.append(("ck", PB[bc][:, 0:256], RB[bc], 896, 256, 8, 32, 18, 160, ropeC_C, ropeC_S))
                            for (nm, src, rsrc, c0, wd, nh, hd, sc0, gc0, rC, rS) in specs:
                                v3 = lambda ap, nh=nh: ap.rearrange("p (h d) -> p h d", h=nh)
                                qv = qn[sl][:, c0:c0 + wd]
                                S.op("dve", lambda src=src, qv=qv, v3=v3, sc0=sc0, nh=nh, hd=hd: nc.vector.tensor_tensor(
                                    v3(qv), v3(src), st2[sl][:, sc0:sc0 + nh].unsqueeze(2).to_broadcast([128, nh, hd]), ALU.mult),
                                    reads=[rsrc, r_st2[sl]], writes=[r_qn[sl]])
                                gain_b = smallg[:, l, gc0:gc0 + hd].unsqueeze(1).to_broadcast([128, nh, hd])
                                if isctx:
                                    if nm == "bq":
                                        outv = qr[sl][:, 0:512].rearrange("p (c hi d) -> p hi c d", c=4, hi=2)
                                        inv = qv.rearrange("p (hi c d) -> p hi c d", hi=2, c=4)
                                        gb = smallg[:, l, gc0:gc0 + hd].unsqueeze(1).unsqueeze(1).to_broadcast([128, 2, 4, hd])
                                        S.op("dve", lambda outv=outv, inv=inv, gb=gb: nc.vector.tensor_tensor(outv, inv, gb, ALU.mult),
                                             reads=[r_qn[sl], r_const], writes=[r_qr[sl]])
                                    else:
                                        S.op("dve", lambda qv=qv, v3=v3, gain_b=gain_b, c0=c0, wd=wd: nc.vector.tensor_tensor(
                                            v3(qr[sl][:, c0:c0 + wd]), v3(qv), gain_b, ALU.mult),
                                            reads=[r_qn[sl], r_const], writes=[r_qr[sl]])
                                    continue
                                S.op("dve", lambda qv=qv, v3=v3, gain_b=gain_b: nc.vector.tensor_tensor(v3(qv), v3(qv), gain_b, ALU.mult),
                                     reads=[r_qn[sl], r_const], writes=[r_qn[sl]])
                                nf = hd // 4
                                v4 = lambda ap, nf=nf: ap.rearrange("p (ha pr f) -> p ha pr f", pr=2, f=nf)
                                tbv = tb[sl][:, c0:c0 + wd]
                                Sn = rS[:, t, 0, :].rearrange("p (a f) -> p a f", a=2).unsqueeze(1).to_broadcast([128, nh, 2, nf])
                                Sp = rS[:, t, 1, :].rearrange("p (a f) -> p a f", a=2).unsqueeze(1).to_broadcast([128, nh, 2, nf])
                                v5 = lambda ap, nh=nh, nf=nf: ap.rearrange("p (h a pr f) -> p h a pr f", h=nh, a=2, pr=2)
                                S.op("dve", lambda tbv=tbv, qv=qv, v5=v5, Sn=Sn: nc.vector.tensor_tensor(
                                    v5(tbv)[:, :, :, 0, :], v5(qv)[:, :, :, 1, :], Sn, ALU.mult),
                                    reads=[r_qn[sl], r_const], writes=[r_tb[sl]])
                                S.op("dve", lambda tbv=tbv, qv=qv, v5=v5, Sp=Sp: nc.vector.tensor_tensor(
                                    v5(tbv)[:, :, :, 1, :], v5(qv)[:, :, :, 0, :], Sp, ALU.mult),
                                    reads=[r_qn[sl], r_const], writes=[r_tb[sl]])
                                Cb = rC[:, t, :].unsqueeze(1).to_broadcast([128, nh, hd])
                                S.op("dve", lambda qv=qv, v3=v3, Cb=Cb: nc.vector.tensor_tensor(v3(qv), v3(qv), Cb, ALU.mult),
                                     reads=[r_qn[sl], r_const], writes=[r_qn[sl]])
                                if nm == "bq":
                                    outv = qr[sl][:, 0:512].rearrange("p (c hi d) -> p hi c d", c=4, hi=2)
                                    a0 = qv.rearrange("p (hi c d) -> p hi c d", hi=2, c=4)
                                    a1 = tbv.rearrange("p (hi c d) -> p hi c d", hi=2, c=4)
                                    S.op("dve", lambda outv=outv, a0=a0, a1=a1: nc.vector.tensor_tensor(outv, a0, a1, ALU.add),
                                         reads=[r_qn[sl], r_tb[sl]], writes=[r_qr[sl]])
                                else:
                                    S.op("dve", lambda qv=qv, tbv=tbv, c0=c0, wd=wd: nc.vector.tensor_tensor(
                                        qr[sl][:, c0:c0 + wd], qv, tbv, ALU.add),
                                        reads=[r_qn[sl], r_tb[sl]], writes=[r_qr[sl]])
                            S.op("act", lambda: nc.scalar.copy(vb[:, t, :, 0:64], PB[bkk][:, 128:256].rearrange("p (h d) -> p h d", h=2)),
                                 reads=[RB[bkk]], writes=[r_vb[t]])
                            S.op("act", lambda: nc.scalar.copy(vc[:, t, :, 0:64], PB[bc][:, 256:512].rearrange("p (h d) -> p h d", h=4)),
                                 reads=[RB[bc]], writes=[r_vc[t]])
                            bt = trp_ring.next()
                            pv = bank_bf(bt)
                            lst = []
                            if full:
                                for c in range(4):
                                    lst.append((c, qr[sl][:, c * 128:(c + 1) * 128]))
                            lst.append((4, qr[sl][:, 512:640]))
                            for c, src in lst:
                                S.op("pe", lambda c=c, src=src: nc.tensor.transpose(pv[:, c * 128:(c + 1) * 128], src, ident_b[:]),
                                     reads=[r_qr[sl], r_ident], writes=[RB[bt]], inc=(c == 4))
                            if full:
                                evac(qbT[:, :, t * 128:(t + 1) * 128], pv[:, 0:512].rearrange("p (c t) -> p c t", c=4),
                                     [RB[bt]], [r_qbT[t]])
                            evac(kbT[:, t * 128:(t + 1) * 128], pv[:, 512:640], [RB[bt]], [r_kbT[t]])
                            bt2 = trp_ring.next()
                            pv2 = bank_bf(bt2)
                            lst = []
                            if full:
                                lst += [(0, qr[sl][:, 640:768]), (1, qr[sl][:, 768:896])]
                            lst += [(2, qr[sl][:, 896:1024]), (3, qr[sl][:, 1024:1152])]
                            for c, src in lst:
                                S.op("pe", lambda c=c, src=src: nc.tensor.transpose(pv2[:, c * 128:(c + 1) * 128], src, ident_b[:]),
                                     reads=[r_qr[sl], r_ident], writes=[RB[bt2]], inc=(c == 3))
                            if full:
                                evac(qcT[:, :, t * 128:(t + 1) * 128], pv2[:, 0:256].rearrange("p (c t) -> p c t", c=2),
                                     [RB[bt2]], [r_qcT[t]])
                            evac(kcT[:, :, t * 128:(t + 1) * 128], pv2[:, 256:512].rearrange("p (c t) -> p c t", c=2),
                                 [RB[bt2]], [r_kcT[t]])

                        import os as _os
                        _ntl = int(_os.environ.get("P1_TILES", NT))
                        _nst = int(_os.environ.get("P1_STAGES", 3))
                        pipeline(_ntl, [(s2, 2), (s1f, 1), (s0, 0)][3 - _nst:])
                        if b == 0 and l == 0:
                            print("sbuf remaining p1", nc.sbuf_bytes_remaining)
                        S.barrier()
                    if stop == "p1x":
                        print("NOPS", getattr(S, "nops", 0))
                        dump("hT0", hT[0][:], r_hT[0])
                        dbg_wait()
                        return nc
                    if stop in ("p1", "p1dbg") and dbg is not None:
                        dump("kbT", kbT[:], r_kbT[0]); dump("kcT", kcT[:], r_kcT[0]); dump("qbT", qbT[:, :, 0:ntq * 128], r_qbT[0])
                        dump("qcT", qcT[:, :, 0:ntq * 128], r_qcT[0]); dump("vb", vb[:], r_vb[0]); dump("vc", vc[:], r_vc[0])
                        dump("catA", catA[:, :, 0:ntq * 128], r_catA[0])
                        dbg_wait()
                        return nc

                    with ExitStack() as p2:
                        catBC = T("catBC", [128, 6, NTOK], BF16, p2)
                        w_out = T("w_out", [128, 8, D], BF16, p2); r_wout = Res("w_out")
                        for kc in range(8):
                            S.dma("pool", g_w[2], w_out[:, kc, :], wout_d[l, kc * 128:(kc + 1) * 128, :], writes=[r_wout])
                        NP = 10
                        pT = [T(f"pT{i}", [128, 512], BF16, p2) for i in range(NP)]; r_pT = [Res(f"pT{i}") for i in range(NP)]
                        pT_ring = Ring(range(NP))
                        zst = [T(f"zst{i}", [128, 512], BF16, p2) for i in range(2)]; r_zst = [Res(f"zst{i}") for i in range(2)]
                        zst_ring = Ring(range(2))
                        rr = [T("rr0", [128, 1024], F32, p2)] * 2; r_rr = [Res("rr0")] * 2
                        Rsb = [T("Rsb0", [128, 1024], F32, p2)] * 2; r_Rsb = [Res("Rsb0")] * 2
                        yy = [T(f"yy{i}", [128, 512], F32, p2) for i in range(2)]; r_yy = [Res(f"yy{i}") for i in range(2)]
                        y1 = [T(f"y1{i}", [128, 512], F32, p2) for i in range(2)]; r_y1 = [Res(f"y1{i}") for i in range(2)]
                        ysq = [T(f"ysq{i}", [128, 512], F32, p2) for i in range(2)]; r_ysq = [Res(f"ysq{i}") for i in range(2)]
                        rsd = [T(f"rsd{i}", [128, 512], F32, p2) for i in range(2)]; r_rsd = [Res(f"rsd{i}") for i in range(2)]

                        s_ring = Ring([0, 1, 2, 3])
                        acc_ring = Ring([4, 5])
                        R_ring = Ring([6, 7])
                        units = [(kvh, n) for n in range(ntq) for kvh in range(2)]
                        bstate = {}
                        bacc = {}

                        def b_scores(ui):
                            kvh, n = units[ui]
                            if n < 16:
                                kts = []
                                if n > 0:
                                    kts.append((n - 1, 0))
                                kts.append((n, None))
                                if n < 15:
                                    kts.append((n + 1, 1))
                                kts += [(16, None), (17, None)]
                            else:
                                kts = [(16, None), (17, None)]
                            pb0 = kvh * 64
                            plist = []
                            for (kt, mk) in kts:
                                sb = s_ring.next()
                                S.op("pe", lambda sb=sb, kt=kt: nc.tensor.matmul(
                                    PB[sb][:, :], lhsT=kbT[pb0:pb0 + 64, kt * 128:(kt + 1) * 128],
                                    rhs=qbT[pb0:pb0 + 64, :, n * 128:(n + 1) * 128], start=True, stop=True),
                                    reads=[r_kbT[kt], r_qbT[n]], writes=[RB[sb]])
                                pi = pT_ring.next()
                                S.op("act", lambda sb=sb, pi=pi: nc.scalar.activation(pT[pi][:], PB[sb][:, :], AF.Exp, scale=0.125),
                                     reads=[RB[sb]], writes=[r_pT[pi]])
                                if mk is not None:
                                    S.op("dve", lambda pi=pi, mk=mk: nc.vector.tensor_tensor(
                                        pT[pi][:].rearrange("p (g q) -> p g q", g=4), pT[pi][:].rearrange("p (g q) -> p g q", g=4),
                                        mask_b[:, mk, :].unsqueeze(1).to_broadcast([128, 4, 128]), ALU.mult),
                                        reads=[r_pT[pi], r_ident], writes=[r_pT[pi]])
                                plist.append((kt, pi))
                            bstate[ui] = plist

                        def b_pv(ui):
                            kvh, n = units[ui]
                            plist = bstate.pop(ui)
                            ab = acc_ring.next()
                            nk = len(plist)
                            for j, (kt, pi) in enumerate(plist):
                                S.op("pe", lambda j=j, kt=kt, pi=pi: nc.tensor.matmul(
                                    PB[ab][0:65, :], lhsT=vb[:, kt, kvh, :], rhs=pT[pi][:], start=(j == 0), stop=(j == nk - 1)),
                                    reads=[r_vb[kt], r_vones, r_pT[pi]], writes=[RB[ab]], inc=(j == nk - 1))
                            bacc[ui] = ab

                        def b_post(ui):
                            kvh, n = units[ui]
                            ab = bacc.pop(ui)
                            ri = ui % 2
                            hd0 = kvh * 4
                            S.op("dve", lambda: nc.vector.tensor_tensor(
                                rr[ri][64:65, 0:512].rearrange("p (g q) -> p g q", g=4),
                                PB[ab][64:65, :].rearrange("p (g q) -> p g q", g=4),
                                esink[64:65, l * 8 + hd0:l * 8 + hd0 + 4].unsqueeze(2).to_broadcast([1, 4, 128]),
                                ALU.add), reads=[RB[ab], r_const], writes=[r_rr[ri]])
                            S.op("dve", lambda: nc.vector.reciprocal(rr[ri][64:65, 0:512], rr[ri][64:65, 0:512]),
                                 reads=[r_rr[ri]], writes=[r_rr[ri]])
                            rb = R_ring.next()
                            S.op("pe", lambda: nc.tensor.matmul(PB[rb][0:64, :], lhsT=ones_f[64:65, 0:64], rhs=rr[ri][64:65, 0:512],
                                                                start=True, stop=True),
                                 reads=[r_rr[ri], r_ident], writes=[RB[rb]])
                            S.op("act", lambda: nc.scalar.copy(Rsb[ri][0:64, 0:512], PB[rb][0:64, :]), reads=[RB[rb]], writes=[r_Rsb[ri]])
                            c0 = kvh * 2
                            v4 = lambda ap: ap.rearrange("p (g q) -> p g q", g=4)
                            S.op("dve", lambda: nc.vector.tensor_tensor(
                                catBC[0:64, c0:c0 + 2, n * 128:(n + 1) * 128], v4(PB[ab][0:64, :])[:, 0:4:2, :],
                                v4(Rsb[ri][0:64, 0:512])[:, 0:4:2, :], ALU.mult),
                                reads=[RB[ab], r_Rsb[ri]], writes=[r_catB[n]])
                            zi = zst_ring.next()
                            S.op("dve", lambda: nc.vector.tensor_tensor(
                                zst[zi][0:64, 0:256].rearrange("p (g q) -> p g q", g=2), v4(PB[ab][0:64, :])[:, 1:4:2, :],
                                v4(Rsb[ri][0:64, 0:512])[:, 1:4:2, :], ALU.mult),
                                reads=[RB[ab], r_Rsb[ri]], writes=[r_zst[zi]])
                            S.dma("sp", g_zst[zi], catBC[64:128, c0:c0 + 2, n * 128:(n + 1) * 128],
                                  zst[zi][0:64, 0:256].rearrange("p (g q) -> p g q", g=2), reads=[r_zst[zi]], writes=[r_catB[n]])

                        if b == 0 and l == 0:
                            print("sbuf remaining p2", nc.sbuf_bytes_remaining)
                        pipeline(len(units), [(b_post, 2), (b_scores, 0), (b_pv, 1)])

                        qgroups = [(g * 512, 512, list(range(NT)), list(range(4 * g, 4 * g + 4))) for g in range(4)]
                        if not last:
                            qgroups.append((2048, 256, [16, 17], [16, 17]))
                        cunits = [(h, qg) for qg in qgroups for h in range(4)]
                        s_ring = Ring([0, 1])
                        scl = 32 ** -0.5
                        acc_sets = [(2, 3), (4, 5)]
                        pending = []

                        def c_post(ui, h, q0, W, qtiles, accb):
                            hc = h // 2
                            ri = ui % 2
                            po = 0
                            pr_ = 64
                            for m in range(2):
                                S.op("dve", lambda m=m: nc.vector.reciprocal(rr[ri][pr_:pr_ + 1, m * 512:m * 512 + W],
                                                                              PB[accb[m]][pr_:pr_ + 1, 0:W]),
                                     reads=[RB[accb[m]]], writes=[r_rr[ri]])
                            S.op("dve", lambda: nc.vector.tensor_scalar(rr[ri][pr_:pr_ + 1, 512:512 + W], rr[ri][pr_:pr_ + 1, 512:512 + W],
                                                                         neglam[pr_:pr_ + 1, l:l + 1], None, ALU.mult),
                                 reads=[r_rr[ri], r_const], writes=[r_rr[ri]])
                            for m in range(2):
                                S.op("pe", lambda m=m: nc.tensor.matmul(PB[6 + m][0:64, 0:W], lhsT=ones_f[pr_:pr_ + 1, 0:64],
                                                                        rhs=rr[ri][pr_:pr_ + 1, m * 512:m * 512 + W], start=True, stop=True),
                                     reads=[r_rr[ri], r_ident], writes=[RB[6 + m]])
                                S.op("act", lambda m=m: nc.scalar.copy(Rsb[ri][po:po + 64, m * 512:m * 512 + W], PB[6 + m][po:po + 64, 0:W]),
                                     reads=[RB[6 + m]], writes=[r_Rsb[ri]])
                            S.op("dve", lambda: nc.vector.tensor_tensor(yy[ri][po:po + 64, 0:W], PB[accb[0]][po:po + 64, 0:W],
                                                                         Rsb[ri][po:po + 64, 0:W], ALU.mult),
                                 reads=[RB[accb[0]], r_Rsb[ri]], writes=[r_yy[ri]])
                            S.op("dve", lambda: nc.vector.tensor_tensor(y1[ri][po:po + 64, 0:W], PB[accb[1]][po:po + 64, 0:W],
                                                                         Rsb[ri][po:po + 64, 512:512 + W], ALU.mult),
                                 reads=[RB[accb[1]], r_Rsb[ri]], writes=[r_y1[ri]])
                            S.op("dve", lambda: nc.vector.tensor_tensor(yy[ri][po:po + 64, 0:W], yy[ri][po:po + 64, 0:W],
                                                                         y1[ri][po:po + 64, 0:W], ALU.add),
                                 reads=[r_yy[ri], r_y1[ri]], writes=[r_yy[ri]])
                            S.op("act", lambda: nc.scalar.activation(ysq[ri][po:po + 64, 0:W], yy[ri][po:po + 64, 0:W], AF.Square),
                                 reads=[r_yy[ri]], writes=[r_ysq[ri]])
                            S.op("pe", lambda: nc.tensor.matmul(PB[6][0:64, 0:W], lhsT=onesmean[po:po + 64, 0:64], rhs=ysq[ri][po:po + 64, 0:W],
                                                                start=True, stop=True),
                                 reads=[r_ysq[ri], r_ident], writes=[RB[6]])
                            S.op("act", lambda: nc.scalar.activation(rsd[ri][po:po + 64, 0:W], PB[6][po:po + 64, 0:W], AF.Sqrt,
                                                                     bias=eps_t[po:po + 64, 0:1]),
                                 reads=[RB[6], r_ident], writes=[r_rsd[ri]])
                            S.op("dve", lambda: nc.vector.reciprocal(rsd[ri][po:po + 64, 0:W], rsd[ri][po:po + 64, 0:W]),
                                 reads=[r_rsd[ri]], writes=[r_rsd[ri]])
                            if h % 2 == 0:
                                S.op("dve", lambda: nc.vector.scalar_tensor_tensor(
                                    catBC[0:64, 4 + hc, q0:q0 + W], yy[ri][0:64, 0:W], gsub[0:64, l:l + 1],
                                    rsd[ri][0:64, 0:W], ALU.mult, ALU.mult),
                                    reads=[r_yy[ri], r_rsd[ri], r_const], writes=[r_catC[qt] for qt in qtiles])
                            else:
                                zi = zst_ring.next()
                                S.op("dve", lambda: nc.vector.scalar_tensor_tensor(
                                    zst[zi][0:64, 0:W], yy[ri][0:64, 0:W], gsub[0:64, l:l + 1],
                                    rsd[ri][0:64, 0:W], ALU.mult, ALU.mult),
                                    reads=[r_yy[ri], r_rsd[ri], r_const], writes=[r_zst[zi]])
                                S.dma("sp", g_zst[zi], catBC[64:128, 4 + hc, q0:q0 + W], zst[zi][0:64, 0:W],
                                      reads=[r_zst[zi]], writes=[r_catC[qt] for qt in qtiles])

                        for ui, (h, (q0, W, kts, qtiles)) in enumerate(cunits):
                            hc = h // 2
                            accb = acc_sets[ui % 2]
                            nk = len(kts)
                            sbs = {}

                            def c_score(ki, h=h, q0=q0, W=W, kts=kts, qtiles=qtiles, hc=hc, sbs=sbs, nk=nk):
                                if ki == min(2, nk - 1) and pending:
                                    c_post(*pending.pop())
                                kt = kts[ki]
                                for m in range(2):
                                    pb = 32 * (2 * (h % 2) + m)
                                    sb = s_ring.next()
                                    kw = dict(tile_position=(96, 0)) if pb == 96 else {}
                                    S.op("pe", lambda sb=sb, pb=pb, kw=kw: nc.tensor.matmul(
                                        PB[sb][:, 0:W], lhsT=kcT[pb:pb + 32, hc, kt * 128:(kt + 1) * 128],
                                        rhs=qcT[pb:pb + 32, hc, q0:q0 + W], start=True, stop=True, **kw),
                                        reads=[r_kcT[kt]] + [r_qcT[qt] for qt in qtiles], writes=[RB[sb]])
                                    pi = pT_ring.next()
                                    S.op("act", lambda sb=sb, pi=pi: nc.scalar.activation(pT[pi][:, 0:W], PB[sb][:, 0:W], AF.Exp, scale=scl),
                                         reads=[RB[sb]], writes=[r_pT[pi]])
                                    sbs[(ki, m)] = pi

                            def c_pv(ki, h=h, W=W, kts=kts, nk=nk, sbs=sbs, accb=accb):
                                kt = kts[ki]
                                lhs = vc[:, kt, h, :]
                                for m in range(2):
                                    pi = sbs.pop((ki, m))
                                    S.op("pe", lambda m=m, pi=pi: nc.tensor.matmul(
                                        PB[accb[m]][0:65, 0:W], lhsT=lhs, rhs=pT[pi][:, 0:W], start=(ki == 0), stop=(ki == nk - 1)),
                                        reads=[r_vc[kt], r_vones, r_pT[pi]], writes=[RB[accb[m]]], inc=(ki == nk - 1))

                            pipeline(nk, [(c_score, 0), (c_pv, 1)])
                            pending.append((ui, h, q0, W, qtiles, accb))
                        while pending:
                            c_post(*pending.pop())

                        if stop == "p2" and dbg is not None:
                            dump("catA", catA[:, :, 0:ntq * 128], r_catA[0])
                            for t in range(ntq):
                                S._emit_waits("sp", S._deps("sp", [r_catB[t], r_catC[t], r_catA[t]], []))
                            dump("catBC", catBC[:, :, 0:ntq * 128], r_catA[0])
                            for t in range(0):
                                S._emit_waits("sp", S._deps("sp", [r_catB[t], r_catC[t], r_catA[t]], []))
                            dbg_wait()
                            return nc

                        load_modb(0, b, l, 2)
                        load_modb(2, 2, l, 2)
                        pair_ring = Ring([(0, 1), (2, 3), (4, 5), (6, 7)])
                        pstate = {}

                        def o_mm(t):
                            b0, b1 = pair_ring.next()
                            for nh, bi in enumerate((b0, b1)):
                                for kc in range(8):
                                    S.op("pe", lambda nh=nh, bi=bi, kc=kc: nc.tensor.matmul(
                                        PB[bi][:, :], lhsT=(catA[:, kc, t * 128:(t + 1) * 128] if kc < 2 else catBC[:, kc - 2, t * 128:(t + 1) * 128]),
                                        rhs=w_out[:, kc, nh * 512:(nh + 1) * 512],
                                        start=(kc == 0), stop=(kc == 7)),
                                        reads=[r_catA[t], r_catB[t], r_catC[t], r_wout], writes=[RB[bi]], inc=(kc == 7))
                            pstate[t] = (b0, b1)

                        def o_res(t):
                            b0, b1 = pstate.pop(t)
                            xi = load_x(b, lsrc, t)
                            xo = xout_ring.next()
                            gi = 0 if t < 16 else 2
                            for nh, bi in enumerate((b0, b1)):
                                S.op("dve", lambda nh=nh, bi=bi: nc.vector.tensor_tensor(
                                    xout[xo][:, nh * 512:(nh + 1) * 512], PB[bi][:, :], modb[gi][:, nh * 512:(nh + 1) * 512], ALU.mult),
                                    reads=[RB[bi], r_modb[gi]], writes=[r_xout[xo]])
                            S.op("dve", lambda: nc.vector.tensor_tensor(xout[xo][:], xout[xo][:], xin[xi][:], ALU.add),
                                 reads=[r_xout[xo], r_xin[xi]], writes=[r_xout[xo]])
                            S.dma("sp", g_xout[xo], x_dst(b, t), xout[xo][:], reads=[r_xout[xo]], writes=[r_dx[b][t]])

                        pipeline(ntq, [(o_mm, 0), (o_res, 1)])
                        S.barrier()
                if stop == "s1":
                    break

                load_modb(0, b, l, 4)
                load_modb(1, b, l, 3)
                load_modb(2, 2, l, 4)
                load_modb(3, 2, l, 3)
                ncols = ntq * 128
                with ExitStack() as s2c:
                    h2T = T("h2T", [128, 8, NTOK], BF16, s2c); r_h2T = [Res(f"h2T{t}") for t in range(NT)]
                    hid = T("hid", [128, 8, NTOK], BF16, s2c); r_hid = [Res(f"hid{j}") for j in range(8)]
                    wdn = T("wdn", [128, 8, D], BF16, s2c); r_wdn = Res("wdn")
                    wup = [T(f"wup{i}", [128, 8, 256], BF16, s2c) for i in range(3)]; r_wup = [Res(f"wup{i}") for i in range(3)]
                    tbuf = [T(f"tbuf{i}", [128, NTOK], F32, s2c) for i in range(3)]; r_tbuf = [[Res(f"tbuf{i}_{k}") for k in range(5)] for i in range(3)]
                    tb_ring = Ring(range(3))
                    hb2 = [T(f"hb2_{i}", [128, D], BF16, s2c) for i in range(2)]; r_hb2 = [Res(f"hb2_{i}") for i in range(2)]
                    t12 = [T("t12_0", [128, D], F32, s2c)] * 2; r_t12 = [Res("t12_0")] * 2
                    trp_ring = Ring([0, 1, 2, 3])
                    if b == 0 and l == 0:
                        print("sbuf remaining ffn", nc.sbuf_bytes_remaining)
                    def f0a(t):
                        isctx = t >= 16
                        xi = load_x(b, l + 1, t)
                        sl = t % 2
                        norm_tile(xi, 2 if isctx else 0, 3 if isctx else 1, hb2[sl][:], r_hb2[sl], t12[sl][:], r_t12[sl], (t % 4) * 2)

                    def f0b(t):
                        sl = t % 2
                        transpose8(hb2[sl], r_hb2[sl], trp_ring.next(), h2T[:, :, t * 128:(t + 1) * 128], r_h2T[t])

                    pipeline(ntq, [(f0b, 1), (f0a, 0)])
                    blocks = [(g * 512, 512, list(range(4 * g, 4 * g + 4)), g > 0) for g in range(4)]
                    if not last:
                        blocks.append((2048, 256, [16, 17], False))
                    wk = 0
                    load_modb(0, b, l, 5)
                    load_modb(2, 2, l, 5)
                    for (j0, npart) in ((0, 8), (8, 7), (15, 7)):
                        for part in range(npart):
                            jj = j0 + part
                            S.dma("pool", g_wdn, wdn[:, part, :], wdn_d[l, jj * 128:(jj + 1) * 128, :],
                                  reads=[], writes=[r_wdn])
                        up_ring = Ring([0, 1, 2, 3, 4, 5, 6, 7])
                        pend_hid = []
                        for j in range(npart):
                            jj = j0 + j
                            wi = wk % 3
                            wk += 1
                            S.dma("pool", g_w[3 + wi], wup[wi][:], wup_d[l, jj], writes=[r_wup[wi]])
                            tbs = []
                            for row in range(2):
                                ch = jj + row * NCH
                                ti_ = tb_ring.next()
                                tbs.append(ti_)
                                tbv = tbuf[ti_]
                                prev = None
                                for bk_i, (c0, W, tiles, cont) in enumerate(blocks):
                                    rtb = r_tbuf[ti_][bk_i]
                                    rtb_prev = r_tbuf[ti_][bk_i - 1] if bk_i > 0 else None
                                    bi = up_ring.next()
                                    for kc in range(8):
                                        S.op("pe", lambda kc=kc, bi=bi, c0=c0, W=W, row=row: nc.tensor.matmul(
                                            PB[bi][:, 0:W], lhsT=wup[wi][:, kc, row * 128:(row + 1) * 128], rhs=h2T[:, kc, c0:c0 + W],
                                            start=(kc == 0), stop=(kc == 7)),
                                            reads=[r_wup[wi]] + [r_h2T[t] for t in tiles], writes=[RB[bi]], inc=(kc == 7))
                                    S.op("act", lambda bi=bi, c0=c0, W=W, ch=ch, tbv=tbv: nc.scalar.activation(
                                        tbv[:, c0:c0 + W], PB[bi][:, 0:W], AF.Identity, scale=cw[:, l, 1, ch:ch + 1], bias=cb[:, l, ch:ch + 1]),
                                        reads=[RB[bi], r_const], writes=[rtb])
                                    S.op("dve", lambda bi=bi, c0=c0, W=W, ch=ch, tbv=tbv: nc.vector.scalar_tensor_tensor(
                                        tbv[:, c0 + 1:c0 + W], PB[bi][:, 0:W - 1], cw[:, l, 0, ch:ch + 1], tbv[:, c0 + 1:c0 + W],
                                        ALU.mult, ALU.add), reads=[RB[bi], r_const, rtb], writes=[rtb])
                                    S.op("dve", lambda bi=bi, c0=c0, W=W, ch=ch, tbv=tbv: nc.vector.scalar_tensor_tensor(
                                        tbv[:, c0:c0 + W - 1], PB[bi][:, 1:W], cw[:, l, 2, ch:ch + 1], tbv[:, c0:c0 + W - 1],
                                        ALU.mult, ALU.add), reads=[RB[bi], r_const, rtb], writes=[rtb])
                                    if cont:
                                        pbi, pW = prev
                                        S.op("dve", lambda bi=bi, c0=c0, ch=ch, tbv=tbv, pbi=pbi, pW=pW: nc.vector.scalar_tensor_tensor(
                                            tbv[:, c0:c0 + 1], PB[pbi][:, pW - 1:pW], cw[:, l, 0, ch:ch + 1], tbv[:, c0:c0 + 1],
                                            ALU.mult, ALU.add), reads=[RB[pbi], r_const, rtb], writes=[rtb])
                                        S.op("dve", lambda bi=bi, c0=c0, ch=ch, tbv=tbv: nc.vector.scalar_tensor_tensor(
                                            tbv[:, c0 - 1:c0], PB[bi][:, 0:1], cw[:, l, 2, ch:ch + 1], tbv[:, c0 - 1:c0],
                                            ALU.mult, ALU.add), reads=[RB[bi], r_const, rtb_prev], writes=[rtb_prev])
                                    prev = (bi, W)
                                    if row == 0 and c0 == 512 and pend_hid:
                                        pend_hid.pop()()
                            tg, tv = tbs

                            def do_hid(tg=tg, tv=tv, j=j):
                                S.op("act", lambda: nc.scalar.activation(tbuf[tg][:, 0:ncols], tbuf[tg][:, 0:ncols], AF.Silu),
                                     reads=r_tbuf[tg], writes=r_tbuf[tg])
                                S.op("dve", lambda: nc.vector.tensor_tensor(
                                    hid[:, j, 0:ncols], tbuf[tg][:, 0:ncols], tbuf[tv][:, 0:ncols], ALU.mult),
                                    reads=r_tbuf[tg] + r_tbuf[tv], writes=[r_hid[j]])
                            pend_hid.append(do_hid)
                        while pend_hid:
                            pend_hid.pop()()
                        pair_ring = Ring([(0, 1), (2, 3), (4, 5), (6, 7)])
                        pstate = {}

                        def d_mm(t):
                            b0, b1 = pair_ring.next()
                            for nh, bi in enumerate((b0, b1)):
                                for j in range(npart):
                                    S.op("pe", lambda nh=nh, bi=bi, j=j: nc.tensor.matmul(
                                        PB[bi][:, :], lhsT=hid[:, j, t * 128:(t + 1) * 128], rhs=wdn[:, j, nh * 512:(nh + 1) * 512],
                                        start=(j == 0), stop=(j == npart - 1)),
                                        reads=[r_hid[j], r_wdn], writes=[RB[bi]], inc=(j == npart - 1))
                            pstate[t] = (b0, b1)

                        def d_res(t):
                            b0, b1 = pstate.pop(t)
                            xi = load_x(b, l + 1, t)
                            xo = xout_ring.next()
                            gi = 0 if t < 16 else 2
                            for nh, bi in enumerate((b0, b1)):
                                S.op("dve", lambda nh=nh, bi=bi: nc.vector.tensor_tensor(
                                    xout[xo][:, nh * 512:(nh + 1) * 512], PB[bi][:, :], modb[gi][:, nh * 512:(nh + 1) * 512], ALU.mult),
                                    reads=[RB[bi], r_modb[gi]], writes=[r_xout[xo]])
                            S.op("dve", lambda: nc.vector.tensor_tensor(xout[xo][:], xout[xo][:], xin[xi][:], ALU.add),
                                 reads=[r_xout[xo], r_xin[xi]], writes=[r_xout[xo]])
                            S.dma("sp", g_xout[xo], x_dst(b, t), xout[xo][:], reads=[r_xout[xo]], writes=[r_dx[b][t]])

                        pipeline(ntq, [(d_mm, 0), (d_res, 1)])
                    S.barrier()
        S.finish()
        build_nc.stats = (S.ninst, S.nwait)
    return nc


def _rope_tables(head_dim):
    rows = SEQ // GRID_W
    row = np.repeat(np.arange(rows, dtype=np.float32), GRID_W)
    col = np.tile(np.arange(GRID_W, dtype=np.float32), rows)
    n_freq = head_dim // 4
    inv_freq = (np.float32(10000.0) ** (-np.arange(n_freq, dtype=np.float32) / np.float32(n_freq))).astype(np.float32)
    ang = np.stack([row[:, None] * inv_freq, col[:, None] * inv_freq], axis=1).astype(np.float32)
    cos = np.cos(ang).astype(np.float32)
    sin = np.sin(ang).astype(np.float32)
    C2 = np.stack([cos, cos], axis=2).reshape(SEQ, 4 * n_freq)
    S2 = np.stack([-sin.reshape(SEQ, 2 * n_freq), sin.reshape(SEQ, 2 * n_freq)], axis=1)
    C2 = C2.reshape(16, 128, 4 * n_freq).transpose(1, 0, 2)
    S2 = S2.reshape(16, 128, 2, 2 * n_freq).transpose(1, 0, 2, 3)
    return np.ascontiguousarray(C2), np.ascontiguousarray(S2)


def prep_shared(inp):
    f = lambda a: np.ascontiguousarray(np.asarray(a, dtype=np.float32))
    w_up = f(inp["w_up"])
    wu = w_up.reshape(L_ALL, 8, 128, 2, NCH, 128)
    w_up_r = np.ascontiguousarray(wu.transpose(0, 4, 2, 1, 3, 5)).reshape(L_ALL, NCH, 128, 8, 256)
    a_wsT = np.ascontiguousarray(f(inp["a_ws"]).transpose(0, 3, 1, 2))
    smallg = np.concatenate([f(inp["b_qnorm"]), f(inp["b_knorm"]), f(inp["c_qnorm"]), f(inp["c_knorm"]),
                             f(inp["c_subln"])], axis=1)
    cwr = f(inp["conv_w"]).reshape(L_ALL, 3, 44, 128).transpose(3, 0, 1, 2)
    cbr = f(inp["conv_b"]).reshape(L_ALL, 44, 128).transpose(2, 0, 1)
    kk = np.arange(128)[:, None]
    qq = np.arange(128)[None, :]
    rbc, rbs = _rope_tables(64)
    rcc, rcs = _rope_tables(32)
    return {
        "w_ada_r": np.ascontiguousarray(f(inp["w_ada"]).reshape(L_ALL, 8, 128, 12, 512).transpose(0, 3, 2, 1, 4)),
        "b_ada": f(inp["b_ada"]),
        "gn": np.ascontiguousarray(np.stack([f(inp["norm1_g"]), f(inp["norm2_g"])], axis=1)),
        "w_in": f(inp["w_in"]), "w_out": f(inp["w_out"]), "w_up_r": w_up_r, "w_down": f(inp["w_down"]),
        "a_wsT": a_wsT, "a_bs": f(inp["a_bs"]), "smallg": np.ascontiguousarray(smallg),
        "b_sink": f(inp["b_sink"]).reshape(1, -1), "c_lam": f(inp["c_lam"]).reshape(1, -1),
        "sublnT": np.ascontiguousarray(f(inp["c_subln"]).T),
        "cw_r": np.ascontiguousarray(cwr), "cb_r": np.ascontiguousarray(cbr),
        "ident": np.eye(128, dtype=np.float32),
        "maskL": (qq <= kk).astype(np.float32), "maskR": (kk <= qq).astype(np.float32),
        "ropeB_C2": rbc, "ropeB_S": rbs, "ropeC_C2": rcc, "ropeC_S": rcs,
    }


def core_inputs(inp, shared, core, nb=NB):
    f = lambda a: np.ascontiguousarray(np.asarray(a, dtype=np.float32))
    b0 = core * nb
    d = dict(shared)
    d["x"] = f(inp["x"][b0:b0 + nb])
    d["ctx"] = f(inp["ctx"][b0:b0 + nb])
    crow = np.zeros((3, D), np.float32)
    crow[0:nb] = np.asarray(inp["c"], np.float32)[b0:b0 + nb]
    crow[2] = np.asarray(inp["c_ctx"], np.float32)
    d["crow"] = crow
    return d


_NC_CACHE = {}


def kernel(**inputs):
    inp = {k: np.asarray(v) for k, v in inputs.items()}
    shared = prep_shared(inp)
    if "nc" not in _NC_CACHE:
        _NC_CACHE["nc"] = build_nc()
    nc = _NC_CACHE["nc"]
    in_maps = [core_inputs(inp, shared, c) for c in range(N_CORES)]
    res = run_bass_kernel_spmd(nc, in_maps, core_ids=list(range(N_CORES)))
    out = np.concatenate([np.asarray(r["y"], dtype=np.float32) for r in res.results], axis=0)
    return out
```

```python
import math
from contextlib import ExitStack

import numpy as np
import concourse.bass as bass
import concourse.mybir as mybir
from concourse.bass_utils import run_bass_kernel_spmd

F32 = mybir.dt.float32
BF16 = mybir.dt.bfloat16
AF = mybir.ActivationFunctionType
ALU = mybir.AluOpType
AX = mybir.AxisListType

L_ALL = 4
D = 1024
SEQ = 2048
LC = 256
NT = 18
NTOK = NT * 128
DFF = 2816
NCH = 22
EPS = 1e-6
GRID_W = 64
N_CORES = 8
NB = 2


class Res:
    __slots__ = ("name", "lw", "rd", "excl")

    def __init__(self, name, excl=False):
        self.name = name
        self.lw = None
        self.rd = {}
        self.excl = excl


class DmaGroup:
    __slots__ = ("sem", "cnt", "name")

    def __init__(self, sem, name):
        self.sem = sem
        self.cnt = 0
        self.name = name


class Sched:
    ENG = ("pe", "act", "dve", "pool", "sp")

    def __init__(self, nc, stack):
        self.nc = nc
        self.stack = stack
        self.eng = {"pe": nc.tensor, "act": nc.scalar, "dve": nc.vector,
                    "pool": nc.gpsimd, "sp": nc.sync}
        self.sem = {e: stack.enter_context(nc.semaphore("sem_" + e)) for e in self.ENG}
        self.cnt = {e: 0 for e in self.ENG}
        self.seen = {e: {} for e in self.ENG}
        self.nwait = 0
        self.ninst = 0
        self.groups = []

    def group(self, name):
        sem = self.stack.enter_context(self.nc.semaphore("dg_" + name))
        g = DmaGroup(sem, name)
        self.groups.append(g)
        return g

    def finish(self):
        for g in self.groups:
            if g.cnt:
                self.nc.sync.wait_ge(g.sem, g.cnt)

    def _deps(self, e, reads, writes):
        need = {}

        def add(ev, raw):
            if ev is None:
                return
            kind, src, count = ev
            if kind == "eng" and src == e:
                if e in ("pe", "sp"):
                    return
            key = (kind, src)
            if need.get(key, (None, 0))[1] < count:
                need[key] = (ev, count)

        for r in reads:
            add(r.lw, True)
            if r.excl:
                for k2, ev in r.rd.items():
                    if k2 != ("eng", e):
                        add(ev, False)
        for w in writes:
            add(w.lw, False)
            for ev in w.rd.values():
                add(ev, False)
        out = []
        for key, (ev, count) in need.items():
            if self.seen[e].get(key, 0) >= count:
                continue
            out.append((key, ev, count))
        return out

    def _emit_waits(self, e, deps):
        eng = self.eng[e]
        for key, ev, count in deps:
            kind, src, _ = ev
            if kind == "eng":
                assert count <= self.cnt[src], f"wait on un-inc'd instr {src} {count}>{self.cnt[src]}"
                sem = self.sem[src]
            else:
                sem = src.sem
            eng.wait_ge(sem, count)
            self.nwait += 1
            self.seen[e][key] = count

    def _mark(self, ev, key, reads, writes):
        for r in reads:
            r.rd[key] = ev
        for w in writes:
            w.lw = ev
            w.rd = {}

    def op(self, e, fn, reads=(), writes=(), inc=True):
        import os as _os
        self.nops = getattr(self, "nops", 0) + 1
        if self.nops > int(_os.environ.get("P1_OPLIMIT", 10 ** 9)):
            if not inc:
                return None
            return None
        self._emit_waits(e, self._deps(e, reads, writes))
        ins = fn()
        self.ninst += 1
        if inc:
            self.cnt[e] += 1
            ins.then_inc(self.sem[e], 1)
            ev = ("eng", e, self.cnt[e])
        else:
            ev = ("eng", e, self.cnt[e] + 1)
        self._mark(ev, ("eng", e), reads, writes)
        return ins

    def dma(self, q, grp, out, in_, reads=(), writes=(), **kw):
        self._emit_waits(q, self._deps(q, reads, writes))
        ins = self.eng[q].dma_start(out=out, in_=in_, **kw)
        grp.cnt += 16
        ins.then_inc(grp.sem, 16)
        self.ninst += 1
        ev = ("dma", grp, grp.cnt)
        self._mark(ev, ("dma", grp), reads, writes)
        return ins

    def dma_batch(self, q, grp, items):
        allr, allw = [], []
        for it in items:
            allr += list(it.get("reads", ()))
            allw += list(it.get("writes", ()))
        self._emit_waits(q, self._deps(q, allr, allw))
        for it in items:
            ins = self.eng[q].dma_start(out=it["out"], in_=it["in_"], **it.get("kw", {}))
            grp.cnt += 16
            ins.then_inc(grp.sem, 16)
            self.ninst += 1
        ev = ("dma", grp, grp.cnt)
        self._mark(ev, ("dma", grp), allr, allw)

    def barrier(self):
        for e in self.ENG:
            for f in self.ENG:
                if self.cnt[f] == 0 or (f == e and e in ("pe", "sp")):
                    continue
                key = ("eng", f)
                if self.seen[e].get(key, 0) >= self.cnt[f]:
                    continue
                self.eng[e].wait_ge(self.sem[f], self.cnt[f])
                self.seen[e][key] = self.cnt[f]
                self.nwait += 1


class Ring:
    def __init__(self, items):
        self.items = list(items)
        self.i = 0

    def next(self):
        it = self.items[self.i % len(self.items)]
        self.i += 1
        return it


def pipeline(n, stages):
    mx = max(s for _, s in stages)
    for step in range(n + mx):
        for fn, sk in stages:
            i = step - sk
            if 0 <= i < n:
                fn(i)


def build_nc(depth=L_ALL, nb=NB, dbg=None, stop=None):
    nc = bass.Bass("TRN2", target_bir_lowering=False)

    def din(name, shape, dt=F32):
        return nc.dram_tensor(name, list(shape), dt, kind="ExternalInput").ap()

    x_d = din("x", [nb, SEQ, D])
    ctx_d = din("ctx", [nb, LC, D])
    crow_d = din("crow", [3, D])
    wada_d = din("w_ada_r", [L_ALL, 12, 128, 8, 512])
    bada_d = din("b_ada", [L_ALL, 6 * D])
    gn_d = din("gn", [L_ALL, 2, D])
    win_d = din("w_in", [L_ALL, D, 2048])
    wout_d = din("w_out", [L_ALL, D, D])
    wup_d = din("w_up_r", [L_ALL, NCH, 128, 8, 256])
    wdn_d = din("w_down", [L_ALL, DFF, D])
    awsT_d = din("a_wsT", [L_ALL, 128, 4, 128])
    abs_d = din("a_bs", [L_ALL, 4, 128])
    smallg_d = din("smallg", [L_ALL, 256])
    sink_d = din("b_sink", [1, L_ALL * 8])
    clam_d = din("c_lam", [1, L_ALL * 128])
    sublnT_d = din("sublnT", [64, L_ALL])
    cw_d = din("cw_r", [128, L_ALL, 3, 44])
    cb_d = din("cb_r", [128, L_ALL, 44])
    ident_d = din("ident", [128, 128])
    maskL_d = din("maskL", [128, 128])
    maskR_d = din("maskR", [128, 128])
    rbc_d = din("ropeB_C2", [128, 16, 64])
    rbs_d = din("ropeB_S", [128, 16, 2, 32])
    rcc_d = din("ropeC_C2", [128, 16, 32])
    rcs_d = din("ropeC_S", [128, 16, 2, 16])
    y_d = nc.dram_tensor("y", [nb, SEQ, D], F32, kind="ExternalOutput").ap()
    ctxs_d = nc.dram_tensor("ctx_s", [nb, LC, D], F32).ap()
    mod_d = nc.dram_tensor("mod_s", [3, L_ALL, 6, D], F32).ap()

    with ExitStack() as st:
        S = Sched(nc, st)

        uid = [0]

        def T(name, shape, dt, stack=st):
            uid[0] += 1
            return stack.enter_context(nc.sbuf_tensor(f"sb{uid[0]}_{name}", list(shape), dt))

        PB = [st.enter_context(nc.psum_tensor(f"pb{i}", [128, 512], F32)) for i in range(8)]
        RB = [Res(f"pb{i}", excl=True) for i in range(8)]

        def bank_bf(i):
            return PB[i][:].bitcast(BF16)

        g_const = S.group("const")
        g_xin = [S.group(f"xin{i}") for i in range(2)]
        g_xout = [S.group(f"xout{i}") for i in range(2)]
        g_modb = [S.group(f"modb{i}") for i in range(4)]
        g_w = [S.group(f"w{i}") for i in range(6)]
        g_wdn = S.group("wdn")
        g_pre = [S.group(f"pre{i}") for i in range(6)]
        g_zst = [S.group(f"zst{i}") for i in range(2)]
        g_mod = S.group("modw")
        g_dbg = S.group("dbg")

        dbg_groups = []

        def dbg_wait():
            S.finish()

        def dump(name, ap, res):
            if dbg is None:
                return
            d = nc.dram_tensor("dbg_" + name, list(ap.shape), ap.dtype, kind="ExternalOutput").ap()
            gg = S.group("dbg_" + name)
            dbg_groups.append(gg)
            S.dma("sp", gg, d, ap, reads=[res])
            dbg.append(name)

        ident_f = T("ident_f", [128, 128], F32); r_ident = Res("ident")
        ident_b = T("ident_b", [128, 128], BF16)
        ones_f = T("ones_f", [128, 128], F32)
        onesmean = T("onesmean", [128, 128], F32)
        mask_f = T("mask_f", [128, 2, 128], F32)
        mask_b = T("mask_b", [128, 2, 128], BF16)
        ropeB_C = T("ropeB_C", [128, 16, 64], F32)
        ropeB_S = T("ropeB_S", [128, 16, 2, 32], F32)
        ropeC_C = T("ropeC_C", [128, 16, 32], F32)
        ropeC_S = T("ropeC_S", [128, 16, 2, 16], F32)
        smallg = T("smallg", [128, L_ALL, 256], F32)
        biasT = T("biasT", [128, L_ALL, 2, 128], F32)
        cw = T("cw", [128, L_ALL, 3, 44], F32)
        cb = T("cb", [128, L_ALL, 44], F32)
        esink = T("esink", [128, L_ALL * 8], F32)
        clam = T("clam", [128, L_ALL, 4, 32], F32)
        lamt = T("lamt", [128, L_ALL, 2, 32], F32)
        lam2 = T("lam2", [128, L_ALL, 2], F32)
        neglam = T("neglam", [128, L_ALL], F32)
        gsub = T("gsub", [128, L_ALL], F32)
        r_const = Res("const")

        items = [
            dict(out=ident_f[:], in_=ident_d),
            dict(out=mask_f[:, 0, :], in_=maskL_d),
            dict(out=mask_f[:, 1, :], in_=maskR_d),
            dict(out=ropeB_C[:], in_=rbc_d),
            dict(out=ropeB_S[:], in_=rbs_d),
            dict(out=ropeC_C[:], in_=rcc_d),
            dict(out=ropeC_S[:], in_=rcs_d),
            dict(out=smallg[:].rearrange("p l c -> p (l c)"),
                 in_=smallg_d.rearrange("l c -> (l c)").partition_broadcast(128)),
            dict(out=cw[:], in_=cw_d),
            dict(out=cb[:], in_=cb_d),
            dict(out=esink[:], in_=sink_d[0, :].partition_broadcast(128)),
            dict(out=clam[:].rearrange("p l a c -> p (l a c)"), in_=clam_d[0, :].partition_broadcast(128)),
            dict(out=gsub[0:64, :], in_=sublnT_d),
            dict(out=gsub[64:128, :], in_=sublnT_d),
        ]
        for l in range(L_ALL):
            for g in range(4):
                items.append(dict(out=biasT[(g % 2) * 64:(g % 2) * 64 + 64, l, g // 2, :],
                                  in_=abs_d[l, g, :].partition_broadcast(64)))
        for it in items:
            it["writes"] = [r_const]
        S.dma_batch("sp", g_const, items)

        S.op("dve", lambda: nc.vector.tensor_copy(ident_b[:], ident_f[:]), reads=[r_const], writes=[r_ident])
        S.op("dve", lambda: nc.vector.tensor_copy(mask_b[:], mask_f[:]), reads=[r_const], writes=[r_ident])
        S.op("dve", lambda: nc.vector.memset(ones_f[:], 1.0), writes=[r_ident])
        S.op("dve", lambda: nc.vector.memset(onesmean[:], 1.0 / 64.0), writes=[r_ident])
        S.op("act", lambda: nc.scalar.activation(esink[:], esink[:], AF.Exp), reads=[r_const], writes=[r_const])
        S.op("dve", lambda: nc.vector.tensor_tensor(lamt[:], clam[:, :, 0:4:2, :], clam[:, :, 1:4:2, :], ALU.mult),
             reads=[r_const], writes=[r_const])
        S.op("dve", lambda: nc.vector.tensor_reduce(lam2[:], lamt[:], AX.X, ALU.add), reads=[r_const], writes=[r_const])
        S.op("act", lambda: nc.scalar.activation(lam2[:], lam2[:], AF.Exp), reads=[r_const], writes=[r_const])
        S.op("dve", lambda: nc.vector.tensor_tensor(neglam[:], lam2[:, :, 1], lam2[:, :, 0], ALU.subtract),
             reads=[r_const], writes=[r_const])
        for l in range(L_ALL):
            lam_init = 0.8 - 0.6 * math.exp(-0.3 * l)
            S.op("dve", lambda l=l, li=lam_init: nc.vector.tensor_scalar(
                neglam[:, l:l + 1], neglam[:, l:l + 1], -li, None, ALU.add), reads=[r_const], writes=[r_const])
            S.op("dve", lambda l=l, li=lam_init: nc.vector.tensor_scalar(
                gsub[:, l:l + 1], gsub[:, l:l + 1], 1.0 - li, None, ALU.mult), reads=[r_const], writes=[r_const])

        if stop == "const":
            dump("neglam", neglam[:], r_const); dump("gsub", gsub[:], r_const); dump("esink", esink[:], r_const)
            dump("biasT", biasT[:], r_const); dump("mask_b", mask_b[:], r_ident)
            dbg_wait()
            return nc
        with ExitStack() as pp:
            crow = T("crow_sb", [3, D], F32, pp); r_crow = Res("crow")
            scT = T("scT", [128, 8, 3], F32, pp); r_scT = Res("scT")
            rows = T("rows", [3, 6 * D], F32, pp); r_rows = Res("rows")
            bada = T("bada", [3, 6 * D], F32, pp); r_bada = Res("bada")
            gnb = T("gnb", [3, 2, D], F32, pp); r_gnb = Res("gnb")
            wslots = [T(f"wada{i}", [128, 8, 512], F32, pp) for i in range(3)]
            r_wslots = [Res(f"wada{i}") for i in range(3)]
            g_ws = g_pre[0:3]
            g_pp = g_pre[3]
            S.dma("sp", g_pp, crow[:], crow_d, writes=[r_crow])
            S.op("act", lambda: nc.scalar.activation(crow[:], crow[:], AF.Silu), reads=[r_crow], writes=[r_crow])
            for c in range(8):
                S.op("pe", lambda c=c: nc.tensor.transpose(PB[0][:, c * 3:c * 3 + 3], crow[0:3, c * 128:(c + 1) * 128],
                                                           ident_f[0:3, 0:3]),
                     reads=[r_crow, r_const], writes=[RB[0]], inc=(c == 7))
            S.op("dve", lambda: nc.vector.tensor_copy(scT[:].rearrange("p c r -> p (c r)"), PB[0][:, 0:24]),
                 reads=[RB[0]], writes=[r_scT])
            pring = Ring([1, 2, 3])
            k = 0
            for l in range(depth):
                S.dma("sp", g_pre[4], bada[:], bada_d[l, :].partition_broadcast(3), writes=[r_bada])
                S.dma("sp", g_pre[5], gnb[:].rearrange("p a d -> p (a d)"),
                      gn_d[l].rearrange("a d -> (a d)").partition_broadcast(3), writes=[r_gnb])
                for n in range(12):
                    si = k % 3
                    k += 1
                    S.dma("sp", g_ws[si], wslots[si][:],
                          wada_d[l, n],
                          writes=[r_wslots[si]])
                    bi = pring.next()
                    for kc in range(8):
                        S.op("pe", lambda kc=kc, si=si, bi=bi: nc.tensor.matmul(
                            PB[bi][0:3, :], lhsT=scT[:, kc, :], rhs=wslots[si][:, kc, :],
                            start=(kc == 0), stop=(kc == 7)),
                            reads=[r_scT, r_wslots[si]], writes=[RB[bi]], inc=(kc == 7))
                    S.op("dve", lambda n=n, bi=bi: nc.vector.tensor_tensor(
                        rows[:, n * 512:(n + 1) * 512], PB[bi][0:3, :], bada[:, n * 512:(n + 1) * 512], ALU.add),
                        reads=[RB[bi], r_bada], writes=[r_rows])
                for a, kidx in ((0, 1), (1, 4)):
                    S.op("dve", lambda a=a, kidx=kidx: nc.vector.scalar_tensor_tensor(
                        rows[:, kidx * D:(kidx + 1) * D], rows[:, kidx * D:(kidx + 1) * D], 1.0, gnb[:, a, :],
                        ALU.add, ALU.mult), reads=[r_rows, r_gnb], writes=[r_rows])
                S.dma("sp", g_mod, mod_d[:, l].rearrange("r k d -> r (k d)"), rows[:], reads=[r_rows], writes=[])
            r_mod = Res("mod_d")
            r_mod.lw = ("dma", g_mod, g_mod.cnt)
            S.barrier()

        if stop == "prepass":
            if dbg is not None:
                d = nc.dram_tensor("dbg_mod", [3, 1, 6, D], F32, kind="ExternalOutput").ap()
                S.dma("sp", g_dbg, d, mod_d[:, 0:1], reads=[r_mod])
                dbg.append("mod")
            dbg_wait()
            return nc
        xin = [T(f"xin{i}", [128, D], F32) for i in range(2)]
        r_xin = [Res(f"xin{i}") for i in range(2)]
        xin_ring = Ring(range(2))
        xout = [T(f"xout{i}", [128, D], F32) for i in range(2)]
        r_xout = [Res(f"xout{i}") for i in range(2)]
        xout_ring = Ring(range(2))
        modb = [T(f"modb{i}", [128, D], F32) for i in range(4)]
        r_modb = [Res(f"modb{i}") for i in range(4)]
        stat = T("stat", [128, 64], F32)
        r_dx = [[Res(f"dx{b}_{t}") for t in range(NT)] for b in range(nb)]

        def x_src(b, l, t):
            if t < 16:
                base = x_d if l == 0 else y_d
                return base[b, t * 128:(t + 1) * 128, :]
            base = ctx_d if l == 0 else ctxs_d
            return base[b, (t - 16) * 128:(t - 15) * 128, :]

        def x_dst(b, t):
            if t < 16:
                return y_d[b, t * 128:(t + 1) * 128, :]
            return ctxs_d[b, (t - 16) * 128:(t - 15) * 128, :]

        def load_x(b, lsrc, t):
            i = xin_ring.next()
            S.dma("sp", g_xin[i], xin[i][:], x_src(b, lsrc, t), reads=[r_dx[b][t]], writes=[r_xin[i]])
            return i

        def load_modb(i, row, l, kind):
            S.dma("sp", g_modb[i], modb[i][:], mod_d[row, l, kind, :].partition_broadcast(128),
                  reads=[r_mod], writes=[r_modb[i]])

        ev_toggle = [0]

        def evac(out, in_, reads, writes):
            ev_toggle[0] ^= 1
            if ev_toggle[0]:
                S.op("act", lambda: nc.scalar.copy(out, in_), reads=reads, writes=writes)
            else:
                S.op("dve", lambda: nc.vector.tensor_copy(out, in_), reads=reads, writes=writes)

        def norm_tile(xi, mi, shi, hb, r_hb, t1, r_t1, ss_col):
            ss = stat[:, ss_col:ss_col + 1]
            rt = stat[:, ss_col + 1:ss_col + 2]
            r_st = r_stat[ss_col // 2]
            import os as _os
            _k = int(_os.environ.get("P1_S0", 99))
            if _k < 2:
                return
            S.op("act", lambda: nc.scalar.activation(t1, xin[xi][:], AF.Square),
                 reads=[r_xin[xi]], writes=[r_t1])
            S.op("dve", lambda: nc.vector.tensor_reduce(ss, t1, AX.X, ALU.add), reads=[r_t1], writes=[r_st])
            if _k < 3:
                return
            S.op("act", lambda: nc.scalar.activation(rt, ss, AF.Sqrt, scale=1.0 / D, bias=eps_t[:, 0:1]),
                 reads=[r_st, r_ident], writes=[r_st])
            if _k < 4:
                return
            S.op("dve", lambda: nc.vector.reciprocal(rt, rt), reads=[r_st], writes=[r_st])
            if _k < 5:
                return
            S.op("dve", lambda: nc.vector.scalar_tensor_tensor(t1, xin[xi][:], rt, modb[mi][:], ALU.mult, ALU.mult),
                 reads=[r_xin[xi], r_st, r_modb[mi]], writes=[r_t1])
            if _k < 6:
                return
            S.op("dve", lambda: nc.vector.tensor_tensor(hb, t1, modb[shi][:], ALU.add),
                 reads=[r_t1, r_modb[shi]], writes=[r_hb])

        r_stat = [Res(f"stat{i}") for i in range(8)]
        eps_t = T("eps_t", [128, 1], F32)
        S.op("dve", lambda: nc.vector.memset(eps_t[:], EPS), writes=[r_ident])

        def transpose8(hb, r_hb, bi, dst, r_dst):
            pv = bank_bf(bi)
            import os as _os
            _k = int(_os.environ.get("P1_S0", 99))
            if _k < 7:
                return
            for c in range(8):
                S.op("pe", lambda c=c: nc.tensor.transpose(pv[:, c * 128:(c + 1) * 128], hb[:, c * 128:(c + 1) * 128],
                                                           ident_b[:]),
                     reads=[r_hb, r_ident], writes=[RB[bi]], inc=(c == 7))
            if _k < 8:
                return
            evac(dst, pv[:, 0:1024].rearrange("p (c t) -> p c t", c=8), [RB[bi]], [r_dst])

        for b in range(nb):
            for l in range(depth):
                last = (l == depth - 1)
                ntq = 16 if last else 18
                lsrc = l
                load_modb(0, b, l, 1)
                load_modb(1, b, l, 0)
                load_modb(2, 2, l, 1)
                load_modb(3, 2, l, 0)
                with ExitStack() as s1:
                    kbT = T("kbT", [128, NTOK], BF16, s1); r_kbT = [Res(f"kbT{t}") for t in range(NT)]
                    vb = T("vb", [128, NT, 2, 65], BF16, s1); r_vb = [Res(f"vb{t}") for t in range(NT)]
                    kcT = T("kcT", [128, 2, NTOK], BF16, s1); r_kcT = [Res(f"kcT{t}") for t in range(NT)]
                    vc = T("vc", [128, NT, 4, 65], BF16, s1); r_vc = [Res(f"vc{t}") for t in range(NT)]
                    qbT = T("qbT", [128, 4, NTOK], BF16, s1); r_qbT = [Res(f"qbT{t}") for t in range(NT)]
                    qcT = T("qcT", [128, 2, NTOK], BF16, s1); r_qcT = [Res(f"qcT{t}") for t in range(NT)]
                    catA = T("catA", [128, 2, NTOK], BF16, s1)
                    r_catA = [Res(f"catA{t}") for t in range(NT)]
                    r_catB = [Res(f"catB{t}") for t in range(NT)]
                    r_catC = [Res(f"catC{t}") for t in range(NT)]
                    r_vones = Res("vones")
                    S.op("dve", lambda: nc.vector.memset(vb[:, :, :, 64:65], 1.0), writes=[r_vones])
                    S.op("dve", lambda: nc.vector.memset(vc[:, :, :, 64:65], 1.0), writes=[r_vones])

                    with ExitStack() as p1:
                        w_in = T("w_in", [128, 8, 2048], BF16, p1); r_win = Res("w_in")
                        awsT = T("awsT", [128, 4, 128], BF16, p1); r_aws = Res("awsT")
                        for kc in range(8):
                            S.dma("pool", g_w[0], w_in[:, kc, :], win_d[l, kc * 128:(kc + 1) * 128, :], writes=[r_win])
                        S.dma("pool", g_w[1], awsT[:], awsT_d[l], writes=[r_aws])
                        NS = 2
                        hb = [T(f"hb{i}", [128, D], BF16, p1) for i in range(NS)]; r_hb = [Res(f"hb{i}") for i in range(NS)]
                        t1 = [T("t1_0", [128, D], F32, p1)] * NS; r_t1 = [Res("t1_0")] * NS
                        hT = [T(f"hT{i}", [128, 8, 128], BF16, p1) for i in range(3)]; r_hT = [Res(f"hT{i}") for i in range(3)]
                        uT = [T(f"uT{i}", [128, 2, 128], BF16, p1) for i in range(NS)]; r_uT = [Res(f"uT{i}") for i in range(NS)]
                        gv = [T(f"gv{i}", [128, 256], F32, p1) for i in range(NS)]; r_gv = [Res(f"gv{i}") for i in range(NS)]
                        vpad = [T(f"vpad{i}", [128, 4, 128], BF16, p1) for i in range(NS)]; r_vpad = [Res(f"vpad{i}") for i in range(NS)]
                        sq = [T("sq0", [128, 1152], F32, p1)] * NS; r_sq = [Res("sq0")] * NS
                        qn = [T("qn0", [128, 1152], F32, p1)] * NS; r_qn = [Res("qn0")] * NS
                        tb = [T("tb0", [128, 1152], F32, p1)] * NS; r_tb = [Res("tb0")] * NS
                        qr = [T("qr0", [128, 1152], BF16, p1)] * NS; r_qr = [Res("qr0")] * NS
                        st2 = [T(f"st2_{i}", [128, 32], F32, p1) for i in range(NS)]; r_st2 = [Res(f"st2_{i}") for i in range(NS)]
                        ta = [T(f"ta{i}", [128, 256], F32, p1) for i in range(NS)]; r_ta = [Res(f"ta{i}") for i in range(NS)]
                        for i in range(NS):
                            S.op("dve", lambda i=i: nc.vector.memset(vpad[i][:], 0.0), writes=[r_vpad[i]])
                        order = [16, 17] + list(range(16))
                        trp_ring = Ring([0, 1, 2])
                        prj_ring = Ring([3, 4, 5, 6, 7])
                        xi_of = {}
                        banks_of = {}

                        def s0(i):
                            t = order[i]
                            isctx = t >= 16
                            xi = load_x(b, lsrc, t)
                            xi_of[i] = xi
                            sl = i % NS
                            norm_tile(xi, 2 if isctx else 0, 3 if isctx else 1, hb[sl][:], r_hb[sl], t1[sl][:], r_t1[sl], (i % 4) * 2)
                            transpose8(hb[sl], r_hb[sl], trp_ring.next(), hT[i % 3][:], r_hT[i % 3])
                            if stop == "p1dbg" and t == 0:
                                dump("xin", xin[xi][:], r_xin[xi]); dump("hb", hb[sl][:], r_hb[sl]); dump("hT", hT[i % 3][:], r_hT[i % 3])
                                dump("m1b", modb[0][:], r_modb[0]); dump("sh1b", modb[1][:], r_modb[1]); dump("w_in", w_in[:], r_win)

                        def s1f(i):
                            t = order[i]
                            isctx = t >= 16
                            full = (not isctx) or (not last)
                            h = hT[i % 3]; rh = r_hT[i % 3]
                            bk = {}
                            if full:
                                bu = prj_ring.next(); bk["u"] = bu
                                for cc in range(2):
                                    for kc in range(8):
                                        S.op("pe", lambda cc=cc, kc=kc: nc.tensor.matmul(
                                            PB[bu][:, cc * 128:(cc + 1) * 128], lhsT=w_in[:, kc, cc * 128:(cc + 1) * 128],
                                            rhs=h[:, kc, :], start=(kc == 0), stop=(kc == 7)),
                                            reads=[r_win, rh], writes=[RB[bu]], inc=(kc == 7 and cc == 1))
                                bv = prj_ring.next(); bk["v"] = bv
                                for kc in range(8):
                                    S.op("pe", lambda kc=kc: nc.tensor.matmul(
                                        PB[bv][:, 0:256], lhsT=h[:, kc, :], rhs=w_in[:, kc, 256:512],
                                        start=(kc == 0), stop=(kc == 7)), reads=[r_win, rh], writes=[RB[bv]], inc=(kc == 7))
                                bq = prj_ring.next(); bk["q"] = bq
                                for kc in range(8):
                                    S.op("pe", lambda kc=kc: nc.tensor.matmul(
                                        PB[bq][:, :], lhsT=h[:, kc, :], rhs=w_in[:, kc, 512:1024],
                                        start=(kc == 0), stop=(kc == 7)), reads=[r_win, rh], writes=[RB[bq]], inc=(kc == 7))
                            bkk = prj_ring.next(); bk["k"] = bkk
                            for kc in range(8):
                                S.op("pe", lambda kc=kc: nc.tensor.matmul(
                                    PB[bkk][:, :], lhsT=h[:, kc, :], rhs=w_in[:, kc, 1024:1536],
                                    start=(kc == 0), stop=(kc == 7)), reads=[r_win, rh], writes=[RB[bkk]], inc=(kc == 7))
                            bc = prj_ring.next(); bk["c"] = bc
                            for kc in range(8):
                                S.op("pe", lambda kc=kc: nc.tensor.matmul(
                                    PB[bc][:, :], lhsT=h[:, kc, :], rhs=w_in[:, kc, 1536:2048],
                                    start=(kc == 0), stop=(kc == 7)), reads=[r_win, rh], writes=[RB[bc]], inc=(kc == 7))
                            banks_of[i] = bk

                        def s2(i):
                            t = order[i]
                            isctx = t >= 16
                            full = (not isctx) or (not last)
                            bk = banks_of[i]
                            sl = i % NS
                            g0 = l * 256
                            if full:
                                bu, bv = bk["u"], bk["v"]
                                S.op("act", lambda: nc.scalar.activation(
                                    uT[sl][:].rearrange("p c t -> p (c t)"), PB[bu][:, 0:256], AF.Gelu_apprx_tanh),
                                    reads=[RB[bu]], writes=[r_uT[sl]])
                                S.op("act", lambda: nc.scalar.activation(gv[sl][:], PB[bv][:, 0:256], AF.Gelu_apprx_tanh),
                                     reads=[RB[bv]], writes=[r_gv[sl]])
                                S.op("dve", lambda: nc.vector.tensor_tensor(ta[sl][:], gv[sl][:], gv[sl][:], ALU.mult),
                                     reads=[r_gv[sl]], writes=[r_ta[sl]])
                                S.op("dve", lambda: nc.vector.tensor_reduce(
                                    st2[sl][:, 26:30], ta[sl][:].rearrange("p (g c) -> p g c", g=4), AX.X, ALU.add),
                                    reads=[r_ta[sl]], writes=[r_st2[sl]])
                                S.op("act", lambda: nc.scalar.activation(st2[sl][:, 26:30], st2[sl][:, 26:30], AF.Sqrt,
                                                                         scale=1.0 / 64, bias=eps_t[:, 0:1]),
                                     reads=[r_st2[sl], r_ident], writes=[r_st2[sl]])
                                S.op("dve", lambda: nc.vector.reciprocal(st2[sl][:, 26:30], st2[sl][:, 26:30]),
                                     reads=[r_st2[sl]], writes=[r_st2[sl]])
                                for par in range(2):
                                    S.op("dve", lambda par=par: nc.vector.tensor_tensor(
                                        vpad[sl][:, par:4:2, par * 64:par * 64 + 64],
                                        gv[sl][:].rearrange("p (g c) -> p g c", g=4)[:, par:4:2, :],
                                        st2[sl][:, 26 + par:30:2].unsqueeze(2).to_broadcast([128, 2, 64]), ALU.mult),
                                        reads=[r_gv[sl], r_st2[sl]], writes=[r_vpad[sl]])
                                bm = prj_ring.next()
                                for g in range(4):
                                    S.op("pe", lambda g=g: nc.tensor.matmul(
                                        PB[bm][:, (g // 2) * 128:(g // 2) * 128 + 128], lhsT=vpad[sl][:, g, :], rhs=awsT[:, g, :],
                                        start=(g % 2 == 0), stop=(g % 2 == 1)),
                                        reads=[r_vpad[sl], r_aws], writes=[RB[bm]], inc=(g == 3))
                                S.op("dve", lambda: nc.vector.tensor_tensor(
                                    ta[sl][:], PB[bm][:, 0:256], biasT[:, l].rearrange("p c t -> p (c t)"), ALU.add),
                                    reads=[RB[bm], r_const], writes=[r_ta[sl]])
                                S.op("dve", lambda: nc.vector.tensor_tensor(
                                    catA[:, 0:2, t * 128:(t + 1) * 128], ta[sl][:].rearrange("p (c t) -> p c t", c=2),
                                    uT[sl][:], ALU.mult), reads=[r_ta[sl], r_uT[sl]], writes=[r_catA[t]])
                            bkk, bc = bk["k"], bk["c"]
                            if full:
                                bq = bk["q"]
                                S.op("act", lambda: nc.scalar.activation(sq[sl][:, 0:512], PB[bq][:, :], AF.Square),
                                     reads=[RB[bq]], writes=[r_sq[sl]])
                                S.op("act", lambda: nc.scalar.activation(sq[sl][:, 640:896], PB[bkk][:, 256:512], AF.Square),
                                     reads=[RB[bkk]], writes=[r_sq[sl]])
                            else:
                                S.op("dve", lambda: nc.vector.memset(sq[sl][:, 0:512], 1.0), writes=[r_sq[sl]])
                                S.op("dve", lambda: nc.vector.memset(sq[sl][:, 640:896], 1.0), writes=[r_sq[sl]])
                            S.op("act", lambda: nc.scalar.activation(sq[sl][:, 512:640], PB[bkk][:, 0:128], AF.Square),
                                 reads=[RB[bkk]], writes=[r_sq[sl]])
                            S.op("act", lambda: nc.scalar.activation(sq[sl][:, 896:1152], PB[bc][:, 0:256], AF.Square),
                                 reads=[RB[bc]], writes=[r_sq[sl]])
                            S.op("dve", lambda: nc.vector.tensor_reduce(
                                st2[sl][:, 0:10], sq[sl][:, 0:640].rearrange("p (h d) -> p h d", d=64), AX.X, ALU.add),
                                reads=[r_sq[sl]], writes=[r_st2[sl]])
                            S.op("dve", lambda: nc.vector.tensor_reduce(
                                st2[sl][:, 10:26], sq[sl][:, 640:1152].rearrange("p (h d) -> p h d", d=32), AX.X, ALU.add),
                                reads=[r_sq[sl]], writes=[r_st2[sl]])
                            S.op("act", lambda: nc.scalar.activation(st2[sl][:, 0:10], st2[sl][:, 0:10], AF.Sqrt,
                                                                     scale=1.0 / 64, bias=eps_t[:, 0:1]),
                                 reads=[r_st2[sl], r_ident], writes=[r_st2[sl]])
                            S.op("act", lambda: nc.scalar.activation(st2[sl][:, 10:26], st2[sl][:, 10:26], AF.Sqrt,
                                                                     scale=1.0 / 32, bias=eps_t[:, 0:1]),
                                 reads=[r_st2[sl], r_ident], writes=[r_st2[sl]])
                            S.op("dve", lambda: nc.vector.reciprocal(st2[sl][:, 0:26], st2[sl][:, 0:26]),
                                 reads=[r_st2[sl]], writes=[r_st2[sl]])
                            specs = []
                            if full:
                                specs.append(("bq", PB[bk["q"]][:, :], RB[bk["q"]], 0, 512, 8, 64, 0, 0, ropeB_C, ropeB_S))
                            specs.append(("bk", PB[bkk][:, 0:128], RB[bkk], 512, 128, 2, 64, 8, 64, ropeB_C, ropeB_S))
                            if full:
                                specs.append(("cq", PB[bkk][:, 256:512], RB[bkk], 640, 256, 8, 32, 10, 128, ropeC_C, ropeC_S))
                            specs.append(("ck", PB[bc][:, 0:256], RB[bc], 896, 256, 8, 32, 18, 160, ropeC_C, ropeC_S))
                            for (nm, src, rsrc, c0, wd, nh, hd, sc0, gc0, rC, rS) in specs:
                                v3 = lambda ap, nh=nh: ap.rearrange("p (h d) -> p h d", h=nh)
                                qv = qn[sl][:, c0:c0 + wd]
                                S.op("dve", lambda src=src, qv=qv, v3=v3, sc0=sc0, nh=nh, hd=hd: nc.vector.tensor_tensor(
                                    v3(qv), v3(src), st2[sl][:, sc0:sc0 + nh].unsqueeze(2).to_broadcast([128, nh, hd]), ALU.mult),
                                    reads=[rsrc, r_st2[sl]], writes=[r_qn[sl]])
                                gain_b = smallg[:, l, gc0:gc0 + hd].unsqueeze(1).to_broadcast([128, nh, hd])
                                if isctx:
                                    if nm == "bq":
                                        outv = qr[sl][:, 0:512].rearrange("p (c hi d) -> p hi c d", c=4, hi=2)
                                        inv = qv.rearrange("p (hi c d) -> p hi c d", hi=2, c=4)
                                        gb = smallg[:, l, gc0:gc0 + hd].unsqueeze(1).unsqueeze(1).to_broadcast([128, 2, 4, hd])
                                        S.op("dve", lambda outv=outv, inv=inv, gb=gb: nc.vector.tensor_tensor(outv, inv, gb, ALU.mult),
                                             reads=[r_qn[sl], r_const], writes=[r_qr[sl]])
                                    else:
                                        S.op("dve", lambda qv=qv, v3=v3, gain_b=gain_b, c0=c0, wd=wd: nc.vector.tensor_tensor(
                                            v3(qr[sl][:, c0:c0 + wd]), v3(qv), gain_b, ALU.mult),
                                            reads=[r_qn[sl], r_const], writes=[r_qr[sl]])
                                    continue
                                S.op("dve", lambda qv=qv, v3=v3, gain_b=gain_b: nc.vector.tensor_tensor(v3(qv), v3(qv), gain_b, ALU.mult),
                                     reads=[r_qn[sl], r_const], writes=[r_qn[sl]])
                                nf = hd // 4
                                v4 = lambda ap, nf=nf: ap.rearrange("p (ha pr f) -> p ha pr f", pr=2, f=nf)
                                tbv = tb[sl][:, c0:c0 + wd]
                                Sn = rS[:, t, 0, :].rearrange("p (a f) -> p a f", a=2).unsqueeze(1).to_broadcast([128, nh, 2, nf])
                                Sp = rS[:, t, 1, :].rearrange("p (a f) -> p a f", a=2).unsqueeze(1).to_broadcast([128, nh, 2, nf])
                                v5 = lambda ap, nh=nh, nf=nf: ap.rearrange("p (h a pr f) -> p h a pr f", h=nh, a=2, pr=2)
                                S.op("dve", lambda tbv=tbv, qv=qv, v5=v5, Sn=Sn: nc.vector.tensor_tensor(
                                    v5(tbv)[:, :, :, 0, :], v5(qv)[:, :, :, 1, :], Sn, ALU.mult),
                                    reads=[r_qn[sl], r_const], writes=[r_tb[sl]])
                                S.op("dve", lambda tbv=tbv, qv=qv, v5=v5, Sp=Sp: nc.vector.tensor_tensor(
                                    v5(tbv)[:, :, :, 1, :], v5(qv)[:, :, :, 0, :], Sp, ALU.mult),
                                    reads=[r_qn[sl], r_const], writes=[r_tb[sl]])
                                Cb = rC[:, t, :].unsqueeze(1).to_broadcast([128, nh, hd])
                                S.op("dve", lambda qv=qv, v3=v3, Cb=Cb: nc.vector.tensor_tensor(v3(qv), v3(qv), Cb, ALU.mult),
                                     reads=[r_qn[sl], r_const], writes=[r_qn[sl]])
                                if nm == "bq":
                                    outv = qr[sl][:, 0:512].rearrange("p (c hi d) -> p hi c d", c=4, hi=2)
                                    a0 = qv.rearrange("p (hi c d) -> p hi c d", hi=2, c=4)
                                    a1 = tbv.rearrange("p (hi c d) -> p hi c d", hi=2, c=4)
                                    S.op("dve", lambda outv=outv, a0=a0, a1=a1: nc.vector.tensor_tensor(outv, a0, a1, ALU.add),
                                         reads=[r_qn[sl], r_tb[sl]], writes=[r_qr[sl]])
                                else:
                                    S.op("dve", lambda qv=qv, tbv=tbv, c0=c0, wd=wd: nc.vector.tensor_tensor(
                                        qr[sl][:, c0:c0 + wd], qv, tbv, ALU.add),
                                        reads=[r_qn[sl], r_tb[sl]], writes=[r_qr[sl]])
                            S.op("act", lambda: nc.scalar.copy(vb[:, t, :, 0:64], PB[bkk][:, 128:256].rearrange("p (h d) -> p h d", h=2)),
                                 reads=[RB[bkk]], writes=[r_vb[t]])
                            S.op("act", lambda: nc.scalar.copy(vc[:, t, :, 0:64], PB[bc][:, 256:512].rearrange("p (h d) -> p h d", h=4)),
                                 reads=[RB[bc]], writes=[r_vc[t]])
                            bt = trp_ring.next()
                            pv = bank_bf(bt)
                            lst = []
                            if full:
                                for c in range(4):
                                    lst.append((c, qr[sl][:, c * 128:(c + 1) * 128]))
                            lst.append((4, qr[sl][:, 512:640]))
                            for c, src in lst:
                                S.op("pe", lambda c=c, src=src: nc.tensor.transpose(pv[:, c * 128:(c + 1) * 128], src, ident_b[:]),
                                     reads=[r_qr[sl], r_ident], writes=[RB[bt]], inc=(c == 4))
                            if full:
                                evac(qbT[:, :, t * 128:(t + 1) * 128], pv[:, 0:512].rearrange("p (c t) -> p c t", c=4),
                                     [RB[bt]], [r_qbT[t]])
                            evac(kbT[:, t * 128:(t + 1) * 128], pv[:, 512:640], [RB[bt]], [r_kbT[t]])
                            bt2 = trp_ring.next()
                            pv2 = bank_bf(bt2)
                            lst = []
                            if full:
                                lst += [(0, qr[sl][:, 640:768]), (1, qr[sl][:, 768:896])]
                            lst += [(2, qr[sl][:, 896:1024]), (3, qr[sl][:, 1024:1152])]
                            for c, src in lst:
                                S.op("pe", lambda c=c, src=src: nc.tensor.transpose(pv2[:, c * 128:(c + 1) * 128], src, ident_b[:]),
                                     reads=[r_qr[sl], r_ident], writes=[RB[bt2]], inc=(c == 3))
                            if full:
                                evac(qcT[:, :, t * 128:(t + 1) * 128], pv2[:, 0:256].rearrange("p (c t) -> p c t", c=2),
                                     [RB[bt2]], [r_qcT[t]])
                            evac(kcT[:, :, t * 128:(t + 1) * 128], pv2[:, 256:512].rearrange("p (c t) -> p c t", c=2),
                                 [RB[bt2]], [r_kcT[t]])

                        import os as _os
                        _ntl = int(_os.environ.get("P1_TILES", NT))
                        _nst = int(_os.environ.get("P1_STAGES", 3))
                        pipeline(_ntl, [(s2, 2), (s1f, 1), (s0, 0)][3 - _nst:])
                        if b == 0 and l == 0:
                            print("sbuf remaining p1", nc.sbuf_bytes_remaining)
                        S.barrier()
                    if stop == "p1x":
                        print("NOPS", getattr(S, "nops", 0))
                        dump("hT0", hT[0][:], r_hT[0])
                        dbg_wait()
                        return nc
                    if stop in ("p1", "p1dbg") and dbg is not None:
                        dump("kbT", kbT[:], r_kbT[0]); dump("kcT", kcT[:], r_kcT[0]); dump("qbT", qbT[:, :, 0:ntq * 128], r_qbT[0])
                        dump("qcT", qcT[:, :, 0:ntq * 128], r_qcT[0]); dump("vb", vb[:], r_vb[0]); dump("vc", vc[:], r_vc[0])
                        dump("catA", catA[:, :, 0:ntq * 128], r_catA[0])
                        dbg_wait()
                        return nc

                    with ExitStack() as p2:
                        catBC = T("catBC", [128, 6, NTOK], BF16, p2)
                        w_out = T("w_out", [128, 8, D], BF16, p2); r_wout = Res("w_out")
                        for kc in range(8):
                            S.dma("pool", g_w[2], w_out[:, kc, :], wout_d[l, kc * 128:(kc + 1) * 128, :], writes=[r_wout])
                        NP = 10
                        pT = [T(f"pT{i}", [128, 512], BF16, p2) for i in range(NP)]; r_pT = [Res(f"pT{i}") for i in range(NP)]
                        pT_ring = Ring(range(NP))
                        zst = [T(f"zst{i}", [128, 512], BF16, p2) for i in range(2)]; r_zst = [Res(f"zst{i}") for i in range(2)]
                        zst_ring = Ring(range(2))
                        rr = [T("rr0", [128, 1024], F32, p2)] * 2; r_rr = [Res("rr0")] * 2
                        Rsb = [T("Rsb0", [128, 1024], F32, p2)] * 2; r_Rsb = [Res("Rsb0")] * 2
                        yy = [T(f"yy{i}", [128, 512], F32, p2) for i in range(2)]; r_yy = [Res(f"yy{i}") for i in range(2)]
                        y1 = [T(f"y1{i}", [128, 512], F32, p2) for i in range(2)]; r_y1 = [Res(f"y1{i}") for i in range(2)]
                        ysq = [T(f"ysq{i}", [128, 512], F32, p2) for i in range(2)]; r_ysq = [Res(f"ysq{i}") for i in range(2)]
                        rsd = [T(f"rsd{i}", [128, 512], F32, p2) for i in range(2)]; r_rsd = [Res(f"rsd{i}") for i in range(2)]

                        s_ring = Ring([0, 1, 2, 3])
                        acc_ring = Ring([4, 5])
                        R_ring = Ring([6, 7])
                        units = [(kvh, n) for n in range(ntq) for kvh in range(2)]
                        bstate = {}
                        bacc = {}

                        def b_scores(ui):
                            kvh, n = units[ui]
                            if n < 16:
                                kts = []
                                if n > 0:
                                    kts.append((n - 1, 0))
                                kts.append((n, None))
                                if n < 15:
                                    kts.append((n + 1, 1))
                                kts += [(16, None), (17, None)]
                            else:
                                kts = [(16, None), (17, None)]
                            pb0 = kvh * 64
                            plist = []
                            for (kt, mk) in kts:
                                sb = s_ring.next()
                                S.op("pe", lambda sb=sb, kt=kt: nc.tensor.matmul(
                                    PB[sb][:, :], lhsT=kbT[pb0:pb0 + 64, kt * 128:(kt + 1) * 128],
                                    rhs=qbT[pb0:pb0 + 64, :, n * 128:(n + 1) * 128], start=True, stop=True),
                                    reads=[r_kbT[kt], r_qbT[n]], writes=[RB[sb]])
                                pi = pT_ring.next()
                                S.op("act", lambda sb=sb, pi=pi: nc.scalar.activation(pT[pi][:], PB[sb][:, :], AF.Exp, scale=0.125),
                                     reads=[RB[sb]], writes=[r_pT[pi]])
                                if mk is not None:
                                    S.op("dve", lambda pi=pi, mk=mk: nc.vector.tensor_tensor(
                                        pT[pi][:].rearrange("p (g q) -> p g q", g=4), pT[pi][:].rearrange("p (g q) -> p g q", g=4),
                                        mask_b[:, mk, :].unsqueeze(1).to_broadcast([128, 4, 128]), ALU.mult),
                                        reads=[r_pT[pi], r_ident], writes=[r_pT[pi]])
                                plist.append((kt, pi))
                            bstate[ui] = plist

                        def b_pv(ui):
                            kvh, n = units[ui]
                            plist = bstate.pop(ui)
                            ab = acc_ring.next()
                            nk = len(plist)
                            for j, (kt, pi) in enumerate(plist):
                                S.op("pe", lambda j=j, kt=kt, pi=pi: nc.tensor.matmul(
                                    PB[ab][0:65, :], lhsT=vb[:, kt, kvh, :], rhs=pT[pi][:], start=(j == 0), stop=(j == nk - 1)),
                                    reads=[r_vb[kt], r_vones, r_pT[pi]], writes=[RB[ab]], inc=(j == nk - 1))
                            bacc[ui] = ab

                        def b_post(ui):
                            kvh, n = units[ui]
                            ab = bacc.pop(ui)
                            ri = ui % 2
                            hd0 = kvh * 4
                            S.op("dve", lambda: nc.vector.tensor_tensor(
                                rr[ri][64:65, 0:512].rearrange("p (g q) -> p g q", g=4),
                                PB[ab][64:65, :].rearrange("p (g q) -> p g q", g=4),
                                esink[64:65, l * 8 + hd0:l * 8 + hd0 + 4].unsqueeze(2).to_broadcast([1, 4, 128]),
                                ALU.add), reads=[RB[ab], r_const], writes=[r_rr[ri]])
                            S.op("dve", lambda: nc.vector.reciprocal(rr[ri][64:65, 0:512], rr[ri][64:65, 0:512]),
                                 reads=[r_rr[ri]], writes=[r_rr[ri]])
                            rb = R_ring.next()
                            S.op("pe", lambda: nc.tensor.matmul(PB[rb][0:64, :], lhsT=ones_f[64:65, 0:64], rhs=rr[ri][64:65, 0:512],
                                                                start=True, stop=True),
                                 reads=[r_rr[ri], r_ident], writes=[RB[rb]])
                            S.op("act", lambda: nc.scalar.copy(Rsb[ri][0:64, 0:512], PB[rb][0:64, :]), reads=[RB[rb]], writes=[r_Rsb[ri]])
                            c0 = kvh * 2
                            v4 = lambda ap: ap.rearrange("p (g q) -> p g q", g=4)
                            S.op("dve", lambda: nc.vector.tensor_tensor(
                                catBC[0:64, c0:c0 + 2, n * 128:(n + 1) * 128], v4(PB[ab][0:64, :])[:, 0:4:2, :],
                                v4(Rsb[ri][0:64, 0:512])[:, 0:4:2, :], ALU.mult),
                                reads=[RB[ab], r_Rsb[ri]], writes=[r_catB[n]])
                            zi = zst_ring.next()
                            S.op("dve", lambda: nc.vector.tensor_tensor(
                                zst[zi][0:64, 0:256].rearrange("p (g q) -> p g q", g=2), v4(PB[ab][0:64, :])[:, 1:4:2, :],
                                v4(Rsb[ri][0:64, 0:512])[:, 1:4:2, :], ALU.mult),
                                reads=[RB[ab], r_Rsb[ri]], writes=[r_zst[zi]])
                            S.dma("sp", g_zst[zi], catBC[64:128, c0:c0 + 2, n * 128:(n + 1) * 128],
                                  zst[zi][0:64, 0:256].rearrange("p (g q) -> p g q", g=2), reads=[r_zst[zi]], writes=[r_catB[n]])

                        if b == 0 and l == 0:
                            print("sbuf remaining p2", nc.sbuf_bytes_remaining)
                        pipeline(len(units), [(b_post, 2), (b_scores, 0), (b_pv, 1)])

                        qgroups = [(g * 512, 512, list(range(NT)), list(range(4 * g, 4 * g + 4))) for g in range(4)]
                        if not last:
                            qgroups.append((2048, 256, [16, 17], [16, 17]))
                        s_ring = Ring([0, 1, 2, 3])
                        scl = 32 ** -0.5
                        def c_post(ui, h, q0, W, qtiles, accb, mb):
                            hc = h // 2
                            ri = ui % 2
                            po = 0
                            pr_ = 64
                            for m in range(2):
                                S.op("dve", lambda m=m: nc.vector.reciprocal(rr[ri][pr_:pr_ + 1, m * 512:m * 512 + W],
                                                                              PB[accb[m]][pr_:pr_ + 1, 0:W]),
                                     reads=[RB[accb[m]]], writes=[r_rr[ri]])
                            S.op("dve", lambda: nc.vector.tensor_scalar(rr[ri][pr_:pr_ + 1, 512:512 + W], rr[ri][pr_:pr_ + 1, 512:512 + W],
                                                                         neglam[pr_:pr_ + 1, l:l + 1], None, ALU.mult),
                                 reads=[r_rr[ri], r_const], writes=[r_rr[ri]])
                            for m in range(2):
                                S.op("pe", lambda m=m: nc.tensor.matmul(PB[mb[m]][0:64, 0:W], lhsT=ones_f[pr_:pr_ + 1, 0:64],
                                                                        rhs=rr[ri][pr_:pr_ + 1, m * 512:m * 512 + W], start=True, stop=True),
                                     reads=[r_rr[ri], r_ident], writes=[RB[mb[m]]])
                                S.op("act", lambda m=m: nc.scalar.copy(Rsb[ri][po:po + 64, m * 512:m * 512 + W], PB[mb[m]][po:po + 64, 0:W]),
                                     reads=[RB[mb[m]]], writes=[r_Rsb[ri]])
                            S.op("dve", lambda: nc.vector.tensor_tensor(yy[ri][po:po + 64, 0:W], PB[accb[0]][po:po + 64, 0:W],
                                                                         Rsb[ri][po:po + 64, 0:W], ALU.mult),
                                 reads=[RB[accb[0]], r_Rsb[ri]], writes=[r_yy[ri]])
                            S.op("dve", lambda: nc.vector.tensor_tensor(y1[ri][po:po + 64, 0:W], PB[accb[1]][po:po + 64, 0:W],
                                                                         Rsb[ri][po:po + 64, 512:512 + W], ALU.mult),
                                 reads=[RB[accb[1]], r_Rsb[ri]], writes=[r_y1[ri]])
                            S.op("dve", lambda: nc.vector.tensor_tensor(yy[ri][po:po + 64, 0:W], yy[ri][po:po + 64, 0:W],
                                                                         y1[ri][po:po + 64, 0:W], ALU.add),
                                 reads=[r_yy[ri], r_y1[ri]], writes=[r_yy[ri]])
                            S.op("act", lambda: nc.scalar.activation(ysq[ri][po:po + 64, 0:W], yy[ri][po:po + 64, 0:W], AF.Square),
                                 reads=[r_yy[ri]], writes=[r_ysq[ri]])
                            S.op("pe", lambda: nc.tensor.matmul(PB[mb[2]][0:64, 0:W], lhsT=onesmean[po:po + 64, 0:64], rhs=ysq[ri][po:po + 64, 0:W],
                                                                start=True, stop=True),
                                 reads=[r_ysq[ri], r_ident], writes=[RB[mb[2]]])
                            S.op("act", lambda: nc.scalar.activation(rsd[ri][po:po + 64, 0:W], PB[mb[2]][po:po + 64, 0:W], AF.Sqrt,
                                                                     bias=eps_t[po:po + 64, 0:1]),
                                 reads=[RB[mb[2]], r_ident], writes=[r_rsd[ri]])
                            S.op("dve", lambda: nc.vector.reciprocal(rsd[ri][po:po + 64, 0:W], rsd[ri][po:po + 64, 0:W]),
                                 reads=[r_rsd[ri]], writes=[r_rsd[ri]])
                            if h % 2 == 0:
                                S.op("dve", lambda: nc.vector.scalar_tensor_tensor(
                                    catBC[0:64, 4 + hc, q0:q0 + W], yy[ri][0:64, 0:W], gsub[0:64, l:l + 1],
                                    rsd[ri][0:64, 0:W], ALU.mult, ALU.mult),
                                    reads=[r_yy[ri], r_rsd[ri], r_const], writes=[r_catC[qt] for qt in qtiles])
                            else:
                                zi = zst_ring.next()
                                S.op("dve", lambda: nc.vector.scalar_tensor_tensor(
                                    zst[zi][0:64, 0:W], yy[ri][0:64, 0:W], gsub[0:64, l:l + 1],
                                    rsd[ri][0:64, 0:W], ALU.mult, ALU.mult),
                                    reads=[r_yy[ri], r_rsd[ri], r_const], writes=[r_zst[zi]])
                                S.dma("sp", g_zst[zi], catBC[64:128, 4 + hc, q0:q0 + W], zst[zi][0:64, 0:W],
                                      reads=[r_zst[zi]], writes=[r_catC[qt] for qt in qtiles])

                        cunits = [(hc, qg) for qg in qgroups for hc in range(2)]
                        for ui, (hc, (q0, W, kts, qtiles)) in enumerate(cunits):
                            nk = len(kts)
                            sbs = {}
                            combos = [(2 * hc + hh, m) for hh in range(2) for m in range(2)]
                            accs = {cm: 4 + k for k, cm in enumerate(combos)}

                            def c_score(ki, q0=q0, W=W, kts=kts, qtiles=qtiles, hc=hc, sbs=sbs, combos=combos):
                                kt = kts[ki]
                                sbl = []
                                for (h, m) in combos:
                                    pb = 32 * (2 * (h % 2) + m)
                                    sb = s_ring.next()
                                    kw = dict(tile_position=(96, 0)) if pb == 96 else {}
                                    S.op("pe", lambda sb=sb, pb=pb, kw=kw: nc.tensor.matmul(
                                        PB[sb][:, 0:W], lhsT=kcT[pb:pb + 32, hc, kt * 128:(kt + 1) * 128],
                                        rhs=qcT[pb:pb + 32, hc, q0:q0 + W], start=True, stop=True, **kw),
                                        reads=[r_kcT[kt]] + [r_qcT[qt] for qt in qtiles], writes=[RB[sb]])
                                    sbl.append(sb)
                                for (h, m), sb in zip(combos, sbl):
                                    pi = pT_ring.next()
                                    S.op("act", lambda sb=sb, pi=pi: nc.scalar.activation(pT[pi][:, 0:W], PB[sb][:, 0:W], AF.Exp, scale=scl),
                                         reads=[RB[sb]], writes=[r_pT[pi]])
                                    sbs[(ki, h, m)] = pi

                            def c_pv(ki, W=W, kts=kts, nk=nk, sbs=sbs, combos=combos, accs=accs):
                                kt = kts[ki]
                                for (h, m) in combos:
                                    pi = sbs.pop((ki, h, m))
                                    ab = accs[(h, m)]
                                    S.op("pe", lambda h=h, pi=pi, ab=ab: nc.tensor.matmul(
                                        PB[ab][0:65, 0:W], lhsT=vc[:, kt, h, :], rhs=pT[pi][:, 0:W], start=(ki == 0), stop=(ki == nk - 1)),
                                        reads=[r_vc[kt], r_vones, r_pT[pi]], writes=[RB[ab]], inc=(ki == nk - 1))

                            pipeline(nk, [(c_score, 0), (c_pv, 1)])
                            for hh in range(2):
                                h = 2 * hc + hh
                                c_post(2 * ui + hh, h, q0, W, qtiles, (accs[(h, 0)], accs[(h, 1)]), (0, 1, 2) if hh == 0 else (3, 0, 1))

                        if stop == "p2" and dbg is not None:
                            dump("catA", catA[:, :, 0:ntq * 128], r_catA[0])
                            for t in range(ntq):
                                S._emit_waits("sp", S._deps("sp", [r_catB[t], r_catC[t], r_catA[t]], []))
                            dump("catBC", catBC[:, :, 0:ntq * 128], r_catA[0])
                            for t in range(0):
                                S._emit_waits("sp", S._deps("sp", [r_catB[t], r_catC[t], r_catA[t]], []))
                            dbg_wait()
                            return nc

                        load_modb(0, b, l, 2)
                        load_modb(2, 2, l, 2)
                        pair_ring = Ring([(0, 1), (2, 3), (4, 5), (6, 7)])
                        pstate = {}

                        def o_mm(t):
                            b0, b1 = pair_ring.next()
                            for nh, bi in enumerate((b0, b1)):
                                for kc in range(8):
                                    S.op("pe", lambda nh=nh, bi=bi, kc=kc: nc.tensor.matmul(
                                        PB[bi][:, :], lhsT=(catA[:, kc, t * 128:(t + 1) * 128] if kc < 2 else catBC[:, kc - 2, t * 128:(t + 1) * 128]),
                                        rhs=w_out[:, kc, nh * 512:(nh + 1) * 512],
                                        start=(kc == 0), stop=(kc == 7)),
                                        reads=[r_catA[t], r_catB[t], r_catC[t], r_wout], writes=[RB[bi]], inc=(kc == 7))
                            pstate[t] = (b0, b1)

                        def o_res(t):
                            b0, b1 = pstate.pop(t)
                            xi = load_x(b, lsrc, t)
                            xo = xout_ring.next()
                            gi = 0 if t < 16 else 2
                            for nh, bi in enumerate((b0, b1)):
                                S.op("dve", lambda nh=nh, bi=bi: nc.vector.tensor_tensor(
                                    xout[xo][:, nh * 512:(nh + 1) * 512], PB[bi][:, :], modb[gi][:, nh * 512:(nh + 1) * 512], ALU.mult),
                                    reads=[RB[bi], r_modb[gi]], writes=[r_xout[xo]])
                            S.op("dve", lambda: nc.vector.tensor_tensor(xout[xo][:], xout[xo][:], xin[xi][:], ALU.add),
                                 reads=[r_xout[xo], r_xin[xi]], writes=[r_xout[xo]])
                            S.dma("sp", g_xout[xo], x_dst(b, t), xout[xo][:], reads=[r_xout[xo]], writes=[r_dx[b][t]])

                        pipeline(ntq, [(o_mm, 0), (o_res, 1)])
                        S.barrier()
                if stop == "s1":
                    break

                load_modb(0, b, l, 4)
                load_modb(1, b, l, 3)
                load_modb(2, 2, l, 4)
                load_modb(3, 2, l, 3)
                ncols = ntq * 128
                with ExitStack() as s2c:
                    h2T = T("h2T", [128, 8, NTOK], BF16, s2c); r_h2T = [Res(f"h2T{t}") for t in range(NT)]
                    hid = T("hid", [128, 8, NTOK], BF16, s2c); r_hid = [Res(f"hid{j}") for j in range(8)]
                    wdn = T("wdn", [128, 8, D], BF16, s2c); r_wdn = Res("wdn")
                    wup = [T(f"wup{i}", [128, 8, 256], BF16, s2c) for i in range(3)]; r_wup = [Res(f"wup{i}") for i in range(3)]
                    tbuf = [T(f"tbuf{i}", [128, NTOK], F32, s2c) for i in range(3)]; r_tbuf = [[Res(f"tbuf{i}_{k}") for k in range(5)] for i in range(3)]
                    tb_ring = Ring(range(3))
                    hb2 = [T(f"hb2_{i}", [128, D], BF16, s2c) for i in range(2)]; r_hb2 = [Res(f"hb2_{i}") for i in range(2)]
                    t12 = [T("t12_0", [128, D], F32, s2c)] * 2; r_t12 = [Res("t12_0")] * 2
                    trp_ring = Ring([0, 1, 2, 3])
                    if b == 0 and l == 0:
                        print("sbuf remaining ffn", nc.sbuf_bytes_remaining)
                    def f0a(t):
                        isctx = t >= 16
                        xi = load_x(b, l + 1, t)
                        sl = t % 2
                        norm_tile(xi, 2 if isctx else 0, 3 if isctx else 1, hb2[sl][:], r_hb2[sl], t12[sl][:], r_t12[sl], (t % 4) * 2)

                    def f0b(t):
                        sl = t % 2
                        transpose8(hb2[sl], r_hb2[sl], trp_ring.next(), h2T[:, :, t * 128:(t + 1) * 128], r_h2T[t])

                    pipeline(ntq, [(f0b, 1), (f0a, 0)])
                    blocks = [(g * 512, 512, list(range(4 * g, 4 * g + 4)), g > 0) for g in range(4)]
                    if not last:
                        blocks.append((2048, 256, [16, 17], False))
                    wk = 0
                    load_modb(0, b, l, 5)
                    load_modb(2, 2, l, 5)
                    for (j0, npart) in ((0, 8), (8, 7), (15, 7)):
                        for part in range(npart):
                            jj = j0 + part
                            S.dma("pool", g_wdn, wdn[:, part, :], wdn_d[l, jj * 128:(jj + 1) * 128, :],
                                  reads=[], writes=[r_wdn])
                        up_ring = Ring([0, 1, 2, 3, 4, 5, 6, 7])
                        pend_hid = []
                        for j in range(npart):
                            jj = j0 + j
                            wi = wk % 3
                            wk += 1
                            S.dma("pool", g_w[3 + wi], wup[wi][:], wup_d[l, jj], writes=[r_wup[wi]])
                            tbs = []
                            for row in range(2):
                                ch = jj + row * NCH
                                ti_ = tb_ring.next()
                                tbs.append(ti_)
                                tbv = tbuf[ti_]
                                prev = None
                                for bk_i, (c0, W, tiles, cont) in enumerate(blocks):
                                    rtb = r_tbuf[ti_][bk_i]
                                    rtb_prev = r_tbuf[ti_][bk_i - 1] if bk_i > 0 else None
                                    bi = up_ring.next()
                                    for kc in range(8):
                                        S.op("pe", lambda kc=kc, bi=bi, c0=c0, W=W, row=row: nc.tensor.matmul(
                                            PB[bi][:, 0:W], lhsT=wup[wi][:, kc, row * 128:(row + 1) * 128], rhs=h2T[:, kc, c0:c0 + W],
                                            start=(kc == 0), stop=(kc == 7)),
                                            reads=[r_wup[wi]] + [r_h2T[t] for t in tiles], writes=[RB[bi]], inc=(kc == 7))
                                    S.op("act", lambda bi=bi, c0=c0, W=W, ch=ch, tbv=tbv: nc.scalar.activation(
                                        tbv[:, c0:c0 + W], PB[bi][:, 0:W], AF.Identity, scale=cw[:, l, 1, ch:ch + 1], bias=cb[:, l, ch:ch + 1]),
                                        reads=[RB[bi], r_const], writes=[rtb])
                                    S.op("dve", lambda bi=bi, c0=c0, W=W, ch=ch, tbv=tbv: nc.vector.scalar_tensor_tensor(
                                        tbv[:, c0 + 1:c0 + W], PB[bi][:, 0:W - 1], cw[:, l, 0, ch:ch + 1], tbv[:, c0 + 1:c0 + W],
                                        ALU.mult, ALU.add), reads=[RB[bi], r_const, rtb], writes=[rtb])
                                    S.op("dve", lambda bi=bi, c0=c0, W=W, ch=ch, tbv=tbv: nc.vector.scalar_tensor_tensor(
                                        tbv[:, c0:c0 + W - 1], PB[bi][:, 1:W], cw[:, l, 2, ch:ch + 1], tbv[:, c0:c0 + W - 1],
                                        ALU.mult, ALU.add), reads=[RB[bi], r_const, rtb], writes=[rtb])
                                    if cont:
                                        pbi, pW = prev
                                        S.op("dve", lambda bi=bi, c0=c0, ch=ch, tbv=tbv, pbi=pbi, pW=pW: nc.vector.scalar_tensor_tensor(
                                            tbv[:, c0:c0 + 1], PB[pbi][:, pW - 1:pW], cw[:, l, 0, ch:ch + 1], tbv[:, c0:c0 + 1],
                                            ALU.mult, ALU.add), reads=[RB[pbi], r_const, rtb], writes=[rtb])
                                        S.op("dve", lambda bi=bi, c0=c0, ch=ch, tbv=tbv: nc.vector.scalar_tensor_tensor(
                                            tbv[:, c0 - 1:c0], PB[bi][:, 0:1], cw[:, l, 2, ch:ch + 1], tbv[:, c0 - 1:c0],
                                            ALU.mult, ALU.add), reads=[RB[bi], r_const, rtb_prev], writes=[rtb_prev])
                                    prev = (bi, W)
                                    if row == 0 and c0 == 512 and pend_hid:
                                        pend_hid.pop()()
                            tg, tv = tbs

                            def do_hid(tg=tg, tv=tv, j=j):
                                S.op("act", lambda: nc.scalar.activation(tbuf[tg][:, 0:ncols], tbuf[tg][:, 0:ncols], AF.Silu),
                                     reads=r_tbuf[tg], writes=r_tbuf[tg])
                                S.op("dve", lambda: nc.vector.tensor_tensor(
                                    hid[:, j, 0:ncols], tbuf[tg][:, 0:ncols], tbuf[tv][:, 0:ncols], ALU.mult),
                                    reads=r_tbuf[tg] + r_tbuf[tv], writes=[r_hid[j]])
                            pend_hid.append(do_hid)
                        while pend_hid:
                            pend_hid.pop()()
                        pair_ring = Ring([(0, 1), (2, 3), (4, 5), (6, 7)])
                        pstate = {}

                        def d_mm(t):
                            b0, b1 = pair_ring.next()
                            for nh, bi in enumerate((b0, b1)):
                                for j in range(npart):
                                    S.op("pe", lambda nh=nh, bi=bi, j=j: nc.tensor.matmul(
                                        PB[bi][:, :], lhsT=hid[:, j, t * 128:(t + 1) * 128], rhs=wdn[:, j, nh * 512:(nh + 1) * 512],
                                        start=(j == 0), stop=(j == npart - 1)),
                                        reads=[r_hid[j], r_wdn], writes=[RB[bi]], inc=(j == npart - 1))
                            pstate[t] = (b0, b1)

                        def d_res(t):
                            b0, b1 = pstate.pop(t)
                            xi = load_x(b, l + 1, t)
                            xo = xout_ring.next()
                            gi = 0 if t < 16 else 2
                            for nh, bi in enumerate((b0, b1)):
                                S.op("dve", lambda nh=nh, bi=bi: nc.vector.tensor_tensor(
                                    xout[xo][:, nh * 512:(nh + 1) * 512], PB[bi][:, :], modb[gi][:, nh * 512:(nh + 1) * 512], ALU.mult),
                                    reads=[RB[bi], r_modb[gi]], writes=[r_xout[xo]])
                            S.op("dve", lambda: nc.vector.tensor_tensor(xout[xo][:], xout[xo][:], xin[xi][:], ALU.add),
                                 reads=[r_xout[xo], r_xin[xi]], writes=[r_xout[xo]])
                            S.dma("sp", g_xout[xo], x_dst(b, t), xout[xo][:], reads=[r_xout[xo]], writes=[r_dx[b][t]])

                        pipeline(ntq, [(d_mm, 0), (d_res, 1)])
                    S.barrier()
        S.finish()
        build_nc.stats = (S.ninst, S.nwait)
    return nc


def _rope_tables(head_dim):
    rows = SEQ // GRID_W
    row = np.repeat(np.arange(rows, dtype=np.float32), GRID_W)
    col = np.tile(np.arange(GRID_W, dtype=np.float32), rows)
    n_freq = head_dim // 4
    inv_freq = (np.float32(10000.0) ** (-np.arange(n_freq, dtype=np.float32) / np.float32(n_freq))).astype(np.float32)
    ang = np.stack([row[:, None] * inv_freq, col[:, None] * inv_freq], axis=1).astype(np.float32)
    cos = np.cos(ang).astype(np.float32)
    sin = np.sin(ang).astype(np.float32)
    C2 = np.stack([cos, cos], axis=2).reshape(SEQ, 4 * n_freq)
    S2 = np.stack([-sin.reshape(SEQ, 2 * n_freq), sin.reshape(SEQ, 2 * n_freq)], axis=1)
    C2 = C2.reshape(16, 128, 4 * n_freq).transpose(1, 0, 2)
    S2 = S2.reshape(16, 128, 2, 2 * n_freq).transpose(1, 0, 2, 3)
    return np.ascontiguousarray(C2), np.ascontiguousarray(S2)


def prep_shared(inp):
    f = lambda a: np.ascontiguousarray(np.asarray(a, dtype=np.float32))
    w_up = f(inp["w_up"])
    wu = w_up.reshape(L_ALL, 8, 128, 2, NCH, 128)
    w_up_r = np.ascontiguousarray(wu.transpose(0, 4, 2, 1, 3, 5)).reshape(L_ALL, NCH, 128, 8, 256)
    a_wsT = np.ascontiguousarray(f(inp["a_ws"]).transpose(0, 3, 1, 2))
    smallg = np.concatenate([f(inp["b_qnorm"]), f(inp["b_knorm"]), f(inp["c_qnorm"]), f(inp["c_knorm"]),
                             f(inp["c_subln"])], axis=1)
    cwr = f(inp["conv_w"]).reshape(L_ALL, 3, 44, 128).transpose(3, 0, 1, 2)
    cbr = f(inp["conv_b"]).reshape(L_ALL, 44, 128).transpose(2, 0, 1)
    kk = np.arange(128)[:, None]
    qq = np.arange(128)[None, :]
    rbc, rbs = _rope_tables(64)
    rcc, rcs = _rope_tables(32)
    return {
        "w_ada_r": np.ascontiguousarray(f(inp["w_ada"]).reshape(L_ALL, 8, 128, 12, 512).transpose(0, 3, 2, 1, 4)),
        "b_ada": f(inp["b_ada"]),
        "gn": np.ascontiguousarray(np.stack([f(inp["norm1_g"]), f(inp["norm2_g"])], axis=1)),
        "w_in": f(inp["w_in"]), "w_out": f(inp["w_out"]), "w_up_r": w_up_r, "w_down": f(inp["w_down"]),
        "a_wsT": a_wsT, "a_bs": f(inp["a_bs"]), "smallg": np.ascontiguousarray(smallg),
        "b_sink": f(inp["b_sink"]).reshape(1, -1), "c_lam": f(inp["c_lam"]).reshape(1, -1),
        "sublnT": np.ascontiguousarray(f(inp["c_subln"]).T),
        "cw_r": np.ascontiguousarray(cwr), "cb_r": np.ascontiguousarray(cbr),
        "ident": np.eye(128, dtype=np.float32),
        "maskL": (qq <= kk).astype(np.float32), "maskR": (kk <= qq).astype(np.float32),
        "ropeB_C2": rbc, "ropeB_S": rbs, "ropeC_C2": rcc, "ropeC_S": rcs,
    }


def core_inputs(inp, shared, core, nb=NB):
    f = lambda a: np.ascontiguousarray(np.asarray(a, dtype=np.float32))
    b0 = core * nb
    d = dict(shared)
    d["x"] = f(inp["x"][b0:b0 + nb])
    d["ctx"] = f(inp["ctx"][b0:b0 + nb])
    crow = np.zeros((3, D), np.float32)
    crow[0:nb] = np.asarray(inp["c"], np.float32)[b0:b0 + nb]
    crow[2] = np.asarray(inp["c_ctx"], np.float32)
    d["crow"] = crow
    return d


_NC_CACHE = {}


def kernel(**inputs):
    inp = {k: np.asarray(v) for k, v in inputs.items()}
    shared = prep_shared(inp)
    if "nc" not in _NC_CACHE:
        _NC_CACHE["nc"] = build_nc()
    nc = _NC_CACHE["nc"]
    in_maps = [core_inputs(inp, shared, c) for c in range(N_CORES)]
    res = run_bass_kernel_spmd(nc, in_maps, core_ids=list(range(N_CORES)))
    out = np.concatenate([np.asarray(r["y"], dtype=np.float32) for r in res.results], axis=0)
    return out
```

```python
import math
from contextlib import ExitStack

import numpy as np
import concourse.bass as bass
import concourse.mybir as mybir
from concourse.bass_utils import run_bass_kernel_spmd

F32 = mybir.dt.float32
BF16 = mybir.dt.bfloat16
AF = mybir.ActivationFunctionType
ALU = mybir.AluOpType
AX = mybir.AxisListType

L_ALL = 4
D = 1024
SEQ = 2048
LC = 256
NT = 18
NTOK = NT * 128
DFF = 2816
NCH = 22
EPS = 1e-6
GRID_W = 64
N_CORES = 8
NB = 2


class Res:
    __slots__ = ("name", "lw", "rd", "excl")

    def __init__(self, name, excl=False):
        self.name = name
        self.lw = None
        self.rd = {}
        self.excl = excl


class DmaGroup:
    __slots__ = ("sem", "cnt", "name")

    def __init__(self, sem, name):
        self.sem = sem
        self.cnt = 0
        self.name = name


class Sched:
    ENG = ("pe", "act", "dve", "pool", "sp")

    def __init__(self, nc, stack):
        self.nc = nc
        self.stack = stack
        self.eng = {"pe": nc.tensor, "act": nc.scalar, "dve": nc.vector,
                    "pool": nc.gpsimd, "sp": nc.sync}
        self.sem = {e: stack.enter_context(nc.semaphore("sem_" + e)) for e in self.ENG}
        self.cnt = {e: 0 for e in self.ENG}
        self.seen = {e: {} for e in self.ENG}
        self.nwait = 0
        self.ninst = 0
        self.groups = []

    def group(self, name):
        sem = self.stack.enter_context(self.nc.semaphore("dg_" + name))
        g = DmaGroup(sem, name)
        self.groups.append(g)
        return g

    def finish(self):
        for g in self.groups:
            if g.cnt:
                self.nc.sync.wait_ge(g.sem, g.cnt)

    def _deps(self, e, reads, writes):
        need = {}

        def add(ev, raw):
            if ev is None:
                return
            kind, src, count = ev
            if kind == "eng" and src == e:
                if e in ("pe", "sp"):
                    return
            key = (kind, src)
            if need.get(key, (None, 0))[1] < count:
                need[key] = (ev, count)

        for r in reads:
            add(r.lw, True)
            if r.excl:
                for k2, ev in r.rd.items():
                    if k2 != ("eng", e):
                        add(ev, False)
        for w in writes:
            add(w.lw, False)
            for ev in w.rd.values():
                add(ev, False)
        out = []
        for key, (ev, count) in need.items():
            if self.seen[e].get(key, 0) >= count:
                continue
            out.append((key, ev, count))
        return out

    def _emit_waits(self, e, deps):
        eng = self.eng[e]
        for key, ev, count in deps:
            kind, src, _ = ev
            if kind == "eng":
                assert count <= self.cnt[src], f"wait on un-inc'd instr {src} {count}>{self.cnt[src]}"
                sem = self.sem[src]
            else:
                sem = src.sem
            eng.wait_ge(sem, count)
            self.nwait += 1
            self.seen[e][key] = count

    def _mark(self, ev, key, reads, writes):
        for r in reads:
            r.rd[key] = ev
        for w in writes:
            w.lw = ev
            w.rd = {}

    def op(self, e, fn, reads=(), writes=(), inc=True):
        import os as _os
        self.nops = getattr(self, "nops", 0) + 1
        if self.nops > int(_os.environ.get("P1_OPLIMIT", 10 ** 9)):
            if not inc:
                return None
            return None
        self._emit_waits(e, self._deps(e, reads, writes))
        ins = fn()
        self.ninst += 1
        if inc:
            self.cnt[e] += 1
            ins.then_inc(self.sem[e], 1)
            ev = ("eng", e, self.cnt[e])
        else:
            ev = ("eng", e, self.cnt[e] + 1)
        self._mark(ev, ("eng", e), reads, writes)
        return ins

    def dma(self, q, grp, out, in_, reads=(), writes=(), **kw):
        self._emit_waits(q, self._deps(q, reads, writes))
        ins = self.eng[q].dma_start(out=out, in_=in_, **kw)
        grp.cnt += 16
        ins.then_inc(grp.sem, 16)
        self.ninst += 1
        ev = ("dma", grp, grp.cnt)
        self._mark(ev, ("dma", grp), reads, writes)
        return ins

    def dma_batch(self, q, grp, items):
        allr, allw = [], []
        for it in items:
            allr += list(it.get("reads", ()))
            allw += list(it.get("writes", ()))
        self._emit_waits(q, self._deps(q, allr, allw))
        for it in items:
            ins = self.eng[q].dma_start(out=it["out"], in_=it["in_"], **it.get("kw", {}))
            grp.cnt += 16
            ins.then_inc(grp.sem, 16)
            self.ninst += 1
        ev = ("dma", grp, grp.cnt)
        self._mark(ev, ("dma", grp), allr, allw)

    def barrier(self):
        for e in self.ENG:
            for f in self.ENG:
                if self.cnt[f] == 0 or (f == e and e in ("pe", "sp")):
                    continue
                key = ("eng", f)
                if self.seen[e].get(key, 0) >= self.cnt[f]:
                    continue
                self.eng[e].wait_ge(self.sem[f], self.cnt[f])
                self.seen[e][key] = self.cnt[f]
                self.nwait += 1


class Ring:
    def __init__(self, items):
        self.items = list(items)
        self.i = 0

    def next(self):
        it = self.items[self.i % len(self.items)]
        self.i += 1
        return it


def pipeline(n, stages):
    mx = max(s for _, s in stages)
    for step in range(n + mx):
        for fn, sk in stages:
            i = step - sk
            if 0 <= i < n:
                fn(i)


def build_nc(depth=L_ALL, nb=NB, dbg=None, stop=None):
    nc = bass.Bass("TRN2", target_bir_lowering=False)

    def din(name, shape, dt=F32):
        return nc.dram_tensor(name, list(shape), dt, kind="ExternalInput").ap()

    x_d = din("x", [nb, SEQ, D])
    ctx_d = din("ctx", [nb, LC, D])
    crow_d = din("crow", [3, D])
    wada_d = din("w_ada_r", [L_ALL, 12, 128, 8, 512])
    bada_d = din("b_ada", [L_ALL, 6 * D])
    gn_d = din("gn", [L_ALL, 2, D])
    win_d = din("w_in", [L_ALL, D, 2048])
    wout_d = din("w_out", [L_ALL, D, D])
    wup_d = din("w_up_r", [L_ALL, NCH, 128, 8, 256])
    wdn_d = din("w_down", [L_ALL, DFF, D])
    awsT_d = din("a_wsT", [L_ALL, 128, 4, 128])
    abs_d = din("a_bs", [L_ALL, 4, 128])
    smallg_d = din("smallg", [L_ALL, 256])
    sink_d = din("b_sink", [1, L_ALL * 8])
    clam_d = din("c_lam", [1, L_ALL * 128])
    sublnT_d = din("sublnT", [64, L_ALL])
    cw_d = din("cw_r", [128, L_ALL, 3, 44])
    cb_d = din("cb_r", [128, L_ALL, 44])
    ident_d = din("ident", [128, 128])
    maskL_d = din("maskL", [128, 128])
    maskR_d = din("maskR", [128, 128])
    rbc_d = din("ropeB_C2", [128, 16, 64])
    rbs_d = din("ropeB_S", [128, 16, 2, 32])
    rcc_d = din("ropeC_C2", [128, 16, 32])
    rcs_d = din("ropeC_S", [128, 16, 2, 16])
    y_d = nc.dram_tensor("y", [nb, SEQ, D], F32, kind="ExternalOutput").ap()
    ctxs_d = nc.dram_tensor("ctx_s", [nb, LC, D], F32).ap()
    mod_d = nc.dram_tensor("mod_s", [3, L_ALL, 6, D], F32).ap()

    with ExitStack() as st:
        S = Sched(nc, st)

        uid = [0]

        def T(name, shape, dt, stack=st):
            uid[0] += 1
            return stack.enter_context(nc.sbuf_tensor(f"sb{uid[0]}_{name}", list(shape), dt))

        PB = [st.enter_context(nc.psum_tensor(f"pb{i}", [128, 512], F32)) for i in range(8)]
        RB = [Res(f"pb{i}", excl=True) for i in range(8)]

        def bank_bf(i):
            return PB[i][:].bitcast(BF16)

        g_const = S.group("const")
        g_xin = [S.group(f"xin{i}") for i in range(2)]
        g_xout = [S.group(f"xout{i}") for i in range(2)]
        g_modb = [S.group(f"modb{i}") for i in range(4)]
        g_w = [S.group(f"w{i}") for i in range(6)]
        g_wdn = S.group("wdn")
        g_pre = [S.group(f"pre{i}") for i in range(6)]
        g_zst = [S.group(f"zst{i}") for i in range(2)]
        g_mod = S.group("modw")
        g_dbg = S.group("dbg")

        dbg_groups = []

        def dbg_wait():
            S.finish()

        def dump(name, ap, res):
            if dbg is None:
                return
            d = nc.dram_tensor("dbg_" + name, list(ap.shape), ap.dtype, kind="ExternalOutput").ap()
            gg = S.group("dbg_" + name)
            dbg_groups.append(gg)
            S.dma("sp", gg, d, ap, reads=[res])
            dbg.append(name)

        ident_f = T("ident_f", [128, 128], F32); r_ident = Res("ident")
        ident_b = T("ident_b", [128, 128], BF16)
        ones_f = T("ones_f", [128, 128], F32)
        onesmean = T("onesmean", [128, 128], F32)
        mask_f = T("mask_f", [128, 2, 128], F32)
        mask_b = T("mask_b", [128, 2, 128], BF16)
        ropeB_C = T("ropeB_C", [128, 16, 64], F32)
        ropeB_S = T("ropeB_S", [128, 16, 2, 32], F32)
        ropeC_C = T("ropeC_C", [128, 16, 32], F32)
        ropeC_S = T("ropeC_S", [128, 16, 2, 16], F32)
        smallg = T("smallg", [128, L_ALL, 256], F32)
        biasT = T("biasT", [128, L_ALL, 2, 128], F32)
        cw = T("cw", [128, L_ALL, 3, 44], F32)
        cb = T("cb", [128, L_ALL, 44], F32)
        esink = T("esink", [128, L_ALL * 8], F32)
        clam = T("clam", [128, L_ALL, 4, 32], F32)
        lamt = T("lamt", [128, L_ALL, 2, 32], F32)
        lam2 = T("lam2", [128, L_ALL, 2], F32)
        neglam = T("neglam", [128, L_ALL], F32)
        gsub = T("gsub", [128, L_ALL], F32)
        r_const = Res("const")

        items = [
            dict(out=ident_f[:], in_=ident_d),
            dict(out=mask_f[:, 0, :], in_=maskL_d),
            dict(out=mask_f[:, 1, :], in_=maskR_d),
            dict(out=ropeB_C[:], in_=rbc_d),
            dict(out=ropeB_S[:], in_=rbs_d),
            dict(out=ropeC_C[:], in_=rcc_d),
            dict(out=ropeC_S[:], in_=rcs_d),
            dict(out=smallg[:].rearrange("p l c -> p (l c)"),
                 in_=smallg_d.rearrange("l c -> (l c)").partition_broadcast(128)),
            dict(out=cw[:], in_=cw_d),
            dict(out=cb[:], in_=cb_d),
            dict(out=esink[:], in_=sink_d[0, :].partition_broadcast(128)),
            dict(out=clam[:].rearrange("p l a c -> p (l a c)"), in_=clam_d[0, :].partition_broadcast(128)),
            dict(out=gsub[0:64, :], in_=sublnT_d),
            dict(out=gsub[64:128, :], in_=sublnT_d),
        ]
        for l in range(L_ALL):
            for g in range(4):
                items.append(dict(out=biasT[(g % 2) * 64:(g % 2) * 64 + 64, l, g // 2, :],
                                  in_=abs_d[l, g, :].partition_broadcast(64)))
        for it in items:
            it["writes"] = [r_const]
        S.dma_batch("sp", g_const, items)

        S.op("dve", lambda: nc.vector.tensor_copy(ident_b[:], ident_f[:]), reads=[r_const], writes=[r_ident])
        S.op("dve", lambda: nc.vector.tensor_copy(mask_b[:], mask_f[:]), reads=[r_const], writes=[r_ident])
        S.op("dve", lambda: nc.vector.memset(ones_f[:], 1.0), writes=[r_ident])
        S.op("dve", lambda: nc.vector.memset(onesmean[:], 1.0 / 64.0), writes=[r_ident])
        S.op("act", lambda: nc.scalar.activation(esink[:], esink[:], AF.Exp), reads=[r_const], writes=[r_const])
        S.op("dve", lambda: nc.vector.tensor_tensor(lamt[:], clam[:, :, 0:4:2, :], clam[:, :, 1:4:2, :], ALU.mult),
             reads=[r_const], writes=[r_const])
        S.op("dve", lambda: nc.vector.tensor_reduce(lam2[:], lamt[:], AX.X, ALU.add), reads=[r_const], writes=[r_const])
        S.op("act", lambda: nc.scalar.activation(lam2[:], lam2[:], AF.Exp), reads=[r_const], writes=[r_const])
        S.op("dve", lambda: nc.vector.tensor_tensor(neglam[:], lam2[:, :, 1], lam2[:, :, 0], ALU.subtract),
             reads=[r_const], writes=[r_const])
        for l in range(L_ALL):
            lam_init = 0.8 - 0.6 * math.exp(-0.3 * l)
            S.op("dve", lambda l=l, li=lam_init: nc.vector.tensor_scalar(
                neglam[:, l:l + 1], neglam[:, l:l + 1], -li, None, ALU.add), reads=[r_const], writes=[r_const])
            S.op("dve", lambda l=l, li=lam_init: nc.vector.tensor_scalar(
                gsub[:, l:l + 1], gsub[:, l:l + 1], 1.0 - li, None, ALU.mult), reads=[r_const], writes=[r_const])

        if stop == "const":
            dump("neglam", neglam[:], r_const); dump("gsub", gsub[:], r_const); dump("esink", esink[:], r_const)
            dump("biasT", biasT[:], r_const); dump("mask_b", mask_b[:], r_ident)
            dbg_wait()
            return nc
        with ExitStack() as pp:
            crow = T("crow_sb", [3, D], F32, pp); r_crow = Res("crow")
            scT = T("scT", [128, 8, 3], F32, pp); r_scT = Res("scT")
            rows = T("rows", [3, 6 * D], F32, pp); r_rows = Res("rows")
            bada = T("bada", [3, 6 * D], F32, pp); r_bada = Res("bada")
            gnb = T("gnb", [3, 2, D], F32, pp); r_gnb = Res("gnb")
            wslots = [T(f"wada{i}", [128, 8, 512], F32, pp) for i in range(3)]
            r_wslots = [Res(f"wada{i}") for i in range(3)]
            g_ws = g_pre[0:3]
            g_pp = g_pre[3]
            S.dma("sp", g_pp, crow[:], crow_d, writes=[r_crow])
            S.op("act", lambda: nc.scalar.activation(crow[:], crow[:], AF.Silu), reads=[r_crow], writes=[r_crow])
            for c in range(8):
                S.op("pe", lambda c=c: nc.tensor.transpose(PB[0][:, c * 3:c * 3 + 3], crow[0:3, c * 128:(c + 1) * 128],
                                                           ident_f[0:3, 0:3]),
                     reads=[r_crow, r_const], writes=[RB[0]], inc=(c == 7))
            S.op("dve", lambda: nc.vector.tensor_copy(scT[:].rearrange("p c r -> p (c r)"), PB[0][:, 0:24]),
                 reads=[RB[0]], writes=[r_scT])
            pring = Ring([1, 2, 3])
            k = 0
            for l in range(depth):
                S.dma("sp", g_pre[4], bada[:], bada_d[l, :].partition_broadcast(3), writes=[r_bada])
                S.dma("sp", g_pre[5], gnb[:].rearrange("p a d -> p (a d)"),
                      gn_d[l].rearrange("a d -> (a d)").partition_broadcast(3), writes=[r_gnb])
                for n in range(12):
                    si = k % 3
                    k += 1
                    S.dma("sp", g_ws[si], wslots[si][:],
                          wada_d[l, n],
                          writes=[r_wslots[si]])
                    bi = pring.next()
                    for kc in range(8):
                        S.op("pe", lambda kc=kc, si=si, bi=bi: nc.tensor.matmul(
                            PB[bi][0:3, :], lhsT=scT[:, kc, :], rhs=wslots[si][:, kc, :],
                            start=(kc == 0), stop=(kc == 7)),
                            reads=[r_scT, r_wslots[si]], writes=[RB[bi]], inc=(kc == 7))
                    S.op("dve", lambda n=n, bi=bi: nc.vector.tensor_tensor(
                        rows[:, n * 512:(n + 1) * 512], PB[bi][0:3, :], bada[:, n * 512:(n + 1) * 512], ALU.add),
                        reads=[RB[bi], r_bada], writes=[r_rows])
                for a, kidx in ((0, 1), (1, 4)):
                    S.op("dve", lambda a=a, kidx=kidx: nc.vector.scalar_tensor_tensor(
                        rows[:, kidx * D:(kidx + 1) * D], rows[:, kidx * D:(kidx + 1) * D], 1.0, gnb[:, a, :],
                        ALU.add, ALU.mult), reads=[r_rows, r_gnb], writes=[r_rows])
                S.dma("sp", g_mod, mod_d[:, l].rearrange("r k d -> r (k d)"), rows[:], reads=[r_rows], writes=[])
            r_mod = Res("mod_d")
            r_mod.lw = ("dma", g_mod, g_mod.cnt)
            S.barrier()

        if stop == "prepass":
            if dbg is not None:
                d = nc.dram_tensor("dbg_mod", [3, 1, 6, D], F32, kind="ExternalOutput").ap()
                S.dma("sp", g_dbg, d, mod_d[:, 0:1], reads=[r_mod])
                dbg.append("mod")
            dbg_wait()
            return nc
        xin = [T(f"xin{i}", [128, D], F32) for i in range(2)]
        r_xin = [Res(f"xin{i}") for i in range(2)]
        xin_ring = Ring(range(2))
        xout = [T(f"xout{i}", [128, D], F32) for i in range(2)]
        r_xout = [Res(f"xout{i}") for i in range(2)]
        xout_ring = Ring(range(2))
        modb = [T(f"modb{i}", [128, D], F32) for i in range(4)]
        r_modb = [Res(f"modb{i}") for i in range(4)]
        stat = T("stat", [128, 64], F32)
        r_dx = [[Res(f"dx{b}_{t}") for t in range(NT)] for b in range(nb)]

        def x_src(b, l, t):
            if t < 16:
                base = x_d if l == 0 else y_d
                return base[b, t * 128:(t + 1) * 128, :]
            base = ctx_d if l == 0 else ctxs_d
            return base[b, (t - 16) * 128:(t - 15) * 128, :]

        def x_dst(b, t):
            if t < 16:
                return y_d[b, t * 128:(t + 1) * 128, :]
            return ctxs_d[b, (t - 16) * 128:(t - 15) * 128, :]

        def load_x(b, lsrc, t):
            i = xin_ring.next()
            S.dma("sp", g_xin[i], xin[i][:], x_src(b, lsrc, t), reads=[r_dx[b][t]], writes=[r_xin[i]])
            return i

        def load_modb(i, row, l, kind):
            S.dma("sp", g_modb[i], modb[i][:], mod_d[row, l, kind, :].partition_broadcast(128),
                  reads=[r_mod], writes=[r_modb[i]])

        ev_toggle = [0]

        def evac(out, in_, reads, writes):
            ev_toggle[0] ^= 1
            if ev_toggle[0]:
                S.op("act", lambda: nc.scalar.copy(out, in_), reads=reads, writes=writes)
            else:
                S.op("dve", lambda: nc.vector.tensor_copy(out, in_), reads=reads, writes=writes)

        def norm_tile(xi, mi, shi, hb, r_hb, t1, r_t1, ss_col):
            ss = stat[:, ss_col:ss_col + 1]
            rt = stat[:, ss_col + 1:ss_col + 2]
            r_st = r_stat[ss_col // 2]
            import os as _os
            _k = int(_os.environ.get("P1_S0", 99))
            if _k < 2:
                return
            S.op("act", lambda: nc.scalar.activation(t1, xin[xi][:], AF.Square),
                 reads=[r_xin[xi]], writes=[r_t1])
            S.op("dve", lambda: nc.vector.tensor_reduce(ss, t1, AX.X, ALU.add), reads=[r_t1], writes=[r_st])
            if _k < 3:
                return
            S.op("act", lambda: nc.scalar.activation(rt, ss, AF.Sqrt, scale=1.0 / D, bias=eps_t[:, 0:1]),
                 reads=[r_st, r_ident], writes=[r_st])
            if _k < 4:
                return
            S.op("dve", lambda: nc.vector.reciprocal(rt, rt), reads=[r_st], writes=[r_st])
            if _k < 5:
                return
            S.op("dve", lambda: nc.vector.scalar_tensor_tensor(t1, xin[xi][:], rt, modb[mi][:], ALU.mult, ALU.mult),
                 reads=[r_xin[xi], r_st, r_modb[mi]], writes=[r_t1])
            if _k < 6:
                return
            S.op("dve", lambda: nc.vector.tensor_tensor(hb, t1, modb[shi][:], ALU.add),
                 reads=[r_t1, r_modb[shi]], writes=[r_hb])

        r_stat = [Res(f"stat{i}") for i in range(8)]
        eps_t = T("eps_t", [128, 1], F32)
        S.op("dve", lambda: nc.vector.memset(eps_t[:], EPS), writes=[r_ident])

        def transpose8(hb, r_hb, bi, dst, r_dst):
            pv = bank_bf(bi)
            import os as _os
            _k = int(_os.environ.get("P1_S0", 99))
            if _k < 7:
                return
            for c in range(8):
                S.op("pe", lambda c=c: nc.tensor.transpose(pv[:, c * 128:(c + 1) * 128], hb[:, c * 128:(c + 1) * 128],
                                                           ident_b[:]),
                     reads=[r_hb, r_ident], writes=[RB[bi]], inc=(c == 7))
            if _k < 8:
                return
            evac(dst, pv[:, 0:1024].rearrange("p (c t) -> p c t", c=8), [RB[bi]], [r_dst])

        for b in range(nb):
            for l in range(depth):
                last = (l == depth - 1)
                ntq = 16 if last else 18
                lsrc = l
                load_modb(0, b, l, 1)
                load_modb(1, b, l, 0)
                load_modb(2, 2, l, 1)
                load_modb(3, 2, l, 0)
                with ExitStack() as s1:
                    kbT = T("kbT", [128, NTOK], BF16, s1); r_kbT = [Res(f"kbT{t}") for t in range(NT)]
                    vb = T("vb", [128, NT, 2, 65], BF16, s1); r_vb = [Res(f"vb{t}") for t in range(NT)]
                    kcT = T("kcT", [128, 2, NTOK], BF16, s1); r_kcT = [Res(f"kcT{t}") for t in range(NT)]
                    vc = T("vc", [128, NT, 4, 65], BF16, s1); r_vc = [Res(f"vc{t}") for t in range(NT)]
                    qbT = T("qbT", [128, 4, NTOK], BF16, s1); r_qbT = [Res(f"qbT{t}") for t in range(NT)]
                    qcT = T("qcT", [128, 2, NTOK], BF16, s1); r_qcT = [Res(f"qcT{t}") for t in range(NT)]
                    catA = T("catA", [128, 2, NTOK], BF16, s1)
                    r_catA = [Res(f"catA{t}") for t in range(NT)]
                    r_catB = [Res(f"catB{t}") for t in range(NT)]
                    r_catC = [Res(f"catC{t}") for t in range(NT)]
                    r_vones = Res("vones")
                    S.op("dve", lambda: nc.vector.memset(vb[:, :, :, 64:65], 1.0), writes=[r_vones])
                    S.op("dve", lambda: nc.vector.memset(vc[:, :, :, 64:65], 1.0), writes=[r_vones])

                    with ExitStack() as p1:
                        w_in = T("w_in", [128, 8, 2048], BF16, p1); r_win = Res("w_in")
                        awsT = T("awsT", [128, 4, 128], BF16, p1); r_aws = Res("awsT")
                        for kc in range(8):
                            S.dma("pool", g_w[0], w_in[:, kc, :], win_d[l, kc * 128:(kc + 1) * 128, :], writes=[r_win])
                        S.dma("pool", g_w[1], awsT[:], awsT_d[l], writes=[r_aws])
                        NS = 2
                        hb = [T(f"hb{i}", [128, D], BF16, p1) for i in range(NS)]; r_hb = [Res(f"hb{i}") for i in range(NS)]
                        t1 = [T("t1_0", [128, D], F32, p1)] * NS; r_t1 = [Res("t1_0")] * NS
                        hT = [T(f"hT{i}", [128, 8, 128], BF16, p1) for i in range(3)]; r_hT = [Res(f"hT{i}") for i in range(3)]
                        uT = [T(f"uT{i}", [128, 2, 128], BF16, p1) for i in range(NS)]; r_uT = [Res(f"uT{i}") for i in range(NS)]
                        gv = [T(f"gv{i}", [128, 256], F32, p1) for i in range(NS)]; r_gv = [Res(f"gv{i}") for i in range(NS)]
                        vpad = [T(f"vpad{i}", [128, 4, 128], BF16, p1) for i in range(NS)]; r_vpad = [Res(f"vpad{i}") for i in range(NS)]
                        sq = [T("sq0", [128, 1152], F32, p1)] * NS; r_sq = [Res("sq0")] * NS
                        qn = [T("qn0", [128, 1152], F32, p1)] * NS; r_qn = [Res("qn0")] * NS
                        tb = [T("tb0", [128, 1152], F32, p1)] * NS; r_tb = [Res("tb0")] * NS
                        qr = [T("qr0", [128, 1152], BF16, p1)] * NS; r_qr = [Res("qr0")] * NS
                        st2 = [T(f"st2_{i}", [128, 32], F32, p1) for i in range(NS)]; r_st2 = [Res(f"st2_{i}") for i in range(NS)]
                        ta = [T(f"ta{i}", [128, 256], F32, p1) for i in range(NS)]; r_ta = [Res(f"ta{i}") for i in range(NS)]
                        for i in range(NS):
                            S.op("dve", lambda i=i: nc.vector.memset(vpad[i][:], 0.0), writes=[r_vpad[i]])
                        order = [16, 17] + list(range(16))
                        trp_ring = Ring([0, 1, 2])
                        prj_ring = Ring([3, 4, 5, 6, 7])
                        xi_of = {}
                        banks_of = {}

                        def s0(i):
                            t = order[i]
                            isctx = t >= 16
                            xi = load_x(b, lsrc, t)
                            xi_of[i] = xi
                            sl = i % NS
                            norm_tile(xi, 2 if isctx else 0, 3 if isctx else 1, hb[sl][:], r_hb[sl], t1[sl][:], r_t1[sl], (i % 4) * 2)
                            transpose8(hb[sl], r_hb[sl], trp_ring.next(), hT[i % 3][:], r_hT[i % 3])
                            if stop == "p1dbg" and t == 0:
                                dump("xin", xin[xi][:], r_xin[xi]); dump("hb", hb[sl][:], r_hb[sl]); dump("hT", hT[i % 3][:], r_hT[i % 3])
                                dump("m1b", modb[0][:], r_modb[0]); dump("sh1b", modb[1][:], r_modb[1]); dump("w_in", w_in[:], r_win)

                        def s1f(i):
                            t = order[i]
                            isctx = t >= 16
                            full = (not isctx) or (not last)
                            h = hT[i % 3]; rh = r_hT[i % 3]
                            bk = {}
                            if full:
                                bu = prj_ring.next(); bk["u"] = bu
                                for cc in range(2):
                                    for kc in range(8):
                                        S.op("pe", lambda cc=cc, kc=kc: nc.tensor.matmul(
                                            PB[bu][:, cc * 128:(cc + 1) * 128], lhsT=w_in[:, kc, cc * 128:(cc + 1) * 128],
                                            rhs=h[:, kc, :], start=(kc == 0), stop=(kc == 7)),
                                            reads=[r_win, rh], writes=[RB[bu]], inc=(kc == 7 and cc == 1))
                                bv = prj_ring.next(); bk["v"] = bv
                                for kc in range(8):
                                    S.op("pe", lambda kc=kc: nc.tensor.matmul(
                                        PB[bv][:, 0:256], lhsT=h[:, kc, :], rhs=w_in[:, kc, 256:512],
                                        start=(kc == 0), stop=(kc == 7)), reads=[r_win, rh], writes=[RB[bv]], inc=(kc == 7))
                                bq = prj_ring.next(); bk["q"] = bq
                                for kc in range(8):
                                    S.op("pe", lambda kc=kc: nc.tensor.matmul(
                                        PB[bq][:, :], lhsT=h[:, kc, :], rhs=w_in[:, kc, 512:1024],
                                        start=(kc == 0), stop=(kc == 7)), reads=[r_win, rh], writes=[RB[bq]], inc=(kc == 7))
                            bkk = prj_ring.next(); bk["k"] = bkk
                            for kc in range(8):
                                S.op("pe", lambda kc=kc: nc.tensor.matmul(
                                    PB[bkk][:, :], lhsT=h[:, kc, :], rhs=w_in[:, kc, 1024:1536],
                                    start=(kc == 0), stop=(kc == 7)), reads=[r_win, rh], writes=[RB[bkk]], inc=(kc == 7))
                            bc = prj_ring.next(); bk["c"] = bc
                            for kc in range(8):
                                S.op("pe", lambda kc=kc: nc.tensor.matmul(
                                    PB[bc][:, :], lhsT=h[:, kc, :], rhs=w_in[:, kc, 1536:2048],
                                    start=(kc == 0), stop=(kc == 7)), reads=[r_win, rh], writes=[RB[bc]], inc=(kc == 7))
                            banks_of[i] = bk

                        def s2(i):
                            t = order[i]
                            isctx = t >= 16
                            full = (not isctx) or (not last)
                            bk = banks_of[i]
                            sl = i % NS
                            g0 = l * 256
                            if full:
                                bu, bv = bk["u"], bk["v"]
                                S.op("act", lambda: nc.scalar.activation(
                                    uT[sl][:].rearrange("p c t -> p (c t)"), PB[bu][:, 0:256], AF.Gelu_apprx_tanh),
                                    reads=[RB[bu]], writes=[r_uT[sl]])
                                S.op("act", lambda: nc.scalar.activation(gv[sl][:], PB[bv][:, 0:256], AF.Gelu_apprx_tanh),
                                     reads=[RB[bv]], writes=[r_gv[sl]])
                                S.op("dve", lambda: nc.vector.tensor_tensor(ta[sl][:], gv[sl][:], gv[sl][:], ALU.mult),
                                     reads=[r_gv[sl]], writes=[r_ta[sl]])
                                S.op("dve", lambda: nc.vector.tensor_reduce(
                                    st2[sl][:, 26:30], ta[sl][:].rearrange("p (g c) -> p g c", g=4), AX.X, ALU.add),
                                    reads=[r_ta[sl]], writes=[r_st2[sl]])
                                S.op("act", lambda: nc.scalar.activation(st2[sl][:, 26:30], st2[sl][:, 26:30], AF.Sqrt,
                                                                         scale=1.0 / 64, bias=eps_t[:, 0:1]),
                                     reads=[r_st2[sl], r_ident], writes=[r_st2[sl]])
                                S.op("dve", lambda: nc.vector.reciprocal(st2[sl][:, 26:30], st2[sl][:, 26:30]),
                                     reads=[r_st2[sl]], writes=[r_st2[sl]])
                                for par in range(2):
                                    S.op("dve", lambda par=par: nc.vector.tensor_tensor(
                                        vpad[sl][:, par:4:2, par * 64:par * 64 + 64],
                                        gv[sl][:].rearrange("p (g c) -> p g c", g=4)[:, par:4:2, :],
                                        st2[sl][:, 26 + par:30:2].unsqueeze(2).to_broadcast([128, 2, 64]), ALU.mult),
                                        reads=[r_gv[sl], r_st2[sl]], writes=[r_vpad[sl]])
                                bm = prj_ring.next()
                                for g in range(4):
                                    S.op("pe", lambda g=g: nc.tensor.matmul(
                                        PB[bm][:, (g // 2) * 128:(g // 2) * 128 + 128], lhsT=vpad[sl][:, g, :], rhs=awsT[:, g, :],
                                        start=(g % 2 == 0), stop=(g % 2 == 1)),
                                        reads=[r_vpad[sl], r_aws], writes=[RB[bm]], inc=(g == 3))
                                S.op("dve", lambda: nc.vector.tensor_tensor(
                                    ta[sl][:], PB[bm][:, 0:256], biasT[:, l].rearrange("p c t -> p (c t)"), ALU.add),
                                    reads=[RB[bm], r_const], writes=[r_ta[sl]])
                                S.op("dve", lambda: nc.vector.tensor_tensor(
                                    catA[:, 0:2, t * 128:(t + 1) * 128], ta[sl][:].rearrange("p (c t) -> p c t", c=2),
                                    uT[sl][:], ALU.mult), reads=[r_ta[sl], r_uT[sl]], writes=[r_catA[t]])
                            bkk, bc = bk["k"], bk["c"]
                            if full:
                                bq = bk["q"]
                                S.op("act", lambda: nc.scalar.activation(sq[sl][:, 0:512], PB[bq][:, :], AF.Square),
                                     reads=[RB[bq]], writes=[r_sq[sl]])
                                S.op("act", lambda: nc.scalar.activation(sq[sl][:, 640:896], PB[bkk][:, 256:512], AF.Square),
                                     reads=[RB[bkk]], writes=[r_sq[sl]])
                            else:
                                S.op("dve", lambda: nc.vector.memset(sq[sl][:, 0:512], 1.0), writes=[r_sq[sl]])
                                S.op("dve", lambda: nc.vector.memset(sq[sl][:, 640:896], 1.0), writes=[r_sq[sl]])
                            S.op("act", lambda: nc.scalar.activation(sq[sl][:, 512:640], PB[bkk][:, 0:128], AF.Square),
                                 reads=[RB[bkk]], writes=[r_sq[sl]])
                            S.op("act", lambda: nc.scalar.activation(sq[sl][:, 896:1152], PB[bc][:, 0:256], AF.Square),
                                 reads=[RB[bc]], writes=[r_sq[sl]])
                            S.op("dve", lambda: nc.vector.tensor_reduce(
                                st2[sl][:, 0:10], sq[sl][:, 0:640].rearrange("p (h d) -> p h d", d=64), AX.X, ALU.add),
                                reads=[r_sq[sl]], writes=[r_st2[sl]])
                            S.op("dve", lambda: nc.vector.tensor_reduce(
                                st2[sl][:, 10:26], sq[sl][:, 640:1152].rearrange("p (h d) -> p h d", d=32), AX.X, ALU.add),
                                reads=[r_sq[sl]], writes=[r_st2[sl]])
                            S.op("act", lambda: nc.scalar.activation(st2[sl][:, 0:10], st2[sl][:, 0:10], AF.Sqrt,
                                                                     scale=1.0 / 64, bias=eps_t[:, 0:1]),
                                 reads=[r_st2[sl], r_ident], writes=[r_st2[sl]])
                            S.op("act", lambda: nc.scalar.activation(st2[sl][:, 10:26], st2[sl][:, 10:26], AF.Sqrt,
                                                                     scale=1.0 / 32, bias=eps_t[:, 0:1]),
                                 reads=[r_st2[sl], r_ident], writes=[r_st2[sl]])
                            S.op("dve", lambda: nc.vector.reciprocal(st2[sl][:, 0:26], st2[sl][:, 0:26]),
                                 reads=[r_st2[sl]], writes=[r_st2[sl]])
                            specs = []
                            if full:
                                specs.append(("bq", PB[bk["q"]][:, :], RB[bk["q"]], 0, 512, 8, 64, 0, 0, ropeB_C, ropeB_S))
                            specs.append(("bk", PB[bkk][:, 0:128], RB[bkk], 512, 128, 2, 64, 8, 64, ropeB_C, ropeB_S))
                            if full:
                                specs.append(("cq", PB[bkk][:, 256:512], RB[bkk], 640, 256, 8, 32, 10, 128, ropeC_C, ropeC_S))
                            specs.append(("ck", PB[bc][:, 0:256], RB[bc], 896, 256, 8, 32, 18, 160, ropeC_C, ropeC_S))
                            for (nm, src, rsrc, c0, wd, nh, hd, sc0, gc0, rC, rS) in specs:
                                v3 = lambda ap, nh=nh: ap.rearrange("p (h d) -> p h d", h=nh)
                                qv = qn[sl][:, c0:c0 + wd]
                                S.op("dve", lambda src=src, qv=qv, v3=v3, sc0=sc0, nh=nh, hd=hd: nc.vector.tensor_tensor(
                                    v3(qv), v3(src), st2[sl][:, sc0:sc0 + nh].unsqueeze(2).to_broadcast([128, nh, hd]), ALU.mult),
                                    reads=[rsrc, r_st2[sl]], writes=[r_qn[sl]])
                                gain_b = smallg[:, l, gc0:gc0 + hd].unsqueeze(1).to_broadcast([128, nh, hd])
                                if isctx:
                                    if nm == "bq":
                                        outv = qr[sl][:, 0:512].rearrange("p (c hi d) -> p hi c d", c=4, hi=2)
                                        inv = qv.rearrange("p (hi c d) -> p hi c d", hi=2, c=4)
                                        gb = smallg[:, l, gc0:gc0 + hd].unsqueeze(1).unsqueeze(1).to_broadcast([128, 2, 4, hd])
                                        S.op("dve", lambda outv=outv, inv=inv, gb=gb: nc.vector.tensor_tensor(outv, inv, gb, ALU.mult),
                                             reads=[r_qn[sl], r_const], writes=[r_qr[sl]])
                                    else:
                                        S.op("dve", lambda qv=qv, v3=v3, gain_b=gain_b, c0=c0, wd=wd: nc.vector.tensor_tensor(
                                            v3(qr[sl][:, c0:c0 + wd]), v3(qv), gain_b, ALU.mult),
                                            reads=[r_qn[sl], r_const], writes=[r_qr[sl]])
                                    continue
                                S.op("dve", lambda qv=qv, v3=v3, gain_b=gain_b: nc.vector.tensor_tensor(v3(qv), v3(qv), gain_b, ALU.mult),
                                     reads=[r_qn[sl], r_const], writes=[r_qn[sl]])
                                nf = hd // 4
                                v4 = lambda ap, nf=nf: ap.rearrange("p (ha pr f) -> p ha pr f", pr=2, f=nf)
                                tbv = tb[sl][:, c0:c0 + wd]
                                Sn = rS[:, t, 0, :].rearrange("p (a f) -> p a f", a=2).unsqueeze(1).to_broadcast([128, nh, 2, nf])
                                Sp = rS[:, t, 1, :].rearrange("p (a f) -> p a f", a=2).unsqueeze(1).to_broadcast([128, nh, 2, nf])
                                v5 = lambda ap, nh=nh, nf=nf: ap.rearrange("p (h a pr f) -> p h a pr f", h=nh, a=2, pr=2)
                                S.op("dve", lambda tbv=tbv, qv=qv, v5=v5, Sn=Sn: nc.vector.tensor_tensor(
                                    v5(tbv)[:, :, :, 0, :], v5(qv)[:, :, :, 1, :], Sn, ALU.mult),
                                    reads=[r_qn[sl], r_const], writes=[r_tb[sl]])
                                S.op("dve", lambda tbv=tbv, qv=qv, v5=v5, Sp=Sp: nc.vector.tensor_tensor(
                                    v5(tbv)[:, :, :, 1, :], v5(qv)[:, :, :, 0, :], Sp, ALU.mult),
                                    reads=[r_qn[sl], r_const], writes=[r_tb[sl]])
                                Cb = rC[:, t, :].unsqueeze(1).to_broadcast([128, nh, hd])
                                S.op("dve", lambda qv=qv, v3=v3, Cb=Cb: nc.vector.tensor_tensor(v3(qv), v3(qv), Cb, ALU.mult),
                                     reads=[r_qn[sl], r_const], writes=[r_qn[sl]])
                                if nm == "bq":
                                    outv = qr[sl][:, 0:512].rearrange("p (c hi d) -> p hi c d", c=4, hi=2)
                                    a0 = qv.rearrange("p (hi c d) -> p hi c d", hi=2, c=4)
                                    a1 = tbv.rearrange("p (hi c d) -> p hi c d", hi=2, c=4)
                                    S.op("dve", lambda outv=outv, a0=a0, a1=a1: nc.vector.tensor_tensor(outv, a0, a1, ALU.add),
                                         reads=[r_qn[sl], r_tb[sl]], writes=[r_qr[sl]])
                                else:
                                    S.op("dve", lambda qv=qv, tbv=tbv, c0=c0, wd=wd: nc.vector.tensor_tensor(
                                        qr[sl][:, c0:c0 + wd], qv, tbv, ALU.add),
                                        reads=[r_qn[sl], r_tb[sl]], writes=[r_qr[sl]])
                            S.op("act", lambda: nc.scalar.copy(vb[:, t, :, 0:64], PB[bkk][:, 128:256].rearrange("p (h d) -> p h d", h=2)),
                                 reads=[RB[bkk]], writes=[r_vb[t]])
                            S.op("act", lambda: nc.scalar.copy(vc[:, t, :, 0:64], PB[bc][:, 256:512].rearrange("p (h d) -> p h d", h=4)),
                                 reads=[RB[bc]], writes=[r_vc[t]])
                            bt = trp_ring.next()
                            pv = bank_bf(bt)
                            lst = []
                            if full:
                                for c in range(4):
                                    lst.append((c, qr[sl][:, c * 128:(c + 1) * 128]))
                            lst.append((4, qr[sl][:, 512:640]))
                            for c, src in lst:
                                S.op("pe", lambda c=c, src=src: nc.tensor.transpose(pv[:, c * 128:(c + 1) * 128], src, ident_b[:]),
                                     reads=[r_qr[sl], r_ident], writes=[RB[bt]], inc=(c == 4))
                            if full:
                                evac(qbT[:, :, t * 128:(t + 1) * 128], pv[:, 0:512].rearrange("p (c t) -> p c t", c=4),
                                     [RB[bt]], [r_qbT[t]])
                            evac(kbT[:, t * 128:(t + 1) * 128], pv[:, 512:640], [RB[bt]], [r_kbT[t]])
                            bt2 = trp_ring.next()
                            pv2 = bank_bf(bt2)
                            lst = []
                            if full:
                                lst += [(0, qr[sl][:, 640:768]), (1, qr[sl][:, 768:896])]
                            lst += [(2, qr[sl][:, 896:1024]), (3, qr[sl][:, 1024:1152])]
                            for c, src in lst:
                                S.op("pe", lambda c=c, src=src: nc.tensor.transpose(pv2[:, c * 128:(c + 1) * 128], src, ident_b[:]),
                                     reads=[r_qr[sl], r_ident], writes=[RB[bt2]], inc=(c == 3))
                            if full:
                                evac(qcT[:, :, t * 128:(t + 1) * 128], pv2[:, 0:256].rearrange("p (c t) -> p c t", c=2),
                                     [RB[bt2]], [r_qcT[t]])
                            evac(kcT[:, :, t * 128:(t + 1) * 128], pv2[:, 256:512].rearrange("p (c t) -> p c t", c=2),
                                 [RB[bt2]], [r_kcT[t]])

                        import os as _os
                        _ntl = int(_os.environ.get("P1_TILES", NT))
                        _nst = int(_os.environ.get("P1_STAGES", 3))
                        pipeline(_ntl, [(s2, 2), (s1f, 1), (s0, 0)][3 - _nst:])
                        if b == 0 and l == 0:
                            print("sbuf remaining p1", nc.sbuf_bytes_remaining)
                        S.barrier()
                    if stop == "p1x":
                        print("NOPS", getattr(S, "nops", 0))
                        dump("hT0", hT[0][:], r_hT[0])
                        dbg_wait()
                        return nc
                    if stop in ("p1", "p1dbg") and dbg is not None:
                        dump("kbT", kbT[:], r_kbT[0]); dump("kcT", kcT[:], r_kcT[0]); dump("qbT", qbT[:, :, 0:ntq * 128], r_qbT[0])
                        dump("qcT", qcT[:, :, 0:ntq * 128], r_qcT[0]); dump("vb", vb[:], r_vb[0]); dump("vc", vc[:], r_vc[0])
                        dump("catA", catA[:, :, 0:ntq * 128], r_catA[0])
                        dbg_wait()
                        return nc

                    with ExitStack() as p2:
                        catBC = T("catBC", [128, 6, NTOK], BF16, p2)
                        w_out = T("w_out", [128, 8, D], BF16, p2); r_wout = Res("w_out")
                        for kc in range(8):
                            S.dma("pool", g_w[2], w_out[:, kc, :], wout_d[l, kc * 128:(kc + 1) * 128, :], writes=[r_wout])
                        NP = 10
                        pT = [T(f"pT{i}", [128, 512], BF16, p2) for i in range(NP)]; r_pT = [Res(f"pT{i}") for i in range(NP)]
                        pT_ring = Ring(range(NP))
                        zst = [T(f"zst{i}", [128, 512], BF16, p2) for i in range(2)]; r_zst = [Res(f"zst{i}") for i in range(2)]
                        zst_ring = Ring(range(2))
                        accsb = [T(f"accsb{i}", [128, 512], F32, p2) for i in range(4)]; r_accsb = [Res(f"accsb{i}") for i in range(4)]
                        rr = [T("rr0", [128, 1024], F32, p2)] * 2; r_rr = [Res("rr0")] * 2
                        Rsb = [T("Rsb0", [128, 1024], F32, p2)] * 2; r_Rsb = [Res("Rsb0")] * 2
                        yy = [T("yy0", [128, 512], F32, p2)] * 2; r_yy = [Res("yy0")] * 2
                        y1 = [T("y10", [128, 512], F32, p2)] * 2; r_y1 = [Res("y10")] * 2
                        ysq = [T("ysq0", [128, 512], F32, p2)] * 2; r_ysq = [Res("ysq0")] * 2
                        rsd = [T("rsd0", [128, 512], F32, p2)] * 2; r_rsd = [Res("rsd0")] * 2

                        s_ring = Ring([0, 1, 2, 3])
                        acc_ring = Ring([4, 5])
                        R_ring = Ring([6, 7])
                        units = [(kvh, n) for n in range(ntq) for kvh in range(2)]
                        bstate = {}
                        bacc = {}

                        def b_scores(ui):
                            kvh, n = units[ui]
                            if n < 16:
                                kts = []
                                if n > 0:
                                    kts.append((n - 1, 0))
                                kts.append((n, None))
                                if n < 15:
                                    kts.append((n + 1, 1))
                                kts += [(16, None), (17, None)]
                            else:
                                kts = [(16, None), (17, None)]
                            pb0 = kvh * 64
                            plist = []
                            for (kt, mk) in kts:
                                sb = s_ring.next()
                                S.op("pe", lambda sb=sb, kt=kt: nc.tensor.matmul(
                                    PB[sb][:, :], lhsT=kbT[pb0:pb0 + 64, kt * 128:(kt + 1) * 128],
                                    rhs=qbT[pb0:pb0 + 64, :, n * 128:(n + 1) * 128], start=True, stop=True),
                                    reads=[r_kbT[kt], r_qbT[n]], writes=[RB[sb]])
                                pi = pT_ring.next()
                                S.op("act", lambda sb=sb, pi=pi: nc.scalar.activation(pT[pi][:], PB[sb][:, :], AF.Exp, scale=0.125),
                                     reads=[RB[sb]], writes=[r_pT[pi]])
                                if mk is not None:
                                    S.op("dve", lambda pi=pi, mk=mk: nc.vector.tensor_tensor(
                                        pT[pi][:].rearrange("p (g q) -> p g q", g=4), pT[pi][:].rearrange("p (g q) -> p g q", g=4),
                                        mask_b[:, mk, :].unsqueeze(1).to_broadcast([128, 4, 128]), ALU.mult),
                                        reads=[r_pT[pi], r_ident], writes=[r_pT[pi]])
                                plist.append((kt, pi))
                            bstate[ui] = plist

                        def b_pv(ui):
                            kvh, n = units[ui]
                            plist = bstate.pop(ui)
                            ab = acc_ring.next()
                            nk = len(plist)
                            for j, (kt, pi) in enumerate(plist):
                                S.op("pe", lambda j=j, kt=kt, pi=pi: nc.tensor.matmul(
                                    PB[ab][0:65, :], lhsT=vb[:, kt, kvh, :], rhs=pT[pi][:], start=(j == 0), stop=(j == nk - 1)),
                                    reads=[r_vb[kt], r_vones, r_pT[pi]], writes=[RB[ab]], inc=(j == nk - 1))
                            bacc[ui] = ab

                        def b_post(ui):
                            kvh, n = units[ui]
                            ab = bacc.pop(ui)
                            ri = ui % 2
                            hd0 = kvh * 4
                            S.op("dve", lambda: nc.vector.tensor_tensor(
                                rr[ri][64:65, 0:512].rearrange("p (g q) -> p g q", g=4),
                                PB[ab][64:65, :].rearrange("p (g q) -> p g q", g=4),
                                esink[64:65, l * 8 + hd0:l * 8 + hd0 + 4].unsqueeze(2).to_broadcast([1, 4, 128]),
                                ALU.add), reads=[RB[ab], r_const], writes=[r_rr[ri]])
                            S.op("dve", lambda: nc.vector.reciprocal(rr[ri][64:65, 0:512], rr[ri][64:65, 0:512]),
                                 reads=[r_rr[ri]], writes=[r_rr[ri]])
                            rb = R_ring.next()
                            S.op("pe", lambda: nc.tensor.matmul(PB[rb][0:64, :], lhsT=ones_f[64:65, 0:64], rhs=rr[ri][64:65, 0:512],
                                                                start=True, stop=True),
                                 reads=[r_rr[ri], r_ident], writes=[RB[rb]])
                            S.op("act", lambda: nc.scalar.copy(Rsb[ri][0:64, 0:512], PB[rb][0:64, :]), reads=[RB[rb]], writes=[r_Rsb[ri]])
                            c0 = kvh * 2
                            v4 = lambda ap: ap.rearrange("p (g q) -> p g q", g=4)
                            S.op("dve", lambda: nc.vector.tensor_tensor(
                                catBC[0:64, c0:c0 + 2, n * 128:(n + 1) * 128], v4(PB[ab][0:64, :])[:, 0:4:2, :],
                                v4(Rsb[ri][0:64, 0:512])[:, 0:4:2, :], ALU.mult),
                                reads=[RB[ab], r_Rsb[ri]], writes=[r_catB[n]])
                            zi = zst_ring.next()
                            S.op("dve", lambda: nc.vector.tensor_tensor(
                                zst[zi][0:64, 0:256].rearrange("p (g q) -> p g q", g=2), v4(PB[ab][0:64, :])[:, 1:4:2, :],
                                v4(Rsb[ri][0:64, 0:512])[:, 1:4:2, :], ALU.mult),
                                reads=[RB[ab], r_Rsb[ri]], writes=[r_zst[zi]])
                            S.dma("sp", g_zst[zi], catBC[64:128, c0:c0 + 2, n * 128:(n + 1) * 128],
                                  zst[zi][0:64, 0:256].rearrange("p (g q) -> p g q", g=2), reads=[r_zst[zi]], writes=[r_catB[n]])

                        if b == 0 and l == 0:
                            print("sbuf remaining p2", nc.sbuf_bytes_remaining)
                        pipeline(len(units), [(b_post, 2), (b_scores, 0), (b_pv, 1)])

                        qgroups = [(g * 512, 512, list(range(NT)), list(range(4 * g, 4 * g + 4))) for g in range(4)]
                        if not last:
                            qgroups.append((2048, 256, [16, 17], [16, 17]))
                        s_ring = Ring([0, 1, 2, 3])
                        scl = 32 ** -0.5
                        def c_post(ui, h, q0, W, qtiles, accb, mb):
                            hc = h // 2
                            ri = ui % 2
                            po = 0
                            pr_ = 64
                            for m in range(2):
                                S.op("dve", lambda m=m: nc.vector.reciprocal(rr[ri][pr_:pr_ + 1, m * 512:m * 512 + W],
                                                                              accsb[accb[m]][pr_:pr_ + 1, 0:W]),
                                     reads=[r_accsb[accb[m]]], writes=[r_rr[ri]])
                            S.op("dve", lambda: nc.vector.tensor_scalar(rr[ri][pr_:pr_ + 1, 512:512 + W], rr[ri][pr_:pr_ + 1, 512:512 + W],
                                                                         neglam[pr_:pr_ + 1, l:l + 1], None, ALU.mult),
                                 reads=[r_rr[ri], r_const], writes=[r_rr[ri]])
                            for m in range(2):
                                S.op("pe", lambda m=m: nc.tensor.matmul(PB[mb[m]][0:64, 0:W], lhsT=ones_f[pr_:pr_ + 1, 0:64],
                                                                        rhs=rr[ri][pr_:pr_ + 1, m * 512:m * 512 + W], start=True, stop=True),
                                     reads=[r_rr[ri], r_ident], writes=[RB[mb[m]]])
                                S.op("act", lambda m=m: nc.scalar.copy(Rsb[ri][po:po + 64, m * 512:m * 512 + W], PB[mb[m]][po:po + 64, 0:W]),
                                     reads=[RB[mb[m]]], writes=[r_Rsb[ri]])
                            S.op("dve", lambda: nc.vector.tensor_tensor(yy[ri][po:po + 64, 0:W], accsb[accb[0]][po:po + 64, 0:W],
                                                                         Rsb[ri][po:po + 64, 0:W], ALU.mult),
                                 reads=[r_accsb[accb[0]], r_Rsb[ri]], writes=[r_yy[ri]])
                            S.op("dve", lambda: nc.vector.tensor_tensor(y1[ri][po:po + 64, 0:W], accsb[accb[1]][po:po + 64, 0:W],
                                                                         Rsb[ri][po:po + 64, 512:512 + W], ALU.mult),
                                 reads=[r_accsb[accb[1]], r_Rsb[ri]], writes=[r_y1[ri]])
                            S.op("dve", lambda: nc.vector.tensor_tensor(yy[ri][po:po + 64, 0:W], yy[ri][po:po + 64, 0:W],
                                                                         y1[ri][po:po + 64, 0:W], ALU.add),
                                 reads=[r_yy[ri], r_y1[ri]], writes=[r_yy[ri]])
                            S.op("act", lambda: nc.scalar.activation(ysq[ri][po:po + 64, 0:W], yy[ri][po:po + 64, 0:W], AF.Square),
                                 reads=[r_yy[ri]], writes=[r_ysq[ri]])
                            S.op("pe", lambda: nc.tensor.matmul(PB[mb[2]][0:64, 0:W], lhsT=onesmean[po:po + 64, 0:64], rhs=ysq[ri][po:po + 64, 0:W],
                                                                start=True, stop=True),
                                 reads=[r_ysq[ri], r_ident], writes=[RB[mb[2]]])
                            S.op("act", lambda: nc.scalar.activation(rsd[ri][po:po + 64, 0:W], PB[mb[2]][po:po + 64, 0:W], AF.Sqrt,
                                                                     bias=eps_t[po:po + 64, 0:1]),
                                 reads=[RB[mb[2]], r_ident], writes=[r_rsd[ri]])
                            S.op("dve", lambda: nc.vector.reciprocal(rsd[ri][po:po + 64, 0:W], rsd[ri][po:po + 64, 0:W]),
                                 reads=[r_rsd[ri]], writes=[r_rsd[ri]])
                            if h % 2 == 0:
                                S.op("dve", lambda: nc.vector.scalar_tensor_tensor(
                                    catBC[0:64, 4 + hc, q0:q0 + W], yy[ri][0:64, 0:W], gsub[0:64, l:l + 1],
                                    rsd[ri][0:64, 0:W], ALU.mult, ALU.mult),
                                    reads=[r_yy[ri], r_rsd[ri], r_const], writes=[r_catC[qt] for qt in qtiles])
                            else:
                                zi = zst_ring.next()
                                S.op("dve", lambda: nc.vector.scalar_tensor_tensor(
                                    zst[zi][0:64, 0:W], yy[ri][0:64, 0:W], gsub[0:64, l:l + 1],
                                    rsd[ri][0:64, 0:W], ALU.mult, ALU.mult),
                                    reads=[r_yy[ri], r_rsd[ri], r_const], writes=[r_zst[zi]])
                                S.dma("sp", g_zst[zi], catBC[64:128, 4 + hc, q0:q0 + W], zst[zi][0:64, 0:W],
                                      reads=[r_zst[zi]], writes=[r_catC[qt] for qt in qtiles])

                        cunits = [(hc, qg) for qg in qgroups for hc in range(2)]
                        for ui, (hc, (q0, W, kts, qtiles)) in enumerate(cunits):
                            nk = len(kts)
                            sbs = {}
                            combos = [(2 * hc + hh, m) for hh in range(2) for m in range(2)]
                            accs = {cm: 4 + k for k, cm in enumerate(combos)}

                            def c_score(ki, q0=q0, W=W, kts=kts, qtiles=qtiles, hc=hc, sbs=sbs, combos=combos):
                                kt = kts[ki]
                                sbl = []
                                for (h, m) in combos:
                                    pb = 32 * (2 * (h % 2) + m)
                                    sb = s_ring.next()
                                    kw = dict(tile_position=(96, 0)) if pb == 96 else {}
                                    S.op("pe", lambda sb=sb, pb=pb, kw=kw: nc.tensor.matmul(
                                        PB[sb][:, 0:W], lhsT=kcT[pb:pb + 32, hc, kt * 128:(kt + 1) * 128],
                                        rhs=qcT[pb:pb + 32, hc, q0:q0 + W], start=True, stop=True, **kw),
                                        reads=[r_kcT[kt]] + [r_qcT[qt] for qt in qtiles], writes=[RB[sb]])
                                    sbl.append(sb)
                                for (h, m), sb in zip(combos, sbl):
                                    pi = pT_ring.next()
                                    S.op("act", lambda sb=sb, pi=pi: nc.scalar.activation(pT[pi][:, 0:W], PB[sb][:, 0:W], AF.Exp, scale=scl),
                                         reads=[RB[sb]], writes=[r_pT[pi]])
                                    sbs[(ki, h, m)] = pi

                            def c_pv(ki, W=W, kts=kts, nk=nk, sbs=sbs, combos=combos, accs=accs):
                                kt = kts[ki]
                                for (h, m) in combos:
                                    pi = sbs.pop((ki, h, m))
                                    ab = accs[(h, m)]
                                    S.op("pe", lambda h=h, pi=pi, ab=ab: nc.tensor.matmul(
                                        PB[ab][0:65, 0:W], lhsT=vc[:, kt, h, :], rhs=pT[pi][:, 0:W], start=(ki == 0), stop=(ki == nk - 1)),
                                        reads=[r_vc[kt], r_vones, r_pT[pi]], writes=[RB[ab]], inc=(ki == nk - 1))

                            pipeline(nk, [(c_score, 0), (c_pv, 1)])
                            for k, cm in enumerate(combos):
                                evac(accsb[k][0:65, 0:W], PB[accs[cm]][0:65, 0:W], [RB[accs[cm]]], [r_accsb[k]])
                            for hh in range(2):
                                h = 2 * hc + hh
                                c_post(2 * ui + hh, h, q0, W, qtiles, (2 * hh, 2 * hh + 1), (0, 1, 2) if hh == 0 else (3, 0, 1))

                        if stop == "p2" and dbg is not None:
                            dump("catA", catA[:, :, 0:ntq * 128], r_catA[0])
                            for t in range(ntq):
                                S._emit_waits("sp", S._deps("sp", [r_catB[t], r_catC[t], r_catA[t]], []))
                            dump("catBC", catBC[:, :, 0:ntq * 128], r_catA[0])
                            for t in range(0):
                                S._emit_waits("sp", S._deps("sp", [r_catB[t], r_catC[t], r_catA[t]], []))
                            dbg_wait()
                            return nc

                        load_modb(0, b, l, 2)
                        load_modb(2, 2, l, 2)
                        pair_ring = Ring([(0, 1), (2, 3), (4, 5), (6, 7)])
                        pstate = {}

                        def o_mm(t):
                            b0, b1 = pair_ring.next()
                            for nh, bi in enumerate((b0, b1)):
                                for kc in range(8):
                                    S.op("pe", lambda nh=nh, bi=bi, kc=kc: nc.tensor.matmul(
                                        PB[bi][:, :], lhsT=(catA[:, kc, t * 128:(t + 1) * 128] if kc < 2 else catBC[:, kc - 2, t * 128:(t + 1) * 128]),
                                        rhs=w_out[:, kc, nh * 512:(nh + 1) * 512],
                                        start=(kc == 0), stop=(kc == 7)),
                                        reads=[r_catA[t], r_catB[t], r_catC[t], r_wout], writes=[RB[bi]], inc=(kc == 7))
                            pstate[t] = (b0, b1)

                        def o_res(t):
                            b0, b1 = pstate.pop(t)
                            xi = load_x(b, lsrc, t)
                            xo = xout_ring.next()
                            gi = 0 if t < 16 else 2
                            for nh, bi in enumerate((b0, b1)):
                                S.op("dve", lambda nh=nh, bi=bi: nc.vector.tensor_tensor(
                                    xout[xo][:, nh * 512:(nh + 1) * 512], PB[bi][:, :], modb[gi][:, nh * 512:(nh + 1) * 512], ALU.mult),
                                    reads=[RB[bi], r_modb[gi]], writes=[r_xout[xo]])
                            S.op("dve", lambda: nc.vector.tensor_tensor(xout[xo][:], xout[xo][:], xin[xi][:], ALU.add),
                                 reads=[r_xout[xo], r_xin[xi]], writes=[r_xout[xo]])
                            S.dma("sp", g_xout[xo], x_dst(b, t), xout[xo][:], reads=[r_xout[xo]], writes=[r_dx[b][t]])

                        pipeline(ntq, [(o_mm, 0), (o_res, 1)])
                        S.barrier()
                if stop == "s1":
                    break

                load_modb(0, b, l, 4)
                load_modb(1, b, l, 3)
                load_modb(2, 2, l, 4)
                load_modb(3, 2, l, 3)
                ncols = ntq * 128
                with ExitStack() as s2c:
                    h2T = T("h2T", [128, 8, NTOK], BF16, s2c); r_h2T = [Res(f"h2T{t}") for t in range(NT)]
                    hid = T("hid", [128, 8, NTOK], BF16, s2c); r_hid = [Res(f"hid{j}") for j in range(8)]
                    wdn = T("wdn", [128, 8, D], BF16, s2c); r_wdn = Res("wdn")
                    wup = [T(f"wup{i}", [128, 8, 256], BF16, s2c) for i in range(3)]; r_wup = [Res(f"wup{i}") for i in range(3)]
                    tbuf = [T(f"tbuf{i}", [128, NTOK], F32, s2c) for i in range(3)]; r_tbuf = [[Res(f"tbuf{i}_{k}") for k in range(5)] for i in range(3)]
                    tb_ring = Ring(range(3))
                    hb2 = [T(f"hb2_{i}", [128, D], BF16, s2c) for i in range(2)]; r_hb2 = [Res(f"hb2_{i}") for i in range(2)]
                    t12 = [T("t12_0", [128, D], F32, s2c)] * 2; r_t12 = [Res("t12_0")] * 2
                    trp_ring = Ring([0, 1, 2, 3])
                    if b == 0 and l == 0:
                        print("sbuf remaining ffn", nc.sbuf_bytes_remaining)
                    def f0a(t):
                        isctx = t >= 16
                        xi = load_x(b, l + 1, t)
                        sl = t % 2
                        norm_tile(xi, 2 if isctx else 0, 3 if isctx else 1, hb2[sl][:], r_hb2[sl], t12[sl][:], r_t12[sl], (t % 4) * 2)

                    def f0b(t):
                        sl = t % 2
                        transpose8(hb2[sl], r_hb2[sl], trp_ring.next(), h2T[:, :, t * 128:(t + 1) * 128], r_h2T[t])

                    pipeline(ntq, [(f0b, 1), (f0a, 0)])
                    blocks = [(g * 512, 512, list(range(4 * g, 4 * g + 4)), g > 0) for g in range(4)]
                    if not last:
                        blocks.append((2048, 256, [16, 17], False))
                    wk = 0
                    load_modb(0, b, l, 5)
                    load_modb(2, 2, l, 5)
                    for (j0, npart) in ((0, 8), (8, 7), (15, 7)):
                        for part in range(npart):
                            jj = j0 + part
                            S.dma("pool", g_wdn, wdn[:, part, :], wdn_d[l, jj * 128:(jj + 1) * 128, :],
                                  reads=[], writes=[r_wdn])
                        up_ring = Ring([0, 1, 2, 3, 4, 5, 6, 7])
                        pend_hid = []
                        for j in range(npart):
                            jj = j0 + j
                            wi = wk % 3
                            wk += 1
                            S.dma("pool", g_w[3 + wi], wup[wi][:], wup_d[l, jj], writes=[r_wup[wi]])
                            tbs = []
                            for row in range(2):
                                ch = jj + row * NCH
                                ti_ = tb_ring.next()
                                tbs.append(ti_)
                                tbv = tbuf[ti_]
                                prev = None
                                for bk_i, (c0, W, tiles, cont) in enumerate(blocks):
                                    rtb = r_tbuf[ti_][bk_i]
                                    rtb_prev = r_tbuf[ti_][bk_i - 1] if bk_i > 0 else None
                                    bi = up_ring.next()
                                    for kc in range(8):
                                        S.op("pe", lambda kc=kc, bi=bi, c0=c0, W=W, row=row: nc.tensor.matmul(
                                            PB[bi][:, 0:W], lhsT=wup[wi][:, kc, row * 128:(row + 1) * 128], rhs=h2T[:, kc, c0:c0 + W],
                                            start=(kc == 0), stop=(kc == 7)),
                                            reads=[r_wup[wi]] + [r_h2T[t] for t in tiles], writes=[RB[bi]], inc=(kc == 7))
                                    S.op("act", lambda bi=bi, c0=c0, W=W, ch=ch, tbv=tbv: nc.scalar.activation(
                                        tbv[:, c0:c0 + W], PB[bi][:, 0:W], AF.Identity, scale=cw[:, l, 1, ch:ch + 1], bias=cb[:, l, ch:ch + 1]),
                                        reads=[RB[bi], r_const], writes=[rtb])
                                    S.op("dve", lambda bi=bi, c0=c0, W=W, ch=ch, tbv=tbv: nc.vector.scalar_tensor_tensor(
                                        tbv[:, c0 + 1:c0 + W], PB[bi][:, 0:W - 1], cw[:, l, 0, ch:ch + 1], tbv[:, c0 + 1:c0 + W],
                                        ALU.mult, ALU.add), reads=[RB[bi], r_const, rtb], writes=[rtb])
                                    S.op("dve", lambda bi=bi, c0=c0, W=W, ch=ch, tbv=tbv: nc.vector.scalar_tensor_tensor(
                                        tbv[:, c0:c0 + W - 1], PB[bi][:, 1:W], cw[:, l, 2, ch:ch + 1], tbv[:, c0:c0 + W - 1],
                                        ALU.mult, ALU.add), reads=[RB[bi], r_const, rtb], writes=[rtb])
                                    if cont:
                                        pbi, pW = prev
                                        S.op("dve", lambda bi=bi, c0=c0, ch=ch, tbv=tbv, pbi=pbi, pW=pW: nc.vector.scalar_tensor_tensor(
                                            tbv[:, c0:c0 + 1], PB[pbi][:, pW - 1:pW], cw[:, l, 0, ch:ch + 1], tbv[:, c0:c0 + 1],
                                            ALU.mult, ALU.add), reads=[RB[pbi], r_const, rtb], writes=[rtb])
                                        S.op("dve", lambda bi=bi, c0=c0, ch=ch, tbv=tbv: nc.vector.scalar_tensor_tensor(
                                            tbv[:, c0 - 1:c0], PB[bi][:, 0:1], cw[:, l, 2, ch:ch + 1], tbv[:, c0 - 1:c0],
                                            ALU.mult, ALU.add), reads=[RB[bi], r_const, rtb_prev], writes=[rtb_prev])
                                    prev = (bi, W)
                                    if row == 0 and c0 == 512 and pend_hid:
                                        pend_hid.pop()()
                            tg, tv = tbs

                            def do_hid(tg=tg, tv=tv, j=j):
                                S.op("act", lambda: nc.scalar.activation(tbuf[tg][:, 0:ncols], tbuf[tg][:, 0:ncols], AF.Silu),
                                     reads=r_tbuf[tg], writes=r_tbuf[tg])
                                S.op("dve", lambda: nc.vector.tensor_tensor(
                                    hid[:, j, 0:ncols], tbuf[tg][:, 0:ncols], tbuf[tv][:, 0:ncols], ALU.mult),
                                    reads=r_tbuf[tg] + r_tbuf[tv], writes=[r_hid[j]])
                            pend_hid.append(do_hid)
                        while pend_hid:
                            pend_hid.pop()()
                        pair_ring = Ring([(0, 1), (2, 3), (4, 5), (6, 7)])
                        pstate = {}

                        def d_mm(t):
                            b0, b1 = pair_ring.next()
                            for nh, bi in enumerate((b0, b1)):
                                for j in range(npart):
                                    S.op("pe", lambda nh=nh, bi=bi, j=j: nc.tensor.matmul(
                                        PB[bi][:, :], lhsT=hid[:, j, t * 128:(t + 1) * 128], rhs=wdn[:, j, nh * 512:(nh + 1) * 512],
                                        start=(j == 0), stop=(j == npart - 1)),
                                        reads=[r_hid[j], r_wdn], writes=[RB[bi]], inc=(j == npart - 1))
                            pstate[t] = (b0, b1)

                        def d_res(t):
                            b0, b1 = pstate.pop(t)
                            xi = load_x(b, l + 1, t)
                            xo = xout_ring.next()
                            gi = 0 if t < 16 else 2
                            for nh, bi in enumerate((b0, b1)):
                                S.op("dve", lambda nh=nh, bi=bi: nc.vector.tensor_tensor(
                                    xout[xo][:, nh * 512:(nh + 1) * 512], PB[bi][:, :], modb[gi][:, nh * 512:(nh + 1) * 512], ALU.mult),
                                    reads=[RB[bi], r_modb[gi]], writes=[r_xout[xo]])
                            S.op("dve", lambda: nc.vector.tensor_tensor(xout[xo][:], xout[xo][:], xin[xi][:], ALU.add),
                                 reads=[r_xout[xo], r_xin[xi]], writes=[r_xout[xo]])
                            S.dma("sp", g_xout[xo], x_dst(b, t), xout[xo][:], reads=[r_xout[xo]], writes=[r_dx[b][t]])

                        pipeline(ntq, [(d_mm, 0), (d_res, 1)])
                    S.barrier()
        S.finish()
        build_nc.stats = (S.ninst, S.nwait)
    return nc


def _rope_tables(head_dim):
    rows = SEQ // GRID_W
    row = np.repeat(np.arange(rows, dtype=np.float32), GRID_W)
    col = np.tile(np.arange(GRID_W, dtype=np.float32), rows)
    n_freq = head_dim // 4
    inv_freq = (np.float32(10000.0) ** (-np.arange(n_freq, dtype=np.float32) / np.float32(n_freq))).astype(np.float32)
    ang = np.stack([row[:, None] * inv_freq, col[:, None] * inv_freq], axis=1).astype(np.float32)
    cos = np.cos(ang).astype(np.float32)
    sin = np.sin(ang).astype(np.float32)
    C2 = np.stack([cos, cos], axis=2).reshape(SEQ, 4 * n_freq)
    S2 = np.stack([-sin.reshape(SEQ, 2 * n_freq), sin.reshape(SEQ, 2 * n_freq)], axis=1)
    C2 = C2.reshape(16, 128, 4 * n_freq).transpose(1, 0, 2)
    S2 = S2.reshape(16, 128, 2, 2 * n_freq).transpose(1, 0, 2, 3)
    return np.ascontiguousarray(C2), np.ascontiguousarray(S2)


def prep_shared(inp):
    f = lambda a: np.ascontiguousarray(np.asarray(a, dtype=np.float32))
    w_up = f(inp["w_up"])
    wu = w_up.reshape(L_ALL, 8, 128, 2, NCH, 128)
    w_up_r = np.ascontiguousarray(wu.transpose(0, 4, 2, 1, 3, 5)).reshape(L_ALL, NCH, 128, 8, 256)
    a_wsT = np.ascontiguousarray(f(inp["a_ws"]).transpose(0, 3, 1, 2))
    smallg = np.concatenate([f(inp["b_qnorm"]), f(inp["b_knorm"]), f(inp["c_qnorm"]), f(inp["c_knorm"]),
                             f(inp["c_subln"])], axis=1)
    cwr = f(inp["conv_w"]).reshape(L_ALL, 3, 44, 128).transpose(3, 0, 1, 2)
    cbr = f(inp["conv_b"]).reshape(L_ALL, 44, 128).transpose(2, 0, 1)
    kk = np.arange(128)[:, None]
    qq = np.arange(128)[None, :]
    rbc, rbs = _rope_tables(64)
    rcc, rcs = _rope_tables(32)
    return {
        "w_ada_r": np.ascontiguousarray(f(inp["w_ada"]).reshape(L_ALL, 8, 128, 12, 512).transpose(0, 3, 2, 1, 4)),
        "b_ada": f(inp["b_ada"]),
        "gn": np.ascontiguousarray(np.stack([f(inp["norm1_g"]), f(inp["norm2_g"])], axis=1)),
        "w_in": f(inp["w_in"]), "w_out": f(inp["w_out"]), "w_up_r": w_up_r, "w_down": f(inp["w_down"]),
        "a_wsT": a_wsT, "a_bs": f(inp["a_bs"]), "smallg": np.ascontiguousarray(smallg),
        "b_sink": f(inp["b_sink"]).reshape(1, -1), "c_lam": f(inp["c_lam"]).reshape(1, -1),
        "sublnT": np.ascontiguousarray(f(inp["c_subln"]).T),
        "cw_r": np.ascontiguousarray(cwr), "cb_r": np.ascontiguousarray(cbr),
        "ident": np.eye(128, dtype=np.float32),
        "maskL": (qq <= kk).astype(np.float32), "maskR": (kk <= qq).astype(np.float32),
        "ropeB_C2": rbc, "ropeB_S": rbs, "ropeC_C2": rcc, "ropeC_S": rcs,
    }


def core_inputs(inp, shared, core, nb=NB):
    f = lambda a: np.ascontiguousarray(np.asarray(a, dtype=np.float32))
    b0 = core * nb
    d = dict(shared)
    d["x"] = f(inp["x"][b0:b0 + nb])
    d["ctx"] = f(inp["ctx"][b0:b0 + nb])
    crow = np.zeros((3, D), np.float32)
    crow[0:nb] = np.asarray(inp["c"], np.float32)[b0:b0 + nb]
    crow[2] = np.asarray(inp["c_ctx"], np.float32)
    d["crow"] = crow
    return d


_NC_CACHE = {}


def kernel(**inputs):
    inp = {k: np.asarray(v) for k, v in inputs.items()}
    shared = prep_shared(inp)
    if "nc" not in _NC_CACHE:
        _NC_CACHE["nc"] = build_nc()
    nc = _NC_CACHE["nc"]
    in_maps = [core_inputs(inp, shared, c) for c in range(N_CORES)]
    res = run_bass_kernel_spmd(nc, in_maps, core_ids=list(range(N_CORES)))
    out = np.concatenate([np.asarray(r["y"], dtype=np.float32) for r in res.results], axis=0)
    return out
```

```python
import math
from contextlib import ExitStack

import numpy as np
import concourse.bass as bass
import concourse.mybir as mybir
from concourse.bass_utils import run_bass_kernel_spmd

F32 = mybir.dt.float32
BF16 = mybir.dt.bfloat16
AF = mybir.ActivationFunctionType
ALU = mybir.AluOpType
AX = mybir.AxisListType

L_ALL = 4
D = 1024
SEQ = 2048
LC = 256
NT = 18
NTOK = NT * 128
DFF = 2816
NCH = 22
EPS = 1e-6
GRID_W = 64
N_CORES = 8
NB = 2


class Res:
    __slots__ = ("name", "lw", "rd", "excl")

    def __init__(self, name, excl=False):
        self.name = name
        self.lw = None
        self.rd = {}
        self.excl = excl


class DmaGroup:
    __slots__ = ("sem", "cnt", "name")

    def __init__(self, sem, name):
        self.sem = sem
        self.cnt = 0
        self.name = name


class Sched:
    ENG = ("pe", "act", "dve", "pool", "sp")

    def __init__(self, nc, stack):
        self.nc = nc
        self.stack = stack
        self.eng = {"pe": nc.tensor, "act": nc.scalar, "dve": nc.vector,
                    "pool": nc.gpsimd, "sp": nc.sync}
        self.sem = {e: stack.enter_context(nc.semaphore("sem_" + e)) for e in self.ENG}
        self.cnt = {e: 0 for e in self.ENG}
        self.seen = {e: {} for e in self.ENG}
        self.nwait = 0
        self.ninst = 0
        self.groups = []

    def group(self, name):
        sem = self.stack.enter_context(self.nc.semaphore("dg_" + name))
        g = DmaGroup(sem, name)
        self.groups.append(g)
        return g

    def finish(self):
        for g in self.groups:
            if g.cnt:
                self.nc.sync.wait_ge(g.sem, g.cnt)

    def _deps(self, e, reads, writes):
        need = {}

        def add(ev, raw):
            if ev is None:
                return
            kind, src, count = ev
            if kind == "eng" and src == e:
                if e in ("pe", "sp"):
                    return
            key = (kind, src)
            if need.get(key, (None, 0))[1] < count:
                need[key] = (ev, count)

        for r in reads:
            add(r.lw, True)
            if r.excl:
                for k2, ev in r.rd.items():
                    if k2 != ("eng", e):
                        add(ev, False)
        for w in writes:
            add(w.lw, False)
            for ev in w.rd.values():
                add(ev, False)
        out = []
        for key, (ev, count) in need.items():
            if self.seen[e].get(key, 0) >= count:
                continue
            out.append((key, ev, count))
        return out

    def _emit_waits(self, e, deps):
        eng = self.eng[e]
        for key, ev, count in deps:
            kind, src, _ = ev
            if kind == "eng":
                assert count <= self.cnt[src], f"wait on un-inc'd instr {src} {count}>{self.cnt[src]}"
                sem = self.sem[src]
            else:
                sem = src.sem
            eng.wait_ge(sem, count)
            self.nwait += 1
            self.seen[e][key] = count

    def _mark(self, ev, key, reads, writes):
        for r in reads:
            r.rd[key] = ev
        for w in writes:
            w.lw = ev
            w.rd = {}

    def op(self, e, fn, reads=(), writes=(), inc=True):
        import os as _os
        self.nops = getattr(self, "nops", 0) + 1
        if self.nops > int(_os.environ.get("P1_OPLIMIT", 10 ** 9)):
            if not inc:
                return None
            return None
        self._emit_waits(e, self._deps(e, reads, writes))
        ins = fn()
        self.ninst += 1
        if inc:
            self.cnt[e] += 1
            ins.then_inc(self.sem[e], 1)
            ev = ("eng", e, self.cnt[e])
        else:
            ev = ("eng", e, self.cnt[e] + 1)
        self._mark(ev, ("eng", e), reads, writes)
        return ins

    def dma(self, q, grp, out, in_, reads=(), writes=(), **kw):
        self._emit_waits(q, self._deps(q, reads, writes))
        ins = self.eng[q].dma_start(out=out, in_=in_, **kw)
        grp.cnt += 16
        ins.then_inc(grp.sem, 16)
        self.ninst += 1
        ev = ("dma", grp, grp.cnt)
        self._mark(ev, ("dma", grp), reads, writes)
        return ins

    def dma_batch(self, q, grp, items):
        allr, allw = [], []
        for it in items:
            allr += list(it.get("reads", ()))
            allw += list(it.get("writes", ()))
        self._emit_waits(q, self._deps(q, allr, allw))
        for it in items:
            ins = self.eng[q].dma_start(out=it["out"], in_=it["in_"], **it.get("kw", {}))
            grp.cnt += 16
            ins.then_inc(grp.sem, 16)
            self.ninst += 1
        ev = ("dma", grp, grp.cnt)
        self._mark(ev, ("dma", grp), allr, allw)

    def barrier(self):
        for e in self.ENG:
            for f in self.ENG:
                if self.cnt[f] == 0 or (f == e and e in ("pe", "sp")):
                    continue
                key = ("eng", f)
                if self.seen[e].get(key, 0) >= self.cnt[f]:
                    continue
                self.eng[e].wait_ge(self.sem[f], self.cnt[f])
                self.seen[e][key] = self.cnt[f]
                self.nwait += 1


class Ring:
    def __init__(self, items):
        self.items = list(items)
        self.i = 0

    def next(self):
        it = self.items[self.i % len(self.items)]
        self.i += 1
        return it


def pipeline(n, stages):
    mx = max(s for _, s in stages)
    for step in range(n + mx):
        for fn, sk in stages:
            i = step - sk
            if 0 <= i < n:
                fn(i)


def build_nc(depth=L_ALL, nb=NB, dbg=None, stop=None):
    nc = bass.Bass("TRN2", target_bir_lowering=False)

    def din(name, shape, dt=F32):
        return nc.dram_tensor(name, list(shape), dt, kind="ExternalInput").ap()

    x_d = din("x", [nb, SEQ, D])
    ctx_d = din("ctx", [nb, LC, D])
    crow_d = din("crow", [3, D])
    wada_d = din("w_ada_r", [L_ALL, 12, 128, 8, 512])
    bada_d = din("b_ada", [L_ALL, 6 * D])
    gn_d = din("gn", [L_ALL, 2, D])
    win_d = din("w_in", [L_ALL, D, 2048])
    wout_d = din("w_out", [L_ALL, D, D])
    wup_d = din("w_up_r", [L_ALL, NCH, 128, 8, 256])
    wdn_d = din("w_down", [L_ALL, DFF, D])
    awsT_d = din("a_wsT", [L_ALL, 128, 4, 128])
    abs_d = din("a_bs", [L_ALL, 4, 128])
    smallg_d = din("smallg", [L_ALL, 256])
    sink_d = din("b_sink", [1, L_ALL * 8])
    clam_d = din("c_lam", [1, L_ALL * 128])
    sublnT_d = din("sublnT", [64, L_ALL])
    cw_d = din("cw_r", [128, L_ALL, 3, 44])
    cb_d = din("cb_r", [128, L_ALL, 44])
    ident_d = din("ident", [128, 128])
    maskL_d = din("maskL", [128, 128])
    maskR_d = din("maskR", [128, 128])
    rbc_d = din("ropeB_C2", [128, 16, 64])
    rbs_d = din("ropeB_S", [128, 16, 2, 32])
    rcc_d = din("ropeC_C2", [128, 16, 32])
    rcs_d = din("ropeC_S", [128, 16, 2, 16])
    y_d = nc.dram_tensor("y", [nb, SEQ, D], F32, kind="ExternalOutput").ap()
    ctxs_d = nc.dram_tensor("ctx_s", [nb, LC, D], F32).ap()
    mod_d = nc.dram_tensor("mod_s", [3, L_ALL, 6, D], F32).ap()

    with ExitStack() as st:
        S = Sched(nc, st)

        uid = [0]

        def T(name, shape, dt, stack=st):
            uid[0] += 1
            return stack.enter_context(nc.sbuf_tensor(f"sb{uid[0]}_{name}", list(shape), dt))

        PB = [st.enter_context(nc.psum_tensor(f"pb{i}", [128, 512], F32)) for i in range(8)]
        RB = [Res(f"pb{i}", excl=True) for i in range(8)]

        def bank_bf(i):
            return PB[i][:].bitcast(BF16)

        g_const = S.group("const")
        g_xin = [S.group(f"xin{i}") for i in range(2)]
        g_xout = [S.group(f"xout{i}") for i in range(2)]
        g_modb = [S.group(f"modb{i}") for i in range(4)]
        g_w = [S.group(f"w{i}") for i in range(6)]
        g_wdn = S.group("wdn")
        g_pre = [S.group(f"pre{i}") for i in range(6)]
        g_zst = [S.group(f"zst{i}") for i in range(2)]
        g_mod = S.group("modw")
        g_dbg = S.group("dbg")

        dbg_groups = []

        def dbg_wait():
            S.finish()

        def dump(name, ap, res):
            if dbg is None:
                return
            d = nc.dram_tensor("dbg_" + name, list(ap.shape), ap.dtype, kind="ExternalOutput").ap()
            gg = S.group("dbg_" + name)
            dbg_groups.append(gg)
            S.dma("sp", gg, d, ap, reads=[res])
            dbg.append(name)

        ident_f = T("ident_f", [128, 128], F32); r_ident = Res("ident")
        ident_b = T("ident_b", [128, 128], BF16)
        ones_f = T("ones_f", [128, 128], F32)
        onesmean = T("onesmean", [128, 128], F32)
        mask_f = T("mask_f", [128, 2, 128], F32)
        mask_b = T("mask_b", [128, 2, 128], BF16)
        ropeB_C = T("ropeB_C", [128, 16, 64], F32)
        ropeB_S = T("ropeB_S", [128, 16, 2, 32], F32)
        ropeC_C = T("ropeC_C", [128, 16, 32], F32)
        ropeC_S = T("ropeC_S", [128, 16, 2, 16], F32)
        smallg = T("smallg", [128, L_ALL, 256], F32)
        biasT = T("biasT", [128, L_ALL, 2, 128], F32)
        cw = T("cw", [128, L_ALL, 3, 44], F32)
        cb = T("cb", [128, L_ALL, 44], F32)
        esink = T("esink", [128, L_ALL * 8], F32)
        clam = T("clam", [128, L_ALL, 4, 32], F32)
        lamt = T("lamt", [128, L_ALL, 2, 32], F32)
        lam2 = T("lam2", [128, L_ALL, 2], F32)
        neglam = T("neglam", [128, L_ALL], F32)
        gsub = T("gsub", [128, L_ALL], F32)
        r_const = Res("const")

        items = [
            dict(out=ident_f[:], in_=ident_d),
            dict(out=mask_f[:, 0, :], in_=maskL_d),
            dict(out=mask_f[:, 1, :], in_=maskR_d),
            dict(out=ropeB_C[:], in_=rbc_d),
            dict(out=ropeB_S[:], in_=rbs_d),
            dict(out=ropeC_C[:], in_=rcc_d),
            dict(out=ropeC_S[:], in_=rcs_d),
            dict(out=smallg[:].rearrange("p l c -> p (l c)"),
                 in_=smallg_d.rearrange("l c -> (l c)").partition_broadcast(128)),
            dict(out=cw[:], in_=cw_d),
            dict(out=cb[:], in_=cb_d),
            dict(out=esink[:], in_=sink_d[0, :].partition_broadcast(128)),
            dict(out=clam[:].rearrange("p l a c -> p (l a c)"), in_=clam_d[0, :].partition_broadcast(128)),
            dict(out=gsub[0:64, :], in_=sublnT_d),
            dict(out=gsub[64:128, :], in_=sublnT_d),
        ]
        for l in range(L_ALL):
            for g in range(4):
                items.append(dict(out=biasT[(g % 2) * 64:(g % 2) * 64 + 64, l, g // 2, :],
                                  in_=abs_d[l, g, :].partition_broadcast(64)))
        for it in items:
            it["writes"] = [r_const]
        S.dma_batch("sp", g_const, items)

        S.op("dve", lambda: nc.vector.tensor_copy(ident_b[:], ident_f[:]), reads=[r_const], writes=[r_ident])
        S.op("dve", lambda: nc.vector.tensor_copy(mask_b[:], mask_f[:]), reads=[r_const], writes=[r_ident])
        S.op("dve", lambda: nc.vector.memset(ones_f[:], 1.0), writes=[r_ident])
        S.op("dve", lambda: nc.vector.memset(onesmean[:], 1.0 / 64.0), writes=[r_ident])
        S.op("act", lambda: nc.scalar.activation(esink[:], esink[:], AF.Exp), reads=[r_const], writes=[r_const])
        S.op("dve", lambda: nc.vector.tensor_tensor(lamt[:], clam[:, :, 0:4:2, :], clam[:, :, 1:4:2, :], ALU.mult),
             reads=[r_const], writes=[r_const])
        S.op("dve", lambda: nc.vector.tensor_reduce(lam2[:], lamt[:], AX.X, ALU.add), reads=[r_const], writes=[r_const])
        S.op("act", lambda: nc.scalar.activation(lam2[:], lam2[:], AF.Exp), reads=[r_const], writes=[r_const])
        S.op("dve", lambda: nc.vector.tensor_tensor(neglam[:], lam2[:, :, 1], lam2[:, :, 0], ALU.subtract),
             reads=[r_const], writes=[r_const])
        for l in range(L_ALL):
            lam_init = 0.8 - 0.6 * math.exp(-0.3 * l)
            S.op("dve", lambda l=l, li=lam_init: nc.vector.tensor_scalar(
                neglam[:, l:l + 1], neglam[:, l:l + 1], -li, None, ALU.add), reads=[r_const], writes=[r_const])
            S.op("dve", lambda l=l, li=lam_init: nc.vector.tensor_scalar(
                gsub[:, l:l + 1], gsub[:, l:l + 1], 1.0 - li, None, ALU.mult), reads=[r_const], writes=[r_const])

        if stop == "const":
            dump("neglam", neglam[:], r_const); dump("gsub", gsub[:], r_const); dump("esink", esink[:], r_const)
            dump("biasT", biasT[:], r_const); dump("mask_b", mask_b[:], r_ident)
            dbg_wait()
            return nc
        with ExitStack() as pp:
            crow = T("crow_sb", [3, D], F32, pp); r_crow = Res("crow")
            scT = T("scT", [128, 8, 3], F32, pp); r_scT = Res("scT")
            rows = T("rows", [3, 6 * D], F32, pp); r_rows = Res("rows")
            bada = T("bada", [3, 6 * D], F32, pp); r_bada = Res("bada")
            gnb = T("gnb", [3, 2, D], F32, pp); r_gnb = Res("gnb")
            wslots = [T(f"wada{i}", [128, 8, 512], F32, pp) for i in range(3)]
            r_wslots = [Res(f"wada{i}") for i in range(3)]
            g_ws = g_pre[0:3]
            g_pp = g_pre[3]
            S.dma("sp", g_pp, crow[:], crow_d, writes=[r_crow])
            S.op("act", lambda: nc.scalar.activation(crow[:], crow[:], AF.Silu), reads=[r_crow], writes=[r_crow])
            for c in range(8):
                S.op("pe", lambda c=c: nc.tensor.transpose(PB[0][:, c * 3:c * 3 + 3], crow[0:3, c * 128:(c + 1) * 128],
                                                           ident_f[0:3, 0:3]),
                     reads=[r_crow, r_const], writes=[RB[0]], inc=(c == 7))
            S.op("dve", lambda: nc.vector.tensor_copy(scT[:].rearrange("p c r -> p (c r)"), PB[0][:, 0:24]),
                 reads=[RB[0]], writes=[r_scT])
            pring = Ring([1, 2, 3])
            k = 0
            for l in range(depth):
                S.dma("sp", g_pre[4], bada[:], bada_d[l, :].partition_broadcast(3), writes=[r_bada])
                S.dma("sp", g_pre[5], gnb[:].rearrange("p a d -> p (a d)"),
                      gn_d[l].rearrange("a d -> (a d)").partition_broadcast(3), writes=[r_gnb])
                for n in range(12):
                    si = k % 3
                    k += 1
                    S.dma("sp", g_ws[si], wslots[si][:],
                          wada_d[l, n],
                          writes=[r_wslots[si]])
                    bi = pring.next()
                    for kc in range(8):
                        S.op("pe", lambda kc=kc, si=si, bi=bi: nc.tensor.matmul(
                            PB[bi][0:3, :], lhsT=scT[:, kc, :], rhs=wslots[si][:, kc, :],
                            start=(kc == 0), stop=(kc == 7)),
                            reads=[r_scT, r_wslots[si]], writes=[RB[bi]], inc=(kc == 7))
                    S.op("dve", lambda n=n, bi=bi: nc.vector.tensor_tensor(
                        rows[:, n * 512:(n + 1) * 512], PB[bi][0:3, :], bada[:, n * 512:(n + 1) * 512], ALU.add),
                        reads=[RB[bi], r_bada], writes=[r_rows])
                for a, kidx in ((0, 1), (1, 4)):
                    S.op("dve", lambda a=a, kidx=kidx: nc.vector.scalar_tensor_tensor(
                        rows[:, kidx * D:(kidx + 1) * D], rows[:, kidx * D:(kidx + 1) * D], 1.0, gnb[:, a, :],
                        ALU.add, ALU.mult), reads=[r_rows, r_gnb], writes=[r_rows])
                S.dma("sp", g_mod, mod_d[:, l].rearrange("r k d -> r (k d)"), rows[:], reads=[r_rows], writes=[])
            r_mod = Res("mod_d")
            r_mod.lw = ("dma", g_mod, g_mod.cnt)
            S.barrier()

        if stop == "prepass":
            if dbg is not None:
                d = nc.dram_tensor("dbg_mod", [3, 1, 6, D], F32, kind="ExternalOutput").ap()
                S.dma("sp", g_dbg, d, mod_d[:, 0:1], reads=[r_mod])
                dbg.append("mod")
            dbg_wait()
            return nc
        xin = [T(f"xin{i}", [128, D], F32) for i in range(2)]
        r_xin = [Res(f"xin{i}") for i in range(2)]
        xin_ring = Ring(range(2))
        xout = [T(f"xout{i}", [128, D], F32) for i in range(2)]
        r_xout = [Res(f"xout{i}") for i in range(2)]
        xout_ring = Ring(range(2))
        modb = [T(f"modb{i}", [128, D], F32) for i in range(4)]
        r_modb = [Res(f"modb{i}") for i in range(4)]
        stat = T("stat", [128, 64], F32)
        r_dx = [[Res(f"dx{b}_{t}") for t in range(NT)] for b in range(nb)]

        def x_src(b, l, t):
            if t < 16:
                base = x_d if l == 0 else y_d
                return base[b, t * 128:(t + 1) * 128, :]
            base = ctx_d if l == 0 else ctxs_d
            return base[b, (t - 16) * 128:(t - 15) * 128, :]

        def x_dst(b, t):
            if t < 16:
                return y_d[b, t * 128:(t + 1) * 128, :]
            return ctxs_d[b, (t - 16) * 128:(t - 15) * 128, :]

        def load_x(b, lsrc, t):
            i = xin_ring.next()
            S.dma("sp", g_xin[i], xin[i][:], x_src(b, lsrc, t), reads=[r_dx[b][t]], writes=[r_xin[i]])
            return i

        def load_modb(i, row, l, kind):
            S.dma("sp", g_modb[i], modb[i][:], mod_d[row, l, kind, :].partition_broadcast(128),
                  reads=[r_mod], writes=[r_modb[i]])

        ev_toggle = [0]

        def evac(out, in_, reads, writes):
            ev_toggle[0] ^= 1
            if ev_toggle[0]:
                S.op("act", lambda: nc.scalar.copy(out, in_), reads=reads, writes=writes)
            else:
                S.op("dve", lambda: nc.vector.tensor_copy(out, in_), reads=reads, writes=writes)

        def norm_tile(xi, mi, shi, hb, r_hb, t1, r_t1, ss_col):
            ss = stat[:, ss_col:ss_col + 1]
            rt = stat[:, ss_col + 1:ss_col + 2]
            r_st = r_stat[ss_col // 2]
            import os as _os
            _k = int(_os.environ.get("P1_S0", 99))
            if _k < 2:
                return
            S.op("act", lambda: nc.scalar.activation(t1, xin[xi][:], AF.Square),
                 reads=[r_xin[xi]], writes=[r_t1])
            S.op("dve", lambda: nc.vector.tensor_reduce(ss, t1, AX.X, ALU.add), reads=[r_t1], writes=[r_st])
            if _k < 3:
                return
            S.op("act", lambda: nc.scalar.activation(rt, ss, AF.Sqrt, scale=1.0 / D, bias=eps_t[:, 0:1]),
                 reads=[r_st, r_ident], writes=[r_st])
            if _k < 4:
                return
            S.op("dve", lambda: nc.vector.reciprocal(rt, rt), reads=[r_st], writes=[r_st])
            if _k < 5:
                return
            S.op("dve", lambda: nc.vector.scalar_tensor_tensor(t1, xin[xi][:], rt, modb[mi][:], ALU.mult, ALU.mult),
                 reads=[r_xin[xi], r_st, r_modb[mi]], writes=[r_t1])
            if _k < 6:
                return
            S.op("dve", lambda: nc.vector.tensor_tensor(hb, t1, modb[shi][:], ALU.add),
                 reads=[r_t1, r_modb[shi]], writes=[r_hb])

        r_stat = [Res(f"stat{i}") for i in range(8)]
        eps_t = T("eps_t", [128, 1], F32)
        S.op("dve", lambda: nc.vector.memset(eps_t[:], EPS), writes=[r_ident])

        def transpose8(hb, r_hb, bi, dst, r_dst):
            pv = bank_bf(bi)
            import os as _os
            _k = int(_os.environ.get("P1_S0", 99))
            if _k < 7:
                return
            for c in range(8):
                S.op("pe", lambda c=c: nc.tensor.transpose(pv[:, c * 128:(c + 1) * 128], hb[:, c * 128:(c + 1) * 128],
                                                           ident_b[:]),
                     reads=[r_hb, r_ident], writes=[RB[bi]], inc=(c == 7))
            if _k < 8:
                return
            evac(dst, pv[:, 0:1024].rearrange("p (c t) -> p c t", c=8), [RB[bi]], [r_dst])

        for b in range(nb):
            for l in range(depth):
                last = (l == depth - 1)
                ntq = 16 if last else 18
                lsrc = l
                load_modb(0, b, l, 1)
                load_modb(1, b, l, 0)
                load_modb(2, 2, l, 1)
                load_modb(3, 2, l, 0)
                with ExitStack() as s1:
                    kbT = T("kbT", [128, NTOK], BF16, s1); r_kbT = [Res(f"kbT{t}") for t in range(NT)]
                    vb = T("vb", [128, NT, 2, 65], BF16, s1); r_vb = [Res(f"vb{t}") for t in range(NT)]
                    kcT = T("kcT", [128, 2, NTOK], BF16, s1); r_kcT = [Res(f"kcT{t}") for t in range(NT)]
                    vc = T("vc", [128, NT, 4, 65], BF16, s1); r_vc = [Res(f"vc{t}") for t in range(NT)]
                    qbT = T("qbT", [128, 4, NTOK], BF16, s1); r_qbT = [Res(f"qbT{t}") for t in range(NT)]
                    qcT = T("qcT", [128, 2, NTOK], BF16, s1); r_qcT = [Res(f"qcT{t}") for t in range(NT)]
                    catA = T("catA", [128, 2, NTOK], BF16, s1)
                    r_catA = [Res(f"catA{t}") for t in range(NT)]
                    r_catB = [Res(f"catB{t}") for t in range(NT)]
                    r_catC = [Res(f"catC{t}") for t in range(NT)]
                    r_vones = Res("vones")
                    S.op("dve", lambda: nc.vector.memset(vb[:, :, :, 64:65], 1.0), writes=[r_vones])
                    S.op("dve", lambda: nc.vector.memset(vc[:, :, :, 64:65], 1.0), writes=[r_vones])

                    with ExitStack() as p1:
                        w_in = T("w_in", [128, 8, 2048], BF16, p1); r_win = Res("w_in")
                        awsT = T("awsT", [128, 4, 128], BF16, p1); r_aws = Res("awsT")
                        for kc in range(8):
                            S.dma("pool", g_w[0], w_in[:, kc, :], win_d[l, kc * 128:(kc + 1) * 128, :], writes=[r_win])
                        S.dma("pool", g_w[1], awsT[:], awsT_d[l], writes=[r_aws])
                        NS = 2
                        hb = [T(f"hb{i}", [128, D], BF16, p1) for i in range(NS)]; r_hb = [Res(f"hb{i}") for i in range(NS)]
                        t1 = [T("t1_0", [128, D], F32, p1)] * NS; r_t1 = [Res("t1_0")] * NS
                        hT = [T(f"hT{i}", [128, 8, 128], BF16, p1) for i in range(3)]; r_hT = [Res(f"hT{i}") for i in range(3)]
                        uT = [T(f"uT{i}", [128, 2, 128], BF16, p1) for i in range(NS)]; r_uT = [Res(f"uT{i}") for i in range(NS)]
                        gv = [T(f"gv{i}", [128, 256], F32, p1) for i in range(NS)]; r_gv = [Res(f"gv{i}") for i in range(NS)]
                        vpad = [T(f"vpad{i}", [128, 4, 128], BF16, p1) for i in range(NS)]; r_vpad = [Res(f"vpad{i}") for i in range(NS)]
                        sq = [T("sq0", [128, 1152], F32, p1)] * NS; r_sq = [Res("sq0")] * NS
                        qn = [T("qn0", [128, 1152], F32, p1)] * NS; r_qn = [Res("qn0")] * NS
                        tb = [T("tb0", [128, 1152], F32, p1)] * NS; r_tb = [Res("tb0")] * NS
                        qr = [T("qr0", [128, 1152], BF16, p1)] * NS; r_qr = [Res("qr0")] * NS
                        st2 = [T(f"st2_{i}", [128, 32], F32, p1) for i in range(NS)]; r_st2 = [Res(f"st2_{i}") for i in range(NS)]
                        ta = [T(f"ta{i}", [128, 256], F32, p1) for i in range(NS)]; r_ta = [Res(f"ta{i}") for i in range(NS)]
                        for i in range(NS):
                            S.op("dve", lambda i=i: nc.vector.memset(vpad[i][:], 0.0), writes=[r_vpad[i]])
                        order = [16, 17] + list(range(16))
                        trp_ring = Ring([0, 1, 2])
                        prj_ring = Ring([3, 4, 5, 6, 7])
                        xi_of = {}
                        banks_of = {}

                        def s0(i):
                            t = order[i]
                            isctx = t >= 16
                            xi = load_x(b, lsrc, t)
                            xi_of[i] = xi
                            sl = i % NS
                            norm_tile(xi, 2 if isctx else 0, 3 if isctx else 1, hb[sl][:], r_hb[sl], t1[sl][:], r_t1[sl], (i % 4) * 2)
                            transpose8(hb[sl], r_hb[sl], trp_ring.next(), hT[i % 3][:], r_hT[i % 3])
                            if stop == "p1dbg" and t == 0:
                                dump("xin", xin[xi][:], r_xin[xi]); dump("hb", hb[sl][:], r_hb[sl]); dump("hT", hT[i % 3][:], r_hT[i % 3])
                                dump("m1b", modb[0][:], r_modb[0]); dump("sh1b", modb[1][:], r_modb[1]); dump("w_in", w_in[:], r_win)

                        def s1f(i):
                            t = order[i]
                            isctx = t >= 16
                            full = (not isctx) or (not last)
                            h = hT[i % 3]; rh = r_hT[i % 3]
                            bk = {}
                            if full:
                                bu = prj_ring.next(); bk["u"] = bu
                                for cc in range(2):
                                    for kc in range(8):
                                        S.op("pe", lambda cc=cc, kc=kc: nc.tensor.matmul(
                                            PB[bu][:, cc * 128:(cc + 1) * 128], lhsT=w_in[:, kc, cc * 128:(cc + 1) * 128],
                                            rhs=h[:, kc, :], start=(kc == 0), stop=(kc == 7)),
                                            reads=[r_win, rh], writes=[RB[bu]], inc=(kc == 7 and cc == 1))
                                bv = prj_ring.next(); bk["v"] = bv
                                for kc in range(8):
                                    S.op("pe", lambda kc=kc: nc.tensor.matmul(
                                        PB[bv][:, 0:256], lhsT=h[:, kc, :], rhs=w_in[:, kc, 256:512],
                                        start=(kc == 0), stop=(kc == 7)), reads=[r_win, rh], writes=[RB[bv]], inc=(kc == 7))
                                bq = prj_ring.next(); bk["q"] = bq
                                for kc in range(8):
                                    S.op("pe", lambda kc=kc: nc.tensor.matmul(
                                        PB[bq][:, :], lhsT=h[:, kc, :], rhs=w_in[:, kc, 512:1024],
                                        start=(kc == 0), stop=(kc == 7)), reads=[r_win, rh], writes=[RB[bq]], inc=(kc == 7))
                            bkk = prj_ring.next(); bk["k"] = bkk
                            for kc in range(8):
                                S.op("pe", lambda kc=kc: nc.tensor.matmul(
                                    PB[bkk][:, :], lhsT=h[:, kc, :], rhs=w_in[:, kc, 1024:1536],
                                    start=(kc == 0), stop=(kc == 7)), reads=[r_win, rh], writes=[RB[bkk]], inc=(kc == 7))
                            bc = prj_ring.next(); bk["c"] = bc
                            for kc in range(8):
                                S.op("pe", lambda kc=kc: nc.tensor.matmul(
                                    PB[bc][:, :], lhsT=h[:, kc, :], rhs=w_in[:, kc, 1536:2048],
                                    start=(kc == 0), stop=(kc == 7)), reads=[r_win, rh], writes=[RB[bc]], inc=(kc == 7))
                            banks_of[i] = bk

                        def s2(i):
                            t = order[i]
                            isctx = t >= 16
                            full = (not isctx) or (not last)
                            bk = banks_of[i]
                            sl = i % NS
                            g0 = l * 256
                            bkk, bc = bk["k"], bk["c"]
                            if full:
                                bu, bv, bq = bk["u"], bk["v"], bk["q"]
                                S.op("act", lambda: nc.scalar.activation(
                                    uT[sl][:].rearrange("p c t -> p (c t)"), PB[bu][:, 0:256], AF.Gelu_apprx_tanh),
                                    reads=[RB[bu]], writes=[r_uT[sl]])
                                S.op("act", lambda: nc.scalar.activation(gv[sl][:], PB[bv][:, 0:256], AF.Gelu_apprx_tanh),
                                     reads=[RB[bv]], writes=[r_gv[sl]])
                                S.op("act", lambda: nc.scalar.activation(sq[sl][:, 0:512], PB[bq][:, :], AF.Square),
                                     reads=[RB[bq]], writes=[r_sq[sl]])
                                S.op("act", lambda: nc.scalar.activation(sq[sl][:, 640:896], PB[bkk][:, 256:512], AF.Square),
                                     reads=[RB[bkk]], writes=[r_sq[sl]])
                            else:
                                S.op("dve", lambda: nc.vector.memset(sq[sl][:, 0:512], 1.0), writes=[r_sq[sl]])
                                S.op("dve", lambda: nc.vector.memset(sq[sl][:, 640:896], 1.0), writes=[r_sq[sl]])
                            S.op("act", lambda: nc.scalar.activation(sq[sl][:, 512:640], PB[bkk][:, 0:128], AF.Square),
                                 reads=[RB[bkk]], writes=[r_sq[sl]])
                            S.op("act", lambda: nc.scalar.activation(sq[sl][:, 896:1152], PB[bc][:, 0:256], AF.Square),
                                 reads=[RB[bc]], writes=[r_sq[sl]])
                            if full:
                                S.op("dve", lambda: nc.vector.tensor_tensor(ta[sl][:], gv[sl][:], gv[sl][:], ALU.mult),
                                     reads=[r_gv[sl]], writes=[r_ta[sl]])
                                S.op("dve", lambda: nc.vector.tensor_reduce(
                                    st2[sl][:, 26:30], ta[sl][:].rearrange("p (g c) -> p g c", g=4), AX.X, ALU.add),
                                    reads=[r_ta[sl]], writes=[r_st2[sl]])
                            S.op("dve", lambda: nc.vector.tensor_reduce(
                                st2[sl][:, 0:10], sq[sl][:, 0:640].rearrange("p (h d) -> p h d", d=64), AX.X, ALU.add),
                                reads=[r_sq[sl]], writes=[r_st2[sl]])
                            S.op("dve", lambda: nc.vector.tensor_reduce(
                                st2[sl][:, 10:26], sq[sl][:, 640:1152].rearrange("p (h d) -> p h d", d=32), AX.X, ALU.add),
                                reads=[r_sq[sl]], writes=[r_st2[sl]])
                            if full:
                                S.op("act", lambda: nc.scalar.activation(st2[sl][:, 26:30], st2[sl][:, 26:30], AF.Sqrt,
                                                                         scale=1.0 / 64, bias=eps_t[:, 0:1]),
                                     reads=[r_st2[sl], r_ident], writes=[r_st2[sl]])
                            S.op("act", lambda: nc.scalar.activation(st2[sl][:, 0:10], st2[sl][:, 0:10], AF.Sqrt,
                                                                     scale=1.0 / 64, bias=eps_t[:, 0:1]),
                                 reads=[r_st2[sl], r_ident], writes=[r_st2[sl]])
                            S.op("act", lambda: nc.scalar.activation(st2[sl][:, 10:26], st2[sl][:, 10:26], AF.Sqrt,
                                                                     scale=1.0 / 32, bias=eps_t[:, 0:1]),
                                 reads=[r_st2[sl], r_ident], writes=[r_st2[sl]])
                            nst = 30 if full else 26
                            S.op("dve", lambda: nc.vector.reciprocal(st2[sl][:, 0:nst], st2[sl][:, 0:nst]),
                                 reads=[r_st2[sl]], writes=[r_st2[sl]])
                            bm = None
                            if full:
                                for par in range(2):
                                    S.op("dve", lambda par=par: nc.vector.tensor_tensor(
                                        vpad[sl][:, par:4:2, par * 64:par * 64 + 64],
                                        gv[sl][:].rearrange("p (g c) -> p g c", g=4)[:, par:4:2, :],
                                        st2[sl][:, 26 + par:30:2].unsqueeze(2).to_broadcast([128, 2, 64]), ALU.mult),
                                        reads=[r_gv[sl], r_st2[sl]], writes=[r_vpad[sl]])
                                bm = prj_ring.next()
                                for g in range(4):
                                    S.op("pe", lambda g=g: nc.tensor.matmul(
                                        PB[bm][:, (g // 2) * 128:(g // 2) * 128 + 128], lhsT=vpad[sl][:, g, :], rhs=awsT[:, g, :],
                                        start=(g % 2 == 0), stop=(g % 2 == 1)),
                                        reads=[r_vpad[sl], r_aws], writes=[RB[bm]], inc=(g == 3))
                            specs = []
                            if full:
                                specs.append(("bq", PB[bk["q"]][:, :], RB[bk["q"]], 0, 512, 8, 64, 0, 0, ropeB_C, ropeB_S))
                            specs.append(("bk", PB[bkk][:, 0:128], RB[bkk], 512, 128, 2, 64, 8, 64, ropeB_C, ropeB_S))
                            if full:
                                specs.append(("cq", PB[bkk][:, 256:512], RB[bkk], 640, 256, 8, 32, 10, 128, ropeC_C, ropeC_S))
                            specs.append(("ck", PB[bc][:, 0:256], RB[bc], 896, 256, 8, 32, 18, 160, ropeC_C, ropeC_S))
                            for (nm, src, rsrc, c0, wd, nh, hd, sc0, gc0, rC, rS) in specs:
                                v3 = lambda ap, nh=nh: ap.rearrange("p (h d) -> p h d", h=nh)
                                qv = qn[sl][:, c0:c0 + wd]
                                S.op("dve", lambda src=src, qv=qv, v3=v3, sc0=sc0, nh=nh, hd=hd: nc.vector.tensor_tensor(
                                    v3(qv), v3(src), st2[sl][:, sc0:sc0 + nh].unsqueeze(2).to_broadcast([128, nh, hd]), ALU.mult),
                                    reads=[rsrc, r_st2[sl]], writes=[r_qn[sl]])
                                gain_b = smallg[:, l, gc0:gc0 + hd].unsqueeze(1).to_broadcast([128, nh, hd])
                                if isctx:
                                    if nm == "bq":
                                        outv = qr[sl][:, 0:512].rearrange("p (c hi d) -> p hi c d", c=4, hi=2)
                                        inv = qv.rearrange("p (hi c d) -> p hi c d", hi=2, c=4)
                                        gb = smallg[:, l, gc0:gc0 + hd].unsqueeze(1).unsqueeze(1).to_broadcast([128, 2, 4, hd])
                                        S.op("dve", lambda outv=outv, inv=inv, gb=gb: nc.vector.tensor_tensor(outv, inv, gb, ALU.mult),
                                             reads=[r_qn[sl], r_const], writes=[r_qr[sl]])
                                    else:
                                        S.op("dve", lambda qv=qv, v3=v3, gain_b=gain_b, c0=c0, wd=wd: nc.vector.tensor_tensor(
                                            v3(qr[sl][:, c0:c0 + wd]), v3(qv), gain_b, ALU.mult),
                                            reads=[r_qn[sl], r_const], writes=[r_qr[sl]])
                                    continue
                                S.op("dve", lambda qv=qv, v3=v3, gain_b=gain_b: nc.vector.tensor_tensor(v3(qv), v3(qv), gain_b, ALU.mult),
                                     reads=[r_qn[sl], r_const], writes=[r_qn[sl]])
                                nf = hd // 4
                                v4 = lambda ap, nf=nf: ap.rearrange("p (ha pr f) -> p ha pr f", pr=2, f=nf)
                                tbv = tb[sl][:, c0:c0 + wd]
                                Sn = rS[:, t, 0, :].rearrange("p (a f) -> p a f", a=2).unsqueeze(1).to_broadcast([128, nh, 2, nf])
                                Sp = rS[:, t, 1, :].rearrange("p (a f) -> p a f", a=2).unsqueeze(1).to_broadcast([128, nh, 2, nf])
                                v5 = lambda ap, nh=nh, nf=nf: ap.rearrange("p (h a pr f) -> p h a pr f", h=nh, a=2, pr=2)
                                S.op("dve", lambda tbv=tbv, qv=qv, v5=v5, Sn=Sn: nc.vector.tensor_tensor(
                                    v5(tbv)[:, :, :, 0, :], v5(qv)[:, :, :, 1, :], Sn, ALU.mult),
                                    reads=[r_qn[sl], r_const], writes=[r_tb[sl]])
                                S.op("dve", lambda tbv=tbv, qv=qv, v5=v5, Sp=Sp: nc.vector.tensor_tensor(
                                    v5(tbv)[:, :, :, 1, :], v5(qv)[:, :, :, 0, :], Sp, ALU.mult),
                                    reads=[r_qn[sl], r_const], writes=[r_tb[sl]])
                                Cb = rC[:, t, :].unsqueeze(1).to_broadcast([128, nh, hd])
                                S.op("dve", lambda qv=qv, v3=v3, Cb=Cb: nc.vector.tensor_tensor(v3(qv), v3(qv), Cb, ALU.mult),
                                     reads=[r_qn[sl], r_const], writes=[r_qn[sl]])
                                if nm == "bq":
                                    outv = qr[sl][:, 0:512].rearrange("p (c hi d) -> p hi c d", c=4, hi=2)
                                    a0 = qv.rearrange("p (hi c d) -> p hi c d", hi=2, c=4)
                                    a1 = tbv.rearrange("p (hi c d) -> p hi c d", hi=2, c=4)
                                    S.op("dve", lambda outv=outv, a0=a0, a1=a1: nc.vector.tensor_tensor(outv, a0, a1, ALU.add),
                                         reads=[r_qn[sl], r_tb[sl]], writes=[r_qr[sl]])
                                else:
                                    S.op("dve", lambda qv=qv, tbv=tbv, c0=c0, wd=wd: nc.vector.tensor_tensor(
                                        qr[sl][:, c0:c0 + wd], qv, tbv, ALU.add),
                                        reads=[r_qn[sl], r_tb[sl]], writes=[r_qr[sl]])
                            if full:
                                S.op("dve", lambda: nc.vector.tensor_tensor(
                                    ta[sl][:], PB[bm][:, 0:256], biasT[:, l].rearrange("p c t -> p (c t)"), ALU.add),
                                    reads=[RB[bm], r_const], writes=[r_ta[sl]])
                                S.op("dve", lambda: nc.vector.tensor_tensor(
                                    catA[:, 0:2, t * 128:(t + 1) * 128], ta[sl][:].rearrange("p (c t) -> p c t", c=2),
                                    uT[sl][:], ALU.mult), reads=[r_ta[sl], r_uT[sl]], writes=[r_catA[t]])
                            S.op("act", lambda: nc.scalar.copy(vb[:, t, :, 0:64], PB[bkk][:, 128:256].rearrange("p (h d) -> p h d", h=2)),
                                 reads=[RB[bkk]], writes=[r_vb[t]])
                            S.op("act", lambda: nc.scalar.copy(vc[:, t, :, 0:64], PB[bc][:, 256:512].rearrange("p (h d) -> p h d", h=4)),
                                 reads=[RB[bc]], writes=[r_vc[t]])
                            bt = trp_ring.next()
                            pv = bank_bf(bt)
                            lst = []
                            if full:
                                for c in range(4):
                                    lst.append((c, qr[sl][:, c * 128:(c + 1) * 128]))
                            lst.append((4, qr[sl][:, 512:640]))
                            for c, src in lst:
                                S.op("pe", lambda c=c, src=src: nc.tensor.transpose(pv[:, c * 128:(c + 1) * 128], src, ident_b[:]),
                                     reads=[r_qr[sl], r_ident], writes=[RB[bt]], inc=(c == 4))
                            if full:
                                evac(qbT[:, :, t * 128:(t + 1) * 128], pv[:, 0:512].rearrange("p (c t) -> p c t", c=4),
                                     [RB[bt]], [r_qbT[t]])
                            evac(kbT[:, t * 128:(t + 1) * 128], pv[:, 512:640], [RB[bt]], [r_kbT[t]])
                            bt2 = trp_ring.next()
                            pv2 = bank_bf(bt2)
                            lst = []
                            if full:
                                lst += [(0, qr[sl][:, 640:768]), (1, qr[sl][:, 768:896])]
                            lst += [(2, qr[sl][:, 896:1024]), (3, qr[sl][:, 1024:1152])]
                            for c, src in lst:
                                S.op("pe", lambda c=c, src=src: nc.tensor.transpose(pv2[:, c * 128:(c + 1) * 128], src, ident_b[:]),
                                     reads=[r_qr[sl], r_ident], writes=[RB[bt2]], inc=(c == 3))
                            if full:
                                evac(qcT[:, :, t * 128:(t + 1) * 128], pv2[:, 0:256].rearrange("p (c t) -> p c t", c=2),
                                     [RB[bt2]], [r_qcT[t]])
                            evac(kcT[:, :, t * 128:(t + 1) * 128], pv2[:, 256:512].rearrange("p (c t) -> p c t", c=2),
                                 [RB[bt2]], [r_kcT[t]])

                        import os as _os
                        _ntl = int(_os.environ.get("P1_TILES", NT))
                        _nst = int(_os.environ.get("P1_STAGES", 3))
                        pipeline(_ntl, [(s2, 2), (s1f, 1), (s0, 0)][3 - _nst:])
                        if b == 0 and l == 0:
                            print("sbuf remaining p1", nc.sbuf_bytes_remaining)
                        S.barrier()
                    if stop == "p1x":
                        print("NOPS", getattr(S, "nops", 0))
                        dump("hT0", hT[0][:], r_hT[0])
                        dbg_wait()
                        return nc
                    if stop in ("p1", "p1dbg") and dbg is not None:
                        dump("kbT", kbT[:], r_kbT[0]); dump("kcT", kcT[:], r_kcT[0]); dump("qbT", qbT[:, :, 0:ntq * 128], r_qbT[0])
                        dump("qcT", qcT[:, :, 0:ntq * 128], r_qcT[0]); dump("vb", vb[:], r_vb[0]); dump("vc", vc[:], r_vc[0])
                        dump("catA", catA[:, :, 0:ntq * 128], r_catA[0])
                        dbg_wait()
                        return nc

                    with ExitStack() as p2:
                        catBC = T("catBC", [128, 6, NTOK], BF16, p2)
                        w_out = T("w_out", [128, 8, D], BF16, p2); r_wout = Res("w_out")
                        for kc in range(8):
                            S.dma("pool", g_w[2], w_out[:, kc, :], wout_d[l, kc * 128:(kc + 1) * 128, :], writes=[r_wout])
                        NP = 10
                        pT = [T(f"pT{i}", [128, 512], BF16, p2) for i in range(NP)]; r_pT = [Res(f"pT{i}") for i in range(NP)]
                        pT_ring = Ring(range(NP))
                        zst = [T(f"zst{i}", [128, 512], BF16, p2) for i in range(2)]; r_zst = [Res(f"zst{i}") for i in range(2)]
                        zst_ring = Ring(range(2))
                        rr = [T("rr0", [128, 1024], F32, p2)] * 2; r_rr = [Res("rr0")] * 2
                        Rsb = [T("Rsb0", [128, 1024], F32, p2)] * 2; r_Rsb = [Res("Rsb0")] * 2
                        yy = [T(f"yy{i}", [128, 512], F32, p2) for i in range(2)]; r_yy = [Res(f"yy{i}") for i in range(2)]
                        y1 = [T(f"y1{i}", [128, 512], F32, p2) for i in range(2)]; r_y1 = [Res(f"y1{i}") for i in range(2)]
                        ysq = [T(f"ysq{i}", [128, 512], F32, p2) for i in range(2)]; r_ysq = [Res(f"ysq{i}") for i in range(2)]
                        rsd = [T(f"rsd{i}", [128, 512], F32, p2) for i in range(2)]; r_rsd = [Res(f"rsd{i}") for i in range(2)]

                        s_ring = Ring([0, 1, 2, 3])
                        acc_ring = Ring([4, 5])
                        R_ring = Ring([6, 7])
                        units = [(kvh, n) for n in range(ntq) for kvh in range(2)]
                        bstate = {}
                        bacc = {}

                        def b_scores(ui):
                            kvh, n = units[ui]
                            if n < 16:
                                kts = []
                                if n > 0:
                                    kts.append((n - 1, 0))
                                kts.append((n, None))
                                if n < 15:
                                    kts.append((n + 1, 1))
                                kts += [(16, None), (17, None)]
                            else:
                                kts = [(16, None), (17, None)]
                            pb0 = kvh * 64
                            plist = []
                            for (kt, mk) in kts:
                                sb = s_ring.next()
                                S.op("pe", lambda sb=sb, kt=kt: nc.tensor.matmul(
                                    PB[sb][:, :], lhsT=kbT[pb0:pb0 + 64, kt * 128:(kt + 1) * 128],
                                    rhs=qbT[pb0:pb0 + 64, :, n * 128:(n + 1) * 128], start=True, stop=True),
                                    reads=[r_kbT[kt], r_qbT[n]], writes=[RB[sb]])
                                pi = pT_ring.next()
                                S.op("act", lambda sb=sb, pi=pi: nc.scalar.activation(pT[pi][:], PB[sb][:, :], AF.Exp, scale=0.125),
                                     reads=[RB[sb]], writes=[r_pT[pi]])
                                if mk is not None:
                                    S.op("dve", lambda pi=pi, mk=mk: nc.vector.tensor_tensor(
                                        pT[pi][:].rearrange("p (g q) -> p g q", g=4), pT[pi][:].rearrange("p (g q) -> p g q", g=4),
                                        mask_b[:, mk, :].unsqueeze(1).to_broadcast([128, 4, 128]), ALU.mult),
                                        reads=[r_pT[pi], r_ident], writes=[r_pT[pi]])
                                plist.append((kt, pi))
                            bstate[ui] = plist

                        def b_pv(ui):
                            kvh, n = units[ui]
                            plist = bstate.pop(ui)
                            ab = acc_ring.next()
                            nk = len(plist)
                            for j, (kt, pi) in enumerate(plist):
                                S.op("pe", lambda j=j, kt=kt, pi=pi: nc.tensor.matmul(
                                    PB[ab][0:65, :], lhsT=vb[:, kt, kvh, :], rhs=pT[pi][:], start=(j == 0), stop=(j == nk - 1)),
                                    reads=[r_vb[kt], r_vones, r_pT[pi]], writes=[RB[ab]], inc=(j == nk - 1))
                            bacc[ui] = ab

                        def b_post(ui):
                            kvh, n = units[ui]
                            ab = bacc.pop(ui)
                            ri = ui % 2
                            hd0 = kvh * 4
                            S.op("dve", lambda: nc.vector.tensor_tensor(
                                rr[ri][64:65, 0:512].rearrange("p (g q) -> p g q", g=4),
                                PB[ab][64:65, :].rearrange("p (g q) -> p g q", g=4),
                                esink[64:65, l * 8 + hd0:l * 8 + hd0 + 4].unsqueeze(2).to_broadcast([1, 4, 128]),
                                ALU.add), reads=[RB[ab], r_const], writes=[r_rr[ri]])
                            S.op("dve", lambda: nc.vector.reciprocal(rr[ri][64:65, 0:512], rr[ri][64:65, 0:512]),
                                 reads=[r_rr[ri]], writes=[r_rr[ri]])
                            rb = R_ring.next()
                            S.op("pe", lambda: nc.tensor.matmul(PB[rb][0:64, :], lhsT=ones_f[64:65, 0:64], rhs=rr[ri][64:65, 0:512],
                                                                start=True, stop=True),
                                 reads=[r_rr[ri], r_ident], writes=[RB[rb]])
                            S.op("act", lambda: nc.scalar.copy(Rsb[ri][0:64, 0:512], PB[rb][0:64, :]), reads=[RB[rb]], writes=[r_Rsb[ri]])
                            c0 = kvh * 2
                            v4 = lambda ap: ap.rearrange("p (g q) -> p g q", g=4)
                            S.op("dve", lambda: nc.vector.tensor_tensor(
                                catBC[0:64, c0:c0 + 2, n * 128:(n + 1) * 128], v4(PB[ab][0:64, :])[:, 0:4:2, :],
                                v4(Rsb[ri][0:64, 0:512])[:, 0:4:2, :], ALU.mult),
                                reads=[RB[ab], r_Rsb[ri]], writes=[r_catB[n]])
                            zi = zst_ring.next()
                            S.op("dve", lambda: nc.vector.tensor_tensor(
                                zst[zi][0:64, 0:256].rearrange("p (g q) -> p g q", g=2), v4(PB[ab][0:64, :])[:, 1:4:2, :],
                                v4(Rsb[ri][0:64, 0:512])[:, 1:4:2, :], ALU.mult),
                                reads=[RB[ab], r_Rsb[ri]], writes=[r_zst[zi]])
                            S.dma("sp", g_zst[zi], catBC[64:128, c0:c0 + 2, n * 128:(n + 1) * 128],
                                  zst[zi][0:64, 0:256].rearrange("p (g q) -> p g q", g=2), reads=[r_zst[zi]], writes=[r_catB[n]])

                        if b == 0 and l == 0:
                            print("sbuf remaining p2", nc.sbuf_bytes_remaining)
                        pipeline(len(units), [(b_post, 2), (b_scores, 0), (b_pv, 1)])

                        qgroups = [(g * 512, 512, list(range(NT)), list(range(4 * g, 4 * g + 4))) for g in range(4)]
                        if not last:
                            qgroups.append((2048, 256, [16, 17], [16, 17]))
                        s_ring = Ring([0, 1, 2, 3])
                        scl = 32 ** -0.5
                        def c_post(ui, h, q0, W, qtiles, accb, mb):
                            hc = h // 2
                            ri = ui % 2
                            po = 0
                            pr_ = 64
                            for m in range(2):
                                S.op("dve", lambda m=m: nc.vector.reciprocal(rr[ri][pr_:pr_ + 1, m * 512:m * 512 + W],
                                                                              PB[accb[m]][pr_:pr_ + 1, 0:W]),
                                     reads=[RB[accb[m]]], writes=[r_rr[ri]])
                            S.op("dve", lambda: nc.vector.tensor_scalar(rr[ri][pr_:pr_ + 1, 512:512 + W], rr[ri][pr_:pr_ + 1, 512:512 + W],
                                                                         neglam[pr_:pr_ + 1, l:l + 1], None, ALU.mult),
                                 reads=[r_rr[ri], r_const], writes=[r_rr[ri]])
                            for m in range(2):
                                S.op("pe", lambda m=m: nc.tensor.matmul(PB[mb[m]][0:64, 0:W], lhsT=ones_f[pr_:pr_ + 1, 0:64],
                                                                        rhs=rr[ri][pr_:pr_ + 1, m * 512:m * 512 + W], start=True, stop=True),
                                     reads=[r_rr[ri], r_ident], writes=[RB[mb[m]]])
                                S.op("act", lambda m=m: nc.scalar.copy(Rsb[ri][po:po + 64, m * 512:m * 512 + W], PB[mb[m]][po:po + 64, 0:W]),
                                     reads=[RB[mb[m]]], writes=[r_Rsb[ri]])
                            S.op("dve", lambda: nc.vector.tensor_tensor(yy[ri][po:po + 64, 0:W], PB[accb[0]][po:po + 64, 0:W],
                                                                         Rsb[ri][po:po + 64, 0:W], ALU.mult),
                                 reads=[RB[accb[0]], r_Rsb[ri]], writes=[r_yy[ri]])
                            S.op("dve", lambda: nc.vector.tensor_tensor(y1[ri][po:po + 64, 0:W], PB[accb[1]][po:po + 64, 0:W],
                                                                         Rsb[ri][po:po + 64, 512:512 + W], ALU.mult),
                                 reads=[RB[accb[1]], r_Rsb[ri]], writes=[r_y1[ri]])
                            S.op("dve", lambda: nc.vector.tensor_tensor(yy[ri][po:po + 64, 0:W], yy[ri][po:po + 64, 0:W],
                                                                         y1[ri][po:po + 64, 0:W], ALU.add),
                                 reads=[r_yy[ri], r_y1[ri]], writes=[r_yy[ri]])
                            S.op("act", lambda: nc.scalar.activation(ysq[ri][po:po + 64, 0:W], yy[ri][po:po + 64, 0:W], AF.Square),
                                 reads=[r_yy[ri]], writes=[r_ysq[ri]])
                            S.op("pe", lambda: nc.tensor.matmul(PB[mb[2]][0:64, 0:W], lhsT=onesmean[po:po + 64, 0:64], rhs=ysq[ri][po:po + 64, 0:W],
                                                                start=True, stop=True),
                                 reads=[r_ysq[ri], r_ident], writes=[RB[mb[2]]])
                            S.op("act", lambda: nc.scalar.activation(rsd[ri][po:po + 64, 0:W], PB[mb[2]][po:po + 64, 0:W], AF.Sqrt,
                                                                     bias=eps_t[po:po + 64, 0:1]),
                                 reads=[RB[mb[2]], r_ident], writes=[r_rsd[ri]])
                            S.op("dve", lambda: nc.vector.reciprocal(rsd[ri][po:po + 64, 0:W], rsd[ri][po:po + 64, 0:W]),
                                 reads=[r_rsd[ri]], writes=[r_rsd[ri]])
                            if h % 2 == 0:
                                S.op("dve", lambda: nc.vector.scalar_tensor_tensor(
                                    catBC[0:64, 4 + hc, q0:q0 + W], yy[ri][0:64, 0:W], gsub[0:64, l:l + 1],
                                    rsd[ri][0:64, 0:W], ALU.mult, ALU.mult),
                                    reads=[r_yy[ri], r_rsd[ri], r_const], writes=[r_catC[qt] for qt in qtiles])
                            else:
                                zi = zst_ring.next()
                                S.op("dve", lambda: nc.vector.scalar_tensor_tensor(
                                    zst[zi][0:64, 0:W], yy[ri][0:64, 0:W], gsub[0:64, l:l + 1],
                                    rsd[ri][0:64, 0:W], ALU.mult, ALU.mult),
                                    reads=[r_yy[ri], r_rsd[ri], r_const], writes=[r_zst[zi]])
                                S.dma("sp", g_zst[zi], catBC[64:128, 4 + hc, q0:q0 + W], zst[zi][0:64, 0:W],
                                      reads=[r_zst[zi]], writes=[r_catC[qt] for qt in qtiles])

                        cunits = [(hc, qg) for qg in qgroups for hc in range(2)]
                        for ui, (hc, (q0, W, kts, qtiles)) in enumerate(cunits):
                            nk = len(kts)
                            sbs = {}
                            combos = [(2 * hc + hh, m) for hh in range(2) for m in range(2)]
                            accs = {cm: 4 + k for k, cm in enumerate(combos)}

                            def c_score(ki, q0=q0, W=W, kts=kts, qtiles=qtiles, hc=hc, sbs=sbs, combos=combos):
                                kt = kts[ki]
                                sbl = []
                                for (h, m) in combos:
                                    pb = 32 * (2 * (h % 2) + m)
                                    sb = s_ring.next()
                                    kw = dict(tile_position=(96, 0)) if pb == 96 else {}
                                    S.op("pe", lambda sb=sb, pb=pb, kw=kw: nc.tensor.matmul(
                                        PB[sb][:, 0:W], lhsT=kcT[pb:pb + 32, hc, kt * 128:(kt + 1) * 128],
                                        rhs=qcT[pb:pb + 32, hc, q0:q0 + W], start=True, stop=True, **kw),
                                        reads=[r_kcT[kt]] + [r_qcT[qt] for qt in qtiles], writes=[RB[sb]])
                                    sbl.append(sb)
                                for (h, m), sb in zip(combos, sbl):
                                    pi = pT_ring.next()
                                    S.op("act", lambda sb=sb, pi=pi: nc.scalar.activation(pT[pi][:, 0:W], PB[sb][:, 0:W], AF.Exp, scale=scl),
                                         reads=[RB[sb]], writes=[r_pT[pi]])
                                    sbs[(ki, h, m)] = pi

                            def c_pv(ki, W=W, kts=kts, nk=nk, sbs=sbs, combos=combos, accs=accs):
                                kt = kts[ki]
                                for (h, m) in combos:
                                    pi = sbs.pop((ki, h, m))
                                    ab = accs[(h, m)]
                                    S.op("pe", lambda h=h, pi=pi, ab=ab: nc.tensor.matmul(
                                        PB[ab][0:65, 0:W], lhsT=vc[:, kt, h, :], rhs=pT[pi][:, 0:W], start=(ki == 0), stop=(ki == nk - 1)),
                                        reads=[r_vc[kt], r_vones, r_pT[pi]], writes=[RB[ab]], inc=(ki == nk - 1))

                            pipeline(nk, [(c_score, 0), (c_pv, 1)])
                            for hh in range(2):
                                h = 2 * hc + hh
                                c_post(2 * ui + hh, h, q0, W, qtiles, (accs[(h, 0)], accs[(h, 1)]), (0, 1, 2) if hh == 0 else (3, 0, 1))

                        if stop == "p2" and dbg is not None:
                            dump("catA", catA[:, :, 0:ntq * 128], r_catA[0])
                            for t in range(ntq):
                                S._emit_waits("sp", S._deps("sp", [r_catB[t], r_catC[t], r_catA[t]], []))
                            dump("catBC", catBC[:, :, 0:ntq * 128], r_catA[0])
                            for t in range(0):
                                S._emit_waits("sp", S._deps("sp", [r_catB[t], r_catC[t], r_catA[t]], []))
                            dbg_wait()
                            return nc

                        load_modb(0, b, l, 2)
                        load_modb(2, 2, l, 2)
                        pair_ring = Ring([(0, 1), (2, 3), (4, 5), (6, 7)])
                        pstate = {}

                        def o_mm(t):
                            b0, b1 = pair_ring.next()
                            for nh, bi in enumerate((b0, b1)):
                                for kc in range(8):
                                    S.op("pe", lambda nh=nh, bi=bi, kc=kc: nc.tensor.matmul(
                                        PB[bi][:, :], lhsT=(catA[:, kc, t * 128:(t + 1) * 128] if kc < 2 else catBC[:, kc - 2, t * 128:(t + 1) * 128]),
                                        rhs=w_out[:, kc, nh * 512:(nh + 1) * 512],
                                        start=(kc == 0), stop=(kc == 7)),
                                        reads=[r_catA[t], r_catB[t], r_catC[t], r_wout], writes=[RB[bi]], inc=(kc == 7))
                            pstate[t] = (b0, b1)

                        def o_res(t):
                            b0, b1 = pstate.pop(t)
                            xi = load_x(b, lsrc, t)
                            xo = xout_ring.next()
                            gi = 0 if t < 16 else 2
                            for nh, bi in enumerate((b0, b1)):
                                S.op("dve", lambda nh=nh, bi=bi: nc.vector.tensor_tensor(
                                    xout[xo][:, nh * 512:(nh + 1) * 512], PB[bi][:, :], modb[gi][:, nh * 512:(nh + 1) * 512], ALU.mult),
                                    reads=[RB[bi], r_modb[gi]], writes=[r_xout[xo]])
                            S.op("dve", lambda: nc.vector.tensor_tensor(xout[xo][:], xout[xo][:], xin[xi][:], ALU.add),
                                 reads=[r_xout[xo], r_xin[xi]], writes=[r_xout[xo]])
                            S.dma("sp", g_xout[xo], x_dst(b, t), xout[xo][:], reads=[r_xout[xo]], writes=[r_dx[b][t]])

                        pipeline(ntq, [(o_mm, 0), (o_res, 1)])
                        S.barrier()
                if stop == "s1":
                    break

                load_modb(0, b, l, 4)
                load_modb(1, b, l, 3)
                load_modb(2, 2, l, 4)
                load_modb(3, 2, l, 3)
                ncols = ntq * 128
                with ExitStack() as s2c:
                    h2T = T("h2T", [128, 8, NTOK], BF16, s2c); r_h2T = [Res(f"h2T{t}") for t in range(NT)]
                    hid = T("hid", [128, 8, NTOK], BF16, s2c); r_hid = [Res(f"hid{j}") for j in range(8)]
                    wdn = T("wdn", [128, 8, D], BF16, s2c); r_wdn = Res("wdn")
                    wup = [T(f"wup{i}", [128, 8, 256], BF16, s2c) for i in range(3)]; r_wup = [Res(f"wup{i}") for i in range(3)]
                    tbuf = [T(f"tbuf{i}", [128, NTOK], F32, s2c) for i in range(3)]; r_tbuf = [[Res(f"tbuf{i}_{k}") for k in range(5)] for i in range(3)]
                    tb_ring = Ring(range(3))
                    hb2 = [T(f"hb2_{i}", [128, D], BF16, s2c) for i in range(2)]; r_hb2 = [Res(f"hb2_{i}") for i in range(2)]
                    t12 = [T("t12_0", [128, D], F32, s2c)] * 2; r_t12 = [Res("t12_0")] * 2
                    trp_ring = Ring([0, 1, 2, 3])
                    if b == 0 and l == 0:
                        print("sbuf remaining ffn", nc.sbuf_bytes_remaining)
                    def f0a(t):
                        isctx = t >= 16
                        xi = load_x(b, l + 1, t)
                        sl = t % 2
                        norm_tile(xi, 2 if isctx else 0, 3 if isctx else 1, hb2[sl][:], r_hb2[sl], t12[sl][:], r_t12[sl], (t % 4) * 2)

                    def f0b(t):
                        sl = t % 2
                        transpose8(hb2[sl], r_hb2[sl], trp_ring.next(), h2T[:, :, t * 128:(t + 1) * 128], r_h2T[t])

                    pipeline(ntq, [(f0b, 1), (f0a, 0)])
                    blocks = [(g * 512, 512, list(range(4 * g, 4 * g + 4)), g > 0) for g in range(4)]
                    if not last:
                        blocks.append((2048, 256, [16, 17], False))
                    wk = 0
                    load_modb(0, b, l, 5)
                    load_modb(2, 2, l, 5)
                    for (j0, npart) in ((0, 8), (8, 7), (15, 7)):
                        for part in range(npart):
                            jj = j0 + part
                            S.dma("pool", g_wdn, wdn[:, part, :], wdn_d[l, jj * 128:(jj + 1) * 128, :],
                                  reads=[], writes=[r_wdn])
                        up_ring = Ring([0, 1, 2, 3, 4, 5, 6, 7])
                        pend_hid = []
                        for j in range(npart):
                            jj = j0 + j
                            wi = wk % 3
                            wk += 1
                            S.dma("pool", g_w[3 + wi], wup[wi][:], wup_d[l, jj], writes=[r_wup[wi]])
                            tbs = []
                            for row in range(2):
                                ch = jj + row * NCH
                                ti_ = tb_ring.next()
                                tbs.append(ti_)
                                tbv = tbuf[ti_]
                                prev = None
                                for bk_i, (c0, W, tiles, cont) in enumerate(blocks):
                                    rtb = r_tbuf[ti_][bk_i]
                                    rtb_prev = r_tbuf[ti_][bk_i - 1] if bk_i > 0 else None
                                    bi = up_ring.next()
                                    for kc in range(8):
                                        S.op("pe", lambda kc=kc, bi=bi, c0=c0, W=W, row=row: nc.tensor.matmul(
                                            PB[bi][:, 0:W], lhsT=wup[wi][:, kc, row * 128:(row + 1) * 128], rhs=h2T[:, kc, c0:c0 + W],
                                            start=(kc == 0), stop=(kc == 7)),
                                            reads=[r_wup[wi]] + [r_h2T[t] for t in tiles], writes=[RB[bi]], inc=(kc == 7))
                                    S.op("act", lambda bi=bi, c0=c0, W=W, ch=ch, tbv=tbv: nc.scalar.activation(
                                        tbv[:, c0:c0 + W], PB[bi][:, 0:W], AF.Identity, scale=cw[:, l, 1, ch:ch + 1], bias=cb[:, l, ch:ch + 1]),
                                        reads=[RB[bi], r_const], writes=[rtb])
                                    S.op("dve", lambda bi=bi, c0=c0, W=W, ch=ch, tbv=tbv: nc.vector.scalar_tensor_tensor(
                                        tbv[:, c0 + 1:c0 + W], PB[bi][:, 0:W - 1], cw[:, l, 0, ch:ch + 1], tbv[:, c0 + 1:c0 + W],
                                        ALU.mult, ALU.add), reads=[RB[bi], r_const, rtb], writes=[rtb])
                                    S.op("dve", lambda bi=bi, c0=c0, W=W, ch=ch, tbv=tbv: nc.vector.scalar_tensor_tensor(
                                        tbv[:, c0:c0 + W - 1], PB[bi][:, 1:W], cw[:, l, 2, ch:ch + 1], tbv[:, c0:c0 + W - 1],
                                        ALU.mult, ALU.add), reads=[RB[bi], r_const, rtb], writes=[rtb])
                                    if cont:
                                        pbi, pW = prev
                                        S.op("dve", lambda bi=bi, c0=c0, ch=ch, tbv=tbv, pbi=pbi, pW=pW: nc.vector.scalar_tensor_tensor(
                                            tbv[:, c0:c0 + 1], PB[pbi][:, pW - 1:pW], cw[:, l, 0, ch:ch + 1], tbv[:, c0:c0 + 1],
                                            ALU.mult, ALU.add), reads=[RB[pbi], r_const, rtb], writes=[rtb])
                                        S.op("dve", lambda bi=bi, c0=c0, ch=ch, tbv=tbv: nc.vector.scalar_tensor_tensor(
                                            tbv[:, c0 - 1:c0], PB[bi][:, 0:1], cw[:, l, 2, ch:ch + 1], tbv[:, c0 - 1:c0],
                                            ALU.mult, ALU.add), reads=[RB[bi], r_const, rtb_prev], writes=[rtb_prev])
                                    prev = (bi, W)
                                    if row == 0 and c0 == 512 and pend_hid:
                                        pend_hid.pop()()
                            tg, tv = tbs

                            def do_hid(tg=tg, tv=tv, j=j):
                                S.op("act", lambda: nc.scalar.activation(tbuf[tg][:, 0:ncols], tbuf[tg][:, 0:ncols], AF.Silu),
                                     reads=r_tbuf[tg], writes=r_tbuf[tg])
                                S.op("dve", lambda: nc.vector.tensor_tensor(
                                    hid[:, j, 0:ncols], tbuf[tg][:, 0:ncols], tbuf[tv][:, 0:ncols], ALU.mult),
                                    reads=r_tbuf[tg] + r_tbuf[tv], writes=[r_hid[j]])
                            pend_hid.append(do_hid)
                        while pend_hid:
                            pend_hid.pop()()
                        pair_ring = Ring([(0, 1), (2, 3), (4, 5), (6, 7)])
                        pstate = {}

                        def d_mm(t):
                            b0, b1 = pair_ring.next()
                            for nh, bi in enumerate((b0, b1)):
                                for j in range(npart):
                                    S.op("pe", lambda nh=nh, bi=bi, j=j: nc.tensor.matmul(
                                        PB[bi][:, :], lhsT=hid[:, j, t * 128:(t + 1) * 128], rhs=wdn[:, j, nh * 512:(nh + 1) * 512],
                                        start=(j == 0), stop=(j == npart - 1)),
                                        reads=[r_hid[j], r_wdn], writes=[RB[bi]], inc=(j == npart - 1))
                            pstate[t] = (b0, b1)

                        def d_res(t):
                            b0, b1 = pstate.pop(t)
                            xi = load_x(b, l + 1, t)
                            xo = xout_ring.next()
                            gi = 0 if t < 16 else 2
                            for nh, bi in enumerate((b0, b1)):
                                S.op("dve", lambda nh=nh, bi=bi: nc.vector.tensor_tensor(
                                    xout[xo][:, nh * 512:(nh + 1) * 512], PB[bi][:, :], modb[gi][:, nh * 512:(nh + 1) * 512], ALU.mult),
                                    reads=[RB[bi], r_modb[gi]], writes=[r_xout[xo]])
                            S.op("dve", lambda: nc.vector.tensor_tensor(xout[xo][:], xout[xo][:], xin[xi][:], ALU.add),
                                 reads=[r_xout[xo], r_xin[xi]], writes=[r_xout[xo]])
                            S.dma("sp", g_xout[xo], x_dst(b, t), xout[xo][:], reads=[r_xout[xo]], writes=[r_dx[b][t]])

                        pipeline(ntq, [(d_mm, 0), (d_res, 1)])
                    S.barrier()
        S.finish()
        build_nc.stats = (S.ninst, S.nwait)
    return nc


def _rope_tables(head_dim):
    rows = SEQ // GRID_W
    row = np.repeat(np.arange(rows, dtype=np.float32), GRID_W)
    col = np.tile(np.arange(GRID_W, dtype=np.float32), rows)
    n_freq = head_dim // 4
    inv_freq = (np.float32(10000.0) ** (-np.arange(n_freq, dtype=np.float32) / np.float32(n_freq))).astype(np.float32)
    ang = np.stack([row[:, None] * inv_freq, col[:, None] * inv_freq], axis=1).astype(np.float32)
    cos = np.cos(ang).astype(np.float32)
    sin = np.sin(ang).astype(np.float32)
    C2 = np.stack([cos, cos], axis=2).reshape(SEQ, 4 * n_freq)
    S2 = np.stack([-sin.reshape(SEQ, 2 * n_freq), sin.reshape(SEQ, 2 * n_freq)], axis=1)
    C2 = C2.reshape(16, 128, 4 * n_freq).transpose(1, 0, 2)
    S2 = S2.reshape(16, 128, 2, 2 * n_freq).transpose(1, 0, 2, 3)
    return np.ascontiguousarray(C2), np.ascontiguousarray(S2)


def prep_shared(inp):
    f = lambda a: np.ascontiguousarray(np.asarray(a, dtype=np.float32))
    w_up = f(inp["w_up"])
    wu = w_up.reshape(L_ALL, 8, 128, 2, NCH, 128)
    w_up_r = np.ascontiguousarray(wu.transpose(0, 4, 2, 1, 3, 5)).reshape(L_ALL, NCH, 128, 8, 256)
    a_wsT = np.ascontiguousarray(f(inp["a_ws"]).transpose(0, 3, 1, 2))
    smallg = np.concatenate([f(inp["b_qnorm"]), f(inp["b_knorm"]), f(inp["c_qnorm"]), f(inp["c_knorm"]),
                             f(inp["c_subln"])], axis=1)
    cwr = f(inp["conv_w"]).reshape(L_ALL, 3, 44, 128).transpose(3, 0, 1, 2)
    cbr = f(inp["conv_b"]).reshape(L_ALL, 44, 128).transpose(2, 0, 1)
    kk = np.arange(128)[:, None]
    qq = np.arange(128)[None, :]
    rbc, rbs = _rope_tables(64)
    rcc, rcs = _rope_tables(32)
    return {
        "w_ada_r": np.ascontiguousarray(f(inp["w_ada"]).reshape(L_ALL, 8, 128, 12, 512).transpose(0, 3, 2, 1, 4)),
        "b_ada": f(inp["b_ada"]),
        "gn": np.ascontiguousarray(np.stack([f(inp["norm1_g"]), f(inp["norm2_g"])], axis=1)),
        "w_in": f(inp["w_in"]), "w_out": f(inp["w_out"]), "w_up_r": w_up_r, "w_down": f(inp["w_down"]),
        "a_wsT": a_wsT, "a_bs": f(inp["a_bs"]), "smallg": np.ascontiguousarray(smallg),
        "b_sink": f(inp["b_sink"]).reshape(1, -1), "c_lam": f(inp["c_lam"]).reshape(1, -1),
        "sublnT": np.ascontiguousarray(f(inp["c_subln"]).T),
        "cw_r": np.ascontiguousarray(cwr), "cb_r": np.ascontiguousarray(cbr),
        "ident": np.eye(128, dtype=np.float32),
        "maskL": (qq <= kk).astype(np.float32), "maskR": (kk <= qq).astype(np.float32),
        "ropeB_C2": rbc, "ropeB_S": rbs, "ropeC_C2": rcc, "ropeC_S": rcs,
    }


def core_inputs(inp, shared, core, nb=NB):
    f = lambda a: np.ascontiguousarray(np.asarray(a, dtype=np.float32))
    b0 = core * nb
    d = dict(shared)
    d["x"] = f(inp["x"][b0:b0 + nb])
    d["ctx"] = f(inp["ctx"][b0:b0 + nb])
    crow = np.zeros((3, D), np.float32)
    crow[0:nb] = np.asarray(inp["c"], np.float32)[b0:b0 + nb]
    crow[2] = np.asarray(inp["c_ctx"], np.float32)
    d["crow"] = crow
    return d


_NC_CACHE = {}


def kernel(**inputs):
    inp = {k: np.asarray(v) for k, v in inputs.items()}
    shared = prep_shared(inp)
    if "nc" not in _NC_CACHE:
        _NC_CACHE["nc"] = build_nc()
    nc = _NC_CACHE["nc"]
    in_maps = [core_inputs(inp, shared, c) for c in range(N_CORES)]
    res = run_bass_kernel_spmd(nc, in_maps, core_ids=list(range(N_CORES)))
    out = np.concatenate([np.asarray(r["y"], dtype=np.float32) for r in res.results], axis=0)
    return out
```
